# Optimizing a Trainium2 kernel written in Bass

```python
import math
import jax
import jax.numpy as jnp
from jax import lax
import numpy as np

D_MODEL = 1024
BATCH = 4
SEQ = 4096
DEPTH = 2
DEC_BATCH = 16
DEC_SEQ = 2048
PAST_LEN = 128

N_EVEN = (DEPTH + 1) // 2
N_ODD = DEPTH // 2
NORM_EPS = 1e-6
ROPE_THETA = 10000.0
CHUNK = 128

SSD_HEADS = 16
SSD_HEAD_DIM = 64
SSD_WIDTH = SSD_HEADS * SSD_HEAD_DIM
SSD_GROUPS = 2
SSD_HPG = SSD_HEADS // SSD_GROUPS
SSD_STATE = 128
SSD_CONV = 5
SSD_CONV_CH = SSD_WIDTH + 2 * SSD_GROUPS * SSD_STATE

RET_HEADS = 8
RET_QK_DIM = 64
RET_V_DIM = 128
RET_QK_WIDTH = RET_HEADS * RET_QK_DIM
RET_V_WIDTH = RET_HEADS * RET_V_DIM

EV_SPLITS = (SSD_WIDTH, SSD_CONV_CH, 2 * SSD_HEADS, RET_QK_WIDTH, RET_QK_WIDTH, RET_V_WIDTH, RET_V_WIDTH)
EV_PROJ = sum(EV_SPLITS)
EV_MIX = SSD_WIDTH + RET_V_WIDTH

ATT_HEADS = 16
ATT_HEAD_DIM = 64
ATT_WIDTH = ATT_HEADS * ATT_HEAD_DIM
DILATED_PATTERNS = ((128, 1), (512, 4), (2048, 16))

GMLP_GROUPS = 8
GMLP_GROUP_DIM = 64
GMLP_WIDTH = GMLP_GROUPS * GMLP_GROUP_DIM
GMLP_CHUNK = 128

OD_SPLITS = (ATT_WIDTH, ATT_WIDTH, ATT_WIDTH, GMLP_WIDTH, GMLP_WIDTH)
OD_PROJ = sum(OD_SPLITS)
OD_MIX = ATT_WIDTH + GMLP_WIDTH

FFN_HIDDEN = 4 * D_MODEL

kernel_name = "hybrid_bidir_ssd_retention_dilated_gmlp_encoder"


def split_cols(t, sizes):
    cuts = [int(c) for c in np.cumsum(sizes)[:-1]]
    return jnp.split(t, cuts, axis=-1)


def flip_seq(t):
    return jnp.flip(t, axis=1)


def rms_norm(x, w):
    xf = x.astype(jnp.float32)
    y = xf * lax.rsqrt(jnp.mean(xf * xf, axis=-1, keepdims=True) + NORM_EPS)
    return (y * w.astype(jnp.float32)).astype(x.dtype)


def rotary(t):
    S, D = t.shape[1], t.shape[-1]
    inv_freq = ROPE_THETA ** (-jnp.arange(0, D, 2, dtype=jnp.float32) / D)
    ang = jnp.arange(S, dtype=jnp.float32)[:, None] * inv_freq[None, :]
    ang = jnp.concatenate([ang, ang], axis=-1)[None, :, None, :]
    tf = t.astype(jnp.float32)
    t1, t2 = jnp.split(tf, 2, axis=-1)
    rot = jnp.concatenate([-t2, t1], axis=-1)
    return (tf * jnp.cos(ang) + rot * jnp.sin(ang)).astype(t.dtype)


def carry_states(states, decay):
    def step(h, inp):
        s_c, d_c = inp
        return d_c * h + s_c, h
    h0 = jnp.zeros_like(states[:, 0])
    _, h_in = lax.scan(step, h0, (jnp.moveaxis(states, 1, 0), jnp.moveaxis(decay, 1, 0)))
    return jnp.moveaxis(h_in, 0, 1)


def ssd_chunked(x, log_a, bm, cm):
    Bsz, L, G, E, P = x.shape
    N = bm.shape[-1]
    nc = L // CHUNK
    xc = x.reshape(Bsz, nc, CHUNK, G, E, P)
    bc = bm.reshape(Bsz, nc, CHUNK, G, N)
    cc = cm.reshape(Bsz, nc, CHUNK, G, N)
    a_cs = jnp.cumsum(log_a.astype(jnp.float32).reshape(Bsz, nc, CHUNK, G, E), axis=2)
    a_cs = jnp.transpose(a_cs, (0, 1, 3, 4, 2))
    lower = jnp.tril(jnp.ones((CHUNK, CHUNK), dtype=bool))
    seg = a_cs[..., :, None] - a_cs[..., None, :]
    decay = jnp.exp(jnp.where(lower, seg, -jnp.inf)).astype(x.dtype)
    cb = jnp.einsum('bclgn,bcsgn->bcgls', cc, bc)
    y_diag = jnp.einsum('bcgels,bcsgep->bclgep', cb[:, :, :, None] * decay, xc)
    to_end = jnp.exp(a_cs[..., -1:] - a_cs).astype(x.dtype)
    states = jnp.einsum('bcsgn,bcges,bcsgep->bcgepn', bc, to_end, xc)
    chunk_decay = jnp.exp(a_cs[..., -1]).astype(x.dtype)[..., None, None]
    h_in = carry_states(states, chunk_decay)
    y_off = jnp.einsum('bclgn,bcgepn,bcgel->bclgep', cc, h_in, jnp.exp(a_cs).astype(x.dtype))
    return (y_diag + y_off).reshape(Bsz, L, G, E, P)


def retention_chunked(q, k, v, log_gamma):
    Bsz, L, H, DK = q.shape
    DV = v.shape[-1]
    nc = L // CHUNK
    qc = q.reshape(Bsz, nc, CHUNK, H, DK)
    kc = k.reshape(Bsz, nc, CHUNK, H, DK)
    vc = v.reshape(Bsz, nc, CHUNK, H, DV)
    pos = jnp.arange(CHUNK, dtype=jnp.float32)
    dist = pos[:, None] - pos[None, :]
    lg = log_gamma[:, None, None]
    dmat = jnp.exp(jnp.where(dist >= 0, lg * dist, -jnp.inf)).astype(q.dtype)
    scores = jnp.einsum('bclhd,bcshd->bchls', qc, kc) * dmat
    y_in = jnp.einsum('bchls,bcshv->bclhv', scores, vc)
    key_decay = jnp.exp(log_gamma[:, None] * (CHUNK - 1.0 - pos)).astype(q.dtype)
    states = jnp.einsum('bcshd,hs,bcshv->bchdv', kc, key_decay, vc)
    chunk_decay = jnp.broadcast_to(jnp.exp(log_gamma * CHUNK).astype(q.dtype)[None, None, :, None, None], (1, nc, H, 1, 1))
    r_in = carry_states(states, chunk_decay)
    q_decay = jnp.exp(log_gamma[:, None] * (pos + 1.0)).astype(q.dtype)
    y_cross = jnp.einsum('bclhd,bchdv,hl->bclhv', qc, r_in, q_decay)
    return (y_in + y_cross).reshape(Bsz, L, H, DV)


def depthwise_conv(x, w, b):
    C = x.shape[-1]
    K = w.shape[0]
    y = lax.conv_general_dilated(x, w[:, None, :].astype(x.dtype), window_strides=(1,), padding=[(K // 2, K // 2)],
                                 dimension_numbers=('NWC', 'WIO', 'NWC'), feature_group_count=C)
    return y + b.astype(x.dtype)


def gated_group_rms(y, z, w):
    g = (y * jax.nn.silu(z)).astype(jnp.float32)
    gs = g.reshape(*g.shape[:-1], SSD_GROUPS, SSD_WIDTH // SSD_GROUPS)
    gs = gs * lax.rsqrt(jnp.mean(gs * gs, axis=-1, keepdims=True) + NORM_EPS)
    return (gs.reshape(g.shape) * w.astype(jnp.float32)).astype(y.dtype)


def head_group_norm(y, w):
    Bsz, S, H, Dv = y.shape
    yf = y.astype(jnp.float32)
    mu = jnp.mean(yf, axis=-1, keepdims=True)
    var = jnp.mean(jnp.square(yf - mu), axis=-1, keepdims=True)
    yn = ((yf - mu) * lax.rsqrt(var + NORM_EPS)).reshape(Bsz, S, H * Dv)
    return (yn * w.astype(jnp.float32)).astype(y.dtype)


def dilated_window_attention(q, k, v, half, dilation):
    Bsz, S, H, D = q.shape
    r = dilation
    L = S // r
    blk = half
    nb = -(-L // blk)
    Lp = nb * blk

    def by_stride(t):
        return jnp.transpose(t.reshape(Bsz, L, r, H, D), (0, 2, 1, 3, 4))

    qs, ks, vs = by_stride(q), by_stride(k), by_stride(v)
    qb = jnp.pad(qs, ((0, 0), (0, 0), (0, Lp - L), (0, 0), (0, 0))).reshape(Bsz, r, nb, blk, H, D)
    pad_kv = ((0, 0), (0, 0), (blk, Lp - L + blk), (0, 0), (0, 0))

    def neighbour_blocks(t):
        tp = jnp.pad(t, pad_kv).reshape(Bsz, r, nb + 2, blk, H, D)
        return jnp.concatenate([tp[:, :, :-2], tp[:, :, 1:-1], tp[:, :, 2:]], axis=3)

    kb, vb = neighbour_blocks(ks), neighbour_blocks(vs)
    qpos = jnp.arange(nb)[:, None] * blk + jnp.arange(blk)[None, :]
    kpos = jnp.arange(nb)[:, None] * blk - blk + jnp.arange(3 * blk)[None, :]
    qp, kp = qpos[:, :, None], kpos[:, None, :]
    valid = ((jnp.abs(kp - qp) <= half) & (kp >= 0) & (kp < L)) | (kp == qp)
    s = jnp.einsum('brnqhd,brnkhd->brnhqk', qb, kb).astype(jnp.float32) * (D ** -0.5)
    s = jnp.where(valid[None, None, :, None], s, -jnp.inf)
    m = jnp.max(s, axis=-1, keepdims=True)
    p = jnp.exp(s - m)
    den = jnp.sum(p, axis=-1)
    o = jnp.einsum('brnhqk,brnkhd->brnqhd', p.astype(v.dtype), vb).astype(jnp.float32)
    o = o / jnp.transpose(den, (0, 1, 2, 4, 3))[..., None]
    lse = jnp.transpose(m[..., 0] + jnp.log(den), (0, 1, 2, 4, 3))
    o = o.reshape(Bsz, r, Lp, H, D)[:, :, :L]
    lse = lse.reshape(Bsz, r, Lp, H)[:, :, :L]
    o = jnp.transpose(o, (0, 2, 1, 3, 4)).reshape(Bsz, S, H, D)
    lse = jnp.transpose(lse, (0, 2, 1, 3)).reshape(Bsz, S, H)
    return o, lse


def chunk_spatial_gate(u, vg, norm_w, w_s, b_s):
    Bsz, S, _ = u.shape
    nc = S // GMLP_CHUNK
    vf = vg.astype(jnp.float32)
    mu = jnp.mean(vf, axis=-1, keepdims=True)
    var = jnp.mean(jnp.square(vf - mu), axis=-1, keepdims=True)
    vn = ((vf - mu) * lax.rsqrt(var + NORM_EPS) * norm_w.astype(jnp.float32)).astype(vg.dtype)
    vc = vn.reshape(Bsz, nc, GMLP_CHUNK, GMLP_GROUPS, GMLP_GROUP_DIM)
    mixed = jnp.einsum('gts,bcsgd->bctgd', w_s.astype(vc.dtype), vc) + jnp.transpose(b_s).astype(vc.dtype)[:, :, None]
    return u * mixed.reshape(Bsz, S, GMLP_WIDTH)


def even_mixer(h, in_proj, conv_w, conv_b, dt_bias, a_log, d_skip, ssd_norm_w, ret_decay, ret_gn_w, out_proj):
    Bsz, S, _ = h.shape
    z, xbc, dt_raw, q, k, v, g = split_cols(h @ in_proj, EV_SPLITS)
    xbc = jax.nn.silu(depthwise_conv(xbc, conv_w, conv_b))
    xs, bm, cm = split_cols(xbc, (SSD_WIDTH, SSD_GROUPS * SSD_STATE, SSD_GROUPS * SSD_STATE))
    xs = xs.reshape(Bsz, S, SSD_GROUPS, SSD_HPG, SSD_HEAD_DIM)
    bm = bm.reshape(Bsz, S, SSD_GROUPS, SSD_STATE)
    cm = cm.reshape(Bsz, S, SSD_GROUPS, SSD_STATE)
    dt = jax.nn.softplus(dt_raw.astype(jnp.float32).reshape(Bsz, S, 2, SSD_HEADS) + dt_bias.astype(jnp.float32))
    a = -jnp.exp(a_log.astype(jnp.float32))
    log_a = (dt * a).reshape(Bsz, S, 2, SSD_GROUPS, SSD_HPG)
    dt = dt.reshape(Bsz, S, 2, SSD_GROUPS, SSD_HPG).astype(xs.dtype)
    y_f = ssd_chunked(xs * dt[:, :, 0, ..., None], log_a[:, :, 0], bm, cm)
    y_b = flip_seq(ssd_chunked(flip_seq(xs * dt[:, :, 1, ..., None]), flip_seq(log_a[:, :, 1]), flip_seq(bm), flip_seq(cm)))
    y = y_f + y_b + xs * d_skip.reshape(SSD_GROUPS, SSD_HPG, 1).astype(xs.dtype)
    y_ssd = gated_group_rms(y.reshape(Bsz, S, SSD_WIDTH), z, ssd_norm_w)
    q = rotary(q.reshape(Bsz, S, RET_HEADS, RET_QK_DIM))
    k = rotary(k.reshape(Bsz, S, RET_HEADS, RET_QK_DIM)) * (RET_QK_DIM ** -0.5)
    v = v.reshape(Bsz, S, RET_HEADS, RET_V_DIM)
    lg = jax.nn.log_sigmoid(ret_decay.astype(jnp.float32))
    r = retention_chunked(q, k, v, lg[0]) + flip_seq(retention_chunked(flip_seq(q), flip_seq(k), flip_seq(v), lg[1]))
    y_ret = head_group_norm(r, ret_gn_w) * jax.nn.silu(g)
    return jnp.concatenate([y_ssd, y_ret], axis=-1) @ out_proj


def odd_mixer(h, in_proj, gmlp_norm_w, gmlp_ws, gmlp_bs, out_proj):
    Bsz, S, _ = h.shape
    q, k, v, u, vg = split_cols(h @ in_proj, OD_SPLITS)
    q = rotary(q.reshape(Bsz, S, ATT_HEADS, ATT_HEAD_DIM))
    k = rotary(k.reshape(Bsz, S, ATT_HEADS, ATT_HEAD_DIM))
    v = v.reshape(Bsz, S, ATT_HEADS, ATT_HEAD_DIM)
    outs, lses = [], []
    for window, dilation in DILATED_PATTERNS:
        o, lse = dilated_window_attention(q, k, v, window // (2 * dilation), dilation)
        outs.append(o)
        lses.append(lse)
    wts = jax.nn.softmax(jnp.stack(lses, axis=0), axis=0)
    att = jnp.einsum('pbsh,pbshd->bshd', wts, jnp.stack(outs, axis=0)).astype(h.dtype).reshape(Bsz, S, ATT_WIDTH)
    sg = chunk_spatial_gate(u, vg, gmlp_norm_w, gmlp_ws, gmlp_bs)
    return jnp.concatenate([att, sg], axis=-1) @ out_proj


def trunk(x, norm_mix_pre, norm_mix_post, norm_ffn_pre, norm_ffn_post, ffn_w1, ffn_w2,
          ev_in_proj, ev_conv_w, ev_conv_b, ssd_dt_bias, ssd_a_log, ssd_d, ssd_norm_w, ret_decay, ret_gn_w, ev_out_proj,
          od_in_proj, gmlp_norm_w, gmlp_ws, gmlp_bs, od_out_proj):
    for i in range(DEPTH):
        h = rms_norm(x, norm_mix_pre[i])
        j = i // 2
        if i % 2 == 0:
            mix = even_mixer(h, ev_in_proj[j], ev_conv_w[j], ev_conv_b[j], ssd_dt_bias[j], ssd_a_log[j], ssd_d[j],
                             ssd_norm_w[j], ret_decay[j], ret_gn_w[j], ev_out_proj[j])
        else:
            mix = odd_mixer(h, od_in_proj[j], gmlp_norm_w[j], gmlp_ws[j], gmlp_bs[j], od_out_proj[j])
        x = x + rms_norm(mix, norm_mix_post[i])
        h = rms_norm(x, norm_ffn_pre[i])
        f = jnp.square(jax.nn.relu(h @ ffn_w1[i])) @ ffn_w2[i]
        x = x + rms_norm(f, norm_ffn_post[i])
    return x


def setup_inputs(seed: int = 0) -> dict:
    key = jax.random.key(seed)
    ks = jax.random.split(key, 23)
    f32 = jnp.float32

    def normal(k, shape, scale):
        return jax.random.normal(k, shape, f32) * scale

    def gain(k, shape):
        return 1.0 + 0.05 * jax.random.normal(k, shape, f32)

    dt0 = jnp.exp(jax.random.uniform(ks[11], (N_EVEN, 2, SSD_HEADS), f32, math.log(1e-3), math.log(1e-1)))
    ssd_dt_bias = dt0 + jnp.log(-jnp.expm1(-dt0))
    ssd_a_log = jnp.log(jax.random.uniform(ks[12], (N_EVEN, 2, SSD_HEADS), f32, 1.0, 16.0))
    ret_base = jnp.log(jnp.exp2(5.0 + jnp.arange(RET_HEADS, dtype=f32)) - 1.0)
    ret_decay = ret_base[None, None, :] + normal(ks[15], (N_EVEN, 2, RET_HEADS), 0.1)
    return {
        "x_prompt": normal(ks[0], (BATCH, SEQ, D_MODEL), 1.0),
        "x_sample": normal(ks[1], (DEC_BATCH, DEC_SEQ, D_MODEL), 1.0),
        "norm_mix_pre": gain(ks[2], (DEPTH, D_MODEL)),
        "norm_mix_post": gain(ks[3], (DEPTH, D_MODEL)),
        "norm_ffn_pre": gain(ks[4], (DEPTH, D_MODEL)),
        "norm_ffn_post": gain(ks[5], (DEPTH, D_MODEL)),
        "ffn_w1": normal(ks[6], (DEPTH, D_MODEL, FFN_HIDDEN), D_MODEL ** -0.5),
        "ffn_w2": normal(ks[7], (DEPTH, FFN_HIDDEN, D_MODEL), FFN_HIDDEN ** -0.5),
        "ev_in_proj": normal(ks[8], (N_EVEN, D_MODEL, EV_PROJ), D_MODEL ** -0.5),
        "ev_conv_w": normal(ks[9], (N_EVEN, SSD_CONV, SSD_CONV_CH), SSD_CONV ** -0.5),
        "ev_conv_b": normal(ks[10], (N_EVEN, SSD_CONV_CH), 0.02),
        "ssd_dt_bias": ssd_dt_bias,
        "ssd_a_log": ssd_a_log,
        "ssd_d": gain(ks[13], (N_EVEN, SSD_HEADS)),
        "ssd_norm_w": gain(ks[14], (N_EVEN, SSD_WIDTH)),
        "ret_decay": ret_decay,
        "ret_gn_w": gain(ks[16], (N_EVEN, RET_V_WIDTH)),
        "ev_out_proj": normal(ks[17], (N_EVEN, EV_MIX, D_MODEL), EV_MIX ** -0.5),
        "od_in_proj": normal(ks[18], (N_ODD, D_MODEL, OD_PROJ), D_MODEL ** -0.5),
        "gmlp_norm_w": gain(ks[19], (N_ODD, GMLP_WIDTH)),
        "gmlp_ws": normal(ks[20], (N_ODD, GMLP_GROUPS, GMLP_CHUNK, GMLP_CHUNK), GMLP_CHUNK ** -0.5),
        "gmlp_bs": 1.0 + normal(ks[21], (N_ODD, GMLP_GROUPS, GMLP_CHUNK), 0.1),
        "od_out_proj": normal(ks[22], (N_ODD, OD_MIX, D_MODEL), OD_MIX ** -0.5),
    }


def reference(x_prompt, x_sample, norm_mix_pre, norm_mix_post, norm_ffn_pre, norm_ffn_post, ffn_w1, ffn_w2,
              ev_in_proj, ev_conv_w, ev_conv_b, ssd_dt_bias, ssd_a_log, ssd_d, ssd_norm_w, ret_decay, ret_gn_w,
              ev_out_proj, od_in_proj, gmlp_norm_w, gmlp_ws, gmlp_bs, od_out_proj):
    weights = (norm_mix_pre, norm_mix_post, norm_ffn_pre, norm_ffn_post, ffn_w1, ffn_w2,
               ev_in_proj, ev_conv_w, ev_conv_b, ssd_dt_bias, ssd_a_log, ssd_d, ssd_norm_w, ret_decay, ret_gn_w,
               ev_out_proj, od_in_proj, gmlp_norm_w, gmlp_ws, gmlp_bs, od_out_proj)
    y_prompt = trunk(x_prompt, *weights)
    y_sample = trunk(x_sample, *weights)
    return (y_prompt, y_sample)
```

```python
import contextlib
import numpy as np
import concourse.bass as bass
import concourse.mybir as mybir
from concourse.bass_utils import run_bass_kernel_spmd

F32 = mybir.dt.float32
BF16 = mybir.dt.bfloat16
ALU = mybir.AluOpType
AF = mybir.ActivationFunctionType
AX = mybir.AxisListType

ENGS = ("pe", "act", "dve", "pool", "sp")
DMA_ENGS = ("sp", "act", "pool")
EPOCH = 30000
NDSEM = 8

NCORES = 8
T = 6144
NCH = 48
SEGCH = 16
D = 1024
EPS = 1e-6
LOOPN = {}
USE_SCHED = True
SCHED_W = 32
SCHED_HOP = 120.0


def crange(name):
    return range(LOOPN.get(name, NCH))


class Tok:
    __slots__ = ("w", "rs")

    def __init__(self):
        self.w = None
        self.rs = []


class Op:
    __slots__ = ("eng", "fn", "deps", "dma", "needed", "seq", "dsem", "dval", "barrier", "snap", "cost", "lat")

    def __init__(self, eng, fn, deps, dma, barrier=False, cost=300.0, lat=0.0):
        self.cost = cost
        self.lat = lat
        self.eng = eng
        self.fn = fn
        self.deps = deps
        self.dma = dma
        self.needed = False
        self.seq = None
        self.dsem = None
        self.dval = None
        self.barrier = barrier
        self.snap = None


class Sched:
    def __init__(self, nc):
        self.nc = nc
        self.ops = []
        self.keep = False
        self.keep_set = set()

    def op(self, eng, fn, reads=(), writes=(), dma=False, cost=300.0, lat=0.0):
        idx = len(self.ops)
        deps = set()
        for t in reads:
            if t.w is not None:
                deps.add(t.w)
        for t in writes:
            if t.w is not None:
                deps.add(t.w)
            for r in t.rs:
                deps.add(r)
        self.ops.append(Op(eng, fn, deps, dma, cost=cost, lat=lat))
        if self.keep:
            self.keep_set.add(idx)
        for t in reads:
            t.rs.append(idx)
        for t in writes:
            t.w = idx
            t.rs = []
        return idx

    def barrier(self):
        for e in ENGS:
            self.ops.append(Op(e, None, set(), False, barrier=True))

    def schedule(self, W=12, hop=120.0):
        ops = self.ops
        n = len(ops)
        fin = [0.0] * n
        done = [False] * n
        order = {e: [] for e in ENGS}
        seg_start = 0
        bounds = []
        i = 0
        while i < n:
            if ops[i].barrier:
                bounds.append((seg_start, i))
                j = i
                while j < n and ops[j].barrier:
                    j += 1
                bounds.append(("barrier", i, j))
                seg_start = j
                i = j
            else:
                i += 1
        bounds.append((seg_start, n))
        for bnd in bounds:
            if bnd[0] == "barrier":
                for k in range(bnd[1], bnd[2]):
                    order[ops[k].eng].append(k)
                    done[k] = True
                continue
            lo, hi = bnd
            if hi <= lo:
                continue
            pend = {e: [] for e in ENGS}
            for k in range(lo, hi):
                pend[ops[k].eng].append(k)
            if lo in self.keep_set:
                for e in ENGS:
                    order[e].extend(pend[e])
                for k in range(lo, hi):
                    done[k] = True
                continue
            pos = {e: 0 for e in ENGS}
            et = {e: 0.0 for e in ENGS}
            remaining = hi - lo
            while remaining:
                best = None
                for e in ENGS:
                    lst = pend[e]
                    cnt = 0
                    p = pos[e]
                    while p < len(lst) and done[lst[p]]:
                        p += 1
                    pos[e] = p
                    q = p
                    while q < len(lst) and cnt < W:
                        k = lst[q]
                        q += 1
                        if done[k]:
                            continue
                        cnt += 1
                        o = ops[k]
                        rdy = 0.0
                        ok = True
                        for d in o.deps:
                            if not done[d]:
                                ok = False
                                break
                            f = fin[d] + (0.0 if ops[d].eng == e and not ops[d].dma else hop)
                            if f > rdy:
                                rdy = f
                        if not ok:
                            continue
                        stt = rdy if rdy > et[e] else et[e]
                        key = (stt, k)
                        if best is None or key < best[0]:
                            best = (key, e, k)
                assert best is not None, "scheduler deadlock"
                (stt, k), e, k = best
                o = ops[k]
                et[e] = stt + o.cost
                fin[k] = stt + o.cost + o.lat
                done[k] = True
                order[e].append(k)
                remaining -= 1
        return order

    def emit(self):
        nc = self.nc
        ops = self.ops
        sched_order = self.schedule(W=SCHED_W, hop=SCHED_HOP) if USE_SCHED else None
        for o in ops:
            for d in o.deps:
                ops[d].needed = True
        cnt = {e: 0 for e in ENGS}
        dcnt = {e: 0 for e in DMA_ENGS}
        if sched_order is not None:
            walk = []
            ptr = {e: 0 for e in ENGS}
            nb = sum(1 for o in ops if o.barrier) // len(ENGS)
            for _ in range(nb + 1):
                for e in ENGS:
                    lst = sched_order[e]
                    p = ptr[e]
                    while p < len(lst) and not ops[lst[p]].barrier:
                        walk.append(lst[p])
                        p += 1
                    ptr[e] = p
                for e in ENGS:
                    lst = sched_order[e]
                    if ptr[e] < len(lst):
                        walk.append(lst[ptr[e]])
                        ptr[e] += 1
            walk_ops = [ops[k] for k in walk]
        else:
            walk_ops = ops
        last = {e: None for e in ENGS}
        for o in walk_ops:
            if o.barrier:
                for e in ENGS:
                    if last[e] is not None:
                        last[e].needed = True
            elif not o.dma:
                last[o.eng] = o
        for o in walk_ops:
            if o.barrier:
                o.snap = (dict(cnt), dict(dcnt))
            elif o.dma:
                i = dcnt[o.eng]
                dcnt[o.eng] += 1
                o.dsem = i % NDSEM
                o.dval = 16 * (i // NDSEM + 1)
                assert o.dval < 60000, "too many DMAs on one queue"
            elif o.needed:
                cnt[o.eng] += 1
                o.seq = cnt[o.eng]
        nep = {e: (cnt[e] + EPOCH - 1) // EPOCH for e in ENGS}
        with contextlib.ExitStack() as st:
            sems = {e: [st.enter_context(nc.semaphore(f"s_{e}_{k}")) for k in range(max(1, nep[e]))]
                    for e in ENGS}
            dsems = {e: [st.enter_context(nc.semaphore(f"d_{e}_{k}")) for k in range(NDSEM)]
                     for e in DMA_ENGS}
            block = st.enter_context(nc.Block())
            if sched_order is not None:
                per_eng = sched_order
            else:
                per_eng = {e: [] for e in ENGS}
                for i, o in enumerate(ops):
                    per_eng[o.eng].append(i)

            def run_engine(ename, engobj):
                seen = {}

                def wait(key, sem, val):
                    if val <= 0 or seen.get(key, 0) >= val:
                        return
                    seen[key] = val
                    engobj.wait_ge(sem, val)

                def wait_all(c_snap, d_snap):
                    for e in ENGS:
                        n = c_snap[e]
                        if n > 0:
                            ep = (n - 1) // EPOCH
                            wait(("c", e, ep), sems[e][ep], n - ep * EPOCH)
                    for e in DMA_ENGS:
                        n = d_snap[e]
                        for k in range(min(n, NDSEM)):
                            lastk = ((n - 1 - k) // NDSEM) * NDSEM + k
                            wait(("d", e, k), dsems[e][k], 16 * (lastk // NDSEM + 1))

                for i in per_eng[ename]:
                    o = ops[i]
                    if o.barrier:
                        wait_all(*o.snap)
                        continue
                    for d in sorted(o.deps):
                        p = ops[d]
                        if ename == "pe" and p.eng == "pe" and not p.dma:
                            continue
                        if p.dma:
                            wait(("d", p.eng, p.dsem), dsems[p.eng][p.dsem], p.dval)
                        else:
                            ep = (p.seq - 1) // EPOCH
                            wait(("c", p.eng, ep), sems[p.eng][ep], p.seq - ep * EPOCH)
                    if o.dma:
                        wait(("d", o.eng, o.dsem), dsems[o.eng][o.dsem], o.dval - 16)
                        ins = o.fn(engobj)
                        ins.then_inc(dsems[o.eng][o.dsem], 16)
                    else:
                        ins = o.fn(engobj)
                        if o.needed:
                            ep = (o.seq - 1) // EPOCH
                            ins.then_inc(sems[o.eng][ep], 1)
                wait_all({e: 0 for e in ENGS}, dcnt)

            @block.tensor
            def _(e):
                run_engine("pe", e)

            @block.scalar
            def _(e):
                run_engine("act", e)

            @block.vector
            def _(e):
                run_engine("dve", e)

            @block.gpsimd
            def _(e):
                run_engine("pool", e)

            @block.sync
            def _(e):
                run_engine("sp", e)
        return {e: len(per_eng[e]) for e in ENGS}


class Tl:
    def __init__(self, h):
        self.h = h
        self.t = Tok()

    def __getitem__(self, k):
        return self.h[k]


class B:
    def __init__(self, nc):
        self.nc = nc
        self.S = Sched(nc)
        self.cap = None

    def _rec(self, eng, fn, R, W, dma=False, cost=300.0, lat=0.0):
        reads = [x.t for x in R]
        writes = [x.t for x in W]
        if self.cap is not None:
            self.cap.append((eng, fn, reads, writes, dma, cost, lat))
        else:
            self.S.op(eng, fn, reads, writes, dma, cost, lat)

    @staticmethod
    def _fd(ap):
        n = 1
        for d in ap.shape[1:]:
            n *= int(d)
        return n

    def _ecost(self, eng, out):
        n = self._fd(out)
        if eng == "act":
            return (200.0 + n) / 1.2
        if eng == "dve":
            return (150.0 + n) / 0.96
        return 150.0 + 2.2 * n

    def capture(self, f, *args):
        old = self.cap
        self.cap = []
        f(*args)
        lst = self.cap
        self.cap = old
        return lst

    def emit_merged(self, la, lb):
        i = j = 0
        while i < len(la) or j < len(lb):
            if j >= len(lb) or (i < len(la) and i * len(lb) <= j * len(la)):
                self.S.op(*la[i])
                i += 1
            else:
                self.S.op(*lb[j])
                j += 1

    def two_stage(self, f1, f2, items):
        prev = None
        for k in items:
            l1 = self.capture(f1, k)
            l2 = self.capture(f2, prev) if prev is not None else []
            self.emit_merged(l2, l1)
            prev = k
        if prev is not None:
            self.emit_merged(self.capture(f2, prev), [])

    def mm(self, out, lhsT, rhs, start, stop, R, W):
        c = max(64, self._fd(rhs)) / 2.4 * (4.0 if lhsT.dtype == F32 else 1.0)
        self._rec("pe", lambda e: e.matmul(out, lhsT=lhsT, rhs=rhs, start=start, stop=stop),
                  R, W, cost=c, lat=60.0)

    def tr(self, out, in_, ident, R, W):
        self._rec("pe", lambda e: e.transpose(out=out, in_=in_, identity=ident),
                  R, W, cost=60.0, lat=60.0)

    def act(self, out, in_, func, R, W, scale=1.0, bias=0.0, accum=None):
        if accum is None:
            self._rec("act", lambda e: e.activation(out=out, in_=in_, func=func, scale=scale, bias=bias),
                      R, W, cost=self._ecost("act", out))
        else:
            self._rec("act", lambda e: e.activation(out=out, in_=in_, func=func, scale=scale, bias=bias,
                                                    accum_out=accum),
                      R, W, cost=self._ecost("act", out) + 80.0)

    def tt(self, eng, out, in0, in1, op, R, W):
        self._rec(eng, lambda e: e.tensor_tensor(out=out, in0=in0, in1=in1, op=op),
                  R, W, cost=self._ecost(eng, out))

    def ts(self, eng, out, in0, s1, s2, op0, op1, R, W):
        if op1 is None:
            self._rec(eng, lambda e: e.tensor_scalar(out=out, in0=in0, scalar1=s1, scalar2=None, op0=op0),
                      R, W, cost=self._ecost(eng, out))
        else:
            self._rec(eng, lambda e: e.tensor_scalar(out=out, in0=in0, scalar1=s1, scalar2=s2, op0=op0, op1=op1),
                      R, W, cost=self._ecost(eng, out))

    def stt(self, out, in0, scalar, in1, op0, op1, R, W):
        self._rec("dve", lambda e: e.scalar_tensor_tensor(out=out, in0=in0, scalar=scalar, in1=in1, op0=op0, op1=op1),
                  R, W, cost=self._ecost("dve", out))

    def cp(self, eng, out, in_, R, W):
        if eng == "act":
            self._rec("act", lambda e: e.copy(out=out, in_=in_), R, W, cost=self._ecost("act", out))
        else:
            self._rec(eng, lambda e: e.tensor_copy(out=out, in_=in_), R, W, cost=self._ecost(eng, out))

    def ms(self, eng, ap, val, W):
        self._rec(eng, lambda e: e.memset(ap, val), [], W, cost=self._ecost(eng, ap))

    def red(self, out, in_, R, W):
        self._rec("dve", lambda e: e.tensor_reduce(out=out, in_=in_, axis=AX.X, op=ALU.add),
                  R, W, cost=self._ecost("dve", in_))

    def ld(self, out, in_, W, eng="sp"):
        nbytes = self._fd(out) * 4 * 128
        self._rec(eng, lambda e: e.dma_start(out=out, in_=in_), [], W, dma=True,
                  cost=(400.0 if eng == "sp" else 700.0), lat=2000.0 + nbytes / 150.0)

    def st(self, out, in_, R, eng="pool"):
        nbytes = self._fd(in_) * 4 * 128
        self._rec(eng, lambda e: e.dma_start(out=out, in_=in_), R, [], dma=True,
                  cost=(400.0 if eng == "sp" else 700.0), lat=2000.0 + nbytes / 150.0)

    def ldw(self, wt, wd, kc_n, ncols):
        for kc in range(kc_n):
            for c0 in range(0, ncols, 2048):
                c1 = min(ncols, c0 + 2048)
                self.ld(wt[:, kc, c0:c1], wd[kc * 128:(kc + 1) * 128, c0:c1], [wt], eng="pool")


def seg_of(c):
    return c // SEGCH


def build_nc(layers=(0, 1), do_mix=True, do_ffn=True):
    nc = bass.Bass("TRN2", target_bir_lowering=False)
    b = B(nc)
    S = b.S

    def din(name, shape):
        return nc.dram_tensor(name, list(shape), F32, kind="ExternalInput").ap()

    def dscr(name, shape, dt=F32):
        return nc.dram_tensor(name, list(shape), dt, kind="Internal").ap()

    xin = din("xin", [T, D])
    yout = nc.dram_tensor("yout", [T, D], F32, kind="ExternalOutput").ap()
    link_d = din("link", [128, 1])
    cos_d = din("rcos", [T, 32])
    sin_d = din("rsin", [T, 32])
    cst_d = din("cst", [128, 9, 128])
    amask_d = din("amask", [128, 17, 128])
    g_mix_pre = din("norm_mix_pre", [2, D])
    g_mix_post = din("norm_mix_post", [2, D])
    g_ffn_pre = din("norm_ffn_pre", [2, D])
    g_ffn_post = din("norm_ffn_post", [2, D])
    w1_d = din("ffn_w1", [2, D, 4096])
    w2_d = din("ffn_w2", [2, 4096, D])
    evin_d = din("ev_in_proj", [1, D, 5664])
    convw_d = din("ev_conv_w", [1, 5, 1536])
    convb_d = din("ev_conv_b", [1, 1536])
    dtb_d = din("ssd_dt_bias", [1, 2, 16])
    alog_d = din("ssd_a_log", [1, 2, 16])
    dsk_d = din("ssd_d", [1, 16])
    snw_d = din("ssd_norm_w", [1, 1024])
    rdec_d = din("ret_decay", [1, 2, 8])
    rgn_d = din("ret_gn_w", [1, 1024])
    evout_d = din("ev_out_proj", [1, 2048, D])
    odin_d = din("od_in_proj", [1, D, 4096])
    gnw_d = din("gmlp_norm_w", [1, 512])
    gws_d = din("gmlp_ws", [1, 8, 128, 128])
    gbs_d = din("gmlp_bs", [1, 8, 128])
    odout_d = din("od_out_proj", [1, 1536, D])

    xa = dscr("xa", [T, D])
    xb = dscr("xb", [T, D])
    sz_d = dscr("sz", [T, 1024])
    sg_d = dscr("sgt", [T, 1024])
    xbc_d = dscr("xbc", [T + 4, 1536])
    dtr_d = dscr("dtr", [T, 32])
    qT0_d = dscr("qT0", [NCH, 128, 4, 128], BF16)
    kT0_d = dscr("kT0", [NCH, 128, 4, 128], BF16)
    k0_d = dscr("k0", [T, 512], BF16)
    v0_d = dscr("v0", [T, 1024], BF16)
    yd_d = dscr("yd", [T, 1024])
    yi_d = dscr("yi", [T, 1024])
    CT_d = dscr("CTd", [NCH, 128, 2, 128], BF16)
    sc_d = dscr("scd", [T, 32])
    Sf_d = dscr("Sfd", [NCH, 128, 1024])
    decf_d = dscr("decf", [NCH, 128, 16])
    Rf_d = dscr("Rfd", [NCH, 128, 512])
    Hb_d = dscr("Hbd", [NCH, 128, 1024], BF16)
    Rb_d = dscr("Rbd", [NCH, 128, 512], BF16)
    xs_d = dscr("xsd", [T, 1024])
    bct_d = dscr("bctd", [NCH, 128, 4, 128], BF16)
    btok_d = dscr("btokd", [T, 256], BF16)
    qT1_d = dscr("qT1", [NCH, 128, 8, 128], BF16)
    kT1_d = dscr("kT1", [NCH, 128, 8, 128], BF16)
    v1_d = dscr("v1", [NCH, 128, 16 * 65], BF16)
    sg1_d = dscr("sg1", [T, 512], BF16)

    def chunk_rows(ap, c):
        return ap[c * 128:(c + 1) * 128, :]

    with contextlib.ExitStack() as gst:
        uid = [0]

        def sb(name, shape, dt, st=gst):
            uid[0] += 1
            return Tl(st.enter_context(nc.sbuf_tensor(f"s{uid[0]}_{name}", list(shape), dt)))

        def psum(name, shape, dt, st=gst):
            uid[0] += 1
            return Tl(st.enter_context(nc.psum_tensor(f"p{uid[0]}_{name}", list(shape), dt)))

        cst = sb("cst", [128, 9, 128], F32)
        b.ld(cst[:], cst_d[:, :, :], [cst])
        idb = sb("idb", [128, 128], BF16)
        b.cp("dve", idb[:], cst[:, 0, :], [cst], [idb])
        Um, Lm, Af, Ab, ONES = (cst[:, 1, :], cst[:, 2, :], cst[:, 3, :], cst[:, 4, :], cst[:, 5, :])
        RDp, RDn = cst[:, 6, :], cst[:, 7, :]
        POS = cst[:, 8, :]
        link = sb("link", [128, 1], F32)
        b.ld(link[:], link_d[:, :], [link])
        def alloc_psum(st, nf32, nbf):
            assert nf32 + nbf <= 8
            return ([psum(f"P{i}", [128, 512], F32, st) for i in range(nf32)],
                    [psum(f"PT{i}", [128, 1024], BF16, st) for i in range(nbf)])

        def rstd_from_ss(ss, n, R):
            b.act(ss[:, 0:1], ss[:, 0:1], AF.Ln, [ss] + R, [ss], scale=1.0 / n, bias=EPS)
            b.act(ss[:, 0:1], ss[:, 0:1], AF.Exp, [ss], [ss], scale=-0.5)

        def norm_to_hT(x_ap, xT, gain, junk, ss, hb, hT_ap, hT, PT):
            b.act(junk[:], x_ap, AF.Square, [xT], [junk, ss], accum=ss[:, 0:1])
            rstd_from_ss(ss, 1024.0, [])
            b.stt(hb[:], x_ap, ss[:, 0:1], gain[:], ALU.mult, ALU.mult, [xT, ss, gain], [hb])
            for k in range(8):
                b.tr(PT[:, k * 128:(k + 1) * 128], hb[:, k * 128:(k + 1) * 128], idb[:], [hb, idb], [PT])
            b.cp("act", hT_ap, PT[:, 0:1024].rearrange("p (k t) -> p k t", k=8), [PT], [hT])

        def post_norm_residual(Pa, Pb, gain, x_ap, xT, junk, ss, tn, xo, out_dram):
            ss2 = ss
            b.act(junk[:, 0:512], Pa[:], AF.Square, [Pa], [junk, ss2], accum=ss2[:, 0:1])
            b.act(junk[:, 512:1024], Pb[:], AF.Square, [Pb], [junk, ss2], accum=ss2[:, 1:2])
            b.tt("dve", ss2[:, 0:1], ss2[:, 0:1], ss2[:, 1:2], ALU.add, [ss2], [ss2])
            b.act(ss2[:, 0:1], ss2[:, 0:1], AF.Ln, [ss2], [ss2], scale=1.0 / 1024, bias=EPS)
            b.act(ss2[:, 0:1], ss2[:, 0:1], AF.Exp, [ss2], [ss2], scale=-0.5)
            b.stt(tn[:, 0:512], Pa[:], ss2[:, 0:1], gain[:, 0:512], ALU.mult, ALU.mult, [Pa, ss2, gain], [tn])
            b.stt(tn[:, 512:1024], Pb[:], ss2[:, 0:1], gain[:, 512:1024], ALU.mult, ALU.mult, [Pb, ss2, gain], [tn])
            b.tt("dve", xo[:], tn[:], x_ap, ALU.add, [tn, xT], [xo])
            b.st(out_dram, xo[:], [xo])

        def rotary(eng, out, src_ap, srcT, cs, H, tmp):
            s3 = src_ap.rearrange("p (h d) -> p h d", h=H)
            t1, t2 = s3[:, :, 0:32], s3[:, :, 32:64]
            cb = cs[:, 0:32].unsqueeze(1).to_broadcast([128, H, 32])
            sn = cs[:, 32:64].unsqueeze(1).to_broadcast([128, H, 32])
            ta = tmp[:, 0:H * 32].rearrange("p (h d) -> p h d", h=H)
            tb = tmp[:, H * 32:H * 64].rearrange("p (h d) -> p h d", h=H)
            b.tt(eng, ta, t1, cb, ALU.mult, [srcT, cs], [tmp])
            b.tt(eng, tb, t2, sn, ALU.mult, [srcT, cs], [tmp])
            b.tt(eng, out[:, :, 0:32], ta, tb, ALU.subtract, [tmp], [out])
            b.tt(eng, ta, t2, cb, ALU.mult, [srcT, cs, out], [tmp])
            b.tt(eng, tb, t1, sn, ALU.mult, [srcT, cs, out], [tmp])
            b.tt(eng, out[:, :, 32:64], ta, tb, ALU.add, [tmp], [out])

        def ffn_loop(li, src, dst):
            with contextlib.ExitStack() as st:
                W1 = sb("W1", [128, 8, 4096], BF16, st)
                W2 = sb("W2", [128, 32, 1024], BF16, st)
                b.ldw(W1, w1_d[li], 8, 4096)
                b.ldw(W2, w2_d[li], 32, 1024)
                gpre = sb("gpre", [128, 1024], F32, st)
                gpost = sb("gpost", [128, 1024], F32, st)
                b.ld(gpre[:], g_ffn_pre[li].partition_broadcast(128), [gpre])
                b.ld(gpost[:], g_ffn_post[li].partition_broadcast(128), [gpost])
                P, PTs = alloc_psum(st, 6, 1)
                xm = [sb(f"xm{i}", [128, 2, 1024], F32, st) for i in range(2)]
                junk = sb("junk", [128, 1024], BF16, st)
                junk2 = sb("junk2", [128, 1024], BF16, st)
                ss = sb("ss", [128, 2], F32, st)
                ss2 = sb("ss2", [128, 2], F32, st)
                hb = sb("hb", [128, 1024], BF16, st)
                hT = [sb(f"hT{i}", [128, 8, 256], BF16, st) for i in range(2)]
                uT = sb("uT", [128, 32, 256], BF16, st)
                rl = [sb(f"rl{i}", [128, 256], F32, st) for i in range(2)]
                tn = sb("tn", [128, 1024], F32, st)
                xo = [sb(f"xo{i}", [128, 1024], F32, st) for i in range(2)]

                def ffn_n(mt):
                    xmt = xm[mt % 2]
                    b.ld(xmt[:], src[mt * 256:(mt + 1) * 256, :].rearrange("(j p) d -> p j d", p=128), [xmt])
                    for j in range(2):
                        norm_to_hT(xmt[:, j, :], xmt, gpre, junk, ss, hb, hT[mt % 2][:, :, j * 128:(j + 1) * 128],
                                   hT[mt % 2], PTs[0])

                def ffn_r(mt):
                    xmt = xm[mt % 2]
                    hTc = hT[mt % 2]
                    for fc in range(32):
                        pu = P[fc % 2]
                        for k in range(8):
                            b.mm(pu[:, 0:256], W1[:, k, fc * 128:(fc + 1) * 128], hTc[:, k, :], k == 0, k == 7,
                                 [W1, hTc], [pu])
                        r = rl[fc % 2]
                        b.act(r[:], pu[:, 0:256], AF.Relu, [pu], [r])
                        b.tt("dve", uT[:, fc, :], r[:], r[:], ALU.mult, [r], [uT])
                    for j in range(2):
                        Pa, Pb = P[2 + 2 * j], P[3 + 2 * j]
                        for nt, Pq in ((0, Pa), (1, Pb)):
                            for fc in range(32):
                                b.mm(Pq[:], uT[:, fc, j * 128:(j + 1) * 128], W2[:, fc, nt * 512:(nt + 1) * 512],
                                     fc == 0, fc == 31, [uT, W2], [Pq])
                        c = mt * 2 + j
                        post_norm_residual(Pa, Pb, gpost, xmt[:, j, :], xmt, junk2, ss2, tn, xo[j], chunk_rows(dst, c))

                b.two_stage(ffn_n, ffn_r, range(NCH // 2))
            S.barrier()

        def odd_layer(src, dst):
            with contextlib.ExitStack() as st:
                P, PTs = alloc_psum(st, 6, 2)
                PT = PTs[0]
                Wi = sb("Wi1", [128, 8, 4096], BF16, st)
                b.ldw(Wi, odin_d[0], 8, 4096)
                gpre = sb("gpre", [128, 1024], F32, st)
                b.ld(gpre[:], g_mix_pre[1].partition_broadcast(128), [gpre])
                gnw = sb("gnw", [128, 512], F32, st)
                b.ld(gnw[:], gnw_d[0].partition_broadcast(128), [gnw])
                wsf = sb("wsf", [128, 8, 128], F32, st)
                b.ld(wsf[:], gws_d[0].rearrange("g t s -> t g s"), [wsf])
                wsb = sb("wsb", [128, 8, 128], BF16, st)
                b.cp("dve", wsb[:], wsf[:], [wsf], [wsb])
                wsT = sb("wsT", [128, 8, 128], BF16, st)
                for g in range(8):
                    b.tr(PT[:, g * 128:(g + 1) * 128], wsb[:, g, :], idb[:], [wsb, idb], [PT])
                b.cp("act", wsT[:], PT[:, 0:1024].rearrange("p (g t) -> p g t", g=8), [PT], [wsT])
                bsf = sb("bsf", [8, 128], F32, st)
                b.ld(bsf[:], gbs_d[0], [bsf])
                bsT = sb("bsT", [128, 8], F32, st)
                b.tr(P[0][:, 0:8], bsf[:], cst[0:8, 0, 0:8], [bsf, cst], [P[0]])
                b.cp("act", bsT[:], P[0][:, 0:8], [P[0]], [bsT])

                xt = [sb(f"xt{i}", [128, 1024], F32, st) for i in range(2)]
                cs = [sb(f"cs{i}", [128, 64], F32, st) for i in range(2)]
                junk = sb("junk", [128, 1024], BF16, st)
                junk2 = sb("junk2", [128, 512], BF16, st)
                ss = sb("ss", [128, 2], F32, st)
                hb = sb("hb", [128, 1024], BF16, st)
                hT = [sb(f"hT{i}", [128, 8, 128], BF16, st) for i in range(2)]
                qsq = sb("qsq", [128, 1024], F32, st)
                qsk = sb("qsk", [128, 1024], F32, st)
                rtq = sb("rtq", [128, 1024], F32, st)
                rtk = sb("rtk", [128, 1024], F32, st)
                qrq = sb("qrq", [128, 16, 64], BF16, st)
                qrk = sb("qrk", [128, 16, 64], BF16, st)
                qTs = [sb(f"qTs{i}", [128, 8, 128], BF16, st) for i in range(2)]
                kTs = [sb(f"kTs{i}", [128, 8, 128], BF16, st) for i in range(2)]
                v1s = [sb(f"v1s{i}", [128, 16, 65], BF16, st) for i in range(2)]
                for i in range(2):
                    b.ms("pool", v1s[i][:], 1.0, [v1s[i]])
                us = sb("us", [128, 512], F32, st)
                vgs = sb("vgs", [128, 512], F32, st)
                st4 = sb("st4", [128, 4], F32, st)
                vn = sb("vn", [128, 512], BF16, st)
                sgs = [sb(f"sgs{i}", [128, 512], BF16, st) for i in range(2)]

                def a1_n(c):
                    i2 = c % 2
                    b.ld(xt[i2][:], chunk_rows(src, c), [xt[i2]])
                    b.ld(cs[i2][:, 0:32], chunk_rows(cos_d, c), [cs[i2]])
                    b.ld(cs[i2][:, 32:64], chunk_rows(sin_d, c), [cs[i2]])
                    norm_to_hT(xt[i2][:], xt[i2], gpre, junk, ss, hb, hT[i2][:], hT[i2], PTs[1])

                def a1_r(c):
                    i2 = c % 2
                    hTc = hT[i2]
                    PT = PTs[0]

                    def proj(nt, Pq):
                        for k in range(8):
                            b.mm(Pq[:], hTc[:, k, :], Wi[:, k, nt * 512:(nt + 1) * 512], k == 0, k == 7, [hTc, Wi], [Pq])
                    for nt in range(6):
                        proj(nt, P[nt])
                    b.cp("act", qsq[:, 0:512], P[0][:], [P[0]], [qsq])
                    b.cp("act", qsq[:, 512:1024], P[1][:], [P[1]], [qsq])
                    b.cp("act", qsk[:, 0:512], P[2][:], [P[2]], [qsk])
                    b.cp("act", qsk[:, 512:1024], P[3][:], [P[3]], [qsk])
                    proj(6, P[0])
                    proj(7, P[1])
                    vt = v1s[i2]
                    b.cp("dve", vt[:, 0:8, 0:64], P[4][:].rearrange("p (h d) -> p h d", h=8), [P[4]], [vt])
                    b.cp("dve", vt[:, 8:16, 0:64], P[5][:].rearrange("p (h d) -> p h d", h=8), [P[5]], [vt])
                    b.st(v1_d[c], vt[:].rearrange("p h d -> p (h d)"), [vt])
                    rotary("dve", qrq, qsq[:], qsq, cs[i2], 16, rtq)
                    rotary("pool", qrk, qsk[:], qsk, cs[i2], 16, rtk)
                    b.cp("act", us[:], P[0][:], [P[0]], [us])
                    b.act(vgs[:], P[1][:], AF.Copy, [P[1]], [vgs, st4], accum=st4[:, 0:1])
                    b.act(junk2[:, 0:512], P[1][:], AF.Square, [P[1]], [junk2, st4], accum=st4[:, 1:2])
                    for qr_, dstT, dram in ((qrq, qTs[i2], qT1_d), (qrk, kTs[i2], kT1_d)):
                        qf = qr_[:].rearrange("p h d -> p (h d)")
                        for k in range(8):
                            b.tr(PT[:, k * 128:(k + 1) * 128], qf[:, k * 128:(k + 1) * 128], idb[:], [qr_, idb], [PT])
                        b.cp("act", dstT[:], PT[:, 0:1024].rearrange("p (k t) -> p k t", k=8), [PT], [dstT])
                        b.st(dram[c], dstT[:], [dstT])
                    b.ts("dve", st4[:, 0:2], st4[:, 0:2], 1.0 / 512, None, ALU.mult, None, [st4], [st4])
                    b.tt("dve", st4[:, 2:3], st4[:, 0:1], st4[:, 0:1], ALU.mult, [st4], [st4])
                    b.tt("dve", st4[:, 1:2], st4[:, 1:2], st4[:, 2:3], ALU.subtract, [st4], [st4])
                    b.act(st4[:, 1:2], st4[:, 1:2], AF.Ln, [st4], [st4], scale=1.0, bias=EPS)
                    b.act(st4[:, 1:2], st4[:, 1:2], AF.Exp, [st4], [st4], scale=-0.5)
                    b.ts("dve", vgs[:], vgs[:], st4[:, 0:1], st4[:, 1:2], ALU.subtract, ALU.mult, [vgs, st4], [vgs])
                    b.tt("dve", vn[:], vgs[:], gnw[:], ALU.mult, [vgs, gnw], [vn])
                    for g in range(8):
                        b.mm(P[2][:, g * 64:(g + 1) * 64], wsT[:, g, :], vn[:, g * 64:(g + 1) * 64], True, True,
                             [wsT, vn], [P[2]])
                    b.tt("dve", vgs[:].rearrange("p (g d) -> p g d", g=8), P[2][:].rearrange("p (g d) -> p g d", g=8),
                         bsT[:].unsqueeze(2).to_broadcast([128, 8, 64]), ALU.add, [P[2], bsT, vn], [vgs])
                    b.tt("dve", sgs[i2][:], vgs[:], us[:], ALU.mult, [vgs, us], [sgs[i2]])
                    b.st(chunk_rows(sg1_d, c), sgs[i2][:], [sgs[i2]])

                b.two_stage(a1_n, a1_r, crange("A1"))
            S.barrier()
            with contextlib.ExitStack() as st:
                NS = 18
                S.keep = True
                P, PTs = alloc_psum(st, 7, 1)
                PT = PTs[0]
                Wo = sb("Wo1", [128, 12, 1024], BF16, st)
                b.ldw(Wo, odout_d[0], 12, 1024)
                gpost = sb("gpost", [128, 1024], F32, st)
                b.ld(gpost[:], g_mix_post[1].partition_broadcast(128), [gpost])
                amf = sb("amf", [128, 17, 128], F32, st)
                b.ld(amf[:], amask_d[:, :, :], [amf])
                am = sb("am", [128, 17, 128], BF16, st)
                amL = sb("amL", [128, 17, 128], BF16, st)
                b.cp("dve", am[:], amf[:], [amf], [am])
                b.ts("dve", amL[:], amf[:], link[:, 0:1], None, ALU.mult, None, [amf, link], [amL])
                kr = [sb(f"kr{i}", [128, 8, 128], BF16, st) for i in range(NS)]
                vr = [sb(f"vr{i}", [128, 16 * 65], BF16, st) for i in range(NS)]
                qze = [sb(f"qze{i}", [128, 8, 128], BF16, st) for i in range(2)]
                qzo = [sb(f"qzo{i}", [128, 8, 128], BF16, st) for i in range(2)]
                for i in range(2):
                    b.ms("dve", qze[i][:], 0.0, [qze[i]])
                    b.ms("dve", qzo[i][:], 0.0, [qzo[i]])
                pe_sb = [sb(f"pe{i}", [128, 512], BF16, st) for i in range(3)]
                pm_sb = [sb(f"pm{i}", [128, 512], BF16, st) for i in range(3)]
                rden = sb("rden", [128, 4], F32, st)
                mix = [sb(f"mix{i}", [128, 1536], BF16, st) for i in range(2)]
                mixT = sb("mixT", [128, 12, 128], BF16, st)
                xt = [sb(f"xt{i}", [128, 1024], F32, st) for i in range(2)]
                junk = sb("junk", [128, 1024], BF16, st)
                ss = sb("ss", [128, 2], F32, st)
                tn = sb("tn", [128, 1024], F32, st)
                xo = [sb(f"xo{i}", [128, 1024], F32, st) for i in range(2)]
                loaded = set()
                gi = 0
                pend = []
                LAG = 2
                for c in crange("AT"):
                    i2 = c % 2
                    sgc = seg_of(c)
                    lo, hi = (0, 2 * SEGCH) if sgc < 2 else (2 * SEGCH, NCH)
                    blocks = [j for j in range(c - 8, c + 9) if lo <= j < hi]
                    for j in blocks:
                        if j not in loaded:
                            loaded.add(j)
                            b.ld(kr[j % NS][:], kT1_d[j], [kr[j % NS]])
                            b.ld(vr[j % NS][:], v1_d[j], [vr[j % NS]])
                    b.ld(qze[i2][0:64], qT1_d[c][0:64], [qze[i2]])
                    b.ld(qzo[i2][64:128], qT1_d[c][64:128], [qzo[i2]])
                    b.ld(mix[i2][:, 1024:1536], chunk_rows(sg1_d, c), [mix[i2]])
                    b.ld(xt[i2][:], chunk_rows(src, c), [xt[i2]])
                    groups = []
                    cur = []
                    for j in blocks:
                        cross = (seg_of(j) != sgc)
                        if cur and (len(cur) == 4 or cur[0][1] != cross):
                            groups.append(cur)
                            cur = []
                        cur.append((j, cross))
                    if cur:
                        groups.append(cur)
                    for h in range(16):
                        hp, base = h // 2, (h % 2) * 64
                        po = P[4 + (h // 4) % 2]
                        pcol = (h % 4) * 65
                        nb = len(blocks)
                        bi = 0
                        for gx, grp in enumerate(groups):
                            ps = P[gi % 3]
                            pe_t = pe_sb[gi % 3]
                            pm_t = pm_sb[gi % 3]
                            gi += 1
                            n = len(grp)
                            for i, (j, cross) in enumerate(grp):
                                qz = (qze if h % 2 == 0 else qzo)[i2]
                                b.mm(ps[:, i * 128:(i + 1) * 128], kr[j % NS][:, hp, :],
                                     qz[:, hp, :], True, True, [kr[j % NS], qz], [ps])
                            b.act(pe_t[:, 0:n * 128], ps[:, 0:n * 128], AF.Exp, [ps], [pe_t], scale=0.125)
                            o0 = grp[0][0] - c + 8
                            msk = amL if grp[0][1] else am
                            b.tt("dve", pm_t[:, 0:n * 128], pe_t[:, 0:n * 128],
                                 msk[:, o0:o0 + n, :].rearrange("p a b -> p (a b)"), ALU.mult, [pe_t, msk], [pm_t])

                            def pv(grp=grp, pm_t=pm_t, po=po, pcol=pcol, bi=bi, nb=nb, h=h, i2=i2, c=c,
                                   last=(gx == len(groups) - 1)):
                                for i, (j, cross) in enumerate(grp):
                                    b.mm(po[:, pcol:pcol + 65], pm_t[:, i * 128:(i + 1) * 128],
                                         vr[j % NS][:, h * 65:(h + 1) * 65], bi + i == 0, bi + i == nb - 1,
                                         [pm_t, vr[j % NS]], [po])
                                if last and h % 4 == 3:
                                    po3 = po[:, 0:260].rearrange("p (h d) -> p h d", h=4)
                                    b.S.op("dve", lambda e, o=rden[:].unsqueeze(2), i_=po3[:, :, 64:65]: e.reciprocal(out=o, in_=i_),
                                           [po.t], [rden.t])
                                    h0 = h - 3
                                    b.tt("dve", mix[i2][:, h0 * 64:(h0 + 4) * 64].rearrange("p (h d) -> p h d", h=4),
                                         po3[:, :, 0:64], rden[:].unsqueeze(2).to_broadcast([128, 4, 64]), ALU.mult,
                                         [po, rden], [mix[i2]])
                                if last and h == 15:
                                    for half in range(2):
                                        for k in range(6):
                                            kk = half * 6 + k
                                            b.tr(PT[:, k * 128:(k + 1) * 128], mix[i2][:, kk * 128:(kk + 1) * 128], idb[:],
                                                 [mix[i2], idb], [PT])
                                        b.cp("act", mixT[:, half * 6:(half + 1) * 6, :],
                                             PT[:, 0:768].rearrange("p (k t) -> p k t", k=6), [PT], [mixT])
                                    for nt, Pq in ((0, P[3]), (1, P[6])):
                                        for k in range(12):
                                            b.mm(Pq[:], mixT[:, k, :], Wo[:, k, nt * 512:(nt + 1) * 512], k == 0, k == 11,
                                                 [mixT, Wo], [Pq])
                                    post_norm_residual(P[3], P[6], gpost, xt[i2][:], xt[i2], junk, ss, tn, xo[i2],
                                                       chunk_rows(dst, c))
                            bi += n
                            pend.append(pv)
                            while len(pend) > LAG:
                                pend.pop(0)()
                while pend:
                    pend.pop(0)()
            S.keep = False
            S.barrier()

        def even_layer(src, dst):
            with contextlib.ExitStack() as st:
                P, PTs = alloc_psum(st, 6, 2)
                Wi = sb("Wi0", [128, 8, 5664], BF16, st)
                b.ldw(Wi, evin_d[0], 8, 5664)
                gpre = sb("gpre", [128, 1024], F32, st)
                b.ld(gpre[:], g_mix_pre[0].partition_broadcast(128), [gpre])
                xt = [sb(f"xt{i}", [128, 1024], F32, st) for i in range(2)]
                cs = [sb(f"cs{i}", [128, 64], F32, st) for i in range(2)]
                junk = sb("junk", [128, 1024], BF16, st)
                ss = sb("ss", [128, 2], F32, st)
                hb = sb("hb", [128, 1024], BF16, st)
                hT = [sb(f"hT{i}", [128, 8, 128], BF16, st) for i in range(2)]
                zo = [sb(f"zo{i}", [128, 1024], F32, st) for i in range(2)]
                go = [sb(f"go{i}", [128, 1024], F32, st) for i in range(2)]
                xpre = [sb(f"xpre{i}", [128, 12, 132], BF16, st) for i in range(3)]
                xsT = sb("xsT", [128, 8, 128], F32, st)
                bcT = [sb(f"bcT{i}", [128, 4, 128], BF16, st) for i in range(2)]
                xso = [sb(f"xso{i}", [128, 1024], F32, st) for i in range(2)]
                btk = [sb(f"btk{i}", [128, 256], BF16, st) for i in range(2)]
                cw = sb("cw", [8, 1536], F32, st)
                b.ms("dve", cw[:], 0.0, [cw])
                b.ld(cw[0:5, :], convw_d[0], [cw])
                cbr = sb("cbr", [12, 128], F32, st)
                b.ld(cbr[:], convb_d[0].rearrange("(a p) -> a p", p=128), [cbr])
                wT = sb("wT", [128, 12, 8], F32, st)
                cbT = sb("cbT", [128, 12], F32, st)
                for cb in range(12):
                    b.tr(P[0][:, cb * 8:(cb + 1) * 8], cw[:, cb * 128:(cb + 1) * 128], cst[0:8, 0, 0:8], [cw, cst], [P[0]])
                b.cp("act", wT[:], P[0][:, 0:96].rearrange("p (a j) -> p a j", a=12), [P[0]], [wT])
                b.tr(P[1][:, 0:12], cbr[:], cst[0:12, 0, 0:12], [cbr, cst], [P[1]])
                b.cp("act", cbT[:], P[1][:, 0:12], [P[1]], [cbT])
                Wd = sb("Wd", [128, 12, 5, 128], BF16, st)
                for cb in range(12):
                    for j in range(5):
                        b.ts("dve", Wd[:, cb, j, :], cst[:, 0, :], wT[:, cb, j:j + 1], None, ALU.mult, None, [cst, wT], [Wd])
                b.ms("dve", xpre[0][:, :, 0:2], 0.0, [xpre[0]])
                dto = [sb(f"dto{i}", [128, 32], F32, st) for i in range(2)]
                qs = sb("qs", [128, 512], F32, st)
                rtmp = sb("rtmp", [128, 512], F32, st)
                qr = sb("qr", [128, 8, 64], BF16, st)
                kro = [sb(f"kro{i}", [128, 8, 64], BF16, st) for i in range(2)]
                qTs = [sb(f"qTs{i}", [128, 4, 128], BF16, st) for i in range(2)]
                kTs = [sb(f"kTs{i}", [128, 4, 128], BF16, st) for i in range(2)]
                vo = [sb(f"vo{i}", [128, 1024], BF16, st) for i in range(2)]
                qs2 = sb("qs2", [128, 512], F32, st)
                rtmp2 = sb("rtmp2", [128, 512], F32, st)

                def a0_n(c):
                    i2 = c % 2
                    b.ld(xt[i2][:], chunk_rows(src, c), [xt[i2]])
                    b.ld(cs[i2][:, 0:32], chunk_rows(cos_d, c), [cs[i2]])
                    b.ld(cs[i2][:, 32:64], chunk_rows(sin_d, c), [cs[i2]])
                    norm_to_hT(xt[i2][:], xt[i2], gpre, junk, ss, hb, hT[i2][:], hT[i2], PTs[1])

                def a0_r(c):
                    i2 = c % 2
                    hTc = hT[i2]
                    PT = PTs[0]
                    pi = [0]

                    def proj(c0, ncol):
                        Pq = P[pi[0] % 6]
                        pi[0] += 1
                        for k in range(8):
                            b.mm(Pq[:, 0:ncol], hTc[:, k, :], Wi[:, k, c0:c0 + ncol], k == 0, k == 7, [hTc, Wi], [Pq])
                        return Pq
                    Pq = proj(2592, 512)
                    b.cp("act", qs[:], Pq[:], [Pq], [qs])
                    Pq = proj(3104, 512)
                    b.act(qs2[:], Pq[:], AF.Copy, [Pq], [qs2], scale=0.125)
                    rotary("dve", qr, qs[:], qs, cs[i2], 8, rtmp)
                    rotary("pool", kro[i2], qs2[:], qs2, cs[i2], 8, rtmp2)
                    for nt in range(2):
                        Pq = proj(nt * 512, 512)
                        b.act(zo[i2][:, nt * 512:(nt + 1) * 512], Pq[:], AF.Silu, [Pq], [zo[i2]])
                    b.st(chunk_rows(sz_d, c), zo[i2][:], [zo[i2]])
                    xp = xpre[c % 3]
                    for q4 in range(3):
                        Pq = P[pi[0] % 6]
                        pi[0] += 1
                        for cbi in range(4):
                            cb = q4 * 4 + cbi
                            for k in range(8):
                                b.mm(Pq[:, cbi * 128:(cbi + 1) * 128], Wi[:, k, 1024 + cb * 128:1024 + (cb + 1) * 128],
                                     hTc[:, k, :], k == 0, k == 7, [hTc, Wi], [Pq])
                        b.cp("dve", xp[:, q4 * 4:(q4 + 1) * 4, 2:130], Pq[:].rearrange("p (a t) -> p a t", a=4), [Pq], [xp])
                    if c > 0:
                        xl = xpre[(c - 1) % 3]
                        if c % SEGCH != 0:
                            b.cp("dve", xl[:, :, 130:132], xp[:, :, 2:4], [xp], [xl])
                        elif c == SEGCH:
                            b.ts("dve", xl[:, :, 130:132], xp[:, :, 2:4], link[:, 0:1], None, ALU.mult, None, [xp, link], [xl])
                        else:
                            b.ms("dve", xl[:, :, 130:132], 0.0, [xl])
                    if c + 1 < NCH:
                        xn = xpre[(c + 1) % 3]
                        if (c + 1) % SEGCH != 0:
                            b.cp("dve", xn[:, :, 0:2], xp[:, :, 128:130], [xp], [xn])
                        elif c + 1 == SEGCH:
                            b.ts("dve", xn[:, :, 0:2], xp[:, :, 128:130], link[:, 0:1], None, ALU.mult, None, [xp, link], [xn])
                        else:
                            b.ms("dve", xn[:, :, 0:2], 0.0, [xn])
                    else:
                        b.ms("dve", xp[:, :, 130:132], 0.0, [xp])
                    Pq = proj(2560, 32)
                    b.cp("dve", dto[i2][:], Pq[:, 0:32], [Pq], [dto[i2]])
                    b.st(chunk_rows(dtr_d, c), dto[i2][:], [dto[i2]])
                    for nt in range(2):
                        Pq = proj(3616 + nt * 512, 512)
                        b.cp("act" if nt == 0 else "dve", vo[i2][:, nt * 512:(nt + 1) * 512], Pq[:], [Pq], [vo[i2]])
                    b.st(chunk_rows(v0_d, c), vo[i2][:], [vo[i2]])
                    for nt in range(2):
                        Pq = proj(4640 + nt * 512, 512)
                        b.act(go[i2][:, nt * 512:(nt + 1) * 512], Pq[:], AF.Silu, [Pq], [go[i2]])
                    b.st(chunk_rows(sg_d, c), go[i2][:], [go[i2]])
                    qf = qr[:].rearrange("p h d -> p (h d)")
                    for k in range(4):
                        b.tr(PT[:, k * 128:(k + 1) * 128], qf[:, k * 128:(k + 1) * 128], idb[:], [qr, idb], [PT])
                    b.cp("act", qTs[i2][:], PT[:, 0:512].rearrange("p (k t) -> p k t", k=4), [PT], [qTs[i2]])
                    b.st(qT0_d[c], qTs[i2][:], [qTs[i2]])
                    kf = kro[i2][:].rearrange("p h d -> p (h d)")
                    b.st(chunk_rows(k0_d, c), kf, [kro[i2]])
                    for k in range(4):
                        b.tr(PT[:, 512 + k * 128:512 + (k + 1) * 128], kf[:, k * 128:(k + 1) * 128], idb[:],
                             [kro[i2], idb], [PT])
                    b.cp("act", kTs[i2][:], PT[:, 512:1024].rearrange("p (k t) -> p k t", k=4), [PT], [kTs[i2]])
                    b.st(kT0_d[c], kTs[i2][:], [kTs[i2]])
                    return pi

                def a0_conv(c, pi):
                    i2 = c % 2
                    xp = xpre[c % 3]
                    PT = PTs[0]
                    for q4 in range(3):
                        Pq = P[pi[0] % 6]
                        pi[0] += 1
                        for cbi in range(4):
                            cb = q4 * 4 + cbi
                            for j in range(5):
                                b.mm(Pq[:, cbi * 128:(cbi + 1) * 128], Wd[:, cb, j, :], xp[:, cb, j:j + 128], j == 0, j == 4,
                                     [Wd, xp], [Pq])
                        for cbi in range(4):
                            cb = q4 * 4 + cbi
                            if cb < 8:
                                b.act(xsT[:, cb, :], Pq[:, cbi * 128:(cbi + 1) * 128], AF.Silu, [Pq, cbT], [xsT],
                                      bias=cbT[:, cb:cb + 1])
                            else:
                                b.act(bcT[i2][:, cb - 8, :], Pq[:, cbi * 128:(cbi + 1) * 128], AF.Silu, [Pq, cbT], [bcT[i2]],
                                      bias=cbT[:, cb:cb + 1])
                    b.st(bct_d[c], bcT[i2][:], [bcT[i2]])
                    for half in range(2):
                        Pq = P[pi[0] % 6]
                        pi[0] += 1
                        for e4 in range(4):
                            cb = half * 4 + e4
                            b.tr(Pq[:, e4 * 128:(e4 + 1) * 128], xsT[:, cb, :], cst[:, 0, :], [xsT, cst], [Pq])
                        b.cp("act", xso[i2][:, half * 512:(half + 1) * 512], Pq[:], [Pq], [xso[i2]])
                    b.st(chunk_rows(xs_d, c), xso[i2][:], [xso[i2]])
                    for g in range(2):
                        b.tr(PT[:, 512 + g * 128:512 + (g + 1) * 128], bcT[i2][:, g, :], idb[:], [bcT[i2], idb], [PT])
                    b.cp("act", btk[i2][:], PT[:, 512:768], [PT], [btk[i2]])
                    b.st(chunk_rows(btok_d, c), btk[i2][:], [btk[i2]])

                def a0_r2(c):
                    pi = a0_r(c)
                    if c > 0:
                        a0_conv(c - 1, pi)
                    if c == len(crange("A")) - 1:
                        a0_conv(c, pi)

                b.two_stage(a0_n, a0_r2, crange("A"))
            S.barrier()
            with contextlib.ExitStack() as st:
                P, PTs = alloc_psum(st, 7, 1)
                PT = PTs[0]
                prm = sb("prm", [128, 96], F32, st)
                b.ld(prm[:, 0:32], dtb_d[0].rearrange("a b -> (a b)").partition_broadcast(128), [prm])
                b.ld(prm[:, 32:64], alog_d[0].rearrange("a b -> (a b)").partition_broadcast(128), [prm])
                b.ld(prm[:, 64:80], dsk_d[0].partition_broadcast(128), [prm])
                b.ld(prm[:, 80:96], rdec_d[0].rearrange("a b -> (a b)").partition_broadcast(128), [prm])
                b.act(prm[:, 32:64], prm[:, 32:64], AF.Exp, [prm], [prm])
                b.ts("dve", prm[:, 32:64], prm[:, 32:64], -1.0, None, ALU.mult, None, [prm], [prm])
                b.act(prm[:, 80:96], prm[:, 80:96], AF.Exp, [prm], [prm], scale=-1.0)
                b.act(prm[:, 80:96], prm[:, 80:96], AF.Ln, [prm], [prm], bias=1.0)
                b.ts("dve", prm[:, 80:96], prm[:, 80:96], -1.0, None, ALU.mult, None, [prm], [prm])
                dtbias, avec, dskip, lg = prm[:, 0:32], prm[:, 32:64], prm[:, 64:80], prm[:, 80:96]
                Dfb = sb("Dfb", [128, 8, 128], F32, st)
                Dtmp = sb("Dtmp", [128, 8, 128], F32, st)
                b.tt("dve", Dfb[:], RDp.unsqueeze(1).to_broadcast([128, 8, 128]),
                     prm[:, 80:88].unsqueeze(2).to_broadcast([128, 8, 128]), ALU.mult, [cst, prm], [Dfb])
                b.act(Dfb[:], Dfb[:], AF.Exp, [Dfb], [Dfb])
                b.tt("dve", Dfb[:], Dfb[:], Um.unsqueeze(1).to_broadcast([128, 8, 128]), ALU.mult, [Dfb, cst], [Dfb])
                b.tt("dve", Dtmp[:], RDn.unsqueeze(1).to_broadcast([128, 8, 128]),
                     prm[:, 88:96].unsqueeze(2).to_broadcast([128, 8, 128]), ALU.mult, [cst, prm], [Dtmp])
                b.act(Dtmp[:], Dtmp[:], AF.Exp, [Dtmp], [Dtmp])
                b.tt("dve", Dtmp[:], Dtmp[:], Lm.unsqueeze(1).to_broadcast([128, 8, 128]), ALU.mult, [Dtmp, cst], [Dtmp])
                b.tt("dve", Dfb[:], Dfb[:], Dtmp[:], ALU.add, [Dfb, Dtmp], [Dfb])
                rc = sb("rc", [128, 32], F32, st)
                b.ts("dve", rc[:, 0:8], prm[:, 80:88], POS[:, 0:1], None, ALU.mult, None, [prm, cst], [rc])
                b.ts("dve", rc[:, 8:16], prm[:, 88:96], POS[:, 1:2], None, ALU.mult, None, [prm, cst], [rc])
                b.ts("dve", rc[:, 16:32], prm[:, 80:96], 128.0, None, ALU.mult, None, [prm], [rc])
                b.act(rc[:], rc[:], AF.Exp, [rc], [rc])

                xbcs = sb("xbcs", [128, 1024], F32, st)
                dtr = sb("dtr", [128, 32], F32, st)
                dts = sb("dts", [128, 32], F32, st)
                dt2 = sb("dt2", [128, 32], F32, st)
                la = sb("la", [128, 32], F32, st)
                csb = sb("csb", [128, 64], F32, st)
                scs = [sb(f"scs{i}", [128, 32], F32, st) for i in range(2)]
                wte = sb("wte", [128, 32], F32, st)
                dec = [sb(f"dec{i}", [128, 32], F32, st) for i in range(2)]
                xdt = sb("xdt", [128, 2, 1024], BF16, st)
                xw = [sb(f"xw{i}", [128, 2, 1024], BF16, st) for i in range(2)]
                bcb = [sb(f"bcb{i}", [128, 256], BF16, st) for i in range(2)]
                BCT = [sb(f"BCT{i}", [128, 4, 128], BF16, st) for i in range(2)]
                cbm = sb("cbm", [128, 4, 128], F32, st)
                rhsq = [sb(f"rhsq{i}", [128, 4, 128], F32, st) for i in range(8)]
                DT = [sb(f"DT{i}", [128, 512], BF16, st) for i in range(2)]
                M = sb("M", [128, 2, 16, 128], BF16, st)
                ydt = sb("ydt", [128, 1024], F32, st)
                ydo = [sb(f"ydo{i}", [128, 1024], F32, st) for i in range(2)]
                Sfo = [sb(f"Sfo{i}", [128, 1024], F32, st) for i in range(2)]
                hbs = sb("hbs", [128, 1024], F32, st)
                hbo = [sb(f"hbo{i}", [128, 1024], BF16, st) for i in range(2)]
                qTl = sb("qTl", [128, 4, 128], BF16, st)
                kTl = sb("kTl", [128, 4, 128], BF16, st)
                kl = sb("kl", [128, 512], BF16, st)
                vl = sb("vl", [128, 1024], BF16, st)
                SM = sb("SM", [128, 8, 128], BF16, st)
                kd = sb("kd", [128, 2, 512], BF16, st)
                yio = [sb(f"yio{i}", [128, 1024], F32, st) for i in range(2)]
                Rfo = [sb(f"Rfo{i}", [128, 512], F32, st) for i in range(2)]
                rbs = sb("rbs", [128, 512], F32, st)
                rbo = [sb(f"rbo{i}", [128, 512], BF16, st) for i in range(2)]
                b.ms("dve", hbs[:], 0.0, [hbs])
                b.ms("dve", rbs[:], 0.0, [rbs])

                def b1_h1(c):
                    i2 = c % 2
                    first_of_seg = (c % SEGCH == 0)
                    last_of_seg = (c % SEGCH == SEGCH - 1)
                    b.ld(xbcs[:], chunk_rows(xs_d, c), [xbcs])
                    xs3 = xbcs[:, 0:1024].rearrange("p (e d) -> p e d", e=16)
                    b.ld(dtr[:], chunk_rows(dtr_d, c), [dtr])
                    b.tt("dve", dtr[:], dtr[:], dtbias, ALU.add, [dtr, prm], [dtr])
                    b.ts("dve", dt2[:], dtr[:], -1.0, None, ALU.mult, None, [dtr], [dt2])
                    b.tt("dve", dt2[:], dt2[:], dtr[:], ALU.max, [dtr, dt2], [dt2])
                    b.act(dt2[:], dt2[:], AF.Exp, [dt2], [dt2], scale=-1.0)
                    b.act(dt2[:], dt2[:], AF.Ln, [dt2], [dt2], bias=1.0)
                    b.ts("dve", dts[:], dtr[:], 0.0, None, ALU.max, None, [dtr], [dts])
                    b.tt("dve", dts[:], dts[:], dt2[:], ALU.add, [dts, dt2], [dts])
                    b.tt("dve", la[:], dts[:], avec, ALU.mult, [dts, prm], [la])
                    pc = P[6]
                    b.mm(pc[:, 0:16], Um, la[:, 0:16], True, True, [cst, la], [pc])
                    b.mm(pc[:, 16:32], Lm, la[:, 16:32], True, True, [cst, la], [pc])
                    b.mm(pc[:, 32:64], ONES, la[:, 0:32], True, True, [cst, la], [pc])
                    b.cp("act", csb[:], pc[:, 0:64], [pc], [csb])
                    sc = scs[i2]
                    b.act(sc[:], csb[:, 0:32], AF.Exp, [csb], [sc])
                    b.st(chunk_rows(sc_d, c), sc[:], [sc])
                    b.act(dec[i2][:], csb[:, 32:64], AF.Exp, [csb], [dec[i2]])
                    b.tt("dve", wte[:], csb[:, 32:64], csb[:, 0:32], ALU.subtract, [csb], [wte])
                    b.act(wte[:], wte[:], AF.Exp, [wte], [wte])
                    b.tt("dve", wte[:], wte[:], dts[:], ALU.mult, [wte, dts], [wte])
                    for d in range(2):
                        b.tt("dve", xdt[:, d, :].rearrange("p (e d) -> p e d", e=16), xs3,
                             dts[:, d * 16:(d + 1) * 16].unsqueeze(2).to_broadcast([128, 16, 64]), ALU.mult,
                             [xbcs, dts], [xdt])
                        b.tt("dve", xw[i2][:, d, :].rearrange("p (e d) -> p e d", e=16), xs3,
                             wte[:, d * 16:(d + 1) * 16].unsqueeze(2).to_broadcast([128, 16, 64]), ALU.mult,
                             [xbcs, wte], [xw[i2]])
                    bct = BCT[i2]
                    b.ld(bct[:], bct_d[c], [bct])
                    for g in range(2):
                        b.mm(pc[:, 128 + g * 128:128 + (g + 1) * 128], bct[:, g, :], bct[:, 2 + g, :], True, True,
                             [bct], [pc])
                    pcb = pc[:, 128:384].rearrange("p (g l) -> p g l", g=2)
                    b.tt("dve", cbm[:, 0:2, :], pcb, Um.unsqueeze(1).to_broadcast([128, 2, 128]), ALU.mult,
                         [pc, cst], [cbm])
                    b.tt("dve", cbm[:, 2:4, :], pcb, Lm.unsqueeze(1).to_broadcast([128, 2, 128]), ALU.mult,
                         [pc, cst], [cbm])
                    u = 0
                    for d in range(2):
                        tri = Um if d == 0 else Lm
                        Amat = Af if d == 0 else Ab
                        for q4 in range(4):
                            rq = rhsq[d * 4 + q4]
                            for e4 in range(4):
                                col = d * 16 + q4 * 4 + e4
                                b.act(rq[:, e4, :], tri, AF.Copy, [cst, la], [rq], scale=la[:, col:col + 1])
                        for q4 in range(4):
                            Pq = P[u % 2]
                            dtt = DT[u % 2]
                            u += 1
                            rq = rhsq[d * 4 + q4]
                            b.mm(Pq[:], Amat, rq[:].rearrange("p a b -> p (a b)"), True, True,
                                 [cst, rq], [Pq])
                            b.act(dtt[:], Pq[:], AF.Exp, [Pq], [dtt])
                            g = q4 // 2
                            b.tt("dve", M[:, d, q4 * 4:(q4 + 1) * 4, :], dtt[:].rearrange("p (a b) -> p a b", a=4),
                                 cbm[:, 2 * d + g, :].unsqueeze(1).to_broadcast([128, 4, 128]), ALU.mult,
                                 [dtt, cbm], [M])
                    for e in range(16):
                        Pq = P[2 + e // 8]
                        sl = slice((e % 8) * 64, (e % 8) * 64 + 64)
                        b.mm(Pq[:, sl], M[:, 0, e, :], xdt[:, 0, e * 64:(e + 1) * 64], True, False, [M, xdt], [Pq])
                        b.mm(Pq[:, sl], M[:, 1, e, :], xdt[:, 1, e * 64:(e + 1) * 64], False, True, [M, xdt], [Pq])
                    b.tt("dve", ydt[:].rearrange("p (e d) -> p e d", e=16), xs3,
                         dskip.unsqueeze(2).to_broadcast([128, 16, 64]), ALU.mult, [xbcs, prm], [ydt])
                    b.tt("dve", ydo[i2][:, 0:512], P[2][:], ydt[:, 0:512], ALU.add, [P[2], ydt], [ydo[i2]])
                    b.tt("dve", ydo[i2][:, 512:1024], P[3][:], ydt[:, 512:1024], ALU.add, [P[3], ydt], [ydo[i2]])
                    b.st(chunk_rows(yd_d, c), ydo[i2][:], [ydo[i2]])

                def b1_h2(c):
                    i2 = c % 2
                    first_of_seg = (c % SEGCH == 0)
                    last_of_seg = (c % SEGCH == SEGCH - 1)
                    bt = bcb[i2]
                    b.ld(bt[:], chunk_rows(btok_d, c), [bt])
                    for g in range(2):
                        b.mm(P[4 + g][:], bt[:, g * 128:(g + 1) * 128], xw[i2][:, 0, g * 512:(g + 1) * 512], True, True,
                             [bcb[i2], xw[i2]], [P[4 + g]])
                    for g in range(2):
                        b.cp("act", Sfo[i2][:, g * 512:(g + 1) * 512], P[4 + g][:], [P[4 + g]], [Sfo[i2]])
                    b.st(Sf_d[c], Sfo[i2][:], [Sfo[i2]])
                    b.st(decf_d[c], dec[i2][:, 0:16], [dec[i2]])
                    if last_of_seg:
                        if c == SEGCH - 1:
                            b.ts("dve", hbs[:], hbs[:], link[:, 0:1], None, ALU.mult, None, [hbs, link], [hbs])
                            b.ts("dve", rbs[:], rbs[:], link[:, 0:1], None, ALU.mult, None, [rbs, link], [rbs])
                        else:
                            b.ms("dve", hbs[:], 0.0, [hbs])
                            b.ms("dve", rbs[:], 0.0, [rbs])
                    b.cp("act", hbo[i2][:], hbs[:], [hbs], [hbo[i2]])
                    b.st(Hb_d[c], hbo[i2][:], [hbo[i2]])
                    for g in range(2):
                        b.mm(P[4 + g][:], bt[:, g * 128:(g + 1) * 128], xw[i2][:, 1, g * 512:(g + 1) * 512], True, True,
                             [bcb[i2], xw[i2]], [P[4 + g]])
                    b.tt("dve", hbs[:].rearrange("p (e d) -> p e d", e=16), hbs[:].rearrange("p (e d) -> p e d", e=16),
                         dec[i2][:, 16:32].unsqueeze(2).to_broadcast([128, 16, 64]), ALU.mult, [hbs, dec[i2]], [hbs])
                    for g in range(2):
                        b.tt("dve", hbs[:, g * 512:(g + 1) * 512], hbs[:, g * 512:(g + 1) * 512], P[4 + g][:], ALU.add,
                             [hbs, P[4 + g]], [hbs])
                    b.ld(qTl[:], qT0_d[c], [qTl])
                    b.ld(kTl[:], kT0_d[c], [kTl])
                    b.ld(kl[:], chunk_rows(k0_d, c), [kl])
                    b.ld(vl[:], chunk_rows(v0_d, c), [vl])
                    for par in range(2):
                        for hp in range(4):
                            base = par * 64
                            Pq = P[4 + par]
                            b.mm(Pq[:, hp * 128:(hp + 1) * 128], kTl[base:base + 64, hp, :], qTl[base:base + 64, hp, :],
                                 True, True, [kTl, qTl], [Pq])
                    for par in range(2):
                        b.tt("dve", SM[:, par:8:2, :], P[4 + par][:].rearrange("p (a b) -> p a b", a=4),
                             Dfb[:, par:8:2, :], ALU.mult, [P[4 + par], Dfb], [SM])
                    for h in range(8):
                        Pq = P[4 + h // 4]
                        b.mm(Pq[:, (h % 4) * 128:(h % 4 + 1) * 128], SM[:, h, :], vl[:, h * 128:(h + 1) * 128], True, True,
                             [SM, vl], [Pq])
                    for hh in range(2):
                        b.cp("act", yio[i2][:, hh * 512:(hh + 1) * 512], P[4 + hh][:], [P[4 + hh]], [yio[i2]])
                    b.st(chunk_rows(yi_d, c), yio[i2][:], [yio[i2]])
                    for d in range(2):
                        b.tt("dve", kd[:, d, :].rearrange("p (h d) -> p h d", h=8), kl[:].rearrange("p (h d) -> p h d", h=8),
                             rc[:, d * 8:(d + 1) * 8].unsqueeze(2).to_broadcast([128, 8, 64]), ALU.mult, [kl, rc], [kd])
                    def rstates(d, Pq):
                        for h in range(8):
                            hp, base = h // 2, (h % 2) * 64
                            b.mm(Pq[base:base + 64, hp * 128:(hp + 1) * 128], kd[:, d, h * 64:(h + 1) * 64],
                                 vl[:, h * 128:(h + 1) * 128], True, True, [kd, vl], [Pq])
                    rstates(0, P[4])
                    b.cp("act", Rfo[i2][:], P[4][:], [P[4]], [Rfo[i2]])
                    b.st(Rf_d[c], Rfo[i2][:], [Rfo[i2]])
                    b.cp("act", rbo[i2][:], rbs[:], [rbs], [rbo[i2]])
                    b.st(Rb_d[c], rbo[i2][:], [rbo[i2]])
                    rstates(1, P[5])
                    for par in range(2):
                        sl = slice(par * 64, par * 64 + 64)
                        gcol = rc[sl, 24 + par:32:2]
                        b.tt("dve", rbs[sl, :].rearrange("p (a v) -> p a v", a=4), rbs[sl, :].rearrange("p (a v) -> p a v", a=4),
                             gcol.unsqueeze(2).to_broadcast([64, 4, 128]), ALU.mult, [rbs, rc], [rbs])
                    b.tt("dve", rbs[:], rbs[:], P[5][:], ALU.add, [rbs, P[5]], [rbs])

                b.two_stage(b1_h1, b1_h2, (range(NCH - 1, -1, -1) if LOOPN.get("B1", NCH) == NCH else range(LOOPN["B1"] - 1, -1, -1)))
            S.barrier()
            with contextlib.ExitStack() as st:
                P, PTs = alloc_psum(st, 7, 1)
                PT = PTs[0]
                Wo = sb("Wo0", [128, 16, 1024], BF16, st)
                b.ldw(Wo, evout_d[0], 16, 1024)
                gpost = sb("gpost", [128, 1024], F32, st)
                b.ld(gpost[:], g_mix_post[0].partition_broadcast(128), [gpost])
                snw = sb("snw", [128, 1024], F32, st)
                b.ld(snw[:], snw_d[0].partition_broadcast(128), [snw])
                rgn = sb("rgn", [128, 1024], F32, st)
                b.ld(rgn[:], rgn_d[0].partition_broadcast(128), [rgn])
                prm = sb("prm2", [128, 16], F32, st)
                b.ld(prm[:, 0:16], rdec_d[0].rearrange("a b -> (a b)").partition_broadcast(128), [prm])
                b.act(prm[:], prm[:], AF.Exp, [prm], [prm], scale=-1.0)
                b.act(prm[:], prm[:], AF.Ln, [prm], [prm], bias=1.0)
                b.ts("dve", prm[:], prm[:], -1.0, None, ALU.mult, None, [prm], [prm])
                rc = sb("rc2", [128, 32], F32, st)
                b.ts("dve", rc[:, 0:8], prm[:, 0:8], POS[:, 2:3], None, ALU.mult, None, [prm, cst], [rc])
                b.ts("dve", rc[:, 8:16], prm[:, 8:16], POS[:, 3:4], None, ALU.mult, None, [prm, cst], [rc])
                b.ts("dve", rc[:, 16:32], prm[:, 0:16], 128.0, None, ALU.mult, None, [prm], [rc])
                b.act(rc[:], rc[:], AF.Exp, [rc], [rc])

                ydl = [sb(f"ydl{i}", [128, 1024], F32, st) for i in range(2)]
                yil = [sb(f"yil{i}", [128, 1024], F32, st) for i in range(2)]
                CTl = [sb(f"CTl{i}", [128, 2, 128], BF16, st) for i in range(2)]
                scl = [sb(f"scl{i}", [128, 32], F32, st) for i in range(2)]
                qTl = [sb(f"qTl{i}", [128, 4, 128], BF16, st) for i in range(2)]
                Hbl = [sb(f"Hbl{i}", [128, 1024], BF16, st) for i in range(2)]
                Rbl = [sb(f"Rbl{i}", [128, 512], BF16, st) for i in range(2)]
                Sfl = [sb(f"Sfl{i}", [128, 1024], F32, st) for i in range(2)]
                dfl = [sb(f"dfl{i}", [128, 16], F32, st) for i in range(2)]
                Rfl = [sb(f"Rfl{i}", [128, 512], F32, st) for i in range(2)]
                szl = [sb(f"szl{i}", [128, 1024], F32, st) for i in range(2)]
                sgl = [sb(f"sgl{i}", [128, 1024], F32, st) for i in range(2)]
                xt = [sb(f"xt{i}", [128, 1024], F32, st) for i in range(2)]
                hfs = sb("hfs", [128, 1024], F32, st)
                hfb = sb("hfb", [128, 1024], BF16, st)
                rfs = sb("rfs", [128, 512], F32, st)
                rfb = sb("rfb", [128, 512], BF16, st)
                t1 = sb("t1", [128, 1024], F32, st)
                ys_ = [sb(f"ys{i}", [128, 1024], F32, st) for i in range(2)]
                yr_ = [sb(f"yr{i}", [128, 1024], F32, st) for i in range(2)]
                st8 = sb("st8", [128, 24], F32, st)
                mix = sb("mix", [128, 2048], BF16, st)
                mixT = sb("mixT", [128, 16, 128], BF16, st)
                junk = sb("junk", [128, 1024], BF16, st)
                jf = sb("jf", [128, 1024], F32, st)
                ss = sb("ss", [128, 2], F32, st)
                tn = sb("tn", [128, 1024], F32, st)
                xo = [sb(f"xo{i}", [128, 1024], F32, st) for i in range(2)]
                b.ms("dve", hfs[:], 0.0, [hfs])
                b.ms("dve", rfs[:], 0.0, [rfs])
                def b2_g1(c):
                    i2 = c % 2
                    ys = ys_[i2]
                    yr = yr_[i2]
                    b.ld(ydl[i2][:], chunk_rows(yd_d, c), [ydl[i2]])
                    b.ld(yil[i2][:], chunk_rows(yi_d, c), [yil[i2]])
                    b.ld(CTl[i2][:], bct_d[c][:, 2:4, :], [CTl[i2]])
                    b.ld(scl[i2][:], chunk_rows(sc_d, c), [scl[i2]])
                    b.ld(qTl[i2][:], qT0_d[c], [qTl[i2]])
                    b.ld(Hbl[i2][:], Hb_d[c], [Hbl[i2]])
                    b.ld(Rbl[i2][:], Rb_d[c], [Rbl[i2]])
                    b.ld(Sfl[i2][:], Sf_d[c], [Sfl[i2]])
                    b.ld(dfl[i2][:], decf_d[c], [dfl[i2]])
                    b.ld(Rfl[i2][:], Rf_d[c], [Rfl[i2]])
                    b.ld(szl[i2][:], chunk_rows(sz_d, c), [szl[i2]])
                    b.ld(sgl[i2][:], chunk_rows(sg_d, c), [sgl[i2]])
                    b.ld(xt[i2][:], chunk_rows(src, c), [xt[i2]])
                    if c % SEGCH == 0 and c > 0:
                        if c == SEGCH:
                            b.ts("dve", hfs[:], hfs[:], link[:, 0:1], None, ALU.mult, None, [hfs, link], [hfs])
                            b.ts("dve", rfs[:], rfs[:], link[:, 0:1], None, ALU.mult, None, [rfs, link], [rfs])
                        else:
                            b.ms("dve", hfs[:], 0.0, [hfs])
                            b.ms("dve", rfs[:], 0.0, [rfs])
                    b.cp("act", hfb[:], hfs[:], [hfs], [hfb])
                    b.cp("act", rfb[:], rfs[:], [rfs], [rfb])
                    for g in range(2):
                        b.mm(P[g][:], CTl[i2][:, g, :], hfb[:, g * 512:(g + 1) * 512], True, True, [CTl[i2], hfb], [P[g]])
                        b.mm(P[2 + g][:], CTl[i2][:, g, :], Hbl[i2][:, g * 512:(g + 1) * 512], True, True,
                             [CTl[i2], Hbl[i2]], [P[2 + g]])
                    for g in range(2):
                        for d in range(2):
                            Pq = P[2 * d + g]
                            b.tt("dve", t1[:, g * 512:(g + 1) * 512].rearrange("p (e d) -> p e d", e=8),
                                 Pq[:].rearrange("p (e d) -> p e d", e=8),
                                 scl[i2][:, d * 16 + g * 8:d * 16 + (g + 1) * 8].unsqueeze(2).to_broadcast([128, 8, 64]),
                                 ALU.mult, [Pq, scl[i2]], [t1])
                            src_y = ydl[i2] if d == 0 else ys
                            b.tt("dve", ys[:, g * 512:(g + 1) * 512], t1[:, g * 512:(g + 1) * 512],
                                 src_y[:, g * 512:(g + 1) * 512], ALU.add, [t1, src_y], [ys])
                    b.tt("dve", hfs[:].rearrange("p (e d) -> p e d", e=16), hfs[:].rearrange("p (e d) -> p e d", e=16),
                         dfl[i2][:].unsqueeze(2).to_broadcast([128, 16, 64]), ALU.mult, [hfs, dfl[i2], hfb], [hfs])
                    b.tt("dve", hfs[:], hfs[:], Sfl[i2][:], ALU.add, [hfs, Sfl[i2]], [hfs])
                    for d, (Pa, Pb2), rsrc in ((0, (P[4], P[0]), rfb), (1, (P[1], P[2]), Rbl[i2])):
                        t1v = t1[:].rearrange("p (h v) -> p h v", h=8)
                        for par, Pq in ((0, Pa), (1, Pb2)):
                            base = par * 64
                            for hp in range(4):
                                b.mm(Pq[:, hp * 128:(hp + 1) * 128], qTl[i2][base:base + 64, hp, :],
                                     rsrc[base:base + 64, hp * 128:(hp + 1) * 128], True, True, [qTl[i2], rsrc], [Pq])
                            b.tt("dve", t1v[:, par:8:2, :], Pq[:].rearrange("p (a v) -> p a v", a=4),
                                 rc[:, d * 8 + par:d * 8 + 8:2].unsqueeze(2).to_broadcast([128, 4, 128]),
                                 ALU.mult, [Pq, rc], [t1])
                        src_y = yil[i2] if d == 0 else yr
                        b.tt("dve", yr[:], t1[:], src_y[:], ALU.add, [t1, src_y], [yr])
                    for par in range(2):
                        sl = slice(par * 64, par * 64 + 64)
                        gcol = rc[sl, 16 + par:24:2]
                        b.tt("dve", rfs[sl, :].rearrange("p (a v) -> p a v", a=4), rfs[sl, :].rearrange("p (a v) -> p a v", a=4),
                             gcol.unsqueeze(2).to_broadcast([64, 4, 128]), ALU.mult, [rfs, rc, rfb], [rfs])
                    b.tt("dve", rfs[:], rfs[:], Rfl[i2][:], ALU.add, [rfs, Rfl[i2]], [rfs])

                def b2_g2(c):
                    i2 = c % 2
                    ys = ys_[i2]
                    yr = yr_[i2]
                    b.tt("dve", ys[:], ys[:], szl[i2][:], ALU.mult, [ys, szl[i2]], [ys])
                    for g in range(2):
                        b.act(junk[:, g * 512:(g + 1) * 512], ys[:, g * 512:(g + 1) * 512], AF.Square, [ys], [junk, st8],
                              accum=st8[:, g:g + 1])
                    b.act(st8[:, 0:2], st8[:, 0:2], AF.Ln, [st8], [st8], scale=1.0 / 512, bias=EPS)
                    b.act(st8[:, 0:2], st8[:, 0:2], AF.Exp, [st8], [st8], scale=-0.5)
                    for g in range(2):
                        b.stt(mix[:, g * 512:(g + 1) * 512], ys[:, g * 512:(g + 1) * 512], st8[:, g:g + 1],
                              snw[:, g * 512:(g + 1) * 512], ALU.mult, ALU.mult, [ys, st8, snw], [mix])
                    yr3 = yr[:].rearrange("p (h v) -> p h v", h=8)
                    b.red(st8[:, 8:16], yr3, [yr], [st8])
                    b.act(jf[:], yr[:], AF.Square, [yr], [jf])
                    b.red(st8[:, 16:24], jf[:].rearrange("p (h v) -> p h v", h=8), [jf], [st8])
                    b.ts("dve", st8[:, 8:24], st8[:, 8:24], 1.0 / 128, None, ALU.mult, None, [st8], [st8])
                    b.tt("dve", jf[:, 0:8], st8[:, 8:16], st8[:, 8:16], ALU.mult, [st8], [jf])
                    b.tt("dve", st8[:, 16:24], st8[:, 16:24], jf[:, 0:8], ALU.subtract, [st8, jf], [st8])
                    b.act(st8[:, 16:24], st8[:, 16:24], AF.Ln, [st8], [st8], bias=EPS)
                    b.act(st8[:, 16:24], st8[:, 16:24], AF.Exp, [st8], [st8], scale=-0.5)
                    b.tt("dve", yr3, yr3, st8[:, 8:16].unsqueeze(2).to_broadcast([128, 8, 128]), ALU.subtract,
                         [yr, st8], [yr])
                    b.tt("dve", yr3, yr3, st8[:, 16:24].unsqueeze(2).to_broadcast([128, 8, 128]), ALU.mult,
                         [yr, st8], [yr])
                    b.tt("dve", yr[:], yr[:], rgn[:], ALU.mult, [yr, rgn], [yr])
                    b.tt("dve", mix[:, 1024:2048], yr[:], sgl[i2][:], ALU.mult, [yr, sgl[i2]], [mix])
                    for half in range(2):
                        for k in range(8):
                            kk = half * 8 + k
                            b.tr(PT[:, k * 128:(k + 1) * 128], mix[:, kk * 128:(kk + 1) * 128], idb[:], [mix, idb], [PT])
                        b.cp("act", mixT[:, half * 8:(half + 1) * 8, :],
                             PT[:, 0:1024].rearrange("p (k t) -> p k t", k=8), [PT], [mixT])
                    for nt, Pq in ((0, P[5]), (1, P[6])):
                        for k in range(16):
                            b.mm(Pq[:], mixT[:, k, :], Wo[:, k, nt * 512:(nt + 1) * 512], k == 0, k == 15, [mixT, Wo], [Pq])
                    post_norm_residual(P[5], P[6], gpost, xt[i2][:], xt[i2], junk, ss, tn, xo[i2], chunk_rows(dst, c))

                b.two_stage(b2_g1, b2_g2, crange("B2"))
            S.barrier()

        hm = sb("hm", [128, 8], F32)
        b.ld(hm[:], cst_d[:, 8, 8:16], [hm])
        b.ts("dve", hm[:, 0:2], hm[:, 4:6], -1.0, 1.0, ALU.mult, ALU.add, [hm], [hm])
        b.ts("dve", hm[:, 0:2], hm[:, 0:2], link[:, 0:1], None, ALU.mult, None, [hm, link], [hm])
        b.tt("dve", hm[:, 6:8], hm[:, 0:2], hm[:, 4:6], ALU.add, [hm], [hm])

        stages = []
        if 0 in layers:
            if do_mix:
                stages.append(("mix0",))
            if do_ffn:
                stages.append(("ffn0",))
        if 1 in layers:
            if do_mix:
                stages.append(("mix1",))
            if do_ffn:
                stages.append(("ffn1",))
        bufs = [xa, xb]
        cur = xin
        S.barrier()
        for si, (sname,) in enumerate(stages):
            dst = yout if si == len(stages) - 1 else bufs[si % 2]
            if sname == "mix0":
                even_layer(cur, dst)
            elif sname == "mix1":
                odd_layer(cur, dst)
            elif sname == "ffn0":
                ffn_loop(0, cur, dst)
            elif sname == "ffn1":
                ffn_loop(1, cur, dst)
            cur = dst
        counts = S.emit()
    return nc, counts


def _consts():
    p = np.arange(128)
    r, l = p[:, None], p[None, :]
    cst = np.zeros((128, 9, 128), np.float32)
    cst[:, 0] = (r == l)
    cst[:, 1] = (r <= l)
    cst[:, 2] = (r >= l)
    cst[:, 3] = (r > l)
    cst[:, 4] = (r < l)
    cst[:, 5] = 1.0
    cst[:, 6] = np.maximum(l - r, 0)
    cst[:, 7] = np.maximum(r - l, 0)
    cst[:, 8, 0] = 127 - p
    cst[:, 8, 1] = p
    cst[:, 8, 2] = p + 1
    cst[:, 8, 3] = 128 - p
    cst[:, 8, 12] = (p != 127)
    cst[:, 8, 13] = (p < 126)
    am = np.zeros((128, 17, 128), np.float32)
    for o in range(17):
        j = o - 8
        d = np.abs(l - r - 128 * j)
        am[:, o, :] = (d <= 64).astype(np.float32) + ((d % 4 == 0) & (d <= 256)) + ((d % 16 == 0) & (d <= 1024))
    return cst, am


def _rope(pos):
    inv_freq = (10000.0 ** (-np.arange(0, 64, 2, dtype=np.float32) / np.float32(64))).astype(np.float32)
    ang = pos.astype(np.float32)[:, None] * inv_freq[None, :]
    return np.cos(ang).astype(np.float32), np.sin(ang).astype(np.float32)


WEIGHT_NAMES = ["norm_mix_pre", "norm_mix_post", "norm_ffn_pre", "norm_ffn_post", "ffn_w1", "ffn_w2",
                "ev_in_proj", "ev_conv_w", "ev_conv_b", "ssd_dt_bias", "ssd_a_log", "ssd_d", "ssd_norm_w",
                "ret_decay", "ret_gn_w", "ev_out_proj", "od_in_proj", "gmlp_norm_w", "gmlp_ws", "gmlp_bs",
                "od_out_proj"]


def make_in_maps(inputs):
    xp = np.asarray(inputs["x_prompt"], np.float32)
    xs = np.asarray(inputs["x_sample"], np.float32)
    cst, am = _consts()
    pos_a = np.concatenate([np.arange(4096), np.arange(2048)])
    pos_b = np.concatenate([np.arange(2048)] * 3)
    ca, sa = _rope(pos_a)
    cb, sb_ = _rope(pos_b)
    w = {k: np.ascontiguousarray(np.asarray(inputs[k], np.float32)) for k in WEIGHT_NAMES}
    maps = []
    for c in range(NCORES):
        if c < 4:
            x = np.concatenate([xp[c], xs[c]], axis=0)
            lk = np.ones((128, 1), np.float32)
            rc, rs = ca, sa
        else:
            i0 = 4 + 3 * (c - 4)
            x = np.concatenate([xs[i0], xs[i0 + 1], xs[i0 + 2]], axis=0)
            lk = np.zeros((128, 1), np.float32)
            rc, rs = cb, sb_
        m = {"xin": np.ascontiguousarray(x), "link": lk, "rcos": rc, "rsin": rs, "cst": cst, "amask": am}
        m.update(w)
        maps.append(m)
    return maps


def gather(results):
    yp = np.zeros((4, 4096, D), np.float32)
    ys = np.zeros((16, 2048, D), np.float32)
    for c in range(NCORES):
        y = np.asarray(results[c]["yout"], np.float32)
        if c < 4:
            yp[c] = y[0:4096]
            ys[c] = y[4096:6144]
        else:
            i0 = 4 + 3 * (c - 4)
            for k in range(3):
                ys[i0 + k] = y[k * 2048:(k + 1) * 2048]
    return yp, ys


_NC_CACHE = {}


def kernel(**inputs):
    if "nc" not in _NC_CACHE:
        _NC_CACHE["nc"] = build_nc()[0]
    nc = _NC_CACHE["nc"]
    maps = make_in_maps(inputs)
    res = run_bass_kernel_spmd(nc, maps, core_ids=list(range(NCORES)))
    return gather(res.results)
```

```python
import contextlib
import numpy as np
import concourse.bass as bass
import concourse.mybir as mybir
from concourse.bass_utils import run_bass_kernel_spmd

F32 = mybir.dt.float32
BF16 = mybir.dt.bfloat16
ALU = mybir.AluOpType
AF = mybir.ActivationFunctionType
AX = mybir.AxisListType

ENGS = ("pe", "act", "dve", "pool", "sp")
DMA_ENGS = ("sp", "act", "pool")
EPOCH = 30000
NDSEM = 8

NCORES = 8
T = 6144
NCH = 48
SEGCH = 16
D = 1024
EPS = 1e-6
LOOPN = {}
USE_SCHED = True
SCHED_W = 64
SCHED_HOP = 120.0
AT_LAG = 8


def crange(name):
    return range(LOOPN.get(name, NCH))


class Tok:
    __slots__ = ("w", "rs")

    def __init__(self):
        self.w = None
        self.rs = []


class Op:
    __slots__ = ("eng", "fn", "deps", "dma", "needed", "seq", "dsem", "dval", "barrier", "snap", "cost", "lat")

    def __init__(self, eng, fn, deps, dma, barrier=False, cost=300.0, lat=0.0):
        self.cost = cost
        self.lat = lat
        self.eng = eng
        self.fn = fn
        self.deps = deps
        self.dma = dma
        self.needed = False
        self.seq = None
        self.dsem = None
        self.dval = None
        self.barrier = barrier
        self.snap = None


class Sched:
    def __init__(self, nc):
        self.nc = nc
        self.ops = []
        self.keep = False
        self.keep_set = set()

    def op(self, eng, fn, reads=(), writes=(), dma=False, cost=300.0, lat=0.0):
        idx = len(self.ops)
        deps = set()
        for t in reads:
            if t.w is not None:
                deps.add(t.w)
        for t in writes:
            if t.w is not None:
                deps.add(t.w)
            for r in t.rs:
                deps.add(r)
        self.ops.append(Op(eng, fn, deps, dma, cost=cost, lat=lat))
        if self.keep:
            self.keep_set.add(idx)
        for t in reads:
            t.rs.append(idx)
        for t in writes:
            t.w = idx
            t.rs = []
        return idx

    def barrier(self):
        for e in ENGS:
            self.ops.append(Op(e, None, set(), False, barrier=True))

    def schedule(self, W=12, hop=120.0):
        ops = self.ops
        n = len(ops)
        fin = [0.0] * n
        done = [False] * n
        order = {e: [] for e in ENGS}
        seg_start = 0
        bounds = []
        i = 0
        while i < n:
            if ops[i].barrier:
                bounds.append((seg_start, i))
                j = i
                while j < n and ops[j].barrier:
                    j += 1
                bounds.append(("barrier", i, j))
                seg_start = j
                i = j
            else:
                i += 1
        bounds.append((seg_start, n))
        for bnd in bounds:
            if bnd[0] == "barrier":
                for k in range(bnd[1], bnd[2]):
                    order[ops[k].eng].append(k)
                    done[k] = True
                continue
            lo, hi = bnd
            if hi <= lo:
                continue
            pend = {e: [] for e in ENGS}
            for k in range(lo, hi):
                pend[ops[k].eng].append(k)
            if lo in self.keep_set:
                for e in ENGS:
                    order[e].extend(pend[e])
                for k in range(lo, hi):
                    done[k] = True
                continue
            pos = {e: 0 for e in ENGS}
            et = {e: 0.0 for e in ENGS}
            remaining = hi - lo
            while remaining:
                best = None
                for e in ENGS:
                    lst = pend[e]
                    cnt = 0
                    p = pos[e]
                    while p < len(lst) and done[lst[p]]:
                        p += 1
                    pos[e] = p
                    q = p
                    while q < len(lst) and cnt < W:
                        k = lst[q]
                        q += 1
                        if done[k]:
                            continue
                        cnt += 1
                        o = ops[k]
                        rdy = 0.0
                        ok = True
                        for d in o.deps:
                            if not done[d]:
                                ok = False
                                break
                            f = fin[d] + (0.0 if ops[d].eng == e and not ops[d].dma else hop)
                            if f > rdy:
                                rdy = f
                        if not ok:
                            continue
                        stt = rdy if rdy > et[e] else et[e]
                        key = (stt, k)
                        if best is None or key < best[0]:
                            best = (key, e, k)
                assert best is not None, "scheduler deadlock"
                (stt, k), e, k = best
                o = ops[k]
                et[e] = stt + o.cost
                fin[k] = stt + o.cost + o.lat
                done[k] = True
                order[e].append(k)
                remaining -= 1
        return order

    def emit(self):
        nc = self.nc
        ops = self.ops
        sched_order = self.schedule(W=SCHED_W, hop=SCHED_HOP) if USE_SCHED else None
        for o in ops:
            for d in o.deps:
                ops[d].needed = True
        cnt = {e: 0 for e in ENGS}
        dcnt = {e: 0 for e in DMA_ENGS}
        if sched_order is not None:
            walk = []
            ptr = {e: 0 for e in ENGS}
            nb = sum(1 for o in ops if o.barrier) // len(ENGS)
            for _ in range(nb + 1):
                for e in ENGS:
                    lst = sched_order[e]
                    p = ptr[e]
                    while p < len(lst) and not ops[lst[p]].barrier:
                        walk.append(lst[p])
                        p += 1
                    ptr[e] = p
                for e in ENGS:
                    lst = sched_order[e]
                    if ptr[e] < len(lst):
                        walk.append(lst[ptr[e]])
                        ptr[e] += 1
            walk_ops = [ops[k] for k in walk]
        else:
            walk_ops = ops
        last = {e: None for e in ENGS}
        for o in walk_ops:
            if o.barrier:
                for e in ENGS:
                    if last[e] is not None:
                        last[e].needed = True
            elif not o.dma:
                last[o.eng] = o
        for o in walk_ops:
            if o.barrier:
                o.snap = (dict(cnt), dict(dcnt))
            elif o.dma:
                i = dcnt[o.eng]
                dcnt[o.eng] += 1
                o.dsem = i % NDSEM
                o.dval = 16 * (i // NDSEM + 1)
                assert o.dval < 60000, "too many DMAs on one queue"
            elif o.needed:
                cnt[o.eng] += 1
                o.seq = cnt[o.eng]
        nep = {e: (cnt[e] + EPOCH - 1) // EPOCH for e in ENGS}
        with contextlib.ExitStack() as st:
            sems = {e: [st.enter_context(nc.semaphore(f"s_{e}_{k}")) for k in range(max(1, nep[e]))]
                    for e in ENGS}
            dsems = {e: [st.enter_context(nc.semaphore(f"d_{e}_{k}")) for k in range(NDSEM)]
                     for e in DMA_ENGS}
            block = st.enter_context(nc.Block())
            if sched_order is not None:
                per_eng = sched_order
            else:
                per_eng = {e: [] for e in ENGS}
                for i, o in enumerate(ops):
                    per_eng[o.eng].append(i)

            def run_engine(ename, engobj):
                seen = {}

                def wait(key, sem, val):
                    if val <= 0 or seen.get(key, 0) >= val:
                        return
                    seen[key] = val
                    engobj.wait_ge(sem, val)

                def wait_all(c_snap, d_snap):
                    for e in ENGS:
                        n = c_snap[e]
                        if n > 0:
                            ep = (n - 1) // EPOCH
                            wait(("c", e, ep), sems[e][ep], n - ep * EPOCH)
                    for e in DMA_ENGS:
                        n = d_snap[e]
                        for k in range(min(n, NDSEM)):
                            lastk = ((n - 1 - k) // NDSEM) * NDSEM + k
                            wait(("d", e, k), dsems[e][k], 16 * (lastk // NDSEM + 1))

                for i in per_eng[ename]:
                    o = ops[i]
                    if o.barrier:
                        wait_all(*o.snap)
                        continue
                    for d in sorted(o.deps):
                        p = ops[d]
                        if ename == "pe" and p.eng == "pe" and not p.dma:
                            continue
                        if p.dma:
                            wait(("d", p.eng, p.dsem), dsems[p.eng][p.dsem], p.dval)
                        else:
                            ep = (p.seq - 1) // EPOCH
                            wait(("c", p.eng, ep), sems[p.eng][ep], p.seq - ep * EPOCH)
                    if o.dma:
                        wait(("d", o.eng, o.dsem), dsems[o.eng][o.dsem], o.dval - 16)
                        ins = o.fn(engobj)
                        ins.then_inc(dsems[o.eng][o.dsem], 16)
                    else:
                        ins = o.fn(engobj)
                        if o.needed:
                            ep = (o.seq - 1) // EPOCH
                            ins.then_inc(sems[o.eng][ep], 1)
                wait_all({e: 0 for e in ENGS}, dcnt)

            @block.tensor
            def _(e):
                run_engine("pe", e)

            @block.scalar
            def _(e):
                run_engine("act", e)

            @block.vector
            def _(e):
                run_engine("dve", e)

            @block.gpsimd
            def _(e):
                run_engine("pool", e)

            @block.sync
            def _(e):
                run_engine("sp", e)
        return {e: len(per_eng[e]) for e in ENGS}


class Tl:
    def __init__(self, h):
        self.h = h
        self.t = Tok()

    def __getitem__(self, k):
        return self.h[k]


class B:
    def __init__(self, nc):
        self.nc = nc
        self.S = Sched(nc)
        self.cap = None

    def _rec(self, eng, fn, R, W, dma=False, cost=300.0, lat=0.0):
        reads = [x.t for x in R]
        writes = [x.t for x in W]
        if self.cap is not None:
            self.cap.append((eng, fn, reads, writes, dma, cost, lat))
        else:
            self.S.op(eng, fn, reads, writes, dma, cost, lat)

    @staticmethod
    def _fd(ap):
        n = 1
        for d in ap.shape[1:]:
            n *= int(d)
        return n

    def _ecost(self, eng, out):
        n = self._fd(out)
        if eng == "act":
            return (200.0 + n) / 1.2
        if eng == "dve":
            return (150.0 + n) / 0.96
        return 150.0 + 2.2 * n

    def capture(self, f, *args):
        old = self.cap
        self.cap = []
        f(*args)
        lst = self.cap
        self.cap = old
        return lst

    def emit_merged(self, la, lb):
        i = j = 0
        while i < len(la) or j < len(lb):
            if j >= len(lb) or (i < len(la) and i * len(lb) <= j * len(la)):
                self.S.op(*la[i])
                i += 1
            else:
                self.S.op(*lb[j])
                j += 1

    def two_stage(self, f1, f2, items):
        prev = None
        for k in items:
            l1 = self.capture(f1, k)
            l2 = self.capture(f2, prev) if prev is not None else []
            self.emit_merged(l2, l1)
            prev = k
        if prev is not None:
            self.emit_merged(self.capture(f2, prev), [])

    def mm(self, out, lhsT, rhs, start, stop, R, W):
        c = max(64, self._fd(rhs)) / 2.4 * (4.0 if lhsT.dtype == F32 else 1.0)
        self._rec("pe", lambda e: e.matmul(out, lhsT=lhsT, rhs=rhs, start=start, stop=stop),
                  R, W, cost=c, lat=60.0)

    def tr(self, out, in_, ident, R, W):
        self._rec("pe", lambda e: e.transpose(out=out, in_=in_, identity=ident),
                  R, W, cost=60.0, lat=60.0)

    def act(self, out, in_, func, R, W, scale=1.0, bias=0.0, accum=None):
        if accum is None:
            self._rec("act", lambda e: e.activation(out=out, in_=in_, func=func, scale=scale, bias=bias),
                      R, W, cost=self._ecost("act", out))
        else:
            self._rec("act", lambda e: e.activation(out=out, in_=in_, func=func, scale=scale, bias=bias,
                                                    accum_out=accum),
                      R, W, cost=self._ecost("act", out) + 80.0)

    def tt(self, eng, out, in0, in1, op, R, W):
        self._rec(eng, lambda e: e.tensor_tensor(out=out, in0=in0, in1=in1, op=op),
                  R, W, cost=self._ecost(eng, out))

    def ts(self, eng, out, in0, s1, s2, op0, op1, R, W):
        if op1 is None:
            self._rec(eng, lambda e: e.tensor_scalar(out=out, in0=in0, scalar1=s1, scalar2=None, op0=op0),
                      R, W, cost=self._ecost(eng, out))
        else:
            self._rec(eng, lambda e: e.tensor_scalar(out=out, in0=in0, scalar1=s1, scalar2=s2, op0=op0, op1=op1),
                      R, W, cost=self._ecost(eng, out))

    def stt(self, out, in0, scalar, in1, op0, op1, R, W):
        self._rec("dve", lambda e: e.scalar_tensor_tensor(out=out, in0=in0, scalar=scalar, in1=in1, op0=op0, op1=op1),
                  R, W, cost=self._ecost("dve", out))

    def cp(self, eng, out, in_, R, W):
        if eng == "act":
            self._rec("act", lambda e: e.copy(out=out, in_=in_), R, W, cost=self._ecost("act", out))
        else:
            self._rec(eng, lambda e: e.tensor_copy(out=out, in_=in_), R, W, cost=self._ecost(eng, out))

    def ms(self, eng, ap, val, W):
        self._rec(eng, lambda e: e.memset(ap, val), [], W, cost=self._ecost(eng, ap))

    def red(self, out, in_, R, W):
        self._rec("dve", lambda e: e.tensor_reduce(out=out, in_=in_, axis=AX.X, op=ALU.add),
                  R, W, cost=self._ecost("dve", in_))

    def ld(self, out, in_, W, eng="sp"):
        nbytes = self._fd(out) * 4 * 128
        self._rec(eng, lambda e: e.dma_start(out=out, in_=in_), [], W, dma=True,
                  cost=(400.0 if eng == "sp" else 700.0), lat=2000.0 + nbytes / 150.0)

    def st(self, out, in_, R, eng="pool"):
        nbytes = self._fd(in_) * 4 * 128
        self._rec(eng, lambda e: e.dma_start(out=out, in_=in_), R, [], dma=True,
                  cost=(400.0 if eng == "sp" else 700.0), lat=2000.0 + nbytes / 150.0)

    def ldw(self, wt, wd, kc_n, ncols):
        for kc in range(kc_n):
            for c0 in range(0, ncols, 2048):
                c1 = min(ncols, c0 + 2048)
                self.ld(wt[:, kc, c0:c1], wd[kc * 128:(kc + 1) * 128, c0:c1], [wt], eng="pool")


def seg_of(c):
    return c // SEGCH


def build_nc(layers=(0, 1), do_mix=True, do_ffn=True):
    nc = bass.Bass("TRN2", target_bir_lowering=False)
    b = B(nc)
    S = b.S

    def din(name, shape):
        return nc.dram_tensor(name, list(shape), F32, kind="ExternalInput").ap()

    def dscr(name, shape, dt=F32):
        return nc.dram_tensor(name, list(shape), dt, kind="Internal").ap()

    xin = din("xin", [T, D])
    yout = nc.dram_tensor("yout", [T, D], F32, kind="ExternalOutput").ap()
    link_d = din("link", [128, 1])
    cos_d = din("rcos", [T, 32])
    sin_d = din("rsin", [T, 32])
    cst_d = din("cst", [128, 9, 128])
    amask_d = din("amask", [128, 17, 128])
    g_mix_pre = din("norm_mix_pre", [2, D])
    g_mix_post = din("norm_mix_post", [2, D])
    g_ffn_pre = din("norm_ffn_pre", [2, D])
    g_ffn_post = din("norm_ffn_post", [2, D])
    w1_d = din("ffn_w1", [2, D, 4096])
    w2_d = din("ffn_w2", [2, 4096, D])
    evin_d = din("ev_in_proj", [1, D, 5664])
    convw_d = din("ev_conv_w", [1, 5, 1536])
    convb_d = din("ev_conv_b", [1, 1536])
    dtb_d = din("ssd_dt_bias", [1, 2, 16])
    alog_d = din("ssd_a_log", [1, 2, 16])
    dsk_d = din("ssd_d", [1, 16])
    snw_d = din("ssd_norm_w", [1, 1024])
    rdec_d = din("ret_decay", [1, 2, 8])
    rgn_d = din("ret_gn_w", [1, 1024])
    evout_d = din("ev_out_proj", [1, 2048, D])
    odin_d = din("od_in_proj", [1, D, 4096])
    gnw_d = din("gmlp_norm_w", [1, 512])
    gws_d = din("gmlp_ws", [1, 8, 128, 128])
    gbs_d = din("gmlp_bs", [1, 8, 128])
    odout_d = din("od_out_proj", [1, 1536, D])

    xa = dscr("xa", [T, D])
    xb = dscr("xb", [T, D])
    sz_d = dscr("sz", [T, 1024])
    sg_d = dscr("sgt", [T, 1024])
    xbc_d = dscr("xbc", [T + 4, 1536])
    dtr_d = dscr("dtr", [T, 32])
    qT0_d = dscr("qT0", [NCH, 128, 4, 128], BF16)
    kT0_d = dscr("kT0", [NCH, 128, 4, 128], BF16)
    k0_d = dscr("k0", [T, 512], BF16)
    v0_d = dscr("v0", [T, 1024], BF16)
    yd_d = dscr("yd", [T, 1024])
    yi_d = dscr("yi", [T, 1024])
    CT_d = dscr("CTd", [NCH, 128, 2, 128], BF16)
    sc_d = dscr("scd", [T, 32])
    Sf_d = dscr("Sfd", [NCH, 128, 1024])
    decf_d = dscr("decf", [NCH, 128, 16])
    Rf_d = dscr("Rfd", [NCH, 128, 512])
    Hb_d = dscr("Hbd", [NCH, 128, 1024], BF16)
    Rb_d = dscr("Rbd", [NCH, 128, 512], BF16)
    xs_d = dscr("xsd", [T, 1024])
    bct_d = dscr("bctd", [NCH, 128, 4, 128], BF16)
    btok_d = dscr("btokd", [T, 256], BF16)
    qT1_d = dscr("qT1", [NCH, 128, 8, 128], BF16)
    kT1_d = dscr("kT1", [NCH, 128, 8, 128], BF16)
    v1_d = dscr("v1", [NCH, 128, 16 * 65], BF16)
    sg1_d = dscr("sg1", [T, 512], BF16)

    def chunk_rows(ap, c):
        return ap[c * 128:(c + 1) * 128, :]

    with contextlib.ExitStack() as gst:
        uid = [0]

        def sb(name, shape, dt, st=gst):
            uid[0] += 1
            return Tl(st.enter_context(nc.sbuf_tensor(f"s{uid[0]}_{name}", list(shape), dt)))

        def psum(name, shape, dt, st=gst):
            uid[0] += 1
            return Tl(st.enter_context(nc.psum_tensor(f"p{uid[0]}_{name}", list(shape), dt)))

        cst = sb("cst", [128, 9, 128], F32)
        b.ld(cst[:], cst_d[:, :, :], [cst])
        idb = sb("idb", [128, 128], BF16)
        b.cp("dve", idb[:], cst[:, 0, :], [cst], [idb])
        Um, Lm, Af, Ab, ONES = (cst[:, 1, :], cst[:, 2, :], cst[:, 3, :], cst[:, 4, :], cst[:, 5, :])
        RDp, RDn = cst[:, 6, :], cst[:, 7, :]
        POS = cst[:, 8, :]
        link = sb("link", [128, 1], F32)
        b.ld(link[:], link_d[:, :], [link])
        def alloc_psum(st, nf32, nbf):
            assert nf32 + nbf <= 8
            return ([psum(f"P{i}", [128, 512], F32, st) for i in range(nf32)],
                    [psum(f"PT{i}", [128, 1024], BF16, st) for i in range(nbf)])

        def rstd_from_ss(ss, n, R):
            b.act(ss[:, 0:1], ss[:, 0:1], AF.Ln, [ss] + R, [ss], scale=1.0 / n, bias=EPS)
            b.act(ss[:, 0:1], ss[:, 0:1], AF.Exp, [ss], [ss], scale=-0.5)

        def norm_to_hT(x_ap, xT, gain, junk, ss, hb, hT_ap, hT, PT):
            b.act(junk[:], x_ap, AF.Square, [xT], [junk, ss], accum=ss[:, 0:1])
            rstd_from_ss(ss, 1024.0, [])
            b.stt(hb[:], x_ap, ss[:, 0:1], gain[:], ALU.mult, ALU.mult, [xT, ss, gain], [hb])
            for k in range(8):
                b.tr(PT[:, k * 128:(k + 1) * 128], hb[:, k * 128:(k + 1) * 128], idb[:], [hb, idb], [PT])
            b.cp("act", hT_ap, PT[:, 0:1024].rearrange("p (k t) -> p k t", k=8), [PT], [hT])

        def post_norm_residual(Pa, Pb, gain, x_ap, xT, junk, ss, tn, xo, out_dram):
            ss2 = ss
            b.act(junk[:, 0:512], Pa[:], AF.Square, [Pa], [junk, ss2], accum=ss2[:, 0:1])
            b.act(junk[:, 512:1024], Pb[:], AF.Square, [Pb], [junk, ss2], accum=ss2[:, 1:2])
            b.tt("dve", ss2[:, 0:1], ss2[:, 0:1], ss2[:, 1:2], ALU.add, [ss2], [ss2])
            b.act(ss2[:, 0:1], ss2[:, 0:1], AF.Ln, [ss2], [ss2], scale=1.0 / 1024, bias=EPS)
            b.act(ss2[:, 0:1], ss2[:, 0:1], AF.Exp, [ss2], [ss2], scale=-0.5)
            b.stt(tn[:, 0:512], Pa[:], ss2[:, 0:1], gain[:, 0:512], ALU.mult, ALU.mult, [Pa, ss2, gain], [tn])
            b.stt(tn[:, 512:1024], Pb[:], ss2[:, 0:1], gain[:, 512:1024], ALU.mult, ALU.mult, [Pb, ss2, gain], [tn])
            b.tt("dve", xo[:], tn[:], x_ap, ALU.add, [tn, xT], [xo])
            b.st(out_dram, xo[:], [xo])

        def rotary(eng, out, src_ap, srcT, cs, H, tmp):
            s3 = src_ap.rearrange("p (h d) -> p h d", h=H)
            t1, t2 = s3[:, :, 0:32], s3[:, :, 32:64]
            cb = cs[:, 0:32].unsqueeze(1).to_broadcast([128, H, 32])
            sn = cs[:, 32:64].unsqueeze(1).to_broadcast([128, H, 32])
            ta = tmp[:, 0:H * 32].rearrange("p (h d) -> p h d", h=H)
            tb = tmp[:, H * 32:H * 64].rearrange("p (h d) -> p h d", h=H)
            b.tt(eng, ta, t1, cb, ALU.mult, [srcT, cs], [tmp])
            b.tt(eng, tb, t2, sn, ALU.mult, [srcT, cs], [tmp])
            b.tt(eng, out[:, :, 0:32], ta, tb, ALU.subtract, [tmp], [out])
            b.tt(eng, ta, t2, cb, ALU.mult, [srcT, cs, out], [tmp])
            b.tt(eng, tb, t1, sn, ALU.mult, [srcT, cs, out], [tmp])
            b.tt(eng, out[:, :, 32:64], ta, tb, ALU.add, [tmp], [out])

        def ffn_loop(li, src, dst):
            with contextlib.ExitStack() as st:
                W1 = sb("W1", [128, 8, 4096], BF16, st)
                W2 = sb("W2", [128, 32, 1024], BF16, st)
                b.ldw(W1, w1_d[li], 8, 4096)
                b.ldw(W2, w2_d[li], 32, 1024)
                gpre = sb("gpre", [128, 1024], F32, st)
                gpost = sb("gpost", [128, 1024], F32, st)
                b.ld(gpre[:], g_ffn_pre[li].partition_broadcast(128), [gpre])
                b.ld(gpost[:], g_ffn_post[li].partition_broadcast(128), [gpost])
                P, PTs = alloc_psum(st, 6, 1)
                xm = [sb(f"xm{i}", [128, 2, 1024], F32, st) for i in range(2)]
                junk = sb("junk", [128, 1024], BF16, st)
                junk2 = sb("junk2", [128, 1024], BF16, st)
                ss = sb("ss", [128, 2], F32, st)
                ss2 = sb("ss2", [128, 2], F32, st)
                hb = sb("hb", [128, 1024], BF16, st)
                hT = [sb(f"hT{i}", [128, 8, 256], BF16, st) for i in range(2)]
                uT = sb("uT", [128, 32, 256], BF16, st)
                rl = [sb(f"rl{i}", [128, 256], F32, st) for i in range(2)]
                tn = sb("tn", [128, 1024], F32, st)
                xo = [sb(f"xo{i}", [128, 1024], F32, st) for i in range(2)]

                def ffn_n(mt):
                    xmt = xm[mt % 2]
                    b.ld(xmt[:], src[mt * 256:(mt + 1) * 256, :].rearrange("(j p) d -> p j d", p=128), [xmt])
                    for j in range(2):
                        norm_to_hT(xmt[:, j, :], xmt, gpre, junk, ss, hb, hT[mt % 2][:, :, j * 128:(j + 1) * 128],
                                   hT[mt % 2], PTs[0])

                def ffn_r(mt):
                    xmt = xm[mt % 2]
                    hTc = hT[mt % 2]
                    for fc in range(32):
                        pu = P[fc % 2]
                        for k in range(8):
                            b.mm(pu[:, 0:256], W1[:, k, fc * 128:(fc + 1) * 128], hTc[:, k, :], k == 0, k == 7,
                                 [W1, hTc], [pu])
                        r = rl[fc % 2]
                        b.act(r[:], pu[:, 0:256], AF.Relu, [pu], [r])
                        b.tt("dve", uT[:, fc, :], r[:], r[:], ALU.mult, [r], [uT])
                    for j in range(2):
                        Pa, Pb = P[2 + 2 * j], P[3 + 2 * j]
                        for nt, Pq in ((0, Pa), (1, Pb)):
                            for fc in range(32):
                                b.mm(Pq[:], uT[:, fc, j * 128:(j + 1) * 128], W2[:, fc, nt * 512:(nt + 1) * 512],
                                     fc == 0, fc == 31, [uT, W2], [Pq])
                        c = mt * 2 + j
                        post_norm_residual(Pa, Pb, gpost, xmt[:, j, :], xmt, junk2, ss2, tn, xo[j], chunk_rows(dst, c))

                b.two_stage(ffn_n, ffn_r, range(NCH // 2))
            S.barrier()

        def odd_layer(src, dst):
            with contextlib.ExitStack() as st:
                P, PTs = alloc_psum(st, 6, 2)
                PT = PTs[0]
                Wi = sb("Wi1", [128, 8, 4096], BF16, st)
                b.ldw(Wi, odin_d[0], 8, 4096)
                gpre = sb("gpre", [128, 1024], F32, st)
                b.ld(gpre[:], g_mix_pre[1].partition_broadcast(128), [gpre])
                gnw = sb("gnw", [128, 512], F32, st)
                b.ld(gnw[:], gnw_d[0].partition_broadcast(128), [gnw])
                wsf = sb("wsf", [128, 8, 128], F32, st)
                b.ld(wsf[:], gws_d[0].rearrange("g t s -> t g s"), [wsf])
                wsb = sb("wsb", [128, 8, 128], BF16, st)
                b.cp("dve", wsb[:], wsf[:], [wsf], [wsb])
                wsT = sb("wsT", [128, 8, 128], BF16, st)
                for g in range(8):
                    b.tr(PT[:, g * 128:(g + 1) * 128], wsb[:, g, :], idb[:], [wsb, idb], [PT])
                b.cp("act", wsT[:], PT[:, 0:1024].rearrange("p (g t) -> p g t", g=8), [PT], [wsT])
                bsf = sb("bsf", [8, 128], F32, st)
                b.ld(bsf[:], gbs_d[0], [bsf])
                bsT = sb("bsT", [128, 8], F32, st)
                b.tr(P[0][:, 0:8], bsf[:], cst[0:8, 0, 0:8], [bsf, cst], [P[0]])
                b.cp("act", bsT[:], P[0][:, 0:8], [P[0]], [bsT])

                xt = [sb(f"xt{i}", [128, 1024], F32, st) for i in range(2)]
                cs = [sb(f"cs{i}", [128, 64], F32, st) for i in range(2)]
                junk = sb("junk", [128, 1024], BF16, st)
                junk2 = sb("junk2", [128, 512], BF16, st)
                ss = sb("ss", [128, 2], F32, st)
                hb = sb("hb", [128, 1024], BF16, st)
                hT = [sb(f"hT{i}", [128, 8, 128], BF16, st) for i in range(2)]
                qsq = sb("qsq", [128, 1024], F32, st)
                qsk = sb("qsk", [128, 1024], F32, st)
                rtq = sb("rtq", [128, 1024], F32, st)
                rtk = sb("rtk", [128, 1024], F32, st)
                qrq = sb("qrq", [128, 16, 64], BF16, st)
                qrk = sb("qrk", [128, 16, 64], BF16, st)
                qTs = [sb(f"qTs{i}", [128, 8, 128], BF16, st) for i in range(2)]
                kTs = [sb(f"kTs{i}", [128, 8, 128], BF16, st) for i in range(2)]
                v1s = [sb(f"v1s{i}", [128, 16, 65], BF16, st) for i in range(2)]
                for i in range(2):
                    b.ms("pool", v1s[i][:], 1.0, [v1s[i]])
                us = sb("us", [128, 512], F32, st)
                vgs = sb("vgs", [128, 512], F32, st)
                st4 = sb("st4", [128, 4], F32, st)
                vn = sb("vn", [128, 512], BF16, st)
                sgs = [sb(f"sgs{i}", [128, 512], BF16, st) for i in range(2)]

                def a1_n(c):
                    i2 = c % 2
                    b.ld(xt[i2][:], chunk_rows(src, c), [xt[i2]])
                    b.ld(cs[i2][:, 0:32], chunk_rows(cos_d, c), [cs[i2]])
                    b.ld(cs[i2][:, 32:64], chunk_rows(sin_d, c), [cs[i2]])
                    norm_to_hT(xt[i2][:], xt[i2], gpre, junk, ss, hb, hT[i2][:], hT[i2], PTs[1])

                def a1_r(c):
                    i2 = c % 2
                    hTc = hT[i2]
                    PT = PTs[0]

                    def proj(nt, Pq):
                        for k in range(8):
                            b.mm(Pq[:], hTc[:, k, :], Wi[:, k, nt * 512:(nt + 1) * 512], k == 0, k == 7, [hTc, Wi], [Pq])
                    for nt in range(6):
                        proj(nt, P[nt])
                    b.cp("act", qsq[:, 0:512], P[0][:], [P[0]], [qsq])
                    b.cp("act", qsq[:, 512:1024], P[1][:], [P[1]], [qsq])
                    b.cp("act", qsk[:, 0:512], P[2][:], [P[2]], [qsk])
                    b.cp("act", qsk[:, 512:1024], P[3][:], [P[3]], [qsk])
                    proj(6, P[0])
                    proj(7, P[1])
                    vt = v1s[i2]
                    b.cp("dve", vt[:, 0:8, 0:64], P[4][:].rearrange("p (h d) -> p h d", h=8), [P[4]], [vt])
                    b.cp("dve", vt[:, 8:16, 0:64], P[5][:].rearrange("p (h d) -> p h d", h=8), [P[5]], [vt])
                    b.st(v1_d[c], vt[:].rearrange("p h d -> p (h d)"), [vt])
                    rotary("dve", qrq, qsq[:], qsq, cs[i2], 16, rtq)
                    rotary("pool", qrk, qsk[:], qsk, cs[i2], 16, rtk)
                    b.cp("act", us[:], P[0][:], [P[0]], [us])
                    b.act(vgs[:], P[1][:], AF.Copy, [P[1]], [vgs, st4], accum=st4[:, 0:1])
                    b.act(junk2[:, 0:512], P[1][:], AF.Square, [P[1]], [junk2, st4], accum=st4[:, 1:2])
                    for qr_, dstT, dram in ((qrq, qTs[i2], qT1_d), (qrk, kTs[i2], kT1_d)):
                        qf = qr_[:].rearrange("p h d -> p (h d)")
                        for k in range(8):
                            b.tr(PT[:, k * 128:(k + 1) * 128], qf[:, k * 128:(k + 1) * 128], idb[:], [qr_, idb], [PT])
                        b.cp("act", dstT[:], PT[:, 0:1024].rearrange("p (k t) -> p k t", k=8), [PT], [dstT])
                        b.st(dram[c], dstT[:], [dstT])
                    b.ts("dve", st4[:, 0:2], st4[:, 0:2], 1.0 / 512, None, ALU.mult, None, [st4], [st4])
                    b.tt("dve", st4[:, 2:3], st4[:, 0:1], st4[:, 0:1], ALU.mult, [st4], [st4])
                    b.tt("dve", st4[:, 1:2], st4[:, 1:2], st4[:, 2:3], ALU.subtract, [st4], [st4])
                    b.act(st4[:, 1:2], st4[:, 1:2], AF.Ln, [st4], [st4], scale=1.0, bias=EPS)
                    b.act(st4[:, 1:2], st4[:, 1:2], AF.Exp, [st4], [st4], scale=-0.5)
                    b.ts("dve", vgs[:], vgs[:], st4[:, 0:1], st4[:, 1:2], ALU.subtract, ALU.mult, [vgs, st4], [vgs])
                    b.tt("dve", vn[:], vgs[:], gnw[:], ALU.mult, [vgs, gnw], [vn])
                    for g in range(8):
                        b.mm(P[2][:, g * 64:(g + 1) * 64], wsT[:, g, :], vn[:, g * 64:(g + 1) * 64], True, True,
                             [wsT, vn], [P[2]])
                    b.tt("dve", vgs[:].rearrange("p (g d) -> p g d", g=8), P[2][:].rearrange("p (g d) -> p g d", g=8),
                         bsT[:].unsqueeze(2).to_broadcast([128, 8, 64]), ALU.add, [P[2], bsT, vn], [vgs])
                    b.tt("dve", sgs[i2][:], vgs[:], us[:], ALU.mult, [vgs, us], [sgs[i2]])
                    b.st(chunk_rows(sg1_d, c), sgs[i2][:], [sgs[i2]])

                b.two_stage(a1_n, a1_r, crange("A1"))
            S.barrier()
            with contextlib.ExitStack() as st:
                NS = 18
                S.keep = True
                P, PTs = alloc_psum(st, 7, 1)
                PT = PTs[0]
                Wo = sb("Wo1", [128, 12, 1024], BF16, st)
                b.ldw(Wo, odout_d[0], 12, 1024)
                gpost = sb("gpost", [128, 1024], F32, st)
                b.ld(gpost[:], g_mix_post[1].partition_broadcast(128), [gpost])
                amf = sb("amf", [128, 17, 128], F32, st)
                b.ld(amf[:], amask_d[:, :, :], [amf])
                am = sb("am", [128, 17, 128], BF16, st)
                amL = sb("amL", [128, 17, 128], BF16, st)
                b.cp("dve", am[:], amf[:], [amf], [am])
                b.ts("dve", amL[:], amf[:], link[:, 0:1], None, ALU.mult, None, [amf, link], [amL])
                kr = [sb(f"kr{i}", [128, 8, 128], BF16, st) for i in range(NS)]
                vr = [sb(f"vr{i}", [128, 16 * 65], BF16, st) for i in range(NS)]
                qze = [sb(f"qze{i}", [128, 8, 128], BF16, st) for i in range(2)]
                qzo = [sb(f"qzo{i}", [128, 8, 128], BF16, st) for i in range(2)]
                for i in range(2):
                    b.ms("dve", qze[i][:], 0.0, [qze[i]])
                    b.ms("dve", qzo[i][:], 0.0, [qzo[i]])
                pe_sb = [sb(f"pe{i}", [128, 512], BF16, st) for i in range(AT_LAG + 1)]
                pm_sb = [sb(f"pm{i}", [128, 512], BF16, st) for i in range(AT_LAG + 1)]
                rden = sb("rden", [128, 4], F32, st)
                mix = [sb(f"mix{i}", [128, 1536], BF16, st) for i in range(2)]
                mixT = sb("mixT", [128, 12, 128], BF16, st)
                xt = [sb(f"xt{i}", [128, 1024], F32, st) for i in range(2)]
                junk = sb("junk", [128, 1024], BF16, st)
                ss = sb("ss", [128, 2], F32, st)
                tn = sb("tn", [128, 1024], F32, st)
                xo = [sb(f"xo{i}", [128, 1024], F32, st) for i in range(2)]
                loaded = set()
                gi = 0
                pend = []
                LAG = AT_LAG
                for c in crange("AT"):
                    i2 = c % 2
                    sgc = seg_of(c)
                    lo, hi = (0, 2 * SEGCH) if sgc < 2 else (2 * SEGCH, NCH)
                    blocks = [j for j in range(c - 8, c + 9) if lo <= j < hi]
                    for j in blocks:
                        if j not in loaded:
                            loaded.add(j)
                            b.ld(kr[j % NS][:], kT1_d[j], [kr[j % NS]])
                            b.ld(vr[j % NS][:], v1_d[j], [vr[j % NS]])
                    b.ld(qze[i2][0:64], qT1_d[c][0:64], [qze[i2]])
                    b.ld(qzo[i2][64:128], qT1_d[c][64:128], [qzo[i2]])
                    b.ld(mix[i2][:, 1024:1536], chunk_rows(sg1_d, c), [mix[i2]])
                    b.ld(xt[i2][:], chunk_rows(src, c), [xt[i2]])
                    groups = []
                    cur = []
                    for j in blocks:
                        cross = (seg_of(j) != sgc)
                        if cur and (len(cur) == 4 or cur[0][1] != cross):
                            groups.append(cur)
                            cur = []
                        cur.append((j, cross))
                    if cur:
                        groups.append(cur)
                    for h in range(16):
                        hp, base = h // 2, (h % 2) * 64
                        po = P[4 + (h // 4) % 2]
                        pcol = (h % 4) * 65
                        nb = len(blocks)
                        bi = 0
                        for gx, grp in enumerate(groups):
                            ps = P[gi % 3]
                            pe_t = pe_sb[gi % (AT_LAG + 1)]
                            pm_t = pm_sb[gi % (AT_LAG + 1)]
                            gi += 1
                            n = len(grp)
                            for i, (j, cross) in enumerate(grp):
                                qz = (qze if h % 2 == 0 else qzo)[i2]
                                b.mm(ps[:, i * 128:(i + 1) * 128], kr[j % NS][:, hp, :],
                                     qz[:, hp, :], True, True, [kr[j % NS], qz], [ps])
                            b.act(pe_t[:, 0:n * 128], ps[:, 0:n * 128], AF.Exp, [ps], [pe_t], scale=0.125)
                            o0 = grp[0][0] - c + 8
                            msk = amL if grp[0][1] else am
                            b.tt("dve", pm_t[:, 0:n * 128], pe_t[:, 0:n * 128],
                                 msk[:, o0:o0 + n, :].rearrange("p a b -> p (a b)"), ALU.mult, [pe_t, msk], [pm_t])

                            def pv(grp=grp, pm_t=pm_t, po=po, pcol=pcol, bi=bi, nb=nb, h=h, i2=i2, c=c,
                                   last=(gx == len(groups) - 1)):
                                for i, (j, cross) in enumerate(grp):
                                    b.mm(po[:, pcol:pcol + 65], pm_t[:, i * 128:(i + 1) * 128],
                                         vr[j % NS][:, h * 65:(h + 1) * 65], bi + i == 0, bi + i == nb - 1,
                                         [pm_t, vr[j % NS]], [po])
                                if last and h % 4 == 3:
                                    po3 = po[:, 0:260].rearrange("p (h d) -> p h d", h=4)
                                    b.S.op("dve", lambda e, o=rden[:].unsqueeze(2), i_=po3[:, :, 64:65]: e.reciprocal(out=o, in_=i_),
                                           [po.t], [rden.t])
                                    h0 = h - 3
                                    b.tt("dve", mix[i2][:, h0 * 64:(h0 + 4) * 64].rearrange("p (h d) -> p h d", h=4),
                                         po3[:, :, 0:64], rden[:].unsqueeze(2).to_broadcast([128, 4, 64]), ALU.mult,
                                         [po, rden], [mix[i2]])
                                if last and h == 15:
                                    for half in range(2):
                                        for k in range(6):
                                            kk = half * 6 + k
                                            b.tr(PT[:, k * 128:(k + 1) * 128], mix[i2][:, kk * 128:(kk + 1) * 128], idb[:],
                                                 [mix[i2], idb], [PT])
                                        b.cp("act", mixT[:, half * 6:(half + 1) * 6, :],
                                             PT[:, 0:768].rearrange("p (k t) -> p k t", k=6), [PT], [mixT])
                                    for nt, Pq in ((0, P[3]), (1, P[6])):
                                        for k in range(12):
                                            b.mm(Pq[:], mixT[:, k, :], Wo[:, k, nt * 512:(nt + 1) * 512], k == 0, k == 11,
                                                 [mixT, Wo], [Pq])
                                    post_norm_residual(P[3], P[6], gpost, xt[i2][:], xt[i2], junk, ss, tn, xo[i2],
                                                       chunk_rows(dst, c))
                            bi += n
                            pend.append(pv)
                            while len(pend) > LAG:
                                pend.pop(0)()
                while pend:
                    pend.pop(0)()
            S.keep = False
            S.barrier()

        def even_layer(src, dst):
            with contextlib.ExitStack() as st:
                P, PTs = alloc_psum(st, 6, 2)
                Wi = sb("Wi0", [128, 8, 5664], BF16, st)
                b.ldw(Wi, evin_d[0], 8, 5664)
                gpre = sb("gpre", [128, 1024], F32, st)
                b.ld(gpre[:], g_mix_pre[0].partition_broadcast(128), [gpre])
                xt = [sb(f"xt{i}", [128, 1024], F32, st) for i in range(2)]
                cs = [sb(f"cs{i}", [128, 64], F32, st) for i in range(2)]
                junk = sb("junk", [128, 1024], BF16, st)
                ss = sb("ss", [128, 2], F32, st)
                hb = sb("hb", [128, 1024], BF16, st)
                hT = [sb(f"hT{i}", [128, 8, 128], BF16, st) for i in range(2)]
                zo = [sb(f"zo{i}", [128, 1024], F32, st) for i in range(2)]
                go = [sb(f"go{i}", [128, 1024], F32, st) for i in range(2)]
                xpre = [sb(f"xpre{i}", [128, 12, 132], BF16, st) for i in range(3)]
                xsT = sb("xsT", [128, 8, 128], F32, st)
                bcT = [sb(f"bcT{i}", [128, 4, 128], BF16, st) for i in range(2)]
                xso = [sb(f"xso{i}", [128, 1024], F32, st) for i in range(2)]
                btk = [sb(f"btk{i}", [128, 256], BF16, st) for i in range(2)]
                cw = sb("cw", [8, 1536], F32, st)
                b.ms("dve", cw[:], 0.0, [cw])
                b.ld(cw[0:5, :], convw_d[0], [cw])
                cbr = sb("cbr", [12, 128], F32, st)
                b.ld(cbr[:], convb_d[0].rearrange("(a p) -> a p", p=128), [cbr])
                wT = sb("wT", [128, 12, 8], F32, st)
                cbT = sb("cbT", [128, 12], F32, st)
                for cb in range(12):
                    b.tr(P[0][:, cb * 8:(cb + 1) * 8], cw[:, cb * 128:(cb + 1) * 128], cst[0:8, 0, 0:8], [cw, cst], [P[0]])
                b.cp("act", wT[:], P[0][:, 0:96].rearrange("p (a j) -> p a j", a=12), [P[0]], [wT])
                b.tr(P[1][:, 0:12], cbr[:], cst[0:12, 0, 0:12], [cbr, cst], [P[1]])
                b.cp("act", cbT[:], P[1][:, 0:12], [P[1]], [cbT])
                Wd = sb("Wd", [128, 12, 5, 128], BF16, st)
                for cb in range(12):
                    for j in range(5):
                        b.ts("dve", Wd[:, cb, j, :], cst[:, 0, :], wT[:, cb, j:j + 1], None, ALU.mult, None, [cst, wT], [Wd])
                b.ms("dve", xpre[0][:, :, 0:2], 0.0, [xpre[0]])
                dto = [sb(f"dto{i}", [128, 32], F32, st) for i in range(2)]
                qs = sb("qs", [128, 512], F32, st)
                rtmp = sb("rtmp", [128, 512], F32, st)
                qr = sb("qr", [128, 8, 64], BF16, st)
                kro = [sb(f"kro{i}", [128, 8, 64], BF16, st) for i in range(2)]
                qTs = [sb(f"qTs{i}", [128, 4, 128], BF16, st) for i in range(2)]
                kTs = [sb(f"kTs{i}", [128, 4, 128], BF16, st) for i in range(2)]
                vo = [sb(f"vo{i}", [128, 1024], BF16, st) for i in range(2)]
                qs2 = sb("qs2", [128, 512], F32, st)
                rtmp2 = sb("rtmp2", [128, 512], F32, st)

                def a0_n(c):
                    i2 = c % 2
                    b.ld(xt[i2][:], chunk_rows(src, c), [xt[i2]])
                    b.ld(cs[i2][:, 0:32], chunk_rows(cos_d, c), [cs[i2]])
                    b.ld(cs[i2][:, 32:64], chunk_rows(sin_d, c), [cs[i2]])
                    norm_to_hT(xt[i2][:], xt[i2], gpre, junk, ss, hb, hT[i2][:], hT[i2], PTs[1])

                def a0_r(c):
                    i2 = c % 2
                    hTc = hT[i2]
                    PT = PTs[0]
                    pi = [0]

                    def proj(c0, ncol):
                        Pq = P[pi[0] % 6]
                        pi[0] += 1
                        for k in range(8):
                            b.mm(Pq[:, 0:ncol], hTc[:, k, :], Wi[:, k, c0:c0 + ncol], k == 0, k == 7, [hTc, Wi], [Pq])
                        return Pq
                    Pq = proj(2592, 512)
                    b.cp("act", qs[:], Pq[:], [Pq], [qs])
                    Pq = proj(3104, 512)
                    b.act(qs2[:], Pq[:], AF.Copy, [Pq], [qs2], scale=0.125)
                    rotary("dve", qr, qs[:], qs, cs[i2], 8, rtmp)
                    rotary("pool", kro[i2], qs2[:], qs2, cs[i2], 8, rtmp2)
                    for nt in range(2):
                        Pq = proj(nt * 512, 512)
                        b.act(zo[i2][:, nt * 512:(nt + 1) * 512], Pq[:], AF.Silu, [Pq], [zo[i2]])
                    b.st(chunk_rows(sz_d, c), zo[i2][:], [zo[i2]])
                    xp = xpre[c % 3]
                    for q4 in range(3):
                        Pq = P[pi[0] % 6]
                        pi[0] += 1
                        for cbi in range(4):
                            cb = q4 * 4 + cbi
                            for k in range(8):
                                b.mm(Pq[:, cbi * 128:(cbi + 1) * 128], Wi[:, k, 1024 + cb * 128:1024 + (cb + 1) * 128],
                                     hTc[:, k, :], k == 0, k == 7, [hTc, Wi], [Pq])
                        b.cp("dve", xp[:, q4 * 4:(q4 + 1) * 4, 2:130], Pq[:].rearrange("p (a t) -> p a t", a=4), [Pq], [xp])
                    if c > 0:
                        xl = xpre[(c - 1) % 3]
                        if c % SEGCH != 0:
                            b.cp("dve", xl[:, :, 130:132], xp[:, :, 2:4], [xp], [xl])
                        elif c == SEGCH:
                            b.ts("dve", xl[:, :, 130:132], xp[:, :, 2:4], link[:, 0:1], None, ALU.mult, None, [xp, link], [xl])
                        else:
                            b.ms("dve", xl[:, :, 130:132], 0.0, [xl])
                    if c + 1 < NCH:
                        xn = xpre[(c + 1) % 3]
                        if (c + 1) % SEGCH != 0:
                            b.cp("dve", xn[:, :, 0:2], xp[:, :, 128:130], [xp], [xn])
                        elif c + 1 == SEGCH:
                            b.ts("dve", xn[:, :, 0:2], xp[:, :, 128:130], link[:, 0:1], None, ALU.mult, None, [xp, link], [xn])
                        else:
                            b.ms("dve", xn[:, :, 0:2], 0.0, [xn])
                    else:
                        b.ms("dve", xp[:, :, 130:132], 0.0, [xp])
                    Pq = proj(2560, 32)
                    b.cp("dve", dto[i2][:], Pq[:, 0:32], [Pq], [dto[i2]])
                    b.st(chunk_rows(dtr_d, c), dto[i2][:], [dto[i2]])
                    for nt in range(2):
                        Pq = proj(3616 + nt * 512, 512)
                        b.cp("act" if nt == 0 else "dve", vo[i2][:, nt * 512:(nt + 1) * 512], Pq[:], [Pq], [vo[i2]])
                    b.st(chunk_rows(v0_d, c), vo[i2][:], [vo[i2]])
                    for nt in range(2):
                        Pq = proj(4640 + nt * 512, 512)
                        b.act(go[i2][:, nt * 512:(nt + 1) * 512], Pq[:], AF.Silu, [Pq], [go[i2]])
                    b.st(chunk_rows(sg_d, c), go[i2][:], [go[i2]])
                    qf = qr[:].rearrange("p h d -> p (h d)")
                    for k in range(4):
                        b.tr(PT[:, k * 128:(k + 1) * 128], qf[:, k * 128:(k + 1) * 128], idb[:], [qr, idb], [PT])
                    b.cp("act", qTs[i2][:], PT[:, 0:512].rearrange("p (k t) -> p k t", k=4), [PT], [qTs[i2]])
                    b.st(qT0_d[c], qTs[i2][:], [qTs[i2]])
                    kf = kro[i2][:].rearrange("p h d -> p (h d)")
                    b.st(chunk_rows(k0_d, c), kf, [kro[i2]])
                    for k in range(4):
                        b.tr(PT[:, 512 + k * 128:512 + (k + 1) * 128], kf[:, k * 128:(k + 1) * 128], idb[:],
                             [kro[i2], idb], [PT])
                    b.cp("act", kTs[i2][:], PT[:, 512:1024].rearrange("p (k t) -> p k t", k=4), [PT], [kTs[i2]])
                    b.st(kT0_d[c], kTs[i2][:], [kTs[i2]])
                    return pi

                def a0_conv(c, pi):
                    i2 = c % 2
                    xp = xpre[c % 3]
                    PT = PTs[0]
                    for q4 in range(3):
                        Pq = P[pi[0] % 6]
                        pi[0] += 1
                        for cbi in range(4):
                            cb = q4 * 4 + cbi
                            for j in range(5):
                                b.mm(Pq[:, cbi * 128:(cbi + 1) * 128], Wd[:, cb, j, :], xp[:, cb, j:j + 128], j == 0, j == 4,
                                     [Wd, xp], [Pq])
                        for cbi in range(4):
                            cb = q4 * 4 + cbi
                            if cb < 8:
                                b.act(xsT[:, cb, :], Pq[:, cbi * 128:(cbi + 1) * 128], AF.Silu, [Pq, cbT], [xsT],
                                      bias=cbT[:, cb:cb + 1])
                            else:
                                b.act(bcT[i2][:, cb - 8, :], Pq[:, cbi * 128:(cbi + 1) * 128], AF.Silu, [Pq, cbT], [bcT[i2]],
                                      bias=cbT[:, cb:cb + 1])
                    b.st(bct_d[c], bcT[i2][:], [bcT[i2]])
                    for half in range(2):
                        Pq = P[pi[0] % 6]
                        pi[0] += 1
                        for e4 in range(4):
                            cb = half * 4 + e4
                            b.tr(Pq[:, e4 * 128:(e4 + 1) * 128], xsT[:, cb, :], cst[:, 0, :], [xsT, cst], [Pq])
                        b.cp("act", xso[i2][:, half * 512:(half + 1) * 512], Pq[:], [Pq], [xso[i2]])
                    b.st(chunk_rows(xs_d, c), xso[i2][:], [xso[i2]])
                    for g in range(2):
                        b.tr(PT[:, 512 + g * 128:512 + (g + 1) * 128], bcT[i2][:, g, :], idb[:], [bcT[i2], idb], [PT])
                    b.cp("act", btk[i2][:], PT[:, 512:768], [PT], [btk[i2]])
                    b.st(chunk_rows(btok_d, c), btk[i2][:], [btk[i2]])

                def a0_r2(c):
                    pi = a0_r(c)
                    if c > 0:
                        a0_conv(c - 1, pi)
                    if c == len(crange("A")) - 1:
                        a0_conv(c, pi)

                b.two_stage(a0_n, a0_r2, crange("A"))
            S.barrier()
            with contextlib.ExitStack() as st:
                P, PTs = alloc_psum(st, 7, 1)
                PT = PTs[0]
                prm = sb("prm", [128, 96], F32, st)
                b.ld(prm[:, 0:32], dtb_d[0].rearrange("a b -> (a b)").partition_broadcast(128), [prm])
                b.ld(prm[:, 32:64], alog_d[0].rearrange("a b -> (a b)").partition_broadcast(128), [prm])
                b.ld(prm[:, 64:80], dsk_d[0].partition_broadcast(128), [prm])
                b.ld(prm[:, 80:96], rdec_d[0].rearrange("a b -> (a b)").partition_broadcast(128), [prm])
                b.act(prm[:, 32:64], prm[:, 32:64], AF.Exp, [prm], [prm])
                b.ts("dve", prm[:, 32:64], prm[:, 32:64], -1.0, None, ALU.mult, None, [prm], [prm])
                b.act(prm[:, 80:96], prm[:, 80:96], AF.Exp, [prm], [prm], scale=-1.0)
                b.act(prm[:, 80:96], prm[:, 80:96], AF.Ln, [prm], [prm], bias=1.0)
                b.ts("dve", prm[:, 80:96], prm[:, 80:96], -1.0, None, ALU.mult, None, [prm], [prm])
                dtbias, avec, dskip, lg = prm[:, 0:32], prm[:, 32:64], prm[:, 64:80], prm[:, 80:96]
                Dfb = sb("Dfb", [128, 8, 128], F32, st)
                Dtmp = sb("Dtmp", [128, 8, 128], F32, st)
                b.tt("dve", Dfb[:], RDp.unsqueeze(1).to_broadcast([128, 8, 128]),
                     prm[:, 80:88].unsqueeze(2).to_broadcast([128, 8, 128]), ALU.mult, [cst, prm], [Dfb])
                b.act(Dfb[:], Dfb[:], AF.Exp, [Dfb], [Dfb])
                b.tt("dve", Dfb[:], Dfb[:], Um.unsqueeze(1).to_broadcast([128, 8, 128]), ALU.mult, [Dfb, cst], [Dfb])
                b.tt("dve", Dtmp[:], RDn.unsqueeze(1).to_broadcast([128, 8, 128]),
                     prm[:, 88:96].unsqueeze(2).to_broadcast([128, 8, 128]), ALU.mult, [cst, prm], [Dtmp])
                b.act(Dtmp[:], Dtmp[:], AF.Exp, [Dtmp], [Dtmp])
                b.tt("dve", Dtmp[:], Dtmp[:], Lm.unsqueeze(1).to_broadcast([128, 8, 128]), ALU.mult, [Dtmp, cst], [Dtmp])
                b.tt("dve", Dfb[:], Dfb[:], Dtmp[:], ALU.add, [Dfb, Dtmp], [Dfb])
                rc = sb("rc", [128, 32], F32, st)
                b.ts("dve", rc[:, 0:8], prm[:, 80:88], POS[:, 0:1], None, ALU.mult, None, [prm, cst], [rc])
                b.ts("dve", rc[:, 8:16], prm[:, 88:96], POS[:, 1:2], None, ALU.mult, None, [prm, cst], [rc])
                b.ts("dve", rc[:, 16:32], prm[:, 80:96], 128.0, None, ALU.mult, None, [prm], [rc])
                b.act(rc[:], rc[:], AF.Exp, [rc], [rc])

                xbcs = sb("xbcs", [128, 1024], F32, st)
                dtr = sb("dtr", [128, 32], F32, st)
                dts = sb("dts", [128, 32], F32, st)
                dt2 = sb("dt2", [128, 32], F32, st)
                la = sb("la", [128, 32], F32, st)
                csb = sb("csb", [128, 64], F32, st)
                scs = [sb(f"scs{i}", [128, 32], F32, st) for i in range(2)]
                wte = sb("wte", [128, 32], F32, st)
                dec = [sb(f"dec{i}", [128, 32], F32, st) for i in range(2)]
                xdt = sb("xdt", [128, 2, 1024], BF16, st)
                xw = [sb(f"xw{i}", [128, 2, 1024], BF16, st) for i in range(2)]
                bcb = [sb(f"bcb{i}", [128, 256], BF16, st) for i in range(2)]
                BCT = [sb(f"BCT{i}", [128, 4, 128], BF16, st) for i in range(2)]
                cbm = sb("cbm", [128, 4, 128], F32, st)
                rhsq = [sb(f"rhsq{i}", [128, 4, 128], F32, st) for i in range(8)]
                DT = [sb(f"DT{i}", [128, 512], BF16, st) for i in range(2)]
                M = sb("M", [128, 2, 16, 128], BF16, st)
                ydt = sb("ydt", [128, 1024], F32, st)
                ydo = [sb(f"ydo{i}", [128, 1024], F32, st) for i in range(2)]
                Sfo = [sb(f"Sfo{i}", [128, 1024], F32, st) for i in range(2)]
                hbs = sb("hbs", [128, 1024], F32, st)
                hbo = [sb(f"hbo{i}", [128, 1024], BF16, st) for i in range(2)]
                qTl = sb("qTl", [128, 4, 128], BF16, st)
                kTl = sb("kTl", [128, 4, 128], BF16, st)
                kl = sb("kl", [128, 512], BF16, st)
                vl = sb("vl", [128, 1024], BF16, st)
                SM = sb("SM", [128, 8, 128], BF16, st)
                kd = sb("kd", [128, 2, 512], BF16, st)
                yio = [sb(f"yio{i}", [128, 1024], F32, st) for i in range(2)]
                Rfo = [sb(f"Rfo{i}", [128, 512], F32, st) for i in range(2)]
                rbs = sb("rbs", [128, 512], F32, st)
                rbo = [sb(f"rbo{i}", [128, 512], BF16, st) for i in range(2)]
                b.ms("dve", hbs[:], 0.0, [hbs])
                b.ms("dve", rbs[:], 0.0, [rbs])

                def b1_h1(c):
                    i2 = c % 2
                    first_of_seg = (c % SEGCH == 0)
                    last_of_seg = (c % SEGCH == SEGCH - 1)
                    b.ld(xbcs[:], chunk_rows(xs_d, c), [xbcs])
                    xs3 = xbcs[:, 0:1024].rearrange("p (e d) -> p e d", e=16)
                    b.ld(dtr[:], chunk_rows(dtr_d, c), [dtr])
                    b.tt("dve", dtr[:], dtr[:], dtbias, ALU.add, [dtr, prm], [dtr])
                    b.ts("dve", dt2[:], dtr[:], -1.0, None, ALU.mult, None, [dtr], [dt2])
                    b.tt("dve", dt2[:], dt2[:], dtr[:], ALU.max, [dtr, dt2], [dt2])
                    b.act(dt2[:], dt2[:], AF.Exp, [dt2], [dt2], scale=-1.0)
                    b.act(dt2[:], dt2[:], AF.Ln, [dt2], [dt2], bias=1.0)
                    b.ts("dve", dts[:], dtr[:], 0.0, None, ALU.max, None, [dtr], [dts])
                    b.tt("dve", dts[:], dts[:], dt2[:], ALU.add, [dts, dt2], [dts])
                    b.tt("dve", la[:], dts[:], avec, ALU.mult, [dts, prm], [la])
                    pc = P[6]
                    b.mm(pc[:, 0:16], Um, la[:, 0:16], True, True, [cst, la], [pc])
                    b.mm(pc[:, 16:32], Lm, la[:, 16:32], True, True, [cst, la], [pc])
                    b.mm(pc[:, 32:64], ONES, la[:, 0:32], True, True, [cst, la], [pc])
                    b.cp("act", csb[:], pc[:, 0:64], [pc], [csb])
                    sc = scs[i2]
                    b.act(sc[:], csb[:, 0:32], AF.Exp, [csb], [sc])
                    b.st(chunk_rows(sc_d, c), sc[:], [sc])
                    b.act(dec[i2][:], csb[:, 32:64], AF.Exp, [csb], [dec[i2]])
                    b.tt("dve", wte[:], csb[:, 32:64], csb[:, 0:32], ALU.subtract, [csb], [wte])
                    b.act(wte[:], wte[:], AF.Exp, [wte], [wte])
                    b.tt("dve", wte[:], wte[:], dts[:], ALU.mult, [wte, dts], [wte])
                    for d in range(2):
                        b.tt("dve", xdt[:, d, :].rearrange("p (e d) -> p e d", e=16), xs3,
                             dts[:, d * 16:(d + 1) * 16].unsqueeze(2).to_broadcast([128, 16, 64]), ALU.mult,
                             [xbcs, dts], [xdt])
                        b.tt("dve", xw[i2][:, d, :].rearrange("p (e d) -> p e d", e=16), xs3,
                             wte[:, d * 16:(d + 1) * 16].unsqueeze(2).to_broadcast([128, 16, 64]), ALU.mult,
                             [xbcs, wte], [xw[i2]])
                    bct = BCT[i2]
                    b.ld(bct[:], bct_d[c], [bct])
                    for g in range(2):
                        b.mm(pc[:, 128 + g * 128:128 + (g + 1) * 128], bct[:, g, :], bct[:, 2 + g, :], True, True,
                             [bct], [pc])
                    pcb = pc[:, 128:384].rearrange("p (g l) -> p g l", g=2)
                    b.tt("dve", cbm[:, 0:2, :], pcb, Um.unsqueeze(1).to_broadcast([128, 2, 128]), ALU.mult,
                         [pc, cst], [cbm])
                    b.tt("dve", cbm[:, 2:4, :], pcb, Lm.unsqueeze(1).to_broadcast([128, 2, 128]), ALU.mult,
                         [pc, cst], [cbm])
                    u = 0
                    for d in range(2):
                        tri = Um if d == 0 else Lm
                        Amat = Af if d == 0 else Ab
                        for q4 in range(4):
                            rq = rhsq[d * 4 + q4]
                            for e4 in range(4):
                                col = d * 16 + q4 * 4 + e4
                                b.act(rq[:, e4, :], tri, AF.Copy, [cst, la], [rq], scale=la[:, col:col + 1])
                        for q4 in range(4):
                            Pq = P[u % 2]
                            dtt = DT[u % 2]
                            u += 1
                            rq = rhsq[d * 4 + q4]
                            b.mm(Pq[:], Amat, rq[:].rearrange("p a b -> p (a b)"), True, True,
                                 [cst, rq], [Pq])
                            b.act(dtt[:], Pq[:], AF.Exp, [Pq], [dtt])
                            g = q4 // 2
                            b.tt("dve", M[:, d, q4 * 4:(q4 + 1) * 4, :], dtt[:].rearrange("p (a b) -> p a b", a=4),
                                 cbm[:, 2 * d + g, :].unsqueeze(1).to_broadcast([128, 4, 128]), ALU.mult,
                                 [dtt, cbm], [M])
                    for e in range(16):
                        Pq = P[2 + e // 8]
                        sl = slice((e % 8) * 64, (e % 8) * 64 + 64)
                        b.mm(Pq[:, sl], M[:, 0, e, :], xdt[:, 0, e * 64:(e + 1) * 64], True, False, [M, xdt], [Pq])
                        b.mm(Pq[:, sl], M[:, 1, e, :], xdt[:, 1, e * 64:(e + 1) * 64], False, True, [M, xdt], [Pq])
                    b.tt("dve", ydt[:].rearrange("p (e d) -> p e d", e=16), xs3,
                         dskip.unsqueeze(2).to_broadcast([128, 16, 64]), ALU.mult, [xbcs, prm], [ydt])
                    b.tt("dve", ydo[i2][:, 0:512], P[2][:], ydt[:, 0:512], ALU.add, [P[2], ydt], [ydo[i2]])
                    b.tt("dve", ydo[i2][:, 512:1024], P[3][:], ydt[:, 512:1024], ALU.add, [P[3], ydt], [ydo[i2]])
                    b.st(chunk_rows(yd_d, c), ydo[i2][:], [ydo[i2]])

                def b1_h2(c):
                    i2 = c % 2
                    first_of_seg = (c % SEGCH == 0)
                    last_of_seg = (c % SEGCH == SEGCH - 1)
                    bt = bcb[i2]
                    b.ld(bt[:], chunk_rows(btok_d, c), [bt])
                    for g in range(2):
                        b.mm(P[4 + g][:], bt[:, g * 128:(g + 1) * 128], xw[i2][:, 0, g * 512:(g + 1) * 512], True, True,
                             [bcb[i2], xw[i2]], [P[4 + g]])
                    for g in range(2):
                        b.cp("act", Sfo[i2][:, g * 512:(g + 1) * 512], P[4 + g][:], [P[4 + g]], [Sfo[i2]])
                    b.st(Sf_d[c], Sfo[i2][:], [Sfo[i2]])
                    b.st(decf_d[c], dec[i2][:, 0:16], [dec[i2]])
                    if last_of_seg:
                        if c == SEGCH - 1:
                            b.ts("dve", hbs[:], hbs[:], link[:, 0:1], None, ALU.mult, None, [hbs, link], [hbs])
                            b.ts("dve", rbs[:], rbs[:], link[:, 0:1], None, ALU.mult, None, [rbs, link], [rbs])
                        else:
                            b.ms("dve", hbs[:], 0.0, [hbs])
                            b.ms("dve", rbs[:], 0.0, [rbs])
                    b.cp("act", hbo[i2][:], hbs[:], [hbs], [hbo[i2]])
                    b.st(Hb_d[c], hbo[i2][:], [hbo[i2]])
                    for g in range(2):
                        b.mm(P[4 + g][:], bt[:, g * 128:(g + 1) * 128], xw[i2][:, 1, g * 512:(g + 1) * 512], True, True,
                             [bcb[i2], xw[i2]], [P[4 + g]])
                    b.tt("dve", hbs[:].rearrange("p (e d) -> p e d", e=16), hbs[:].rearrange("p (e d) -> p e d", e=16),
                         dec[i2][:, 16:32].unsqueeze(2).to_broadcast([128, 16, 64]), ALU.mult, [hbs, dec[i2]], [hbs])
                    for g in range(2):
                        b.tt("dve", hbs[:, g * 512:(g + 1) * 512], hbs[:, g * 512:(g + 1) * 512], P[4 + g][:], ALU.add,
                             [hbs, P[4 + g]], [hbs])
                    b.ld(qTl[:], qT0_d[c], [qTl])
                    b.ld(kTl[:], kT0_d[c], [kTl])
                    b.ld(kl[:], chunk_rows(k0_d, c), [kl])
                    b.ld(vl[:], chunk_rows(v0_d, c), [vl])
                    for par in range(2):
                        for hp in range(4):
                            base = par * 64
                            Pq = P[4 + par]
                            b.mm(Pq[:, hp * 128:(hp + 1) * 128], kTl[base:base + 64, hp, :], qTl[base:base + 64, hp, :],
                                 True, True, [kTl, qTl], [Pq])
                    for par in range(2):
                        b.tt("dve", SM[:, par:8:2, :], P[4 + par][:].rearrange("p (a b) -> p a b", a=4),
                             Dfb[:, par:8:2, :], ALU.mult, [P[4 + par], Dfb], [SM])
                    for h in range(8):
                        Pq = P[4 + h // 4]
                        b.mm(Pq[:, (h % 4) * 128:(h % 4 + 1) * 128], SM[:, h, :], vl[:, h * 128:(h + 1) * 128], True, True,
                             [SM, vl], [Pq])
                    for hh in range(2):
                        b.cp("act", yio[i2][:, hh * 512:(hh + 1) * 512], P[4 + hh][:], [P[4 + hh]], [yio[i2]])
                    b.st(chunk_rows(yi_d, c), yio[i2][:], [yio[i2]])
                    for d in range(2):
                        b.tt("dve", kd[:, d, :].rearrange("p (h d) -> p h d", h=8), kl[:].rearrange("p (h d) -> p h d", h=8),
                             rc[:, d * 8:(d + 1) * 8].unsqueeze(2).to_broadcast([128, 8, 64]), ALU.mult, [kl, rc], [kd])
                    def rstates(d, Pq):
                        for h in range(8):
                            hp, base = h // 2, (h % 2) * 64
                            b.mm(Pq[base:base + 64, hp * 128:(hp + 1) * 128], kd[:, d, h * 64:(h + 1) * 64],
                                 vl[:, h * 128:(h + 1) * 128], True, True, [kd, vl], [Pq])
                    rstates(0, P[4])
                    b.cp("act", Rfo[i2][:], P[4][:], [P[4]], [Rfo[i2]])
                    b.st(Rf_d[c], Rfo[i2][:], [Rfo[i2]])
                    b.cp("act", rbo[i2][:], rbs[:], [rbs], [rbo[i2]])
                    b.st(Rb_d[c], rbo[i2][:], [rbo[i2]])
                    rstates(1, P[5])
                    for par in range(2):
                        sl = slice(par * 64, par * 64 + 64)
                        gcol = rc[sl, 24 + par:32:2]
                        b.tt("dve", rbs[sl, :].rearrange("p (a v) -> p a v", a=4), rbs[sl, :].rearrange("p (a v) -> p a v", a=4),
                             gcol.unsqueeze(2).to_broadcast([64, 4, 128]), ALU.mult, [rbs, rc], [rbs])
                    b.tt("dve", rbs[:], rbs[:], P[5][:], ALU.add, [rbs, P[5]], [rbs])

                b.two_stage(b1_h1, b1_h2, (range(NCH - 1, -1, -1) if LOOPN.get("B1", NCH) == NCH else range(LOOPN["B1"] - 1, -1, -1)))
            S.barrier()
            with contextlib.ExitStack() as st:
                P, PTs = alloc_psum(st, 7, 1)
                PT = PTs[0]
                Wo = sb("Wo0", [128, 16, 1024], BF16, st)
                b.ldw(Wo, evout_d[0], 16, 1024)
                gpost = sb("gpost", [128, 1024], F32, st)
                b.ld(gpost[:], g_mix_post[0].partition_broadcast(128), [gpost])
                snw = sb("snw", [128, 1024], F32, st)
                b.ld(snw[:], snw_d[0].partition_broadcast(128), [snw])
                rgn = sb("rgn", [128, 1024], F32, st)
                b.ld(rgn[:], rgn_d[0].partition_broadcast(128), [rgn])
                prm = sb("prm2", [128, 16], F32, st)
                b.ld(prm[:, 0:16], rdec_d[0].rearrange("a b -> (a b)").partition_broadcast(128), [prm])
                b.act(prm[:], prm[:], AF.Exp, [prm], [prm], scale=-1.0)
                b.act(prm[:], prm[:], AF.Ln, [prm], [prm], bias=1.0)
                b.ts("dve", prm[:], prm[:], -1.0, None, ALU.mult, None, [prm], [prm])
                rc = sb("rc2", [128, 32], F32, st)
                b.ts("dve", rc[:, 0:8], prm[:, 0:8], POS[:, 2:3], None, ALU.mult, None, [prm, cst], [rc])
                b.ts("dve", rc[:, 8:16], prm[:, 8:16], POS[:, 3:4], None, ALU.mult, None, [prm, cst], [rc])
                b.ts("dve", rc[:, 16:32], prm[:, 0:16], 128.0, None, ALU.mult, None, [prm], [rc])
                b.act(rc[:], rc[:], AF.Exp, [rc], [rc])

                ydl = [sb(f"ydl{i}", [128, 1024], F32, st) for i in range(2)]
                yil = [sb(f"yil{i}", [128, 1024], F32, st) for i in range(2)]
                CTl = [sb(f"CTl{i}", [128, 2, 128], BF16, st) for i in range(2)]
                scl = [sb(f"scl{i}", [128, 32], F32, st) for i in range(2)]
                qTl = [sb(f"qTl{i}", [128, 4, 128], BF16, st) for i in range(2)]
                Hbl = [sb(f"Hbl{i}", [128, 1024], BF16, st) for i in range(2)]
                Rbl = [sb(f"Rbl{i}", [128, 512], BF16, st) for i in range(2)]
                Sfl = [sb(f"Sfl{i}", [128, 1024], F32, st) for i in range(2)]
                dfl = [sb(f"dfl{i}", [128, 16], F32, st) for i in range(2)]
                Rfl = [sb(f"Rfl{i}", [128, 512], F32, st) for i in range(2)]
                szl = [sb(f"szl{i}", [128, 1024], F32, st) for i in range(2)]
                sgl = [sb(f"sgl{i}", [128, 1024], F32, st) for i in range(2)]
                xt = [sb(f"xt{i}", [128, 1024], F32, st) for i in range(2)]
                hfs = sb("hfs", [128, 1024], F32, st)
                hfb = sb("hfb", [128, 1024], BF16, st)
                rfs = sb("rfs", [128, 512], F32, st)
                rfb = sb("rfb", [128, 512], BF16, st)
                t1 = sb("t1", [128, 1024], F32, st)
                ys_ = [sb(f"ys{i}", [128, 1024], F32, st) for i in range(2)]
                yr_ = [sb(f"yr{i}", [128, 1024], F32, st) for i in range(2)]
                st8 = sb("st8", [128, 24], F32, st)
                mix = sb("mix", [128, 2048], BF16, st)
                mixT = sb("mixT", [128, 16, 128], BF16, st)
                junk = sb("junk", [128, 1024], BF16, st)
                jf = sb("jf", [128, 1024], F32, st)
                ss = sb("ss", [128, 2], F32, st)
                tn = sb("tn", [128, 1024], F32, st)
                xo = [sb(f"xo{i}", [128, 1024], F32, st) for i in range(2)]
                b.ms("dve", hfs[:], 0.0, [hfs])
                b.ms("dve", rfs[:], 0.0, [rfs])
                def b2_g1(c):
                    i2 = c % 2
                    ys = ys_[i2]
                    yr = yr_[i2]
                    b.ld(ydl[i2][:], chunk_rows(yd_d, c), [ydl[i2]])
                    b.ld(yil[i2][:], chunk_rows(yi_d, c), [yil[i2]])
                    b.ld(CTl[i2][:], bct_d[c][:, 2:4, :], [CTl[i2]])
                    b.ld(scl[i2][:], chunk_rows(sc_d, c), [scl[i2]])
                    b.ld(qTl[i2][:], qT0_d[c], [qTl[i2]])
                    b.ld(Hbl[i2][:], Hb_d[c], [Hbl[i2]])
                    b.ld(Rbl[i2][:], Rb_d[c], [Rbl[i2]])
                    b.ld(Sfl[i2][:], Sf_d[c], [Sfl[i2]])
                    b.ld(dfl[i2][:], decf_d[c], [dfl[i2]])
                    b.ld(Rfl[i2][:], Rf_d[c], [Rfl[i2]])
                    b.ld(szl[i2][:], chunk_rows(sz_d, c), [szl[i2]])
                    b.ld(sgl[i2][:], chunk_rows(sg_d, c), [sgl[i2]])
                    b.ld(xt[i2][:], chunk_rows(src, c), [xt[i2]])
                    if c % SEGCH == 0 and c > 0:
                        if c == SEGCH:
                            b.ts("dve", hfs[:], hfs[:], link[:, 0:1], None, ALU.mult, None, [hfs, link], [hfs])
                            b.ts("dve", rfs[:], rfs[:], link[:, 0:1], None, ALU.mult, None, [rfs, link], [rfs])
                        else:
                            b.ms("dve", hfs[:], 0.0, [hfs])
                            b.ms("dve", rfs[:], 0.0, [rfs])
                    b.cp("act", hfb[:], hfs[:], [hfs], [hfb])
                    b.cp("act", rfb[:], rfs[:], [rfs], [rfb])
                    for g in range(2):
                        b.mm(P[g][:], CTl[i2][:, g, :], hfb[:, g * 512:(g + 1) * 512], True, True, [CTl[i2], hfb], [P[g]])
                        b.mm(P[2 + g][:], CTl[i2][:, g, :], Hbl[i2][:, g * 512:(g + 1) * 512], True, True,
                             [CTl[i2], Hbl[i2]], [P[2 + g]])
                    for g in range(2):
                        for d in range(2):
                            Pq = P[2 * d + g]
                            b.tt("dve", t1[:, g * 512:(g + 1) * 512].rearrange("p (e d) -> p e d", e=8),
                                 Pq[:].rearrange("p (e d) -> p e d", e=8),
                                 scl[i2][:, d * 16 + g * 8:d * 16 + (g + 1) * 8].unsqueeze(2).to_broadcast([128, 8, 64]),
                                 ALU.mult, [Pq, scl[i2]], [t1])
                            src_y = ydl[i2] if d == 0 else ys
                            b.tt("dve", ys[:, g * 512:(g + 1) * 512], t1[:, g * 512:(g + 1) * 512],
                                 src_y[:, g * 512:(g + 1) * 512], ALU.add, [t1, src_y], [ys])
                    b.tt("dve", hfs[:].rearrange("p (e d) -> p e d", e=16), hfs[:].rearrange("p (e d) -> p e d", e=16),
                         dfl[i2][:].unsqueeze(2).to_broadcast([128, 16, 64]), ALU.mult, [hfs, dfl[i2], hfb], [hfs])
                    b.tt("dve", hfs[:], hfs[:], Sfl[i2][:], ALU.add, [hfs, Sfl[i2]], [hfs])
                    for d, (Pa, Pb2), rsrc in ((0, (P[4], P[0]), rfb), (1, (P[1], P[2]), Rbl[i2])):
                        t1v = t1[:].rearrange("p (h v) -> p h v", h=8)
                        for par, Pq in ((0, Pa), (1, Pb2)):
                            base = par * 64
                            for hp in range(4):
                                b.mm(Pq[:, hp * 128:(hp + 1) * 128], qTl[i2][base:base + 64, hp, :],
                                     rsrc[base:base + 64, hp * 128:(hp + 1) * 128], True, True, [qTl[i2], rsrc], [Pq])
                            b.tt("dve", t1v[:, par:8:2, :], Pq[:].rearrange("p (a v) -> p a v", a=4),
                                 rc[:, d * 8 + par:d * 8 + 8:2].unsqueeze(2).to_broadcast([128, 4, 128]),
                                 ALU.mult, [Pq, rc], [t1])
                        src_y = yil[i2] if d == 0 else yr
                        b.tt("dve", yr[:], t1[:], src_y[:], ALU.add, [t1, src_y], [yr])
                    for par in range(2):
                        sl = slice(par * 64, par * 64 + 64)
                        gcol = rc[sl, 16 + par:24:2]
                        b.tt("dve", rfs[sl, :].rearrange("p (a v) -> p a v", a=4), rfs[sl, :].rearrange("p (a v) -> p a v", a=4),
                             gcol.unsqueeze(2).to_broadcast([64, 4, 128]), ALU.mult, [rfs, rc, rfb], [rfs])
                    b.tt("dve", rfs[:], rfs[:], Rfl[i2][:], ALU.add, [rfs, Rfl[i2]], [rfs])

                def b2_g2(c):
                    i2 = c % 2
                    ys = ys_[i2]
                    yr = yr_[i2]
                    b.tt("dve", ys[:], ys[:], szl[i2][:], ALU.mult, [ys, szl[i2]], [ys])
                    for g in range(2):
                        b.act(junk[:, g * 512:(g + 1) * 512], ys[:, g * 512:(g + 1) * 512], AF.Square, [ys], [junk, st8],
                              accum=st8[:, g:g + 1])
                    b.act(st8[:, 0:2], st8[:, 0:2], AF.Ln, [st8], [st8], scale=1.0 / 512, bias=EPS)
                    b.act(st8[:, 0:2], st8[:, 0:2], AF.Exp, [st8], [st8], scale=-0.5)
                    for g in range(2):
                        b.stt(mix[:, g * 512:(g + 1) * 512], ys[:, g * 512:(g + 1) * 512], st8[:, g:g + 1],
                              snw[:, g * 512:(g + 1) * 512], ALU.mult, ALU.mult, [ys, st8, snw], [mix])
                    yr3 = yr[:].rearrange("p (h v) -> p h v", h=8)
                    b.red(st8[:, 8:16], yr3, [yr], [st8])
                    b.act(jf[:], yr[:], AF.Square, [yr], [jf])
                    b.red(st8[:, 16:24], jf[:].rearrange("p (h v) -> p h v", h=8), [jf], [st8])
                    b.ts("dve", st8[:, 8:24], st8[:, 8:24], 1.0 / 128, None, ALU.mult, None, [st8], [st8])
                    b.tt("dve", jf[:, 0:8], st8[:, 8:16], st8[:, 8:16], ALU.mult, [st8], [jf])
                    b.tt("dve", st8[:, 16:24], st8[:, 16:24], jf[:, 0:8], ALU.subtract, [st8, jf], [st8])
                    b.act(st8[:, 16:24], st8[:, 16:24], AF.Ln, [st8], [st8], bias=EPS)
                    b.act(st8[:, 16:24], st8[:, 16:24], AF.Exp, [st8], [st8], scale=-0.5)
                    b.tt("dve", yr3, yr3, st8[:, 8:16].unsqueeze(2).to_broadcast([128, 8, 128]), ALU.subtract,
                         [yr, st8], [yr])
                    b.tt("dve", yr3, yr3, st8[:, 16:24].unsqueeze(2).to_broadcast([128, 8, 128]), ALU.mult,
                         [yr, st8], [yr])
                    b.tt("dve", yr[:], yr[:], rgn[:], ALU.mult, [yr, rgn], [yr])
                    b.tt("dve", mix[:, 1024:2048], yr[:], sgl[i2][:], ALU.mult, [yr, sgl[i2]], [mix])
                    for half in range(2):
                        for k in range(8):
                            kk = half * 8 + k
                            b.tr(PT[:, k * 128:(k + 1) * 128], mix[:, kk * 128:(kk + 1) * 128], idb[:], [mix, idb], [PT])
                        b.cp("act", mixT[:, half * 8:(half + 1) * 8, :],
                             PT[:, 0:1024].rearrange("p (k t) -> p k t", k=8), [PT], [mixT])
                    for nt, Pq in ((0, P[5]), (1, P[6])):
                        for k in range(16):
                            b.mm(Pq[:], mixT[:, k, :], Wo[:, k, nt * 512:(nt + 1) * 512], k == 0, k == 15, [mixT, Wo], [Pq])
                    post_norm_residual(P[5], P[6], gpost, xt[i2][:], xt[i2], junk, ss, tn, xo[i2], chunk_rows(dst, c))

                b.two_stage(b2_g1, b2_g2, crange("B2"))
            S.barrier()

        hm = sb("hm", [128, 8], F32)
        b.ld(hm[:], cst_d[:, 8, 8:16], [hm])
        b.ts("dve", hm[:, 0:2], hm[:, 4:6], -1.0, 1.0, ALU.mult, ALU.add, [hm], [hm])
        b.ts("dve", hm[:, 0:2], hm[:, 0:2], link[:, 0:1], None, ALU.mult, None, [hm, link], [hm])
        b.tt("dve", hm[:, 6:8], hm[:, 0:2], hm[:, 4:6], ALU.add, [hm], [hm])

        stages = []
        if 0 in layers:
            if do_mix:
                stages.append(("mix0",))
            if do_ffn:
                stages.append(("ffn0",))
        if 1 in layers:
            if do_mix:
                stages.append(("mix1",))
            if do_ffn:
                stages.append(("ffn1",))
        bufs = [xa, xb]
        cur = xin
        S.barrier()
        for si, (sname,) in enumerate(stages):
            dst = yout if si == len(stages) - 1 else bufs[si % 2]
            if sname == "mix0":
                even_layer(cur, dst)
            elif sname == "mix1":
                odd_layer(cur, dst)
            elif sname == "ffn0":
                ffn_loop(0, cur, dst)
            elif sname == "ffn1":
                ffn_loop(1, cur, dst)
            cur = dst
        counts = S.emit()
    return nc, counts


def _consts():
    p = np.arange(128)
    r, l = p[:, None], p[None, :]
    cst = np.zeros((128, 9, 128), np.float32)
    cst[:, 0] = (r == l)
    cst[:, 1] = (r <= l)
    cst[:, 2] = (r >= l)
    cst[:, 3] = (r > l)
    cst[:, 4] = (r < l)
    cst[:, 5] = 1.0
    cst[:, 6] = np.maximum(l - r, 0)
    cst[:, 7] = np.maximum(r - l, 0)
    cst[:, 8, 0] = 127 - p
    cst[:, 8, 1] = p
    cst[:, 8, 2] = p + 1
    cst[:, 8, 3] = 128 - p
    cst[:, 8, 12] = (p != 127)
    cst[:, 8, 13] = (p < 126)
    am = np.zeros((128, 17, 128), np.float32)
    for o in range(17):
        j = o - 8
        d = np.abs(l - r - 128 * j)
        am[:, o, :] = (d <= 64).astype(np.float32) + ((d % 4 == 0) & (d <= 256)) + ((d % 16 == 0) & (d <= 1024))
    return cst, am


def _rope(pos):
    inv_freq = (10000.0 ** (-np.arange(0, 64, 2, dtype=np.float32) / np.float32(64))).astype(np.float32)
    ang = pos.astype(np.float32)[:, None] * inv_freq[None, :]
    return np.cos(ang).astype(np.float32), np.sin(ang).astype(np.float32)


WEIGHT_NAMES = ["norm_mix_pre", "norm_mix_post", "norm_ffn_pre", "norm_ffn_post", "ffn_w1", "ffn_w2",
                "ev_in_proj", "ev_conv_w", "ev_conv_b", "ssd_dt_bias", "ssd_a_log", "ssd_d", "ssd_norm_w",
                "ret_decay", "ret_gn_w", "ev_out_proj", "od_in_proj", "gmlp_norm_w", "gmlp_ws", "gmlp_bs",
                "od_out_proj"]


def make_in_maps(inputs):
    xp = np.asarray(inputs["x_prompt"], np.float32)
    xs = np.asarray(inputs["x_sample"], np.float32)
    cst, am = _consts()
    pos_a = np.concatenate([np.arange(4096), np.arange(2048)])
    pos_b = np.concatenate([np.arange(2048)] * 3)
    ca, sa = _rope(pos_a)
    cb, sb_ = _rope(pos_b)
    w = {k: np.ascontiguousarray(np.asarray(inputs[k], np.float32)) for k in WEIGHT_NAMES}
    maps = []
    for c in range(NCORES):
        if c < 4:
            x = np.concatenate([xp[c], xs[c]], axis=0)
            lk = np.ones((128, 1), np.float32)
            rc, rs = ca, sa
        else:
            i0 = 4 + 3 * (c - 4)
            x = np.concatenate([xs[i0], xs[i0 + 1], xs[i0 + 2]], axis=0)
            lk = np.zeros((128, 1), np.float32)
            rc, rs = cb, sb_
        m = {"xin": np.ascontiguousarray(x), "link": lk, "rcos": rc, "rsin": rs, "cst": cst, "amask": am}
        m.update(w)
        maps.append(m)
    return maps


def gather(results):
    yp = np.zeros((4, 4096, D), np.float32)
    ys = np.zeros((16, 2048, D), np.float32)
    for c in range(NCORES):
        y = np.asarray(results[c]["yout"], np.float32)
        if c < 4:
            yp[c] = y[0:4096]
            ys[c] = y[4096:6144]
        else:
            i0 = 4 + 3 * (c - 4)
            for k in range(3):
                ys[i0 + k] = y[k * 2048:(k + 1) * 2048]
    return yp, ys


_NC_CACHE = {}


def kernel(**inputs):
    if "nc" not in _NC_CACHE:
        _NC_CACHE["nc"] = build_nc()[0]
    nc = _NC_CACHE["nc"]
    maps = make_in_maps(inputs)
    res = run_bass_kernel_spmd(nc, maps, core_ids=list(range(NCORES)))
    return gather(res.results)
```

```python
import contextlib
import numpy as np
import concourse.bass as bass
import concourse.mybir as mybir
from concourse.bass_utils import run_bass_kernel_spmd

F32 = mybir.dt.float32
BF16 = mybir.dt.bfloat16
ALU = mybir.AluOpType
AF = mybir.ActivationFunctionType
AX = mybir.AxisListType

ENGS = ("pe", "act", "dve", "pool", "sp")
DMA_ENGS = ("sp", "act", "pool")
EPOCH = 30000
NDSEM = 8

NCORES = 8
T = 6144
NCH = 48
SEGCH = 16
D = 1024
EPS = 1e-6
LOOPN = {}
USE_SCHED = True
SCHED_W = 64
SCHED_HOP = 120.0
AT_LAG = 8


def crange(name):
    return range(LOOPN.get(name, NCH))


class Tok:
    __slots__ = ("w", "rs")

    def __init__(self):
        self.w = None
        self.rs = []


class Op:
    __slots__ = ("eng", "fn", "deps", "dma", "needed", "seq", "dsem", "dval", "barrier", "snap", "cost", "lat")

    def __init__(self, eng, fn, deps, dma, barrier=False, cost=300.0, lat=0.0):
        self.cost = cost
        self.lat = lat
        self.eng = eng
        self.fn = fn
        self.deps = deps
        self.dma = dma
        self.needed = False
        self.seq = None
        self.dsem = None
        self.dval = None
        self.barrier = barrier
        self.snap = None


class Sched:
    def __init__(self, nc):
        self.nc = nc
        self.ops = []
        self.keep = False
        self.keep_set = set()

    def op(self, eng, fn, reads=(), writes=(), dma=False, cost=300.0, lat=0.0):
        idx = len(self.ops)
        deps = set()
        for t in reads:
            if t.w is not None:
                deps.add(t.w)
        for t in writes:
            if t.w is not None:
                deps.add(t.w)
            for r in t.rs:
                deps.add(r)
        self.ops.append(Op(eng, fn, deps, dma, cost=cost, lat=lat))
        if self.keep:
            self.keep_set.add(idx)
        for t in reads:
            t.rs.append(idx)
        for t in writes:
            t.w = idx
            t.rs = []
        return idx

    def barrier(self):
        for e in ENGS:
            self.ops.append(Op(e, None, set(), False, barrier=True))

    def schedule(self, W=12, hop=120.0):
        ops = self.ops
        n = len(ops)
        fin = [0.0] * n
        done = [False] * n
        order = {e: [] for e in ENGS}
        seg_start = 0
        bounds = []
        i = 0
        while i < n:
            if ops[i].barrier:
                bounds.append((seg_start, i))
                j = i
                while j < n and ops[j].barrier:
                    j += 1
                bounds.append(("barrier", i, j))
                seg_start = j
                i = j
            else:
                i += 1
        bounds.append((seg_start, n))
        for bnd in bounds:
            if bnd[0] == "barrier":
                for k in range(bnd[1], bnd[2]):
                    order[ops[k].eng].append(k)
                    done[k] = True
                continue
            lo, hi = bnd
            if hi <= lo:
                continue
            pend = {e: [] for e in ENGS}
            for k in range(lo, hi):
                pend[ops[k].eng].append(k)
            if lo in self.keep_set:
                for e in ENGS:
                    order[e].extend(pend[e])
                for k in range(lo, hi):
                    done[k] = True
                continue
            pos = {e: 0 for e in ENGS}
            et = {e: 0.0 for e in ENGS}
            remaining = hi - lo
            while remaining:
                best = None
                for e in ENGS:
                    lst = pend[e]
                    cnt = 0
                    p = pos[e]
                    while p < len(lst) and done[lst[p]]:
                        p += 1
                    pos[e] = p
                    q = p
                    while q < len(lst) and cnt < W:
                        k = lst[q]
                        q += 1
                        if done[k]:
                            continue
                        cnt += 1
                        o = ops[k]
                        rdy = 0.0
                        ok = True
                        for d in o.deps:
                            if not done[d]:
                                ok = False
                                break
                            f = fin[d] + (0.0 if ops[d].eng == e and not ops[d].dma else hop)
                            if f > rdy:
                                rdy = f
                        if not ok:
                            continue
                        stt = rdy if rdy > et[e] else et[e]
                        key = (stt, k)
                        if best is None or key < best[0]:
                            best = (key, e, k)
                assert best is not None, "scheduler deadlock"
                (stt, k), e, k = best
                o = ops[k]
                et[e] = stt + o.cost
                fin[k] = stt + o.cost + o.lat
                done[k] = True
                order[e].append(k)
                remaining -= 1
        return order

    def emit(self):
        nc = self.nc
        ops = self.ops
        sched_order = self.schedule(W=SCHED_W, hop=SCHED_HOP) if USE_SCHED else None
        for o in ops:
            for d in o.deps:
                ops[d].needed = True
        cnt = {e: 0 for e in ENGS}
        dcnt = {e: 0 for e in DMA_ENGS}
        if sched_order is not None:
            walk = []
            ptr = {e: 0 for e in ENGS}
            nb = sum(1 for o in ops if o.barrier) // len(ENGS)
            for _ in range(nb + 1):
                for e in ENGS:
                    lst = sched_order[e]
                    p = ptr[e]
                    while p < len(lst) and not ops[lst[p]].barrier:
                        walk.append(lst[p])
                        p += 1
                    ptr[e] = p
                for e in ENGS:
                    lst = sched_order[e]
                    if ptr[e] < len(lst):
                        walk.append(lst[ptr[e]])
                        ptr[e] += 1
            walk_ops = [ops[k] for k in walk]
        else:
            walk_ops = ops
        last = {e: None for e in ENGS}
        for o in walk_ops:
            if o.barrier:
                for e in ENGS:
                    if last[e] is not None:
                        last[e].needed = True
            elif not o.dma:
                last[o.eng] = o
        for o in walk_ops:
            if o.barrier:
                o.snap = (dict(cnt), dict(dcnt))
            elif o.dma:
                i = dcnt[o.eng]
                dcnt[o.eng] += 1
                o.dsem = i % NDSEM
                o.dval = 16 * (i // NDSEM + 1)
                assert o.dval < 60000, "too many DMAs on one queue"
            elif o.needed:
                cnt[o.eng] += 1
                o.seq = cnt[o.eng]
        nep = {e: (cnt[e] + EPOCH - 1) // EPOCH for e in ENGS}
        with contextlib.ExitStack() as st:
            sems = {e: [st.enter_context(nc.semaphore(f"s_{e}_{k}")) for k in range(max(1, nep[e]))]
                    for e in ENGS}
            dsems = {e: [st.enter_context(nc.semaphore(f"d_{e}_{k}")) for k in range(NDSEM)]
                     for e in DMA_ENGS}
            block = st.enter_context(nc.Block())
            if sched_order is not None:
                per_eng = sched_order
            else:
                per_eng = {e: [] for e in ENGS}
                for i, o in enumerate(ops):
                    per_eng[o.eng].append(i)

            def run_engine(ename, engobj):
                seen = {}

                def wait(key, sem, val):
                    if val <= 0 or seen.get(key, 0) >= val:
                        return
                    seen[key] = val
                    engobj.wait_ge(sem, val)

                def wait_all(c_snap, d_snap):
                    for e in ENGS:
                        n = c_snap[e]
                        if n > 0:
                            ep = (n - 1) // EPOCH
                            wait(("c", e, ep), sems[e][ep], n - ep * EPOCH)
                    for e in DMA_ENGS:
                        n = d_snap[e]
                        for k in range(min(n, NDSEM)):
                            lastk = ((n - 1 - k) // NDSEM) * NDSEM + k
                            wait(("d", e, k), dsems[e][k], 16 * (lastk // NDSEM + 1))

                for i in per_eng[ename]:
                    o = ops[i]
                    if o.barrier:
                        wait_all(*o.snap)
                        continue
                    for d in sorted(o.deps):
                        p = ops[d]
                        if ename == "pe" and p.eng == "pe" and not p.dma:
                            continue
                        if p.dma:
                            wait(("d", p.eng, p.dsem), dsems[p.eng][p.dsem], p.dval)
                        else:
                            ep = (p.seq - 1) // EPOCH
                            wait(("c", p.eng, ep), sems[p.eng][ep], p.seq - ep * EPOCH)
                    if o.dma:
                        wait(("d", o.eng, o.dsem), dsems[o.eng][o.dsem], o.dval - 16)
                        ins = o.fn(engobj)
                        ins.then_inc(dsems[o.eng][o.dsem], 16)
                    else:
                        ins = o.fn(engobj)
                        if o.needed:
                            ep = (o.seq - 1) // EPOCH
                            ins.then_inc(sems[o.eng][ep], 1)
                wait_all({e: 0 for e in ENGS}, dcnt)

            @block.tensor
            def _(e):
                run_engine("pe", e)

            @block.scalar
            def _(e):
                run_engine("act", e)

            @block.vector
            def _(e):
                run_engine("dve", e)

            @block.gpsimd
            def _(e):
                run_engine("pool", e)

            @block.sync
            def _(e):
                run_engine("sp", e)
        return {e: len(per_eng[e]) for e in ENGS}


class Tl:
    def __init__(self, h):
        self.h = h
        self.t = Tok()

    def __getitem__(self, k):
        return self.h[k]


class B:
    def __init__(self, nc):
        self.nc = nc
        self.S = Sched(nc)
        self.cap = None

    def _rec(self, eng, fn, R, W, dma=False, cost=300.0, lat=0.0):
        reads = [x.t for x in R]
        writes = [x.t for x in W]
        if self.cap is not None:
            self.cap.append((eng, fn, reads, writes, dma, cost, lat))
        else:
            self.S.op(eng, fn, reads, writes, dma, cost, lat)

    @staticmethod
    def _fd(ap):
        n = 1
        for d in ap.shape[1:]:
            n *= int(d)
        return n

    def _ecost(self, eng, out):
        n = self._fd(out)
        if eng == "act":
            return (200.0 + n) / 1.2
        if eng == "dve":
            return (150.0 + n) / 0.96
        return 150.0 + 2.2 * n

    def capture(self, f, *args):
        old = self.cap
        self.cap = []
        f(*args)
        lst = self.cap
        self.cap = old
        return lst

    def emit_merged(self, la, lb):
        i = j = 0
        while i < len(la) or j < len(lb):
            if j >= len(lb) or (i < len(la) and i * len(lb) <= j * len(la)):
                self.S.op(*la[i])
                i += 1
            else:
                self.S.op(*lb[j])
                j += 1

    def two_stage(self, f1, f2, items):
        prev = None
        for k in items:
            l1 = self.capture(f1, k)
            l2 = self.capture(f2, prev) if prev is not None else []
            self.emit_merged(l2, l1)
            prev = k
        if prev is not None:
            self.emit_merged(self.capture(f2, prev), [])

    def mm(self, out, lhsT, rhs, start, stop, R, W):
        c = max(64, self._fd(rhs)) / 2.4 * (4.0 if lhsT.dtype == F32 else 1.0)
        self._rec("pe", lambda e: e.matmul(out, lhsT=lhsT, rhs=rhs, start=start, stop=stop),
                  R, W, cost=c, lat=60.0)

    def tr(self, out, in_, ident, R, W):
        self._rec("pe", lambda e: e.transpose(out=out, in_=in_, identity=ident),
                  R, W, cost=60.0, lat=60.0)

    def act(self, out, in_, func, R, W, scale=1.0, bias=0.0, accum=None):
        if accum is None:
            self._rec("act", lambda e: e.activation(out=out, in_=in_, func=func, scale=scale, bias=bias),
                      R, W, cost=self._ecost("act", out))
        else:
            self._rec("act", lambda e: e.activation(out=out, in_=in_, func=func, scale=scale, bias=bias,
                                                    accum_out=accum),
                      R, W, cost=self._ecost("act", out) + 80.0)

    def tt(self, eng, out, in0, in1, op, R, W):
        self._rec(eng, lambda e: e.tensor_tensor(out=out, in0=in0, in1=in1, op=op),
                  R, W, cost=self._ecost(eng, out))

    def ts(self, eng, out, in0, s1, s2, op0, op1, R, W):
        if op1 is None:
            self._rec(eng, lambda e: e.tensor_scalar(out=out, in0=in0, scalar1=s1, scalar2=None, op0=op0),
                      R, W, cost=self._ecost(eng, out))
        else:
            self._rec(eng, lambda e: e.tensor_scalar(out=out, in0=in0, scalar1=s1, scalar2=s2, op0=op0, op1=op1),
                      R, W, cost=self._ecost(eng, out))

    def stt(self, out, in0, scalar, in1, op0, op1, R, W):
        self._rec("dve", lambda e: e.scalar_tensor_tensor(out=out, in0=in0, scalar=scalar, in1=in1, op0=op0, op1=op1),
                  R, W, cost=self._ecost("dve", out))

    def cp(self, eng, out, in_, R, W):
        if eng == "act":
            self._rec("act", lambda e: e.copy(out=out, in_=in_), R, W, cost=self._ecost("act", out))
        else:
            self._rec(eng, lambda e: e.tensor_copy(out=out, in_=in_), R, W, cost=self._ecost(eng, out))

    def ms(self, eng, ap, val, W):
        self._rec(eng, lambda e: e.memset(ap, val), [], W, cost=self._ecost(eng, ap))

    def red(self, out, in_, R, W):
        self._rec("dve", lambda e: e.tensor_reduce(out=out, in_=in_, axis=AX.X, op=ALU.add),
                  R, W, cost=self._ecost("dve", in_))

    def ld(self, out, in_, W, eng="sp"):
        nbytes = self._fd(out) * 4 * 128
        self._rec(eng, lambda e: e.dma_start(out=out, in_=in_), [], W, dma=True,
                  cost=(400.0 if eng == "sp" else 700.0), lat=2000.0 + nbytes / 150.0)

    def st(self, out, in_, R, eng="pool"):
        nbytes = self._fd(in_) * 4 * 128
        self._rec(eng, lambda e: e.dma_start(out=out, in_=in_), R, [], dma=True,
                  cost=(400.0 if eng == "sp" else 700.0), lat=2000.0 + nbytes / 150.0)

    def ldw(self, wt, wd, kc_n, ncols):
        for kc in range(kc_n):
            for c0 in range(0, ncols, 2048):
                c1 = min(ncols, c0 + 2048)
                self.ld(wt[:, kc, c0:c1], wd[kc * 128:(kc + 1) * 128, c0:c1], [wt], eng="pool")


def seg_of(c):
    return c // SEGCH


def build_nc(layers=(0, 1), do_mix=True, do_ffn=True):
    nc = bass.Bass("TRN2", target_bir_lowering=False)
    b = B(nc)
    S = b.S

    def din(name, shape):
        return nc.dram_tensor(name, list(shape), F32, kind="ExternalInput").ap()

    def dscr(name, shape, dt=F32):
        return nc.dram_tensor(name, list(shape), dt, kind="Internal").ap()

    xin = din("xin", [T, D])
    yout = nc.dram_tensor("yout", [T, D], F32, kind="ExternalOutput").ap()
    link_d = din("link", [128, 1])
    cos_d = din("rcos", [T, 32])
    sin_d = din("rsin", [T, 32])
    cst_d = din("cst", [128, 9, 128])
    amask_d = din("amask", [128, 17, 128])
    g_mix_pre = din("norm_mix_pre", [2, D])
    g_mix_post = din("norm_mix_post", [2, D])
    g_ffn_pre = din("norm_ffn_pre", [2, D])
    g_ffn_post = din("norm_ffn_post", [2, D])
    w1_d = din("ffn_w1", [2, D, 4096])
    w2_d = din("ffn_w2", [2, 4096, D])
    evin_d = din("ev_in_proj", [1, D, 5664])
    convw_d = din("ev_conv_w", [1, 5, 1536])
    convb_d = din("ev_conv_b", [1, 1536])
    dtb_d = din("ssd_dt_bias", [1, 2, 16])
    alog_d = din("ssd_a_log", [1, 2, 16])
    dsk_d = din("ssd_d", [1, 16])
    snw_d = din("ssd_norm_w", [1, 1024])
    rdec_d = din("ret_decay", [1, 2, 8])
    rgn_d = din("ret_gn_w", [1, 1024])
    evout_d = din("ev_out_proj", [1, 2048, D])
    odin_d = din("od_in_proj", [1, D, 4096])
    gnw_d = din("gmlp_norm_w", [1, 512])
    gws_d = din("gmlp_ws", [1, 8, 128, 128])
    gbs_d = din("gmlp_bs", [1, 8, 128])
    odout_d = din("od_out_proj", [1, 1536, D])

    xa = dscr("xa", [T, D])
    xb = dscr("xb", [T, D])
    sz_d = dscr("sz", [T, 1024])
    sg_d = dscr("sgt", [T, 1024])
    xbc_d = dscr("xbc", [T + 4, 1536])
    dtr_d = dscr("dtr", [T, 32])
    qT0_d = dscr("qT0", [NCH, 128, 4, 128], BF16)
    kT0_d = dscr("kT0", [NCH, 128, 4, 128], BF16)
    k0_d = dscr("k0", [T, 512], BF16)
    v0_d = dscr("v0", [T, 1024], BF16)
    yd_d = dscr("yd", [T, 1024])
    yi_d = dscr("yi", [T, 1024])
    CT_d = dscr("CTd", [NCH, 128, 2, 128], BF16)
    sc_d = dscr("scd", [T, 32])
    Sf_d = dscr("Sfd", [NCH, 128, 1024])
    decf_d = dscr("decf", [NCH, 128, 16])
    Rf_d = dscr("Rfd", [NCH, 128, 512])
    Hb_d = dscr("Hbd", [NCH, 128, 1024], BF16)
    Rb_d = dscr("Rbd", [NCH, 128, 512], BF16)
    xs_d = dscr("xsd", [T, 1024])
    bct_d = dscr("bctd", [NCH, 128, 4, 128], BF16)
    btok_d = dscr("btokd", [T, 256], BF16)
    qT1_d = dscr("qT1", [NCH, 128, 8, 128], BF16)
    kT1_d = dscr("kT1", [NCH, 128, 8, 128], BF16)
    v1_d = dscr("v1", [NCH, 128, 16 * 65], BF16)
    sg1_d = dscr("sg1", [T, 512], BF16)

    def chunk_rows(ap, c):
        return ap[c * 128:(c + 1) * 128, :]

    with contextlib.ExitStack() as gst:
        uid = [0]

        def sb(name, shape, dt, st=gst):
            uid[0] += 1
            return Tl(st.enter_context(nc.sbuf_tensor(f"s{uid[0]}_{name}", list(shape), dt)))

        def psum(name, shape, dt, st=gst):
            uid[0] += 1
            return Tl(st.enter_context(nc.psum_tensor(f"p{uid[0]}_{name}", list(shape), dt)))

        cst = sb("cst", [128, 9, 128], F32)
        b.ld(cst[:], cst_d[:, :, :], [cst])
        idb = sb("idb", [128, 128], BF16)
        b.cp("dve", idb[:], cst[:, 0, :], [cst], [idb])
        Um, Lm, Af, Ab, ONES = (cst[:, 1, :], cst[:, 2, :], cst[:, 3, :], cst[:, 4, :], cst[:, 5, :])
        RDp, RDn = cst[:, 6, :], cst[:, 7, :]
        POS = cst[:, 8, :]
        link = sb("link", [128, 1], F32)
        b.ld(link[:], link_d[:, :], [link])
        def alloc_psum(st, nf32, nbf):
            assert nf32 + nbf <= 8
            return ([psum(f"P{i}", [128, 512], F32, st) for i in range(nf32)],
                    [psum(f"PT{i}", [128, 1024], BF16, st) for i in range(nbf)])

        def rstd_from_ss(ss, n, R):
            b.act(ss[:, 0:1], ss[:, 0:1], AF.Ln, [ss] + R, [ss], scale=1.0 / n, bias=EPS)
            b.act(ss[:, 0:1], ss[:, 0:1], AF.Exp, [ss], [ss], scale=-0.5)

        def norm_to_hT(x_ap, xT, gain, junk, ss, hb, hT_ap, hT, PT):
            b.act(junk[:], x_ap, AF.Square, [xT], [junk, ss], accum=ss[:, 0:1])
            rstd_from_ss(ss, 1024.0, [])
            b.stt(hb[:], x_ap, ss[:, 0:1], gain[:], ALU.mult, ALU.mult, [xT, ss, gain], [hb])
            for k in range(8):
                b.tr(PT[:, k * 128:(k + 1) * 128], hb[:, k * 128:(k + 1) * 128], idb[:], [hb, idb], [PT])
            b.cp("act", hT_ap, PT[:, 0:1024].rearrange("p (k t) -> p k t", k=8), [PT], [hT])

        def post_norm_residual(Pa, Pb, gain, x_ap, xT, junk, ss, tn, xo, out_dram):
            ss2 = ss
            b.act(junk[:, 0:512], Pa[:], AF.Square, [Pa], [junk, ss2], accum=ss2[:, 0:1])
            b.act(junk[:, 512:1024], Pb[:], AF.Square, [Pb], [junk, ss2], accum=ss2[:, 1:2])
            b.tt("dve", ss2[:, 0:1], ss2[:, 0:1], ss2[:, 1:2], ALU.add, [ss2], [ss2])
            b.act(ss2[:, 0:1], ss2[:, 0:1], AF.Ln, [ss2], [ss2], scale=1.0 / 1024, bias=EPS)
            b.act(ss2[:, 0:1], ss2[:, 0:1], AF.Exp, [ss2], [ss2], scale=-0.5)
            b.stt(tn[:, 0:512], Pa[:], ss2[:, 0:1], gain[:, 0:512], ALU.mult, ALU.mult, [Pa, ss2, gain], [tn])
            b.stt(tn[:, 512:1024], Pb[:], ss2[:, 0:1], gain[:, 512:1024], ALU.mult, ALU.mult, [Pb, ss2, gain], [tn])
            b.tt("dve", xo[:], tn[:], x_ap, ALU.add, [tn, xT], [xo])
            b.st(out_dram, xo[:], [xo])

        def rotary(eng, out, src_ap, srcT, cs, H, tmp):
            s3 = src_ap.rearrange("p (h d) -> p h d", h=H)
            t1, t2 = s3[:, :, 0:32], s3[:, :, 32:64]
            cb = cs[:, 0:32].unsqueeze(1).to_broadcast([128, H, 32])
            sn = cs[:, 32:64].unsqueeze(1).to_broadcast([128, H, 32])
            ta = tmp[:, 0:H * 32].rearrange("p (h d) -> p h d", h=H)
            tb = tmp[:, H * 32:H * 64].rearrange("p (h d) -> p h d", h=H)
            b.tt(eng, ta, t1, cb, ALU.mult, [srcT, cs], [tmp])
            b.tt(eng, tb, t2, sn, ALU.mult, [srcT, cs], [tmp])
            b.tt(eng, out[:, :, 0:32], ta, tb, ALU.subtract, [tmp], [out])
            b.tt(eng, ta, t2, cb, ALU.mult, [srcT, cs, out], [tmp])
            b.tt(eng, tb, t1, sn, ALU.mult, [srcT, cs, out], [tmp])
            b.tt(eng, out[:, :, 32:64], ta, tb, ALU.add, [tmp], [out])

        def ffn_loop(li, src, dst):
            with contextlib.ExitStack() as st:
                W1b = [sb(f"W1b{i}", [128, 8, 512], BF16, st) for i in range(8)]
                W2b = [sb(f"W2b{i}", [128, 4, 1024], BF16, st) for i in range(8)]
                for cbk in range(8):
                    for kc in range(8):
                        b.ld(W1b[cbk][:, kc, :], w1_d[li][kc * 128:(kc + 1) * 128, cbk * 512:(cbk + 1) * 512], [W1b[cbk]],
                             eng="pool")
                for g8 in range(8):
                    for i4 in range(4):
                        r0 = (g8 * 4 + i4) * 128
                        b.ld(W2b[g8][:, i4, :], w2_d[li][r0:r0 + 128, :], [W2b[g8]], eng="pool")
                gpre = sb("gpre", [128, 1024], F32, st)
                gpost = sb("gpost", [128, 1024], F32, st)
                b.ld(gpre[:], g_ffn_pre[li].partition_broadcast(128), [gpre])
                b.ld(gpost[:], g_ffn_post[li].partition_broadcast(128), [gpost])
                P, PTs = alloc_psum(st, 6, 1)
                xm = [sb(f"xm{i}", [128, 2, 1024], F32, st) for i in range(2)]
                junk = sb("junk", [128, 1024], BF16, st)
                junk2 = sb("junk2", [128, 1024], BF16, st)
                ss = sb("ss", [128, 2], F32, st)
                ss2 = sb("ss2", [128, 2], F32, st)
                hb = sb("hb", [128, 1024], BF16, st)
                hT = [sb(f"hT{i}", [128, 8, 256], BF16, st) for i in range(2)]
                uT = sb("uT", [128, 32, 256], BF16, st)
                rl = [sb(f"rl{i}", [128, 256], F32, st) for i in range(2)]
                tn = sb("tn", [128, 1024], F32, st)
                xo = [sb(f"xo{i}", [128, 1024], F32, st) for i in range(2)]

                def ffn_n(mt):
                    xmt = xm[mt % 2]
                    b.ld(xmt[:], src[mt * 256:(mt + 1) * 256, :].rearrange("(j p) d -> p j d", p=128), [xmt])
                    for j in range(2):
                        norm_to_hT(xmt[:, j, :], xmt, gpre, junk, ss, hb, hT[mt % 2][:, :, j * 128:(j + 1) * 128],
                                   hT[mt % 2], PTs[0])

                def ffn_r(mt):
                    xmt = xm[mt % 2]
                    hTc = hT[mt % 2]
                    for fc in range(32):
                        pu = P[fc % 2]
                        for k in range(8):
                            b.mm(pu[:, 0:256], W1b[fc // 4][:, k, (fc % 4) * 128:(fc % 4 + 1) * 128], hTc[:, k, :], k == 0, k == 7,
                                 [W1b[fc // 4], hTc], [pu])
                        r = rl[fc % 2]
                        b.act(r[:], pu[:, 0:256], AF.Relu, [pu], [r])
                        b.tt("dve", uT[:, fc, :], r[:], r[:], ALU.mult, [r], [uT])
                    for j in range(2):
                        Pa, Pb = P[2 + 2 * j], P[3 + 2 * j]
                        for nt, Pq in ((0, Pa), (1, Pb)):
                            for fc in range(32):
                                b.mm(Pq[:], uT[:, fc, j * 128:(j + 1) * 128], W2b[fc // 4][:, fc % 4, nt * 512:(nt + 1) * 512],
                                     fc == 0, fc == 31, [uT, W2b[fc // 4]], [Pq])
                        c = mt * 2 + j
                        post_norm_residual(Pa, Pb, gpost, xmt[:, j, :], xmt, junk2, ss2, tn, xo[j], chunk_rows(dst, c))

                b.two_stage(ffn_n, ffn_r, range(NCH // 2))
            S.barrier()

        def odd_layer(src, dst):
            with contextlib.ExitStack() as st:
                P, PTs = alloc_psum(st, 6, 2)
                PT = PTs[0]
                Wi = sb("Wi1", [128, 8, 4096], BF16, st)
                b.ldw(Wi, odin_d[0], 8, 4096)
                gpre = sb("gpre", [128, 1024], F32, st)
                b.ld(gpre[:], g_mix_pre[1].partition_broadcast(128), [gpre])
                gnw = sb("gnw", [128, 512], F32, st)
                b.ld(gnw[:], gnw_d[0].partition_broadcast(128), [gnw])
                wsf = sb("wsf", [128, 8, 128], F32, st)
                b.ld(wsf[:], gws_d[0].rearrange("g t s -> t g s"), [wsf])
                wsb = sb("wsb", [128, 8, 128], BF16, st)
                b.cp("dve", wsb[:], wsf[:], [wsf], [wsb])
                wsT = sb("wsT", [128, 8, 128], BF16, st)
                for g in range(8):
                    b.tr(PT[:, g * 128:(g + 1) * 128], wsb[:, g, :], idb[:], [wsb, idb], [PT])
                b.cp("act", wsT[:], PT[:, 0:1024].rearrange("p (g t) -> p g t", g=8), [PT], [wsT])
                bsf = sb("bsf", [8, 128], F32, st)
                b.ld(bsf[:], gbs_d[0], [bsf])
                bsT = sb("bsT", [128, 8], F32, st)
                b.tr(P[0][:, 0:8], bsf[:], cst[0:8, 0, 0:8], [bsf, cst], [P[0]])
                b.cp("act", bsT[:], P[0][:, 0:8], [P[0]], [bsT])

                xt = [sb(f"xt{i}", [128, 1024], F32, st) for i in range(2)]
                cs = [sb(f"cs{i}", [128, 64], F32, st) for i in range(2)]
                junk = sb("junk", [128, 1024], BF16, st)
                junk2 = sb("junk2", [128, 512], BF16, st)
                ss = sb("ss", [128, 2], F32, st)
                hb = sb("hb", [128, 1024], BF16, st)
                hT = [sb(f"hT{i}", [128, 8, 128], BF16, st) for i in range(2)]
                qsq = sb("qsq", [128, 1024], F32, st)
                qsk = sb("qsk", [128, 1024], F32, st)
                rtq = sb("rtq", [128, 1024], F32, st)
                rtk = sb("rtk", [128, 1024], F32, st)
                qrq = sb("qrq", [128, 16, 64], BF16, st)
                qrk = sb("qrk", [128, 16, 64], BF16, st)
                qTs = [sb(f"qTs{i}", [128, 8, 128], BF16, st) for i in range(2)]
                kTs = [sb(f"kTs{i}", [128, 8, 128], BF16, st) for i in range(2)]
                v1s = [sb(f"v1s{i}", [128, 16, 65], BF16, st) for i in range(2)]
                for i in range(2):
                    b.ms("pool", v1s[i][:], 1.0, [v1s[i]])
                us = sb("us", [128, 512], F32, st)
                vgs = sb("vgs", [128, 512], F32, st)
                st4 = sb("st4", [128, 4], F32, st)
                vn = sb("vn", [128, 512], BF16, st)
                sgs = [sb(f"sgs{i}", [128, 512], BF16, st) for i in range(2)]

                def a1_n(c):
                    i2 = c % 2
                    b.ld(xt[i2][:], chunk_rows(src, c), [xt[i2]])
                    b.ld(cs[i2][:, 0:32], chunk_rows(cos_d, c), [cs[i2]])
                    b.ld(cs[i2][:, 32:64], chunk_rows(sin_d, c), [cs[i2]])
                    norm_to_hT(xt[i2][:], xt[i2], gpre, junk, ss, hb, hT[i2][:], hT[i2], PTs[1])

                def a1_r(c):
                    i2 = c % 2
                    hTc = hT[i2]
                    PT = PTs[0]

                    def proj(nt, Pq):
                        for k in range(8):
                            b.mm(Pq[:], hTc[:, k, :], Wi[:, k, nt * 512:(nt + 1) * 512], k == 0, k == 7, [hTc, Wi], [Pq])
                    for nt in range(6):
                        proj(nt, P[nt])
                    b.cp("act", qsq[:, 0:512], P[0][:], [P[0]], [qsq])
                    b.cp("act", qsq[:, 512:1024], P[1][:], [P[1]], [qsq])
                    b.cp("act", qsk[:, 0:512], P[2][:], [P[2]], [qsk])
                    b.cp("act", qsk[:, 512:1024], P[3][:], [P[3]], [qsk])
                    proj(6, P[0])
                    proj(7, P[1])
                    vt = v1s[i2]
                    b.cp("dve", vt[:, 0:8, 0:64], P[4][:].rearrange("p (h d) -> p h d", h=8), [P[4]], [vt])
                    b.cp("dve", vt[:, 8:16, 0:64], P[5][:].rearrange("p (h d) -> p h d", h=8), [P[5]], [vt])
                    b.st(v1_d[c], vt[:].rearrange("p h d -> p (h d)"), [vt])
                    rotary("dve", qrq, qsq[:], qsq, cs[i2], 16, rtq)
                    rotary("dve", qrk, qsk[:], qsk, cs[i2], 16, rtk)
                    b.cp("act", us[:], P[0][:], [P[0]], [us])
                    b.act(vgs[:], P[1][:], AF.Copy, [P[1]], [vgs, st4], accum=st4[:, 0:1])
                    b.act(junk2[:, 0:512], P[1][:], AF.Square, [P[1]], [junk2, st4], accum=st4[:, 1:2])
                    for qr_, dstT, dram in ((qrq, qTs[i2], qT1_d), (qrk, kTs[i2], kT1_d)):
                        qf = qr_[:].rearrange("p h d -> p (h d)")
                        for k in range(8):
                            b.tr(PT[:, k * 128:(k + 1) * 128], qf[:, k * 128:(k + 1) * 128], idb[:], [qr_, idb], [PT])
                        b.cp("act", dstT[:], PT[:, 0:1024].rearrange("p (k t) -> p k t", k=8), [PT], [dstT])
                        b.st(dram[c], dstT[:], [dstT])
                    b.ts("dve", st4[:, 0:2], st4[:, 0:2], 1.0 / 512, None, ALU.mult, None, [st4], [st4])
                    b.tt("dve", st4[:, 2:3], st4[:, 0:1], st4[:, 0:1], ALU.mult, [st4], [st4])
                    b.tt("dve", st4[:, 1:2], st4[:, 1:2], st4[:, 2:3], ALU.subtract, [st4], [st4])
                    b.act(st4[:, 1:2], st4[:, 1:2], AF.Ln, [st4], [st4], scale=1.0, bias=EPS)
                    b.act(st4[:, 1:2], st4[:, 1:2], AF.Exp, [st4], [st4], scale=-0.5)
                    b.ts("dve", vgs[:], vgs[:], st4[:, 0:1], st4[:, 1:2], ALU.subtract, ALU.mult, [vgs, st4], [vgs])
                    b.tt("dve", vn[:], vgs[:], gnw[:], ALU.mult, [vgs, gnw], [vn])
                    for g in range(8):
                        b.mm(P[2][:, g * 64:(g + 1) * 64], wsT[:, g, :], vn[:, g * 64:(g + 1) * 64], True, True,
                             [wsT, vn], [P[2]])
                    b.tt("dve", vgs[:].rearrange("p (g d) -> p g d", g=8), P[2][:].rearrange("p (g d) -> p g d", g=8),
                         bsT[:].unsqueeze(2).to_broadcast([128, 8, 64]), ALU.add, [P[2], bsT, vn], [vgs])
                    b.tt("dve", sgs[i2][:], vgs[:], us[:], ALU.mult, [vgs, us], [sgs[i2]])
                    b.st(chunk_rows(sg1_d, c), sgs[i2][:], [sgs[i2]])

                b.two_stage(a1_n, a1_r, crange("A1"))
            S.barrier()
            with contextlib.ExitStack() as st:
                NS = 18
                S.keep = True
                P, PTs = alloc_psum(st, 7, 1)
                PT = PTs[0]
                Wo = sb("Wo1", [128, 12, 1024], BF16, st)
                b.ldw(Wo, odout_d[0], 12, 1024)
                gpost = sb("gpost", [128, 1024], F32, st)
                b.ld(gpost[:], g_mix_post[1].partition_broadcast(128), [gpost])
                amf = sb("amf", [128, 17, 128], F32, st)
                b.ld(amf[:], amask_d[:, :, :], [amf])
                am = sb("am", [128, 17, 128], BF16, st)
                amL = sb("amL", [128, 17, 128], BF16, st)
                b.cp("dve", am[:], amf[:], [amf], [am])
                b.ts("dve", amL[:], amf[:], link[:, 0:1], None, ALU.mult, None, [amf, link], [amL])
                kr = [sb(f"kr{i}", [128, 8, 128], BF16, st) for i in range(NS)]
                vr = [sb(f"vr{i}", [128, 16 * 65], BF16, st) for i in range(NS)]
                qze = [sb(f"qze{i}", [128, 8, 128], BF16, st) for i in range(2)]
                qzo = [sb(f"qzo{i}", [128, 8, 128], BF16, st) for i in range(2)]
                for i in range(2):
                    b.ms("dve", qze[i][:], 0.0, [qze[i]])
                    b.ms("dve", qzo[i][:], 0.0, [qzo[i]])
                pe_sb = [sb(f"pe{i}", [128, 512], BF16, st) for i in range(AT_LAG + 1)]
                pm_sb = [sb(f"pm{i}", [128, 512], BF16, st) for i in range(AT_LAG + 1)]
                rden = sb("rden", [128, 4], F32, st)
                mix = [sb(f"mix{i}", [128, 1536], BF16, st) for i in range(2)]
                mixT = sb("mixT", [128, 12, 128], BF16, st)
                xt = [sb(f"xt{i}", [128, 1024], F32, st) for i in range(2)]
                junk = sb("junk", [128, 1024], BF16, st)
                ss = sb("ss", [128, 2], F32, st)
                tn = sb("tn", [128, 1024], F32, st)
                xo = [sb(f"xo{i}", [128, 1024], F32, st) for i in range(2)]
                loaded = set()
                gi = 0
                pend = []
                LAG = AT_LAG
                for c in crange("AT"):
                    i2 = c % 2
                    sgc = seg_of(c)
                    lo, hi = (0, 2 * SEGCH) if sgc < 2 else (2 * SEGCH, NCH)
                    blocks = [j for j in range(c - 8, c + 9) if lo <= j < hi]
                    for j in blocks:
                        if j not in loaded:
                            loaded.add(j)
                            b.ld(kr[j % NS][:], kT1_d[j], [kr[j % NS]])
                            b.ld(vr[j % NS][:], v1_d[j], [vr[j % NS]])
                    b.ld(qze[i2][0:64], qT1_d[c][0:64], [qze[i2]])
                    b.ld(qzo[i2][64:128], qT1_d[c][64:128], [qzo[i2]])
                    b.ld(mix[i2][:, 1024:1536], chunk_rows(sg1_d, c), [mix[i2]])
                    b.ld(xt[i2][:], chunk_rows(src, c), [xt[i2]])
                    groups = []
                    cur = []
                    for j in blocks:
                        cross = (seg_of(j) != sgc)
                        if cur and (len(cur) == 4 or cur[0][1] != cross):
                            groups.append(cur)
                            cur = []
                        cur.append((j, cross))
                    if cur:
                        groups.append(cur)
                    for h in range(16):
                        hp, base = h // 2, (h % 2) * 64
                        po = P[4 + (h // 4) % 2]
                        pcol = (h % 4) * 65
                        nb = len(blocks)
                        bi = 0
                        for gx, grp in enumerate(groups):
                            ps = P[gi % 3]
                            pe_t = pe_sb[gi % (AT_LAG + 1)]
                            pm_t = pm_sb[gi % (AT_LAG + 1)]
                            gi += 1
                            n = len(grp)
                            for i, (j, cross) in enumerate(grp):
                                qz = (qze if h % 2 == 0 else qzo)[i2]
                                b.mm(ps[:, i * 128:(i + 1) * 128], kr[j % NS][:, hp, :],
                                     qz[:, hp, :], True, True, [kr[j % NS], qz], [ps])
                            b.act(pe_t[:, 0:n * 128], ps[:, 0:n * 128], AF.Exp, [ps], [pe_t], scale=0.125)
                            o0 = grp[0][0] - c + 8
                            msk = amL if grp[0][1] else am
                            b.tt("dve", pm_t[:, 0:n * 128], pe_t[:, 0:n * 128],
                                 msk[:, o0:o0 + n, :].rearrange("p a b -> p (a b)"), ALU.mult, [pe_t, msk], [pm_t])

                            def pv(grp=grp, pm_t=pm_t, po=po, pcol=pcol, bi=bi, nb=nb, h=h, i2=i2, c=c,
                                   last=(gx == len(groups) - 1)):
                                for i, (j, cross) in enumerate(grp):
                                    b.mm(po[:, pcol:pcol + 65], pm_t[:, i * 128:(i + 1) * 128],
                                         vr[j % NS][:, h * 65:(h + 1) * 65], bi + i == 0, bi + i == nb - 1,
                                         [pm_t, vr[j % NS]], [po])
                                if last and h % 4 == 3:
                                    po3 = po[:, 0:260].rearrange("p (h d) -> p h d", h=4)
                                    b.S.op("dve", lambda e, o=rden[:].unsqueeze(2), i_=po3[:, :, 64:65]: e.reciprocal(out=o, in_=i_),
                                           [po.t], [rden.t])
                                    h0 = h - 3
                                    b.tt("dve", mix[i2][:, h0 * 64:(h0 + 4) * 64].rearrange("p (h d) -> p h d", h=4),
                                         po3[:, :, 0:64], rden[:].unsqueeze(2).to_broadcast([128, 4, 64]), ALU.mult,
                                         [po, rden], [mix[i2]])
                                if last and h == 15:
                                    for half in range(2):
                                        for k in range(6):
                                            kk = half * 6 + k
                                            b.tr(PT[:, k * 128:(k + 1) * 128], mix[i2][:, kk * 128:(kk + 1) * 128], idb[:],
                                                 [mix[i2], idb], [PT])
                                        b.cp("act", mixT[:, half * 6:(half + 1) * 6, :],
                                             PT[:, 0:768].rearrange("p (k t) -> p k t", k=6), [PT], [mixT])
                                    for nt, Pq in ((0, P[3]), (1, P[6])):
                                        for k in range(12):
                                            b.mm(Pq[:], mixT[:, k, :], Wo[:, k, nt * 512:(nt + 1) * 512], k == 0, k == 11,
                                                 [mixT, Wo], [Pq])
                                    post_norm_residual(P[3], P[6], gpost, xt[i2][:], xt[i2], junk, ss, tn, xo[i2],
                                                       chunk_rows(dst, c))
                            bi += n
                            pend.append(pv)
                            while len(pend) > LAG:
                                pend.pop(0)()
                while pend:
                    pend.pop(0)()
            S.keep = False
            S.barrier()

        def even_layer(src, dst):
            with contextlib.ExitStack() as st:
                P, PTs = alloc_psum(st, 6, 2)
                Wi = sb("Wi0", [128, 8, 5664], BF16, st)
                b.ldw(Wi, evin_d[0], 8, 5664)
                gpre = sb("gpre", [128, 1024], F32, st)
                b.ld(gpre[:], g_mix_pre[0].partition_broadcast(128), [gpre])
                xt = [sb(f"xt{i}", [128, 1024], F32, st) for i in range(2)]
                cs = [sb(f"cs{i}", [128, 64], F32, st) for i in range(2)]
                junk = sb("junk", [128, 1024], BF16, st)
                ss = sb("ss", [128, 2], F32, st)
                hb = sb("hb", [128, 1024], BF16, st)
                hT = [sb(f"hT{i}", [128, 8, 128], BF16, st) for i in range(2)]
                zo = [sb(f"zo{i}", [128, 1024], F32, st) for i in range(2)]
                go = [sb(f"go{i}", [128, 1024], F32, st) for i in range(2)]
                xpre = [sb(f"xpre{i}", [128, 12, 132], BF16, st) for i in range(3)]
                xsT = sb("xsT", [128, 8, 128], F32, st)
                bcT = [sb(f"bcT{i}", [128, 4, 128], BF16, st) for i in range(2)]
                xso = [sb(f"xso{i}", [128, 1024], F32, st) for i in range(2)]
                btk = [sb(f"btk{i}", [128, 256], BF16, st) for i in range(2)]
                cw = sb("cw", [8, 1536], F32, st)
                b.ms("dve", cw[:], 0.0, [cw])
                b.ld(cw[0:5, :], convw_d[0], [cw])
                cbr = sb("cbr", [12, 128], F32, st)
                b.ld(cbr[:], convb_d[0].rearrange("(a p) -> a p", p=128), [cbr])
                wT = sb("wT", [128, 12, 8], F32, st)
                cbT = sb("cbT", [128, 12], F32, st)
                for cb in range(12):
                    b.tr(P[0][:, cb * 8:(cb + 1) * 8], cw[:, cb * 128:(cb + 1) * 128], cst[0:8, 0, 0:8], [cw, cst], [P[0]])
                b.cp("act", wT[:], P[0][:, 0:96].rearrange("p (a j) -> p a j", a=12), [P[0]], [wT])
                b.tr(P[1][:, 0:12], cbr[:], cst[0:12, 0, 0:12], [cbr, cst], [P[1]])
                b.cp("act", cbT[:], P[1][:, 0:12], [P[1]], [cbT])
                Wd = sb("Wd", [128, 12, 5, 128], BF16, st)
                for cb in range(12):
                    for j in range(5):
                        b.ts("dve", Wd[:, cb, j, :], cst[:, 0, :], wT[:, cb, j:j + 1], None, ALU.mult, None, [cst, wT], [Wd])
                b.ms("dve", xpre[0][:, :, 0:2], 0.0, [xpre[0]])
                dto = [sb(f"dto{i}", [128, 32], F32, st) for i in range(2)]
                qs = sb("qs", [128, 512], F32, st)
                rtmp = sb("rtmp", [128, 512], F32, st)
                qr = sb("qr", [128, 8, 64], BF16, st)
                kro = [sb(f"kro{i}", [128, 8, 64], BF16, st) for i in range(2)]
                qTs = [sb(f"qTs{i}", [128, 4, 128], BF16, st) for i in range(2)]
                kTs = [sb(f"kTs{i}", [128, 4, 128], BF16, st) for i in range(2)]
                vo = [sb(f"vo{i}", [128, 1024], BF16, st) for i in range(2)]
                qs2 = sb("qs2", [128, 512], F32, st)
                rtmp2 = sb("rtmp2", [128, 512], F32, st)

                def a0_n(c):
                    i2 = c % 2
                    b.ld(xt[i2][:], chunk_rows(src, c), [xt[i2]])
                    b.ld(cs[i2][:, 0:32], chunk_rows(cos_d, c), [cs[i2]])
                    b.ld(cs[i2][:, 32:64], chunk_rows(sin_d, c), [cs[i2]])
                    norm_to_hT(xt[i2][:], xt[i2], gpre, junk, ss, hb, hT[i2][:], hT[i2], PTs[1])

                def a0_r(c):
                    i2 = c % 2
                    hTc = hT[i2]
                    PT = PTs[0]
                    pi = [0]

                    def proj(c0, ncol):
                        Pq = P[pi[0] % 6]
                        pi[0] += 1
                        for k in range(8):
                            b.mm(Pq[:, 0:ncol], hTc[:, k, :], Wi[:, k, c0:c0 + ncol], k == 0, k == 7, [hTc, Wi], [Pq])
                        return Pq
                    Pq = proj(2592, 512)
                    b.cp("act", qs[:], Pq[:], [Pq], [qs])
                    Pq = proj(3104, 512)
                    b.act(qs2[:], Pq[:], AF.Copy, [Pq], [qs2], scale=0.125)
                    rotary("dve", qr, qs[:], qs, cs[i2], 8, rtmp)
                    rotary("dve", kro[i2], qs2[:], qs2, cs[i2], 8, rtmp2)
                    for nt in range(2):
                        Pq = proj(nt * 512, 512)
                        b.act(zo[i2][:, nt * 512:(nt + 1) * 512], Pq[:], AF.Silu, [Pq], [zo[i2]])
                    b.st(chunk_rows(sz_d, c), zo[i2][:], [zo[i2]])
                    xp = xpre[c % 3]
                    for q4 in range(3):
                        Pq = P[pi[0] % 6]
                        pi[0] += 1
                        for cbi in range(4):
                            cb = q4 * 4 + cbi
                            for k in range(8):
                                b.mm(Pq[:, cbi * 128:(cbi + 1) * 128], Wi[:, k, 1024 + cb * 128:1024 + (cb + 1) * 128],
                                     hTc[:, k, :], k == 0, k == 7, [hTc, Wi], [Pq])
                        b.cp("dve", xp[:, q4 * 4:(q4 + 1) * 4, 2:130], Pq[:].rearrange("p (a t) -> p a t", a=4), [Pq], [xp])
                    if c > 0:
                        xl = xpre[(c - 1) % 3]
                        if c % SEGCH != 0:
                            b.cp("dve", xl[:, :, 130:132], xp[:, :, 2:4], [xp], [xl])
                        elif c == SEGCH:
                            b.ts("dve", xl[:, :, 130:132], xp[:, :, 2:4], link[:, 0:1], None, ALU.mult, None, [xp, link], [xl])
                        else:
                            b.ms("dve", xl[:, :, 130:132], 0.0, [xl])
                    if c + 1 < NCH:
                        xn = xpre[(c + 1) % 3]
                        if (c + 1) % SEGCH != 0:
                            b.cp("dve", xn[:, :, 0:2], xp[:, :, 128:130], [xp], [xn])
                        elif c + 1 == SEGCH:
                            b.ts("dve", xn[:, :, 0:2], xp[:, :, 128:130], link[:, 0:1], None, ALU.mult, None, [xp, link], [xn])
                        else:
                            b.ms("dve", xn[:, :, 0:2], 0.0, [xn])
                    else:
                        b.ms("dve", xp[:, :, 130:132], 0.0, [xp])
                    Pq = proj(2560, 32)
                    b.cp("dve", dto[i2][:], Pq[:, 0:32], [Pq], [dto[i2]])
                    b.st(chunk_rows(dtr_d, c), dto[i2][:], [dto[i2]])
                    for nt in range(2):
                        Pq = proj(3616 + nt * 512, 512)
                        b.cp("act" if nt == 0 else "dve", vo[i2][:, nt * 512:(nt + 1) * 512], Pq[:], [Pq], [vo[i2]])
                    b.st(chunk_rows(v0_d, c), vo[i2][:], [vo[i2]])
                    for nt in range(2):
                        Pq = proj(4640 + nt * 512, 512)
                        b.act(go[i2][:, nt * 512:(nt + 1) * 512], Pq[:], AF.Silu, [Pq], [go[i2]])
                    b.st(chunk_rows(sg_d, c), go[i2][:], [go[i2]])
                    qf = qr[:].rearrange("p h d -> p (h d)")
                    for k in range(4):
                        b.tr(PT[:, k * 128:(k + 1) * 128], qf[:, k * 128:(k + 1) * 128], idb[:], [qr, idb], [PT])
                    b.cp("act", qTs[i2][:], PT[:, 0:512].rearrange("p (k t) -> p k t", k=4), [PT], [qTs[i2]])
                    b.st(qT0_d[c], qTs[i2][:], [qTs[i2]])
                    kf = kro[i2][:].rearrange("p h d -> p (h d)")
                    b.st(chunk_rows(k0_d, c), kf, [kro[i2]])
                    for k in range(4):
                        b.tr(PT[:, 512 + k * 128:512 + (k + 1) * 128], kf[:, k * 128:(k + 1) * 128], idb[:],
                             [kro[i2], idb], [PT])
                    b.cp("act", kTs[i2][:], PT[:, 512:1024].rearrange("p (k t) -> p k t", k=4), [PT], [kTs[i2]])
                    b.st(kT0_d[c], kTs[i2][:], [kTs[i2]])
                    return pi

                def a0_conv(c, pi):
                    i2 = c % 2
                    xp = xpre[c % 3]
                    PT = PTs[0]
                    for q4 in range(3):
                        Pq = P[pi[0] % 6]
                        pi[0] += 1
                        for cbi in range(4):
                            cb = q4 * 4 + cbi
                            for j in range(5):
                                b.mm(Pq[:, cbi * 128:(cbi + 1) * 128], Wd[:, cb, j, :], xp[:, cb, j:j + 128], j == 0, j == 4,
                                     [Wd, xp], [Pq])
                        for cbi in range(4):
                            cb = q4 * 4 + cbi
                            if cb < 8:
                                b.act(xsT[:, cb, :], Pq[:, cbi * 128:(cbi + 1) * 128], AF.Silu, [Pq, cbT], [xsT],
                                      bias=cbT[:, cb:cb + 1])
                            else:
                                b.act(bcT[i2][:, cb - 8, :], Pq[:, cbi * 128:(cbi + 1) * 128], AF.Silu, [Pq, cbT], [bcT[i2]],
                                      bias=cbT[:, cb:cb + 1])
                    b.st(bct_d[c], bcT[i2][:], [bcT[i2]])
                    for half in range(2):
                        Pq = P[pi[0] % 6]
                        pi[0] += 1
                        for e4 in range(4):
                            cb = half * 4 + e4
                            b.tr(Pq[:, e4 * 128:(e4 + 1) * 128], xsT[:, cb, :], cst[:, 0, :], [xsT, cst], [Pq])
                        b.cp("act", xso[i2][:, half * 512:(half + 1) * 512], Pq[:], [Pq], [xso[i2]])
                    b.st(chunk_rows(xs_d, c), xso[i2][:], [xso[i2]])
                    for g in range(2):
                        b.tr(PT[:, 512 + g * 128:512 + (g + 1) * 128], bcT[i2][:, g, :], idb[:], [bcT[i2], idb], [PT])
                    b.cp("act", btk[i2][:], PT[:, 512:768], [PT], [btk[i2]])
                    b.st(chunk_rows(btok_d, c), btk[i2][:], [btk[i2]])

                def a0_r2(c):
                    pi = a0_r(c)
                    if c > 0:
                        a0_conv(c - 1, pi)
                    if c == len(crange("A")) - 1:
                        a0_conv(c, pi)

                b.two_stage(a0_n, a0_r2, crange("A"))
            S.barrier()
            with contextlib.ExitStack() as st:
                P, PTs = alloc_psum(st, 7, 1)
                PT = PTs[0]
                prm = sb("prm", [128, 96], F32, st)
                b.ld(prm[:, 0:32], dtb_d[0].rearrange("a b -> (a b)").partition_broadcast(128), [prm])
                b.ld(prm[:, 32:64], alog_d[0].rearrange("a b -> (a b)").partition_broadcast(128), [prm])
                b.ld(prm[:, 64:80], dsk_d[0].partition_broadcast(128), [prm])
                b.ld(prm[:, 80:96], rdec_d[0].rearrange("a b -> (a b)").partition_broadcast(128), [prm])
                b.act(prm[:, 32:64], prm[:, 32:64], AF.Exp, [prm], [prm])
                b.ts("dve", prm[:, 32:64], prm[:, 32:64], -1.0, None, ALU.mult, None, [prm], [prm])
                b.act(prm[:, 80:96], prm[:, 80:96], AF.Exp, [prm], [prm], scale=-1.0)
                b.act(prm[:, 80:96], prm[:, 80:96], AF.Ln, [prm], [prm], bias=1.0)
                b.ts("dve", prm[:, 80:96], prm[:, 80:96], -1.0, None, ALU.mult, None, [prm], [prm])
                dtbias, avec, dskip, lg = prm[:, 0:32], prm[:, 32:64], prm[:, 64:80], prm[:, 80:96]
                Dfb = sb("Dfb", [128, 8, 128], F32, st)
                Dtmp = sb("Dtmp", [128, 8, 128], F32, st)
                b.tt("dve", Dfb[:], RDp.unsqueeze(1).to_broadcast([128, 8, 128]),
                     prm[:, 80:88].unsqueeze(2).to_broadcast([128, 8, 128]), ALU.mult, [cst, prm], [Dfb])
                b.act(Dfb[:], Dfb[:], AF.Exp, [Dfb], [Dfb])
                b.tt("dve", Dfb[:], Dfb[:], Um.unsqueeze(1).to_broadcast([128, 8, 128]), ALU.mult, [Dfb, cst], [Dfb])
                b.tt("dve", Dtmp[:], RDn.unsqueeze(1).to_broadcast([128, 8, 128]),
                     prm[:, 88:96].unsqueeze(2).to_broadcast([128, 8, 128]), ALU.mult, [cst, prm], [Dtmp])
                b.act(Dtmp[:], Dtmp[:], AF.Exp, [Dtmp], [Dtmp])
                b.tt("dve", Dtmp[:], Dtmp[:], Lm.unsqueeze(1).to_broadcast([128, 8, 128]), ALU.mult, [Dtmp, cst], [Dtmp])
                b.tt("dve", Dfb[:], Dfb[:], Dtmp[:], ALU.add, [Dfb, Dtmp], [Dfb])
                rc = sb("rc", [128, 32], F32, st)
                b.ts("dve", rc[:, 0:8], prm[:, 80:88], POS[:, 0:1], None, ALU.mult, None, [prm, cst], [rc])
                b.ts("dve", rc[:, 8:16], prm[:, 88:96], POS[:, 1:2], None, ALU.mult, None, [prm, cst], [rc])
                b.ts("dve", rc[:, 16:32], prm[:, 80:96], 128.0, None, ALU.mult, None, [prm], [rc])
                b.act(rc[:], rc[:], AF.Exp, [rc], [rc])

                xbcs = sb("xbcs", [128, 1024], F32, st)
                dtr = sb("dtr", [128, 32], F32, st)
                dts = sb("dts", [128, 32], F32, st)
                dt2 = sb("dt2", [128, 32], F32, st)
                la = sb("la", [128, 32], F32, st)
                csb = sb("csb", [128, 64], F32, st)
                scs = [sb(f"scs{i}", [128, 32], F32, st) for i in range(2)]
                wte = sb("wte", [128, 32], F32, st)
                dec = [sb(f"dec{i}", [128, 32], F32, st) for i in range(2)]
                xdt = sb("xdt", [128, 2, 1024], BF16, st)
                xw = [sb(f"xw{i}", [128, 2, 1024], BF16, st) for i in range(2)]
                bcb = [sb(f"bcb{i}", [128, 256], BF16, st) for i in range(2)]
                BCT = [sb(f"BCT{i}", [128, 4, 128], BF16, st) for i in range(2)]
                cbm = sb("cbm", [128, 4, 128], F32, st)
                rhsq = [sb(f"rhsq{i}", [128, 4, 128], F32, st) for i in range(8)]
                DT = [sb(f"DT{i}", [128, 512], BF16, st) for i in range(2)]
                M = sb("M", [128, 2, 16, 128], BF16, st)
                ydt = sb("ydt", [128, 1024], F32, st)
                ydo = [sb(f"ydo{i}", [128, 1024], F32, st) for i in range(2)]
                Sfo = [sb(f"Sfo{i}", [128, 1024], F32, st) for i in range(2)]
                hbs = sb("hbs", [128, 1024], F32, st)
                hbo = [sb(f"hbo{i}", [128, 1024], BF16, st) for i in range(2)]
                qTl = sb("qTl", [128, 4, 128], BF16, st)
                kTl = sb("kTl", [128, 4, 128], BF16, st)
                kl = sb("kl", [128, 512], BF16, st)
                vl = sb("vl", [128, 1024], BF16, st)
                SM = sb("SM", [128, 8, 128], BF16, st)
                kd = sb("kd", [128, 2, 512], BF16, st)
                yio = [sb(f"yio{i}", [128, 1024], F32, st) for i in range(2)]
                Rfo = [sb(f"Rfo{i}", [128, 512], F32, st) for i in range(2)]
                rbs = sb("rbs", [128, 512], F32, st)
                rbo = [sb(f"rbo{i}", [128, 512], BF16, st) for i in range(2)]
                b.ms("dve", hbs[:], 0.0, [hbs])
                b.ms("dve", rbs[:], 0.0, [rbs])

                def b1_h1(c):
                    i2 = c % 2
                    first_of_seg = (c % SEGCH == 0)
                    last_of_seg = (c % SEGCH == SEGCH - 1)
                    b.ld(xbcs[:], chunk_rows(xs_d, c), [xbcs])
                    xs3 = xbcs[:, 0:1024].rearrange("p (e d) -> p e d", e=16)
                    b.ld(dtr[:], chunk_rows(dtr_d, c), [dtr])
                    b.tt("dve", dtr[:], dtr[:], dtbias, ALU.add, [dtr, prm], [dtr])
                    b.ts("dve", dt2[:], dtr[:], -1.0, None, ALU.mult, None, [dtr], [dt2])
                    b.tt("dve", dt2[:], dt2[:], dtr[:], ALU.max, [dtr, dt2], [dt2])
                    b.act(dt2[:], dt2[:], AF.Exp, [dt2], [dt2], scale=-1.0)
                    b.act(dt2[:], dt2[:], AF.Ln, [dt2], [dt2], bias=1.0)
                    b.ts("dve", dts[:], dtr[:], 0.0, None, ALU.max, None, [dtr], [dts])
                    b.tt("dve", dts[:], dts[:], dt2[:], ALU.add, [dts, dt2], [dts])
                    b.tt("dve", la[:], dts[:], avec, ALU.mult, [dts, prm], [la])
                    pc = P[6]
                    b.mm(pc[:, 0:16], Um, la[:, 0:16], True, True, [cst, la], [pc])
                    b.mm(pc[:, 16:32], Lm, la[:, 16:32], True, True, [cst, la], [pc])
                    b.mm(pc[:, 32:64], ONES, la[:, 0:32], True, True, [cst, la], [pc])
                    b.cp("act", csb[:], pc[:, 0:64], [pc], [csb])
                    sc = scs[i2]
                    b.act(sc[:], csb[:, 0:32], AF.Exp, [csb], [sc])
                    b.st(chunk_rows(sc_d, c), sc[:], [sc])
                    b.act(dec[i2][:], csb[:, 32:64], AF.Exp, [csb], [dec[i2]])
                    b.tt("dve", wte[:], csb[:, 32:64], csb[:, 0:32], ALU.subtract, [csb], [wte])
                    b.act(wte[:], wte[:], AF.Exp, [wte], [wte])
                    b.tt("dve", wte[:], wte[:], dts[:], ALU.mult, [wte, dts], [wte])
                    for d in range(2):
                        b.tt("dve", xdt[:, d, :].rearrange("p (e d) -> p e d", e=16), xs3,
                             dts[:, d * 16:(d + 1) * 16].unsqueeze(2).to_broadcast([128, 16, 64]), ALU.mult,
                             [xbcs, dts], [xdt])
                        b.tt("dve", xw[i2][:, d, :].rearrange("p (e d) -> p e d", e=16), xs3,
                             wte[:, d * 16:(d + 1) * 16].unsqueeze(2).to_broadcast([128, 16, 64]), ALU.mult,
                             [xbcs, wte], [xw[i2]])
                    bct = BCT[i2]
                    b.ld(bct[:], bct_d[c], [bct])
                    for g in range(2):
                        b.mm(pc[:, 128 + g * 128:128 + (g + 1) * 128], bct[:, g, :], bct[:, 2 + g, :], True, True,
                             [bct], [pc])
                    pcb = pc[:, 128:384].rearrange("p (g l) -> p g l", g=2)
                    b.tt("dve", cbm[:, 0:2, :], pcb, Um.unsqueeze(1).to_broadcast([128, 2, 128]), ALU.mult,
                         [pc, cst], [cbm])
                    b.tt("dve", cbm[:, 2:4, :], pcb, Lm.unsqueeze(1).to_broadcast([128, 2, 128]), ALU.mult,
                         [pc, cst], [cbm])
                    u = 0
                    for d in range(2):
                        tri = Um if d == 0 else Lm
                        Amat = Af if d == 0 else Ab
                        for q4 in range(4):
                            rq = rhsq[d * 4 + q4]
                            for e4 in range(4):
                                col = d * 16 + q4 * 4 + e4
                                b.act(rq[:, e4, :], tri, AF.Copy, [cst, la], [rq], scale=la[:, col:col + 1])
                        for q4 in range(4):
                            Pq = P[u % 2]
                            dtt = DT[u % 2]
                            u += 1
                            rq = rhsq[d * 4 + q4]
                            b.mm(Pq[:], Amat, rq[:].rearrange("p a b -> p (a b)"), True, True,
                                 [cst, rq], [Pq])
                            b.act(dtt[:], Pq[:], AF.Exp, [Pq], [dtt])
                            g = q4 // 2
                            b.tt("dve", M[:, d, q4 * 4:(q4 + 1) * 4, :], dtt[:].rearrange("p (a b) -> p a b", a=4),
                                 cbm[:, 2 * d + g, :].unsqueeze(1).to_broadcast([128, 4, 128]), ALU.mult,
                                 [dtt, cbm], [M])
                    for e in range(16):
                        Pq = P[2 + e // 8]
                        sl = slice((e % 8) * 64, (e % 8) * 64 + 64)
                        b.mm(Pq[:, sl], M[:, 0, e, :], xdt[:, 0, e * 64:(e + 1) * 64], True, False, [M, xdt], [Pq])
                        b.mm(Pq[:, sl], M[:, 1, e, :], xdt[:, 1, e * 64:(e + 1) * 64], False, True, [M, xdt], [Pq])
                    b.tt("dve", ydt[:].rearrange("p (e d) -> p e d", e=16), xs3,
                         dskip.unsqueeze(2).to_broadcast([128, 16, 64]), ALU.mult, [xbcs, prm], [ydt])
                    b.tt("dve", ydo[i2][:, 0:512], P[2][:], ydt[:, 0:512], ALU.add, [P[2], ydt], [ydo[i2]])
                    b.tt("dve", ydo[i2][:, 512:1024], P[3][:], ydt[:, 512:1024], ALU.add, [P[3], ydt], [ydo[i2]])
                    b.st(chunk_rows(yd_d, c), ydo[i2][:], [ydo[i2]])

                def b1_h2(c):
                    i2 = c % 2
                    first_of_seg = (c % SEGCH == 0)
                    last_of_seg = (c % SEGCH == SEGCH - 1)
                    bt = bcb[i2]
                    b.ld(bt[:], chunk_rows(btok_d, c), [bt])
                    for g in range(2):
                        b.mm(P[4 + g][:], bt[:, g * 128:(g + 1) * 128], xw[i2][:, 0, g * 512:(g + 1) * 512], True, True,
                             [bcb[i2], xw[i2]], [P[4 + g]])
                    for g in range(2):
                        b.cp("act", Sfo[i2][:, g * 512:(g + 1) * 512], P[4 + g][:], [P[4 + g]], [Sfo[i2]])
                    b.st(Sf_d[c], Sfo[i2][:], [Sfo[i2]])
                    b.st(decf_d[c], dec[i2][:, 0:16], [dec[i2]])
                    if last_of_seg:
                        if c == SEGCH - 1:
                            b.ts("dve", hbs[:], hbs[:], link[:, 0:1], None, ALU.mult, None, [hbs, link], [hbs])
                            b.ts("dve", rbs[:], rbs[:], link[:, 0:1], None, ALU.mult, None, [rbs, link], [rbs])
                        else:
                            b.ms("dve", hbs[:], 0.0, [hbs])
                            b.ms("dve", rbs[:], 0.0, [rbs])
                    b.cp("act", hbo[i2][:], hbs[:], [hbs], [hbo[i2]])
                    b.st(Hb_d[c], hbo[i2][:], [hbo[i2]])
                    for g in range(2):
                        b.mm(P[4 + g][:], bt[:, g * 128:(g + 1) * 128], xw[i2][:, 1, g * 512:(g + 1) * 512], True, True,
                             [bcb[i2], xw[i2]], [P[4 + g]])
                    b.tt("dve", hbs[:].rearrange("p (e d) -> p e d", e=16), hbs[:].rearrange("p (e d) -> p e d", e=16),
                         dec[i2][:, 16:32].unsqueeze(2).to_broadcast([128, 16, 64]), ALU.mult, [hbs, dec[i2]], [hbs])
                    for g in range(2):
                        b.tt("dve", hbs[:, g * 512:(g + 1) * 512], hbs[:, g * 512:(g + 1) * 512], P[4 + g][:], ALU.add,
                             [hbs, P[4 + g]], [hbs])
                    b.ld(qTl[:], qT0_d[c], [qTl])
                    b.ld(kTl[:], kT0_d[c], [kTl])
                    b.ld(kl[:], chunk_rows(k0_d, c), [kl])
                    b.ld(vl[:], chunk_rows(v0_d, c), [vl])
                    for par in range(2):
                        for hp in range(4):
                            base = par * 64
                            Pq = P[4 + par]
                            b.mm(Pq[:, hp * 128:(hp + 1) * 128], kTl[base:base + 64, hp, :], qTl[base:base + 64, hp, :],
                                 True, True, [kTl, qTl], [Pq])
                    for par in range(2):
                        b.tt("dve", SM[:, par:8:2, :], P[4 + par][:].rearrange("p (a b) -> p a b", a=4),
                             Dfb[:, par:8:2, :], ALU.mult, [P[4 + par], Dfb], [SM])
                    for h in range(8):
                        Pq = P[4 + h // 4]
                        b.mm(Pq[:, (h % 4) * 128:(h % 4 + 1) * 128], SM[:, h, :], vl[:, h * 128:(h + 1) * 128], True, True,
                             [SM, vl], [Pq])
                    for hh in range(2):
                        b.cp("act", yio[i2][:, hh * 512:(hh + 1) * 512], P[4 + hh][:], [P[4 + hh]], [yio[i2]])
                    b.st(chunk_rows(yi_d, c), yio[i2][:], [yio[i2]])
                    for d in range(2):
                        b.tt("dve", kd[:, d, :].rearrange("p (h d) -> p h d", h=8), kl[:].rearrange("p (h d) -> p h d", h=8),
                             rc[:, d * 8:(d + 1) * 8].unsqueeze(2).to_broadcast([128, 8, 64]), ALU.mult, [kl, rc], [kd])
                    def rstates(d, Pq):
                        for h in range(8):
                            hp, base = h // 2, (h % 2) * 64
                            b.mm(Pq[base:base + 64, hp * 128:(hp + 1) * 128], kd[:, d, h * 64:(h + 1) * 64],
                                 vl[:, h * 128:(h + 1) * 128], True, True, [kd, vl], [Pq])
                    rstates(0, P[4])
                    b.cp("act", Rfo[i2][:], P[4][:], [P[4]], [Rfo[i2]])
                    b.st(Rf_d[c], Rfo[i2][:], [Rfo[i2]])
                    b.cp("act", rbo[i2][:], rbs[:], [rbs], [rbo[i2]])
                    b.st(Rb_d[c], rbo[i2][:], [rbo[i2]])
                    rstates(1, P[5])
                    for par in range(2):
                        sl = slice(par * 64, par * 64 + 64)
                        gcol = rc[sl, 24 + par:32:2]
                        b.tt("dve", rbs[sl, :].rearrange("p (a v) -> p a v", a=4), rbs[sl, :].rearrange("p (a v) -> p a v", a=4),
                             gcol.unsqueeze(2).to_broadcast([64, 4, 128]), ALU.mult, [rbs, rc], [rbs])
                    b.tt("dve", rbs[:], rbs[:], P[5][:], ALU.add, [rbs, P[5]], [rbs])

                b.two_stage(b1_h1, b1_h2, (range(NCH - 1, -1, -1) if LOOPN.get("B1", NCH) == NCH else range(LOOPN["B1"] - 1, -1, -1)))
            S.barrier()
            with contextlib.ExitStack() as st:
                P, PTs = alloc_psum(st, 7, 1)
                PT = PTs[0]
                Wo = sb("Wo0", [128, 16, 1024], BF16, st)
                b.ldw(Wo, evout_d[0], 16, 1024)
                gpost = sb("gpost", [128, 1024], F32, st)
                b.ld(gpost[:], g_mix_post[0].partition_broadcast(128), [gpost])
                snw = sb("snw", [128, 1024], F32, st)
                b.ld(snw[:], snw_d[0].partition_broadcast(128), [snw])
                rgn = sb("rgn", [128, 1024], F32, st)
                b.ld(rgn[:], rgn_d[0].partition_broadcast(128), [rgn])
                prm = sb("prm2", [128, 16], F32, st)
                b.ld(prm[:, 0:16], rdec_d[0].rearrange("a b -> (a b)").partition_broadcast(128), [prm])
                b.act(prm[:], prm[:], AF.Exp, [prm], [prm], scale=-1.0)
                b.act(prm[:], prm[:], AF.Ln, [prm], [prm], bias=1.0)
                b.ts("dve", prm[:], prm[:], -1.0, None, ALU.mult, None, [prm], [prm])
                rc = sb("rc2", [128, 32], F32, st)
                b.ts("dve", rc[:, 0:8], prm[:, 0:8], POS[:, 2:3], None, ALU.mult, None, [prm, cst], [rc])
                b.ts("dve", rc[:, 8:16], prm[:, 8:16], POS[:, 3:4], None, ALU.mult, None, [prm, cst], [rc])
                b.ts("dve", rc[:, 16:32], prm[:, 0:16], 128.0, None, ALU.mult, None, [prm], [rc])
                b.act(rc[:], rc[:], AF.Exp, [rc], [rc])

                ydl = [sb(f"ydl{i}", [128, 1024], F32, st) for i in range(2)]
                yil = [sb(f"yil{i}", [128, 1024], F32, st) for i in range(2)]
                CTl = [sb(f"CTl{i}", [128, 2, 128], BF16, st) for i in range(2)]
                scl = [sb(f"scl{i}", [128, 32], F32, st) for i in range(2)]
                qTl = [sb(f"qTl{i}", [128, 4, 128], BF16, st) for i in range(2)]
                Hbl = [sb(f"Hbl{i}", [128, 1024], BF16, st) for i in range(2)]
                Rbl = [sb(f"Rbl{i}", [128, 512], BF16, st) for i in range(2)]
                Sfl = [sb(f"Sfl{i}", [128, 1024], F32, st) for i in range(2)]
                dfl = [sb(f"dfl{i}", [128, 16], F32, st) for i in range(2)]
                Rfl = [sb(f"Rfl{i}", [128, 512], F32, st) for i in range(2)]
                szl = [sb(f"szl{i}", [128, 1024], F32, st) for i in range(2)]
                sgl = [sb(f"sgl{i}", [128, 1024], F32, st) for i in range(2)]
                xt = [sb(f"xt{i}", [128, 1024], F32, st) for i in range(2)]
                hfs = sb("hfs", [128, 1024], F32, st)
                hfb = sb("hfb", [128, 1024], BF16, st)
                rfs = sb("rfs", [128, 512], F32, st)
                rfb = sb("rfb", [128, 512], BF16, st)
                t1 = sb("t1", [128, 1024], F32, st)
                ys_ = [sb(f"ys{i}", [128, 1024], F32, st) for i in range(2)]
                yr_ = [sb(f"yr{i}", [128, 1024], F32, st) for i in range(2)]
                st8 = sb("st8", [128, 24], F32, st)
                mix = sb("mix", [128, 2048], BF16, st)
                mixT = sb("mixT", [128, 16, 128], BF16, st)
                junk = sb("junk", [128, 1024], BF16, st)
                jf = sb("jf", [128, 1024], F32, st)
                ss = sb("ss", [128, 2], F32, st)
                tn = sb("tn", [128, 1024], F32, st)
                xo = [sb(f"xo{i}", [128, 1024], F32, st) for i in range(2)]
                b.ms("dve", hfs[:], 0.0, [hfs])
                b.ms("dve", rfs[:], 0.0, [rfs])
                def b2_g1(c):
                    i2 = c % 2
                    ys = ys_[i2]
                    yr = yr_[i2]
                    b.ld(ydl[i2][:], chunk_rows(yd_d, c), [ydl[i2]])
                    b.ld(yil[i2][:], chunk_rows(yi_d, c), [yil[i2]])
                    b.ld(CTl[i2][:], bct_d[c][:, 2:4, :], [CTl[i2]])
                    b.ld(scl[i2][:], chunk_rows(sc_d, c), [scl[i2]])
                    b.ld(qTl[i2][:], qT0_d[c], [qTl[i2]])
                    b.ld(Hbl[i2][:], Hb_d[c], [Hbl[i2]])
                    b.ld(Rbl[i2][:], Rb_d[c], [Rbl[i2]])
                    b.ld(Sfl[i2][:], Sf_d[c], [Sfl[i2]])
                    b.ld(dfl[i2][:], decf_d[c], [dfl[i2]])
                    b.ld(Rfl[i2][:], Rf_d[c], [Rfl[i2]])
                    b.ld(szl[i2][:], chunk_rows(sz_d, c), [szl[i2]])
                    b.ld(sgl[i2][:], chunk_rows(sg_d, c), [sgl[i2]])
                    b.ld(xt[i2][:], chunk_rows(src, c), [xt[i2]])
                    if c % SEGCH == 0 and c > 0:
                        if c == SEGCH:
                            b.ts("dve", hfs[:], hfs[:], link[:, 0:1], None, ALU.mult, None, [hfs, link], [hfs])
                            b.ts("dve", rfs[:], rfs[:], link[:, 0:1], None, ALU.mult, None, [rfs, link], [rfs])
                        else:
                            b.ms("dve", hfs[:], 0.0, [hfs])
                            b.ms("dve", rfs[:], 0.0, [rfs])
                    b.cp("act", hfb[:], hfs[:], [hfs], [hfb])
                    b.cp("act", rfb[:], rfs[:], [rfs], [rfb])
                    for g in range(2):
                        b.mm(P[g][:], CTl[i2][:, g, :], hfb[:, g * 512:(g + 1) * 512], True, True, [CTl[i2], hfb], [P[g]])
                        b.mm(P[2 + g][:], CTl[i2][:, g, :], Hbl[i2][:, g * 512:(g + 1) * 512], True, True,
                             [CTl[i2], Hbl[i2]], [P[2 + g]])
                    for g in range(2):
                        for d in range(2):
                            Pq = P[2 * d + g]
                            b.tt("dve", t1[:, g * 512:(g + 1) * 512].rearrange("p (e d) -> p e d", e=8),
                                 Pq[:].rearrange("p (e d) -> p e d", e=8),
                                 scl[i2][:, d * 16 + g * 8:d * 16 + (g + 1) * 8].unsqueeze(2).to_broadcast([128, 8, 64]),
                                 ALU.mult, [Pq, scl[i2]], [t1])
                            src_y = ydl[i2] if d == 0 else ys
                            b.tt("dve", ys[:, g * 512:(g + 1) * 512], t1[:, g * 512:(g + 1) * 512],
                                 src_y[:, g * 512:(g + 1) * 512], ALU.add, [t1, src_y], [ys])
                    b.tt("dve", hfs[:].rearrange("p (e d) -> p e d", e=16), hfs[:].rearrange("p (e d) -> p e d", e=16),
                         dfl[i2][:].unsqueeze(2).to_broadcast([128, 16, 64]), ALU.mult, [hfs, dfl[i2], hfb], [hfs])
                    b.tt("dve", hfs[:], hfs[:], Sfl[i2][:], ALU.add, [hfs, Sfl[i2]], [hfs])
                    for d, (Pa, Pb2), rsrc in ((0, (P[4], P[0]), rfb), (1, (P[1], P[2]), Rbl[i2])):
                        t1v = t1[:].rearrange("p (h v) -> p h v", h=8)
                        for par, Pq in ((0, Pa), (1, Pb2)):
                            base = par * 64
                            for hp in range(4):
                                b.mm(Pq[:, hp * 128:(hp + 1) * 128], qTl[i2][base:base + 64, hp, :],
                                     rsrc[base:base + 64, hp * 128:(hp + 1) * 128], True, True, [qTl[i2], rsrc], [Pq])
                            b.tt("dve", t1v[:, par:8:2, :], Pq[:].rearrange("p (a v) -> p a v", a=4),
                                 rc[:, d * 8 + par:d * 8 + 8:2].unsqueeze(2).to_broadcast([128, 4, 128]),
                                 ALU.mult, [Pq, rc], [t1])
                        src_y = yil[i2] if d == 0 else yr
                        b.tt("dve", yr[:], t1[:], src_y[:], ALU.add, [t1, src_y], [yr])
                    for par in range(2):
                        sl = slice(par * 64, par * 64 + 64)
                        gcol = rc[sl, 16 + par:24:2]
                        b.tt("dve", rfs[sl, :].rearrange("p (a v) -> p a v", a=4), rfs[sl, :].rearrange("p (a v) -> p a v", a=4),
                             gcol.unsqueeze(2).to_broadcast([64, 4, 128]), ALU.mult, [rfs, rc, rfb], [rfs])
                    b.tt("dve", rfs[:], rfs[:], Rfl[i2][:], ALU.add, [rfs, Rfl[i2]], [rfs])

                def b2_g2(c):
                    i2 = c % 2
                    ys = ys_[i2]
                    yr = yr_[i2]
                    b.tt("dve", ys[:], ys[:], szl[i2][:], ALU.mult, [ys, szl[i2]], [ys])
                    for g in range(2):
                        b.act(junk[:, g * 512:(g + 1) * 512], ys[:, g * 512:(g + 1) * 512], AF.Square, [ys], [junk, st8],
                              accum=st8[:, g:g + 1])
                    b.act(st8[:, 0:2], st8[:, 0:2], AF.Ln, [st8], [st8], scale=1.0 / 512, bias=EPS)
                    b.act(st8[:, 0:2], st8[:, 0:2], AF.Exp, [st8], [st8], scale=-0.5)
                    for g in range(2):
                        b.stt(mix[:, g * 512:(g + 1) * 512], ys[:, g * 512:(g + 1) * 512], st8[:, g:g + 1],
                              snw[:, g * 512:(g + 1) * 512], ALU.mult, ALU.mult, [ys, st8, snw], [mix])
                    yr3 = yr[:].rearrange("p (h v) -> p h v", h=8)
                    b.red(st8[:, 8:16], yr3, [yr], [st8])
                    b.act(jf[:], yr[:], AF.Square, [yr], [jf])
                    b.red(st8[:, 16:24], jf[:].rearrange("p (h v) -> p h v", h=8), [jf], [st8])
                    b.ts("dve", st8[:, 8:24], st8[:, 8:24], 1.0 / 128, None, ALU.mult, None, [st8], [st8])
                    b.tt("dve", jf[:, 0:8], st8[:, 8:16], st8[:, 8:16], ALU.mult, [st8], [jf])
                    b.tt("dve", st8[:, 16:24], st8[:, 16:24], jf[:, 0:8], ALU.subtract, [st8, jf], [st8])
                    b.act(st8[:, 16:24], st8[:, 16:24], AF.Ln, [st8], [st8], bias=EPS)
                    b.act(st8[:, 16:24], st8[:, 16:24], AF.Exp, [st8], [st8], scale=-0.5)
                    b.tt("dve", yr3, yr3, st8[:, 8:16].unsqueeze(2).to_broadcast([128, 8, 128]), ALU.subtract,
                         [yr, st8], [yr])
                    b.tt("dve", yr3, yr3, st8[:, 16:24].unsqueeze(2).to_broadcast([128, 8, 128]), ALU.mult,
                         [yr, st8], [yr])
                    b.tt("dve", yr[:], yr[:], rgn[:], ALU.mult, [yr, rgn], [yr])
                    b.tt("dve", mix[:, 1024:2048], yr[:], sgl[i2][:], ALU.mult, [yr, sgl[i2]], [mix])
                    for half in range(2):
                        for k in range(8):
                            kk = half * 8 + k
                            b.tr(PT[:, k * 128:(k + 1) * 128], mix[:, kk * 128:(kk + 1) * 128], idb[:], [mix, idb], [PT])
                        b.cp("act", mixT[:, half * 8:(half + 1) * 8, :],
                             PT[:, 0:1024].rearrange("p (k t) -> p k t", k=8), [PT], [mixT])
                    for nt, Pq in ((0, P[5]), (1, P[6])):
                        for k in range(16):
                            b.mm(Pq[:], mixT[:, k, :], Wo[:, k, nt * 512:(nt + 1) * 512], k == 0, k == 15, [mixT, Wo], [Pq])
                    post_norm_residual(P[5], P[6], gpost, xt[i2][:], xt[i2], junk, ss, tn, xo[i2], chunk_rows(dst, c))

                b.two_stage(b2_g1, b2_g2, crange("B2"))
            S.barrier()

        hm = sb("hm", [128, 8], F32)
        b.ld(hm[:], cst_d[:, 8, 8:16], [hm])
        b.ts("dve", hm[:, 0:2], hm[:, 4:6], -1.0, 1.0, ALU.mult, ALU.add, [hm], [hm])
        b.ts("dve", hm[:, 0:2], hm[:, 0:2], link[:, 0:1], None, ALU.mult, None, [hm, link], [hm])
        b.tt("dve", hm[:, 6:8], hm[:, 0:2], hm[:, 4:6], ALU.add, [hm], [hm])

        stages = []
        if 0 in layers:
            if do_mix:
                stages.append(("mix0",))
            if do_ffn:
                stages.append(("ffn0",))
        if 1 in layers:
            if do_mix:
                stages.append(("mix1",))
            if do_ffn:
                stages.append(("ffn1",))
        bufs = [xa, xb]
        cur = xin
        S.barrier()
        for si, (sname,) in enumerate(stages):
            dst = yout if si == len(stages) - 1 else bufs[si % 2]
            if sname == "mix0":
                even_layer(cur, dst)
            elif sname == "mix1":
                odd_layer(cur, dst)
            elif sname == "ffn0":
                ffn_loop(0, cur, dst)
            elif sname == "ffn1":
                ffn_loop(1, cur, dst)
            cur = dst
        counts = S.emit()
    return nc, counts


def _consts():
    p = np.arange(128)
    r, l = p[:, None], p[None, :]
    cst = np.zeros((128, 9, 128), np.float32)
    cst[:, 0] = (r == l)
    cst[:, 1] = (r <= l)
    cst[:, 2] = (r >= l)
    cst[:, 3] = (r > l)
    cst[:, 4] = (r < l)
    cst[:, 5] = 1.0
    cst[:, 6] = np.maximum(l - r, 0)
    cst[:, 7] = np.maximum(r - l, 0)
    cst[:, 8, 0] = 127 - p
    cst[:, 8, 1] = p
    cst[:, 8, 2] = p + 1
    cst[:, 8, 3] = 128 - p
    cst[:, 8, 12] = (p != 127)
    cst[:, 8, 13] = (p < 126)
    am = np.zeros((128, 17, 128), np.float32)
    for o in range(17):
        j = o - 8
        d = np.abs(l - r - 128 * j)
        am[:, o, :] = (d <= 64).astype(np.float32) + ((d % 4 == 0) & (d <= 256)) + ((d % 16 == 0) & (d <= 1024))
    return cst, am


def _rope(pos):
    inv_freq = (10000.0 ** (-np.arange(0, 64, 2, dtype=np.float32) / np.float32(64))).astype(np.float32)
    ang = pos.astype(np.float32)[:, None] * inv_freq[None, :]
    return np.cos(ang).astype(np.float32), np.sin(ang).astype(np.float32)


WEIGHT_NAMES = ["norm_mix_pre", "norm_mix_post", "norm_ffn_pre", "norm_ffn_post", "ffn_w1", "ffn_w2",
                "ev_in_proj", "ev_conv_w", "ev_conv_b", "ssd_dt_bias", "ssd_a_log", "ssd_d", "ssd_norm_w",
                "ret_decay", "ret_gn_w", "ev_out_proj", "od_in_proj", "gmlp_norm_w", "gmlp_ws", "gmlp_bs",
                "od_out_proj"]


def make_in_maps(inputs):
    xp = np.asarray(inputs["x_prompt"], np.float32)
    xs = np.asarray(inputs["x_sample"], np.float32)
    cst, am = _consts()
    pos_a = np.concatenate([np.arange(4096), np.arange(2048)])
    pos_b = np.concatenate([np.arange(2048)] * 3)
    ca, sa = _rope(pos_a)
    cb, sb_ = _rope(pos_b)
    w = {k: np.ascontiguousarray(np.asarray(inputs[k], np.float32)) for k in WEIGHT_NAMES}
    maps = []
    for c in range(NCORES):
        if c < 4:
            x = np.concatenate([xp[c], xs[c]], axis=0)
            lk = np.ones((128, 1), np.float32)
            rc, rs = ca, sa
        else:
            i0 = 4 + 3 * (c - 4)
            x = np.concatenate([xs[i0], xs[i0 + 1], xs[i0 + 2]], axis=0)
            lk = np.zeros((128, 1), np.float32)
            rc, rs = cb, sb_
        m = {"xin": np.ascontiguousarray(x), "link": lk, "rcos": rc, "rsin": rs, "cst": cst, "amask": am}
        m.update(w)
        maps.append(m)
    return maps


def gather(results):
    yp = np.zeros((4, 4096, D), np.float32)
    ys = np.zeros((16, 2048, D), np.float32)
    for c in range(NCORES):
        y = np.asarray(results[c]["yout"], np.float32)
        if c < 4:
            yp[c] = y[0:4096]
            ys[c] = y[4096:6144]
        else:
            i0 = 4 + 3 * (c - 4)
            for k in range(3):
                ys[i0 + k] = y[k * 2048:(k + 1) * 2048]
    return yp, ys


_NC_CACHE = {}


def kernel(**inputs):
    if "nc" not in _NC_CACHE:
        _NC_CACHE["nc"] = build_nc()[0]
    nc = _NC_CACHE["nc"]
    maps = make_in_maps(inputs)
    res = run_bass_kernel_spmd(nc, maps, core_ids=list(range(NCORES)))
    return gather(res.results)
```

```python
import contextlib
import numpy as np
import concourse.bass as bass
import concourse.mybir as mybir
from concourse.bass_utils import run_bass_kernel_spmd

F32 = mybir.dt.float32
BF16 = mybir.dt.bfloat16
ALU = mybir.AluOpType
AF = mybir.ActivationFunctionType
AX = mybir.AxisListType

ENGS = ("pe", "act", "dve", "pool", "sp")
DMA_ENGS = ("sp", "act", "pool")
EPOCH = 30000
NDSEM = 8

NCORES = 8
T = 6144
NCH = 48
SEGCH = 16
D = 1024
EPS = 1e-6
LOOPN = {}
USE_SCHED = True
SCHED_W = 64
SCHED_HOP = 120.0
AT_LAG = 8
AT_KEEP = False


def crange(name):
    return range(LOOPN.get(name, NCH))


class Tok:
    __slots__ = ("w", "rs")

    def __init__(self):
        self.w = None
        self.rs = []


class Op:
    __slots__ = ("eng", "fn", "deps", "dma", "needed", "seq", "dsem", "dval", "barrier", "snap", "cost", "lat")

    def __init__(self, eng, fn, deps, dma, barrier=False, cost=300.0, lat=0.0):
        self.cost = cost
        self.lat = lat
        self.eng = eng
        self.fn = fn
        self.deps = deps
        self.dma = dma
        self.needed = False
        self.seq = None
        self.dsem = None
        self.dval = None
        self.barrier = barrier
        self.snap = None


class Sched:
    def __init__(self, nc):
        self.nc = nc
        self.ops = []
        self.keep = False
        self.keep_set = set()

    def op(self, eng, fn, reads=(), writes=(), dma=False, cost=300.0, lat=0.0):
        idx = len(self.ops)
        deps = set()
        for t in reads:
            if t.w is not None:
                deps.add(t.w)
        for t in writes:
            if t.w is not None:
                deps.add(t.w)
            for r in t.rs:
                deps.add(r)
        self.ops.append(Op(eng, fn, deps, dma, cost=cost, lat=lat))
        if self.keep:
            self.keep_set.add(idx)
        for t in reads:
            t.rs.append(idx)
        for t in writes:
            t.w = idx
            t.rs = []
        return idx

    def barrier(self):
        for e in ENGS:
            self.ops.append(Op(e, None, set(), False, barrier=True))

    def schedule(self, W=12, hop=120.0):
        ops = self.ops
        n = len(ops)
        fin = [0.0] * n
        done = [False] * n
        order = {e: [] for e in ENGS}
        seg_start = 0
        bounds = []
        i = 0
        while i < n:
            if ops[i].barrier:
                bounds.append((seg_start, i))
                j = i
                while j < n and ops[j].barrier:
                    j += 1
                bounds.append(("barrier", i, j))
                seg_start = j
                i = j
            else:
                i += 1
        bounds.append((seg_start, n))
        for bnd in bounds:
            if bnd[0] == "barrier":
                for k in range(bnd[1], bnd[2]):
                    order[ops[k].eng].append(k)
                    done[k] = True
                continue
            lo, hi = bnd
            if hi <= lo:
                continue
            pend = {e: [] for e in ENGS}
            for k in range(lo, hi):
                pend[ops[k].eng].append(k)
            if lo in self.keep_set:
                for e in ENGS:
                    order[e].extend(pend[e])
                for k in range(lo, hi):
                    done[k] = True
                continue
            pos = {e: 0 for e in ENGS}
            et = {e: 0.0 for e in ENGS}
            remaining = hi - lo
            while remaining:
                best = None
                for e in ENGS:
                    lst = pend[e]
                    cnt = 0
                    p = pos[e]
                    while p < len(lst) and done[lst[p]]:
                        p += 1
                    pos[e] = p
                    q = p
                    while q < len(lst) and cnt < W:
                        k = lst[q]
                        q += 1
                        if done[k]:
                            continue
                        cnt += 1
                        o = ops[k]
                        rdy = 0.0
                        ok = True
                        for d in o.deps:
                            if not done[d]:
                                ok = False
                                break
                            f = fin[d] + (0.0 if ops[d].eng == e and not ops[d].dma else hop)
                            if f > rdy:
                                rdy = f
                        if not ok:
                            continue
                        stt = rdy if rdy > et[e] else et[e]
                        key = (stt, k)
                        if best is None or key < best[0]:
                            best = (key, e, k)
                assert best is not None, "scheduler deadlock"
                (stt, k), e, k = best
                o = ops[k]
                et[e] = stt + o.cost
                fin[k] = stt + o.cost + o.lat
                done[k] = True
                order[e].append(k)
                remaining -= 1
        return order

    def emit(self):
        nc = self.nc
        ops = self.ops
        sched_order = self.schedule(W=SCHED_W, hop=SCHED_HOP) if USE_SCHED else None
        for o in ops:
            for d in o.deps:
                ops[d].needed = True
        cnt = {e: 0 for e in ENGS}
        dcnt = {e: 0 for e in DMA_ENGS}
        if sched_order is not None:
            walk = []
            ptr = {e: 0 for e in ENGS}
            nb = sum(1 for o in ops if o.barrier) // len(ENGS)
            for _ in range(nb + 1):
                for e in ENGS:
                    lst = sched_order[e]
                    p = ptr[e]
                    while p < len(lst) and not ops[lst[p]].barrier:
                        walk.append(lst[p])
                        p += 1
                    ptr[e] = p
                for e in ENGS:
                    lst = sched_order[e]
                    if ptr[e] < len(lst):
                        walk.append(lst[ptr[e]])
                        ptr[e] += 1
            walk_ops = [ops[k] for k in walk]
        else:
            walk_ops = ops
        last = {e: None for e in ENGS}
        for o in walk_ops:
            if o.barrier:
                for e in ENGS:
                    if last[e] is not None:
                        last[e].needed = True
            elif not o.dma:
                last[o.eng] = o
        for o in walk_ops:
            if o.barrier:
                o.snap = (dict(cnt), dict(dcnt))
            elif o.dma:
                i = dcnt[o.eng]
                dcnt[o.eng] += 1
                o.dsem = i % NDSEM
                o.dval = 16 * (i // NDSEM + 1)
                assert o.dval < 60000, "too many DMAs on one queue"
            elif o.needed:
                cnt[o.eng] += 1
                o.seq = cnt[o.eng]
        nep = {e: (cnt[e] + EPOCH - 1) // EPOCH for e in ENGS}
        with contextlib.ExitStack() as st:
            sems = {e: [st.enter_context(nc.semaphore(f"s_{e}_{k}")) for k in range(max(1, nep[e]))]
                    for e in ENGS}
            dsems = {e: [st.enter_context(nc.semaphore(f"d_{e}_{k}")) for k in range(NDSEM)]
                     for e in DMA_ENGS}
            block = st.enter_context(nc.Block())
            if sched_order is not None:
                per_eng = sched_order
            else:
                per_eng = {e: [] for e in ENGS}
                for i, o in enumerate(ops):
                    per_eng[o.eng].append(i)

            def run_engine(ename, engobj):
                seen = {}

                def wait(key, sem, val):
                    if val <= 0 or seen.get(key, 0) >= val:
                        return
                    seen[key] = val
                    engobj.wait_ge(sem, val)

                def wait_all(c_snap, d_snap):
                    for e in ENGS:
                        n = c_snap[e]
                        if n > 0:
                            ep = (n - 1) // EPOCH
                            wait(("c", e, ep), sems[e][ep], n - ep * EPOCH)
                    for e in DMA_ENGS:
                        n = d_snap[e]
                        for k in range(min(n, NDSEM)):
                            lastk = ((n - 1 - k) // NDSEM) * NDSEM + k
                            wait(("d", e, k), dsems[e][k], 16 * (lastk // NDSEM + 1))

                for i in per_eng[ename]:
                    o = ops[i]
                    if o.barrier:
                        wait_all(*o.snap)
                        continue
                    for d in sorted(o.deps):
                        p = ops[d]
                        if ename == "pe" and p.eng == "pe" and not p.dma:
                            continue
                        if p.dma:
                            wait(("d", p.eng, p.dsem), dsems[p.eng][p.dsem], p.dval)
                        else:
                            ep = (p.seq - 1) // EPOCH
                            wait(("c", p.eng, ep), sems[p.eng][ep], p.seq - ep * EPOCH)
                    if o.dma:
                        wait(("d", o.eng, o.dsem), dsems[o.eng][o.dsem], o.dval - 16)
                        ins = o.fn(engobj)
                        ins.then_inc(dsems[o.eng][o.dsem], 16)
                    else:
                        ins = o.fn(engobj)
                        if o.needed:
                            ep = (o.seq - 1) // EPOCH
                            ins.then_inc(sems[o.eng][ep], 1)
                wait_all({e: 0 for e in ENGS}, dcnt)

            @block.tensor
            def _(e):
                run_engine("pe", e)

            @block.scalar
            def _(e):
                run_engine("act", e)

            @block.vector
            def _(e):
                run_engine("dve", e)

            @block.gpsimd
            def _(e):
                run_engine("pool", e)

            @block.sync
            def _(e):
                run_engine("sp", e)
        return {e: len(per_eng[e]) for e in ENGS}


class Tl:
    def __init__(self, h):
        self.h = h
        self.t = Tok()

    def __getitem__(self, k):
        return self.h[k]


class B:
    def __init__(self, nc):
        self.nc = nc
        self.S = Sched(nc)
        self.cap = None

    def _rec(self, eng, fn, R, W, dma=False, cost=300.0, lat=0.0):
        reads = [x.t for x in R]
        writes = [x.t for x in W]
        if self.cap is not None:
            self.cap.append((eng, fn, reads, writes, dma, cost, lat))
        else:
            self.S.op(eng, fn, reads, writes, dma, cost, lat)

    @staticmethod
    def _fd(ap):
        n = 1
        for d in ap.shape[1:]:
            n *= int(d)
        return n

    def _ecost(self, eng, out):
        n = self._fd(out)
        if eng == "act":
            return (200.0 + n) / 1.2
        if eng == "dve":
            return (150.0 + n) / 0.96
        return 150.0 + 2.2 * n

    def capture(self, f, *args):
        old = self.cap
        self.cap = []
        f(*args)
        lst = self.cap
        self.cap = old
        return lst

    def emit_merged(self, la, lb):
        i = j = 0
        while i < len(la) or j < len(lb):
            if j >= len(lb) or (i < len(la) and i * len(lb) <= j * len(la)):
                self.S.op(*la[i])
                i += 1
            else:
                self.S.op(*lb[j])
                j += 1

    def two_stage(self, f1, f2, items):
        prev = None
        for k in items:
            l1 = self.capture(f1, k)
            l2 = self.capture(f2, prev) if prev is not None else []
            self.emit_merged(l2, l1)
            prev = k
        if prev is not None:
            self.emit_merged(self.capture(f2, prev), [])

    def mm(self, out, lhsT, rhs, start, stop, R, W):
        c = max(64, self._fd(rhs)) / 2.4 * (4.0 if lhsT.dtype == F32 else 1.0)
        self._rec("pe", lambda e: e.matmul(out, lhsT=lhsT, rhs=rhs, start=start, stop=stop),
                  R, W, cost=c, lat=60.0)

    def tr(self, out, in_, ident, R, W):
        self._rec("pe", lambda e: e.transpose(out=out, in_=in_, identity=ident),
                  R, W, cost=60.0, lat=60.0)

    def act(self, out, in_, func, R, W, scale=1.0, bias=0.0, accum=None):
        if accum is None:
            self._rec("act", lambda e: e.activation(out=out, in_=in_, func=func, scale=scale, bias=bias),
                      R, W, cost=self._ecost("act", out))
        else:
            self._rec("act", lambda e: e.activation(out=out, in_=in_, func=func, scale=scale, bias=bias,
                                                    accum_out=accum),
                      R, W, cost=self._ecost("act", out) + 80.0)

    def tt(self, eng, out, in0, in1, op, R, W):
        self._rec(eng, lambda e: e.tensor_tensor(out=out, in0=in0, in1=in1, op=op),
                  R, W, cost=self._ecost(eng, out))

    def ts(self, eng, out, in0, s1, s2, op0, op1, R, W):
        if op1 is None:
            self._rec(eng, lambda e: e.tensor_scalar(out=out, in0=in0, scalar1=s1, scalar2=None, op0=op0),
                      R, W, cost=self._ecost(eng, out))
        else:
            self._rec(eng, lambda e: e.tensor_scalar(out=out, in0=in0, scalar1=s1, scalar2=s2, op0=op0, op1=op1),
                      R, W, cost=self._ecost(eng, out))

    def stt(self, out, in0, scalar, in1, op0, op1, R, W):
        self._rec("dve", lambda e: e.scalar_tensor_tensor(out=out, in0=in0, scalar=scalar, in1=in1, op0=op0, op1=op1),
                  R, W, cost=self._ecost("dve", out))

    def cp(self, eng, out, in_, R, W):
        if eng == "act":
            self._rec("act", lambda e: e.copy(out=out, in_=in_), R, W, cost=self._ecost("act", out))
        else:
            self._rec(eng, lambda e: e.tensor_copy(out=out, in_=in_), R, W, cost=self._ecost(eng, out))

    def ms(self, eng, ap, val, W):
        self._rec(eng, lambda e: e.memset(ap, val), [], W, cost=self._ecost(eng, ap))

    def red(self, out, in_, R, W):
        self._rec("dve", lambda e: e.tensor_reduce(out=out, in_=in_, axis=AX.X, op=ALU.add),
                  R, W, cost=self._ecost("dve", in_))

    def ld(self, out, in_, W, eng="sp"):
        nbytes = self._fd(out) * 4 * 128
        self._rec(eng, lambda e: e.dma_start(out=out, in_=in_), [], W, dma=True,
                  cost=(400.0 if eng == "sp" else 700.0), lat=2000.0 + nbytes / 150.0)

    def st(self, out, in_, R, eng="pool"):
        nbytes = self._fd(in_) * 4 * 128
        self._rec(eng, lambda e: e.dma_start(out=out, in_=in_), R, [], dma=True,
                  cost=(400.0 if eng == "sp" else 700.0), lat=2000.0 + nbytes / 150.0)

    def ldw(self, wt, wd, kc_n, ncols):
        for kc in range(kc_n):
            for c0 in range(0, ncols, 2048):
                c1 = min(ncols, c0 + 2048)
                self.ld(wt[:, kc, c0:c1], wd[kc * 128:(kc + 1) * 128, c0:c1], [wt], eng="pool")


def seg_of(c):
    return c // SEGCH


def build_nc(layers=(0, 1), do_mix=True, do_ffn=True):
    nc = bass.Bass("TRN2", target_bir_lowering=False)
    b = B(nc)
    S = b.S

    def din(name, shape):
        return nc.dram_tensor(name, list(shape), F32, kind="ExternalInput").ap()

    def dscr(name, shape, dt=F32):
        return nc.dram_tensor(name, list(shape), dt, kind="Internal").ap()

    xin = din("xin", [T, D])
    yout = nc.dram_tensor("yout", [T, D], F32, kind="ExternalOutput").ap()
    link_d = din("link", [128, 1])
    cos_d = din("rcos", [T, 32])
    sin_d = din("rsin", [T, 32])
    cst_d = din("cst", [128, 9, 128])
    amask_d = din("amask", [128, 17, 128])
    g_mix_pre = din("norm_mix_pre", [2, D])
    g_mix_post = din("norm_mix_post", [2, D])
    g_ffn_pre = din("norm_ffn_pre", [2, D])
    g_ffn_post = din("norm_ffn_post", [2, D])
    w1_d = din("ffn_w1", [2, D, 4096])
    w2_d = din("ffn_w2", [2, 4096, D])
    evin_d = din("ev_in_proj", [1, D, 5664])
    convw_d = din("ev_conv_w", [1, 5, 1536])
    convb_d = din("ev_conv_b", [1, 1536])
    dtb_d = din("ssd_dt_bias", [1, 2, 16])
    alog_d = din("ssd_a_log", [1, 2, 16])
    dsk_d = din("ssd_d", [1, 16])
    snw_d = din("ssd_norm_w", [1, 1024])
    rdec_d = din("ret_decay", [1, 2, 8])
    rgn_d = din("ret_gn_w", [1, 1024])
    evout_d = din("ev_out_proj", [1, 2048, D])
    odin_d = din("od_in_proj", [1, D, 4096])
    gnw_d = din("gmlp_norm_w", [1, 512])
    gws_d = din("gmlp_ws", [1, 8, 128, 128])
    gbs_d = din("gmlp_bs", [1, 8, 128])
    odout_d = din("od_out_proj", [1, 1536, D])

    xa = dscr("xa", [T, D])
    xb = dscr("xb", [T, D])
    sz_d = dscr("sz", [T, 1024])
    sg_d = dscr("sgt", [T, 1024])
    xbc_d = dscr("xbc", [T + 4, 1536])
    dtr_d = dscr("dtr", [T, 32])
    qT0_d = dscr("qT0", [NCH, 128, 4, 128], BF16)
    kT0_d = dscr("kT0", [NCH, 128, 4, 128], BF16)
    k0_d = dscr("k0", [T, 512], BF16)
    v0_d = dscr("v0", [T, 1024], BF16)
    yd_d = dscr("yd", [T, 1024])
    yi_d = dscr("yi", [T, 1024])
    CT_d = dscr("CTd", [NCH, 128, 2, 128], BF16)
    sc_d = dscr("scd", [T, 32])
    Sf_d = dscr("Sfd", [NCH, 128, 1024])
    decf_d = dscr("decf", [NCH, 128, 16])
    Rf_d = dscr("Rfd", [NCH, 128, 512])
    Hb_d = dscr("Hbd", [NCH, 128, 1024], BF16)
    Rb_d = dscr("Rbd", [NCH, 128, 512], BF16)
    xs_d = dscr("xsd", [T, 1024])
    bct_d = dscr("bctd", [NCH, 128, 4, 128], BF16)
    btok_d = dscr("btokd", [T, 256], BF16)
    qT1_d = dscr("qT1", [NCH, 128, 8, 128], BF16)
    kT1_d = dscr("kT1", [NCH, 128, 8, 128], BF16)
    v1_d = dscr("v1", [NCH, 128, 16 * 65], BF16)
    sg1_d = dscr("sg1", [T, 512], BF16)

    def chunk_rows(ap, c):
        return ap[c * 128:(c + 1) * 128, :]

    with contextlib.ExitStack() as gst:
        uid = [0]

        def sb(name, shape, dt, st=gst):
            uid[0] += 1
            return Tl(st.enter_context(nc.sbuf_tensor(f"s{uid[0]}_{name}", list(shape), dt)))

        def psum(name, shape, dt, st=gst):
            uid[0] += 1
            return Tl(st.enter_context(nc.psum_tensor(f"p{uid[0]}_{name}", list(shape), dt)))

        cst = sb("cst", [128, 9, 128], F32)
        b.ld(cst[:], cst_d[:, :, :], [cst])
        idb = sb("idb", [128, 128], BF16)
        b.cp("dve", idb[:], cst[:, 0, :], [cst], [idb])
        Um, Lm, Af, Ab, ONES = (cst[:, 1, :], cst[:, 2, :], cst[:, 3, :], cst[:, 4, :], cst[:, 5, :])
        RDp, RDn = cst[:, 6, :], cst[:, 7, :]
        POS = cst[:, 8, :]
        link = sb("link", [128, 1], F32)
        b.ld(link[:], link_d[:, :], [link])
        def alloc_psum(st, nf32, nbf):
            assert nf32 + nbf <= 8
            return ([psum(f"P{i}", [128, 512], F32, st) for i in range(nf32)],
                    [psum(f"PT{i}", [128, 1024], BF16, st) for i in range(nbf)])

        def rstd_from_ss(ss, n, R):
            b.act(ss[:, 0:1], ss[:, 0:1], AF.Ln, [ss] + R, [ss], scale=1.0 / n, bias=EPS)
            b.act(ss[:, 0:1], ss[:, 0:1], AF.Exp, [ss], [ss], scale=-0.5)

        def norm_to_hT(x_ap, xT, gain, junk, ss, hb, hT_ap, hT, PT):
            b.act(junk[:], x_ap, AF.Square, [xT], [junk, ss], accum=ss[:, 0:1])
            rstd_from_ss(ss, 1024.0, [])
            b.stt(hb[:], x_ap, ss[:, 0:1], gain[:], ALU.mult, ALU.mult, [xT, ss, gain], [hb])
            for k in range(8):
                b.tr(PT[:, k * 128:(k + 1) * 128], hb[:, k * 128:(k + 1) * 128], idb[:], [hb, idb], [PT])
            b.cp("act", hT_ap, PT[:, 0:1024].rearrange("p (k t) -> p k t", k=8), [PT], [hT])

        def post_norm_residual(Pa, Pb, gain, x_ap, xT, junk, ss, tn, xo, out_dram):
            ss2 = ss
            b.act(junk[:, 0:512], Pa[:], AF.Square, [Pa], [junk, ss2], accum=ss2[:, 0:1])
            b.act(junk[:, 512:1024], Pb[:], AF.Square, [Pb], [junk, ss2], accum=ss2[:, 1:2])
            b.tt("dve", ss2[:, 0:1], ss2[:, 0:1], ss2[:, 1:2], ALU.add, [ss2], [ss2])
            b.act(ss2[:, 0:1], ss2[:, 0:1], AF.Ln, [ss2], [ss2], scale=1.0 / 1024, bias=EPS)
            b.act(ss2[:, 0:1], ss2[:, 0:1], AF.Exp, [ss2], [ss2], scale=-0.5)
            b.stt(tn[:, 0:512], Pa[:], ss2[:, 0:1], gain[:, 0:512], ALU.mult, ALU.mult, [Pa, ss2, gain], [tn])
            b.stt(tn[:, 512:1024], Pb[:], ss2[:, 0:1], gain[:, 512:1024], ALU.mult, ALU.mult, [Pb, ss2, gain], [tn])
            b.tt("dve", xo[:], tn[:], x_ap, ALU.add, [tn, xT], [xo])
            b.st(out_dram, xo[:], [xo])

        def rotary(eng, out, src_ap, srcT, cs, H, tmp):
            s3 = src_ap.rearrange("p (h d) -> p h d", h=H)
            t1, t2 = s3[:, :, 0:32], s3[:, :, 32:64]
            cb = cs[:, 0:32].unsqueeze(1).to_broadcast([128, H, 32])
            sn = cs[:, 32:64].unsqueeze(1).to_broadcast([128, H, 32])
            ta = tmp[:, 0:H * 32].rearrange("p (h d) -> p h d", h=H)
            tb = tmp[:, H * 32:H * 64].rearrange("p (h d) -> p h d", h=H)
            b.tt(eng, ta, t1, cb, ALU.mult, [srcT, cs], [tmp])
            b.tt(eng, tb, t2, sn, ALU.mult, [srcT, cs], [tmp])
            b.tt(eng, out[:, :, 0:32], ta, tb, ALU.subtract, [tmp], [out])
            b.tt(eng, ta, t2, cb, ALU.mult, [srcT, cs, out], [tmp])
            b.tt(eng, tb, t1, sn, ALU.mult, [srcT, cs, out], [tmp])
            b.tt(eng, out[:, :, 32:64], ta, tb, ALU.add, [tmp], [out])

        def ffn_loop(li, src, dst):
            with contextlib.ExitStack() as st:
                W1b = [sb(f"W1b{i}", [128, 8, 512], BF16, st) for i in range(8)]
                W2b = [sb(f"W2b{i}", [128, 4, 1024], BF16, st) for i in range(8)]
                for cbk in range(8):
                    for kc in range(8):
                        b.ld(W1b[cbk][:, kc, :], w1_d[li][kc * 128:(kc + 1) * 128, cbk * 512:(cbk + 1) * 512], [W1b[cbk]],
                             eng="pool")
                for g8 in range(8):
                    for i4 in range(4):
                        r0 = (g8 * 4 + i4) * 128
                        b.ld(W2b[g8][:, i4, :], w2_d[li][r0:r0 + 128, :], [W2b[g8]], eng="pool")
                gpre = sb("gpre", [128, 1024], F32, st)
                gpost = sb("gpost", [128, 1024], F32, st)
                b.ld(gpre[:], g_ffn_pre[li].partition_broadcast(128), [gpre])
                b.ld(gpost[:], g_ffn_post[li].partition_broadcast(128), [gpost])
                P, PTs = alloc_psum(st, 6, 1)
                xm = [sb(f"xm{i}", [128, 2, 1024], F32, st) for i in range(2)]
                junk = sb("junk", [128, 1024], BF16, st)
                junk2 = sb("junk2", [128, 1024], BF16, st)
                ss = sb("ss", [128, 2], F32, st)
                ss2 = sb("ss2", [128, 2], F32, st)
                hb = sb("hb", [128, 1024], BF16, st)
                hT = [sb(f"hT{i}", [128, 8, 256], BF16, st) for i in range(2)]
                uT = sb("uT", [128, 32, 256], BF16, st)
                rl = [sb(f"rl{i}", [128, 256], F32, st) for i in range(2)]
                tn = sb("tn", [128, 1024], F32, st)
                xo = [sb(f"xo{i}", [128, 1024], F32, st) for i in range(2)]

                def ffn_n(mt):
                    xmt = xm[mt % 2]
                    b.ld(xmt[:], src[mt * 256:(mt + 1) * 256, :].rearrange("(j p) d -> p j d", p=128), [xmt])
                    for j in range(2):
                        norm_to_hT(xmt[:, j, :], xmt, gpre, junk, ss, hb, hT[mt % 2][:, :, j * 128:(j + 1) * 128],
                                   hT[mt % 2], PTs[0])

                def ffn_r(mt):
                    xmt = xm[mt % 2]
                    hTc = hT[mt % 2]
                    for fc in range(32):
                        pu = P[fc % 2]
                        for k in range(8):
                            b.mm(pu[:, 0:256], W1b[fc // 4][:, k, (fc % 4) * 128:(fc % 4 + 1) * 128], hTc[:, k, :], k == 0, k == 7,
                                 [W1b[fc // 4], hTc], [pu])
                        r = rl[fc % 2]
                        b.act(r[:], pu[:, 0:256], AF.Relu, [pu], [r])
                        b.tt("dve", uT[:, fc, :], r[:], r[:], ALU.mult, [r], [uT])
                    for j in range(2):
                        Pa, Pb = P[2 + 2 * j], P[3 + 2 * j]
                        for nt, Pq in ((0, Pa), (1, Pb)):
                            for fc in range(32):
                                b.mm(Pq[:], uT[:, fc, j * 128:(j + 1) * 128], W2b[fc // 4][:, fc % 4, nt * 512:(nt + 1) * 512],
                                     fc == 0, fc == 31, [uT, W2b[fc // 4]], [Pq])
                        c = mt * 2 + j
                        post_norm_residual(Pa, Pb, gpost, xmt[:, j, :], xmt, junk2, ss2, tn, xo[j], chunk_rows(dst, c))

                b.two_stage(ffn_n, ffn_r, range(NCH // 2))
            S.barrier()

        def odd_layer(src, dst):
            with contextlib.ExitStack() as st:
                P, PTs = alloc_psum(st, 6, 2)
                PT = PTs[0]
                Wi = sb("Wi1", [128, 8, 4096], BF16, st)
                b.ldw(Wi, odin_d[0], 8, 4096)
                gpre = sb("gpre", [128, 1024], F32, st)
                b.ld(gpre[:], g_mix_pre[1].partition_broadcast(128), [gpre])
                gnw = sb("gnw", [128, 512], F32, st)
                b.ld(gnw[:], gnw_d[0].partition_broadcast(128), [gnw])
                wsf = sb("wsf", [128, 8, 128], F32, st)
                b.ld(wsf[:], gws_d[0].rearrange("g t s -> t g s"), [wsf])
                wsb = sb("wsb", [128, 8, 128], BF16, st)
                b.cp("dve", wsb[:], wsf[:], [wsf], [wsb])
                wsT = sb("wsT", [128, 8, 128], BF16, st)
                for g in range(8):
                    b.tr(PT[:, g * 128:(g + 1) * 128], wsb[:, g, :], idb[:], [wsb, idb], [PT])
                b.cp("act", wsT[:], PT[:, 0:1024].rearrange("p (g t) -> p g t", g=8), [PT], [wsT])
                bsf = sb("bsf", [8, 128], F32, st)
                b.ld(bsf[:], gbs_d[0], [bsf])
                bsT = sb("bsT", [128, 8], F32, st)
                b.tr(P[0][:, 0:8], bsf[:], cst[0:8, 0, 0:8], [bsf, cst], [P[0]])
                b.cp("act", bsT[:], P[0][:, 0:8], [P[0]], [bsT])

                xt = [sb(f"xt{i}", [128, 1024], F32, st) for i in range(2)]
                cs = [sb(f"cs{i}", [128, 64], F32, st) for i in range(2)]
                junk = sb("junk", [128, 1024], BF16, st)
                junk2 = sb("junk2", [128, 512], BF16, st)
                ss = sb("ss", [128, 2], F32, st)
                hb = sb("hb", [128, 1024], BF16, st)
                hT = [sb(f"hT{i}", [128, 8, 128], BF16, st) for i in range(2)]
                qsq = sb("qsq", [128, 1024], F32, st)
                qsk = sb("qsk", [128, 1024], F32, st)
                rtq = sb("rtq", [128, 1024], F32, st)
                rtk = sb("rtk", [128, 1024], F32, st)
                qrq = sb("qrq", [128, 16, 64], BF16, st)
                qrk = sb("qrk", [128, 16, 64], BF16, st)
                qTs = [sb(f"qTs{i}", [128, 8, 128], BF16, st) for i in range(2)]
                kTs = [sb(f"kTs{i}", [128, 8, 128], BF16, st) for i in range(2)]
                v1s = [sb(f"v1s{i}", [128, 16, 65], BF16, st) for i in range(2)]
                for i in range(2):
                    b.ms("pool", v1s[i][:], 1.0, [v1s[i]])
                us = sb("us", [128, 512], F32, st)
                vgs = sb("vgs", [128, 512], F32, st)
                st4 = sb("st4", [128, 4], F32, st)
                vn = sb("vn", [128, 512], BF16, st)
                sgs = [sb(f"sgs{i}", [128, 512], BF16, st) for i in range(2)]

                def a1_n(c):
                    i2 = c % 2
                    b.ld(xt[i2][:], chunk_rows(src, c), [xt[i2]])
                    b.ld(cs[i2][:, 0:32], chunk_rows(cos_d, c), [cs[i2]])
                    b.ld(cs[i2][:, 32:64], chunk_rows(sin_d, c), [cs[i2]])
                    norm_to_hT(xt[i2][:], xt[i2], gpre, junk, ss, hb, hT[i2][:], hT[i2], PTs[1])

                def a1_r(c):
                    i2 = c % 2
                    hTc = hT[i2]
                    PT = PTs[0]

                    def proj(nt, Pq):
                        for k in range(8):
                            b.mm(Pq[:], hTc[:, k, :], Wi[:, k, nt * 512:(nt + 1) * 512], k == 0, k == 7, [hTc, Wi], [Pq])
                    for nt in range(6):
                        proj(nt, P[nt])
                    b.cp("act", qsq[:, 0:512], P[0][:], [P[0]], [qsq])
                    b.cp("act", qsq[:, 512:1024], P[1][:], [P[1]], [qsq])
                    b.cp("act", qsk[:, 0:512], P[2][:], [P[2]], [qsk])
                    b.cp("act", qsk[:, 512:1024], P[3][:], [P[3]], [qsk])
                    proj(6, P[0])
                    proj(7, P[1])
                    vt = v1s[i2]
                    b.cp("dve", vt[:, 0:8, 0:64], P[4][:].rearrange("p (h d) -> p h d", h=8), [P[4]], [vt])
                    b.cp("dve", vt[:, 8:16, 0:64], P[5][:].rearrange("p (h d) -> p h d", h=8), [P[5]], [vt])
                    b.st(v1_d[c], vt[:].rearrange("p h d -> p (h d)"), [vt])
                    rotary("dve", qrq, qsq[:], qsq, cs[i2], 16, rtq)
                    rotary("dve", qrk, qsk[:], qsk, cs[i2], 16, rtk)
                    b.cp("act", us[:], P[0][:], [P[0]], [us])
                    b.act(vgs[:], P[1][:], AF.Copy, [P[1]], [vgs, st4], accum=st4[:, 0:1])
                    b.act(junk2[:, 0:512], P[1][:], AF.Square, [P[1]], [junk2, st4], accum=st4[:, 1:2])
                    for qr_, dstT, dram in ((qrq, qTs[i2], qT1_d), (qrk, kTs[i2], kT1_d)):
                        qf = qr_[:].rearrange("p h d -> p (h d)")
                        for k in range(8):
                            b.tr(PT[:, k * 128:(k + 1) * 128], qf[:, k * 128:(k + 1) * 128], idb[:], [qr_, idb], [PT])
                        b.cp("act", dstT[:], PT[:, 0:1024].rearrange("p (k t) -> p k t", k=8), [PT], [dstT])
                        b.st(dram[c], dstT[:], [dstT])
                    b.ts("dve", st4[:, 0:2], st4[:, 0:2], 1.0 / 512, None, ALU.mult, None, [st4], [st4])
                    b.tt("dve", st4[:, 2:3], st4[:, 0:1], st4[:, 0:1], ALU.mult, [st4], [st4])
                    b.tt("dve", st4[:, 1:2], st4[:, 1:2], st4[:, 2:3], ALU.subtract, [st4], [st4])
                    b.act(st4[:, 1:2], st4[:, 1:2], AF.Ln, [st4], [st4], scale=1.0, bias=EPS)
                    b.act(st4[:, 1:2], st4[:, 1:2], AF.Exp, [st4], [st4], scale=-0.5)
                    b.ts("dve", vgs[:], vgs[:], st4[:, 0:1], st4[:, 1:2], ALU.subtract, ALU.mult, [vgs, st4], [vgs])
                    b.tt("dve", vn[:], vgs[:], gnw[:], ALU.mult, [vgs, gnw], [vn])
                    for g in range(8):
                        b.mm(P[2][:, g * 64:(g + 1) * 64], wsT[:, g, :], vn[:, g * 64:(g + 1) * 64], True, True,
                             [wsT, vn], [P[2]])
                    b.tt("dve", vgs[:].rearrange("p (g d) -> p g d", g=8), P[2][:].rearrange("p (g d) -> p g d", g=8),
                         bsT[:].unsqueeze(2).to_broadcast([128, 8, 64]), ALU.add, [P[2], bsT, vn], [vgs])
                    b.tt("dve", sgs[i2][:], vgs[:], us[:], ALU.mult, [vgs, us], [sgs[i2]])
                    b.st(chunk_rows(sg1_d, c), sgs[i2][:], [sgs[i2]])

                b.two_stage(a1_n, a1_r, crange("A1"))
            S.barrier()
            with contextlib.ExitStack() as st:
                NS = 18
                S.keep = AT_KEEP
                P, PTs = alloc_psum(st, 7, 1)
                PT = PTs[0]
                Wo = sb("Wo1", [128, 12, 1024], BF16, st)
                b.ldw(Wo, odout_d[0], 12, 1024)
                gpost = sb("gpost", [128, 1024], F32, st)
                b.ld(gpost[:], g_mix_post[1].partition_broadcast(128), [gpost])
                amf = sb("amf", [128, 17, 128], F32, st)
                b.ld(amf[:], amask_d[:, :, :], [amf])
                am = sb("am", [128, 17, 128], BF16, st)
                amL = sb("amL", [128, 17, 128], BF16, st)
                b.cp("dve", am[:], amf[:], [amf], [am])
                b.ts("dve", amL[:], amf[:], link[:, 0:1], None, ALU.mult, None, [amf, link], [amL])
                kr = [sb(f"kr{i}", [128, 8, 128], BF16, st) for i in range(NS)]
                vr = [sb(f"vr{i}", [128, 16 * 65], BF16, st) for i in range(NS)]
                qze = [sb(f"qze{i}", [128, 8, 128], BF16, st) for i in range(2)]
                qzo = [sb(f"qzo{i}", [128, 8, 128], BF16, st) for i in range(2)]
                for i in range(2):
                    b.ms("dve", qze[i][:], 0.0, [qze[i]])
                    b.ms("dve", qzo[i][:], 0.0, [qzo[i]])
                pe_sb = [sb(f"pe{i}", [128, 512], BF16, st) for i in range(AT_LAG + 1)]
                pm_sb = [sb(f"pm{i}", [128, 512], BF16, st) for i in range(AT_LAG + 1)]
                rden = sb("rden", [128, 4], F32, st)
                mix = [sb(f"mix{i}", [128, 1536], BF16, st) for i in range(2)]
                mixT = sb("mixT", [128, 12, 128], BF16, st)
                xt = [sb(f"xt{i}", [128, 1024], F32, st) for i in range(2)]
                junk = sb("junk", [128, 1024], BF16, st)
                ss = sb("ss", [128, 2], F32, st)
                tn = sb("tn", [128, 1024], F32, st)
                xo = [sb(f"xo{i}", [128, 1024], F32, st) for i in range(2)]
                loaded = set()
                gi = 0
                pend = []
                LAG = AT_LAG
                for c in crange("AT"):
                    i2 = c % 2
                    sgc = seg_of(c)
                    lo, hi = (0, 2 * SEGCH) if sgc < 2 else (2 * SEGCH, NCH)
                    blocks = [j for j in range(c - 8, c + 9) if lo <= j < hi]
                    for j in blocks:
                        if j not in loaded:
                            loaded.add(j)
                            b.ld(kr[j % NS][:], kT1_d[j], [kr[j % NS]])
                            b.ld(vr[j % NS][:], v1_d[j], [vr[j % NS]])
                    b.ld(qze[i2][0:64], qT1_d[c][0:64], [qze[i2]])
                    b.ld(qzo[i2][64:128], qT1_d[c][64:128], [qzo[i2]])
                    b.ld(mix[i2][:, 1024:1536], chunk_rows(sg1_d, c), [mix[i2]])
                    b.ld(xt[i2][:], chunk_rows(src, c), [xt[i2]])
                    groups = []
                    cur = []
                    for j in blocks:
                        cross = (seg_of(j) != sgc)
                        if cur and (len(cur) == 4 or cur[0][1] != cross):
                            groups.append(cur)
                            cur = []
                        cur.append((j, cross))
                    if cur:
                        groups.append(cur)
                    for h in range(16):
                        hp, base = h // 2, (h % 2) * 64
                        po = P[4 + (h // 4) % 2]
                        pcol = (h % 4) * 65
                        nb = len(blocks)
                        bi = 0
                        for gx, grp in enumerate(groups):
                            ps = P[gi % 3]
                            pe_t = pe_sb[gi % (AT_LAG + 1)]
                            pm_t = pm_sb[gi % (AT_LAG + 1)]
                            gi += 1
                            n = len(grp)
                            for i, (j, cross) in enumerate(grp):
                                qz = (qze if h % 2 == 0 else qzo)[i2]
                                b.mm(ps[:, i * 128:(i + 1) * 128], kr[j % NS][:, hp, :],
                                     qz[:, hp, :], True, True, [kr[j % NS], qz], [ps])
                            b.act(pe_t[:, 0:n * 128], ps[:, 0:n * 128], AF.Exp, [ps], [pe_t], scale=0.125)
                            o0 = grp[0][0] - c + 8
                            msk = amL if grp[0][1] else am
                            b.tt("dve", pm_t[:, 0:n * 128], pe_t[:, 0:n * 128],
                                 msk[:, o0:o0 + n, :].rearrange("p a b -> p (a b)"), ALU.mult, [pe_t, msk], [pm_t])

                            def pv(grp=grp, pm_t=pm_t, po=po, pcol=pcol, bi=bi, nb=nb, h=h, i2=i2, c=c,
                                   last=(gx == len(groups) - 1)):
                                for i, (j, cross) in enumerate(grp):
                                    b.mm(po[:, pcol:pcol + 65], pm_t[:, i * 128:(i + 1) * 128],
                                         vr[j % NS][:, h * 65:(h + 1) * 65], bi + i == 0, bi + i == nb - 1,
                                         [pm_t, vr[j % NS]], [po])
                                if last and h % 4 == 3:
                                    po3 = po[:, 0:260].rearrange("p (h d) -> p h d", h=4)
                                    b.S.op("dve", lambda e, o=rden[:].unsqueeze(2), i_=po3[:, :, 64:65]: e.reciprocal(out=o, in_=i_),
                                           [po.t], [rden.t])
                                    h0 = h - 3
                                    b.tt("dve", mix[i2][:, h0 * 64:(h0 + 4) * 64].rearrange("p (h d) -> p h d", h=4),
                                         po3[:, :, 0:64], rden[:].unsqueeze(2).to_broadcast([128, 4, 64]), ALU.mult,
                                         [po, rden], [mix[i2]])
                                if last and h == 15:
                                    for half in range(2):
                                        for k in range(6):
                                            kk = half * 6 + k
                                            b.tr(PT[:, k * 128:(k + 1) * 128], mix[i2][:, kk * 128:(kk + 1) * 128], idb[:],
                                                 [mix[i2], idb], [PT])
                                        b.cp("act", mixT[:, half * 6:(half + 1) * 6, :],
                                             PT[:, 0:768].rearrange("p (k t) -> p k t", k=6), [PT], [mixT])
                                    for nt, Pq in ((0, P[3]), (1, P[6])):
                                        for k in range(12):
                                            b.mm(Pq[:], mixT[:, k, :], Wo[:, k, nt * 512:(nt + 1) * 512], k == 0, k == 11,
                                                 [mixT, Wo], [Pq])
                                    post_norm_residual(P[3], P[6], gpost, xt[i2][:], xt[i2], junk, ss, tn, xo[i2],
                                                       chunk_rows(dst, c))
                            bi += n
                            pend.append(pv)
                            while len(pend) > LAG:
                                pend.pop(0)()
                while pend:
                    pend.pop(0)()
            S.keep = False
            S.barrier()

        def even_layer(src, dst):
            with contextlib.ExitStack() as st:
                P, PTs = alloc_psum(st, 6, 2)
                Wi = sb("Wi0", [128, 8, 5664], BF16, st)
                b.ldw(Wi, evin_d[0], 8, 5664)
                gpre = sb("gpre", [128, 1024], F32, st)
                b.ld(gpre[:], g_mix_pre[0].partition_broadcast(128), [gpre])
                xt = [sb(f"xt{i}", [128, 1024], F32, st) for i in range(2)]
                cs = [sb(f"cs{i}", [128, 64], F32, st) for i in range(2)]
                junk = sb("junk", [128, 1024], BF16, st)
                ss = sb("ss", [128, 2], F32, st)
                hb = sb("hb", [128, 1024], BF16, st)
                hT = [sb(f"hT{i}", [128, 8, 128], BF16, st) for i in range(2)]
                zo = [sb(f"zo{i}", [128, 1024], F32, st) for i in range(2)]
                go = [sb(f"go{i}", [128, 1024], F32, st) for i in range(2)]
                xpre = [sb(f"xpre{i}", [128, 12, 132], BF16, st) for i in range(3)]
                xsT = sb("xsT", [128, 8, 128], F32, st)
                bcT = [sb(f"bcT{i}", [128, 4, 128], BF16, st) for i in range(2)]
                xso = [sb(f"xso{i}", [128, 1024], F32, st) for i in range(2)]
                btk = [sb(f"btk{i}", [128, 256], BF16, st) for i in range(2)]
                cw = sb("cw", [8, 1536], F32, st)
                b.ms("dve", cw[:], 0.0, [cw])
                b.ld(cw[0:5, :], convw_d[0], [cw])
                cbr = sb("cbr", [12, 128], F32, st)
                b.ld(cbr[:], convb_d[0].rearrange("(a p) -> a p", p=128), [cbr])
                wT = sb("wT", [128, 12, 8], F32, st)
                cbT = sb("cbT", [128, 12], F32, st)
                for cb in range(12):
                    b.tr(P[0][:, cb * 8:(cb + 1) * 8], cw[:, cb * 128:(cb + 1) * 128], cst[0:8, 0, 0:8], [cw, cst], [P[0]])
                b.cp("act", wT[:], P[0][:, 0:96].rearrange("p (a j) -> p a j", a=12), [P[0]], [wT])
                b.tr(P[1][:, 0:12], cbr[:], cst[0:12, 0, 0:12], [cbr, cst], [P[1]])
                b.cp("act", cbT[:], P[1][:, 0:12], [P[1]], [cbT])
                Wd = sb("Wd", [128, 12, 5, 128], BF16, st)
                for cb in range(12):
                    for j in range(5):
                        b.ts("dve", Wd[:, cb, j, :], cst[:, 0, :], wT[:, cb, j:j + 1], None, ALU.mult, None, [cst, wT], [Wd])
                b.ms("dve", xpre[0][:, :, 0:2], 0.0, [xpre[0]])
                dto = [sb(f"dto{i}", [128, 32], F32, st) for i in range(2)]
                qs = sb("qs", [128, 512], F32, st)
                rtmp = sb("rtmp", [128, 512], F32, st)
                qr = sb("qr", [128, 8, 64], BF16, st)
                kro = [sb(f"kro{i}", [128, 8, 64], BF16, st) for i in range(2)]
                qTs = [sb(f"qTs{i}", [128, 4, 128], BF16, st) for i in range(2)]
                kTs = [sb(f"kTs{i}", [128, 4, 128], BF16, st) for i in range(2)]
                vo = [sb(f"vo{i}", [128, 1024], BF16, st) for i in range(2)]
                qs2 = sb("qs2", [128, 512], F32, st)
                rtmp2 = sb("rtmp2", [128, 512], F32, st)

                def a0_n(c):
                    i2 = c % 2
                    b.ld(xt[i2][:], chunk_rows(src, c), [xt[i2]])
                    b.ld(cs[i2][:, 0:32], chunk_rows(cos_d, c), [cs[i2]])
                    b.ld(cs[i2][:, 32:64], chunk_rows(sin_d, c), [cs[i2]])
                    norm_to_hT(xt[i2][:], xt[i2], gpre, junk, ss, hb, hT[i2][:], hT[i2], PTs[1])

                def a0_r(c):
                    i2 = c % 2
                    hTc = hT[i2]
                    PT = PTs[0]
                    pi = [0]

                    def proj(c0, ncol):
                        Pq = P[pi[0] % 6]
                        pi[0] += 1
                        for k in range(8):
                            b.mm(Pq[:, 0:ncol], hTc[:, k, :], Wi[:, k, c0:c0 + ncol], k == 0, k == 7, [hTc, Wi], [Pq])
                        return Pq
                    Pq = proj(2592, 512)
                    b.cp("act", qs[:], Pq[:], [Pq], [qs])
                    Pq = proj(3104, 512)
                    b.act(qs2[:], Pq[:], AF.Copy, [Pq], [qs2], scale=0.125)
                    rotary("dve", qr, qs[:], qs, cs[i2], 8, rtmp)
                    rotary("dve", kro[i2], qs2[:], qs2, cs[i2], 8, rtmp2)
                    for nt in range(2):
                        Pq = proj(nt * 512, 512)
                        b.act(zo[i2][:, nt * 512:(nt + 1) * 512], Pq[:], AF.Silu, [Pq], [zo[i2]])
                    b.st(chunk_rows(sz_d, c), zo[i2][:], [zo[i2]])
                    xp = xpre[c % 3]
                    for q4 in range(3):
                        Pq = P[pi[0] % 6]
                        pi[0] += 1
                        for cbi in range(4):
                            cb = q4 * 4 + cbi
                            for k in range(8):
                                b.mm(Pq[:, cbi * 128:(cbi + 1) * 128], Wi[:, k, 1024 + cb * 128:1024 + (cb + 1) * 128],
                                     hTc[:, k, :], k == 0, k == 7, [hTc, Wi], [Pq])
                        b.cp("dve", xp[:, q4 * 4:(q4 + 1) * 4, 2:130], Pq[:].rearrange("p (a t) -> p a t", a=4), [Pq], [xp])
                    if c > 0:
                        xl = xpre[(c - 1) % 3]
                        if c % SEGCH != 0:
                            b.cp("dve", xl[:, :, 130:132], xp[:, :, 2:4], [xp], [xl])
                        elif c == SEGCH:
                            b.ts("dve", xl[:, :, 130:132], xp[:, :, 2:4], link[:, 0:1], None, ALU.mult, None, [xp, link], [xl])
                        else:
                            b.ms("dve", xl[:, :, 130:132], 0.0, [xl])
                    if c + 1 < NCH:
                        xn = xpre[(c + 1) % 3]
                        if (c + 1) % SEGCH != 0:
                            b.cp("dve", xn[:, :, 0:2], xp[:, :, 128:130], [xp], [xn])
                        elif c + 1 == SEGCH:
                            b.ts("dve", xn[:, :, 0:2], xp[:, :, 128:130], link[:, 0:1], None, ALU.mult, None, [xp, link], [xn])
                        else:
                            b.ms("dve", xn[:, :, 0:2], 0.0, [xn])
                    else:
                        b.ms("dve", xp[:, :, 130:132], 0.0, [xp])
                    Pq = proj(2560, 32)
                    b.cp("dve", dto[i2][:], Pq[:, 0:32], [Pq], [dto[i2]])
                    b.st(chunk_rows(dtr_d, c), dto[i2][:], [dto[i2]])
                    for nt in range(2):
                        Pq = proj(3616 + nt * 512, 512)
                        b.cp("act" if nt == 0 else "dve", vo[i2][:, nt * 512:(nt + 1) * 512], Pq[:], [Pq], [vo[i2]])
                    b.st(chunk_rows(v0_d, c), vo[i2][:], [vo[i2]])
                    for nt in range(2):
                        Pq = proj(4640 + nt * 512, 512)
                        b.act(go[i2][:, nt * 512:(nt + 1) * 512], Pq[:], AF.Silu, [Pq], [go[i2]])
                    b.st(chunk_rows(sg_d, c), go[i2][:], [go[i2]])
                    qf = qr[:].rearrange("p h d -> p (h d)")
                    for k in range(4):
                        b.tr(PT[:, k * 128:(k + 1) * 128], qf[:, k * 128:(k + 1) * 128], idb[:], [qr, idb], [PT])
                    b.cp("act", qTs[i2][:], PT[:, 0:512].rearrange("p (k t) -> p k t", k=4), [PT], [qTs[i2]])
                    b.st(qT0_d[c], qTs[i2][:], [qTs[i2]])
                    kf = kro[i2][:].rearrange("p h d -> p (h d)")
                    b.st(chunk_rows(k0_d, c), kf, [kro[i2]])
                    for k in range(4):
                        b.tr(PT[:, 512 + k * 128:512 + (k + 1) * 128], kf[:, k * 128:(k + 1) * 128], idb[:],
                             [kro[i2], idb], [PT])
                    b.cp("act", kTs[i2][:], PT[:, 512:1024].rearrange("p (k t) -> p k t", k=4), [PT], [kTs[i2]])
                    b.st(kT0_d[c], kTs[i2][:], [kTs[i2]])
                    return pi

                def a0_conv(c, pi):
                    i2 = c % 2
                    xp = xpre[c % 3]
                    PT = PTs[0]
                    for q4 in range(3):
                        Pq = P[pi[0] % 6]
                        pi[0] += 1
                        for cbi in range(4):
                            cb = q4 * 4 + cbi
                            for j in range(5):
                                b.mm(Pq[:, cbi * 128:(cbi + 1) * 128], Wd[:, cb, j, :], xp[:, cb, j:j + 128], j == 0, j == 4,
                                     [Wd, xp], [Pq])
                        for cbi in range(4):
                            cb = q4 * 4 + cbi
                            if cb < 8:
                                b.act(xsT[:, cb, :], Pq[:, cbi * 128:(cbi + 1) * 128], AF.Silu, [Pq, cbT], [xsT],
                                      bias=cbT[:, cb:cb + 1])
                            else:
                                b.act(bcT[i2][:, cb - 8, :], Pq[:, cbi * 128:(cbi + 1) * 128], AF.Silu, [Pq, cbT], [bcT[i2]],
                                      bias=cbT[:, cb:cb + 1])
                    b.st(bct_d[c], bcT[i2][:], [bcT[i2]])
                    for half in range(2):
                        Pq = P[pi[0] % 6]
                        pi[0] += 1
                        for e4 in range(4):
                            cb = half * 4 + e4
                            b.tr(Pq[:, e4 * 128:(e4 + 1) * 128], xsT[:, cb, :], cst[:, 0, :], [xsT, cst], [Pq])
                        b.cp("act", xso[i2][:, half * 512:(half + 1) * 512], Pq[:], [Pq], [xso[i2]])
                    b.st(chunk_rows(xs_d, c), xso[i2][:], [xso[i2]])
                    for g in range(2):
                        b.tr(PT[:, 512 + g * 128:512 + (g + 1) * 128], bcT[i2][:, g, :], idb[:], [bcT[i2], idb], [PT])
                    b.cp("act", btk[i2][:], PT[:, 512:768], [PT], [btk[i2]])
                    b.st(chunk_rows(btok_d, c), btk[i2][:], [btk[i2]])

                def a0_r2(c):
                    pi = a0_r(c)
                    if c > 0:
                        a0_conv(c - 1, pi)
                    if c == len(crange("A")) - 1:
                        a0_conv(c, pi)

                b.two_stage(a0_n, a0_r2, crange("A"))
            S.barrier()
            with contextlib.ExitStack() as st:
                P, PTs = alloc_psum(st, 7, 1)
                PT = PTs[0]
                prm = sb("prm", [128, 96], F32, st)
                b.ld(prm[:, 0:32], dtb_d[0].rearrange("a b -> (a b)").partition_broadcast(128), [prm])
                b.ld(prm[:, 32:64], alog_d[0].rearrange("a b -> (a b)").partition_broadcast(128), [prm])
                b.ld(prm[:, 64:80], dsk_d[0].partition_broadcast(128), [prm])
                b.ld(prm[:, 80:96], rdec_d[0].rearrange("a b -> (a b)").partition_broadcast(128), [prm])
                b.act(prm[:, 32:64], prm[:, 32:64], AF.Exp, [prm], [prm])
                b.ts("dve", prm[:, 32:64], prm[:, 32:64], -1.0, None, ALU.mult, None, [prm], [prm])
                b.act(prm[:, 80:96], prm[:, 80:96], AF.Exp, [prm], [prm], scale=-1.0)
                b.act(prm[:, 80:96], prm[:, 80:96], AF.Ln, [prm], [prm], bias=1.0)
                b.ts("dve", prm[:, 80:96], prm[:, 80:96], -1.0, None, ALU.mult, None, [prm], [prm])
                dtbias, avec, dskip, lg = prm[:, 0:32], prm[:, 32:64], prm[:, 64:80], prm[:, 80:96]
                Dfb = sb("Dfb", [128, 8, 128], F32, st)
                Dtmp = sb("Dtmp", [128, 8, 128], F32, st)
                b.tt("dve", Dfb[:], RDp.unsqueeze(1).to_broadcast([128, 8, 128]),
                     prm[:, 80:88].unsqueeze(2).to_broadcast([128, 8, 128]), ALU.mult, [cst, prm], [Dfb])
                b.act(Dfb[:], Dfb[:], AF.Exp, [Dfb], [Dfb])
                b.tt("dve", Dfb[:], Dfb[:], Um.unsqueeze(1).to_broadcast([128, 8, 128]), ALU.mult, [Dfb, cst], [Dfb])
                b.tt("dve", Dtmp[:], RDn.unsqueeze(1).to_broadcast([128, 8, 128]),
                     prm[:, 88:96].unsqueeze(2).to_broadcast([128, 8, 128]), ALU.mult, [cst, prm], [Dtmp])
                b.act(Dtmp[:], Dtmp[:], AF.Exp, [Dtmp], [Dtmp])
                b.tt("dve", Dtmp[:], Dtmp[:], Lm.unsqueeze(1).to_broadcast([128, 8, 128]), ALU.mult, [Dtmp, cst], [Dtmp])
                b.tt("dve", Dfb[:], Dfb[:], Dtmp[:], ALU.add, [Dfb, Dtmp], [Dfb])
                rc = sb("rc", [128, 32], F32, st)
                b.ts("dve", rc[:, 0:8], prm[:, 80:88], POS[:, 0:1], None, ALU.mult, None, [prm, cst], [rc])
                b.ts("dve", rc[:, 8:16], prm[:, 88:96], POS[:, 1:2], None, ALU.mult, None, [prm, cst], [rc])
                b.ts("dve", rc[:, 16:32], prm[:, 80:96], 128.0, None, ALU.mult, None, [prm], [rc])
                b.act(rc[:], rc[:], AF.Exp, [rc], [rc])

                xbcs = sb("xbcs", [128, 1024], F32, st)
                dtr = sb("dtr", [128, 32], F32, st)
                dts = sb("dts", [128, 32], F32, st)
                dt2 = sb("dt2", [128, 32], F32, st)
                la = sb("la", [128, 32], F32, st)
                csb = sb("csb", [128, 64], F32, st)
                scs = [sb(f"scs{i}", [128, 32], F32, st) for i in range(2)]
                wte = sb("wte", [128, 32], F32, st)
                dec = [sb(f"dec{i}", [128, 32], F32, st) for i in range(2)]
                xdt = sb("xdt", [128, 2, 1024], BF16, st)
                xw = [sb(f"xw{i}", [128, 2, 1024], BF16, st) for i in range(2)]
                bcb = [sb(f"bcb{i}", [128, 256], BF16, st) for i in range(2)]
                BCT = [sb(f"BCT{i}", [128, 4, 128], BF16, st) for i in range(2)]
                cbm = sb("cbm", [128, 4, 128], F32, st)
                rhsq = [sb(f"rhsq{i}", [128, 4, 128], F32, st) for i in range(8)]
                DT = [sb(f"DT{i}", [128, 512], BF16, st) for i in range(2)]
                M = sb("M", [128, 2, 16, 128], BF16, st)
                ydt = sb("ydt", [128, 1024], F32, st)
                ydo = [sb(f"ydo{i}", [128, 1024], F32, st) for i in range(2)]
                Sfo = [sb(f"Sfo{i}", [128, 1024], F32, st) for i in range(2)]
                hbs = sb("hbs", [128, 1024], F32, st)
                hbo = [sb(f"hbo{i}", [128, 1024], BF16, st) for i in range(2)]
                qTl = sb("qTl", [128, 4, 128], BF16, st)
                kTl = sb("kTl", [128, 4, 128], BF16, st)
                kl = sb("kl", [128, 512], BF16, st)
                vl = sb("vl", [128, 1024], BF16, st)
                SM = sb("SM", [128, 8, 128], BF16, st)
                kd = sb("kd", [128, 2, 512], BF16, st)
                yio = [sb(f"yio{i}", [128, 1024], F32, st) for i in range(2)]
                Rfo = [sb(f"Rfo{i}", [128, 512], F32, st) for i in range(2)]
                rbs = sb("rbs", [128, 512], F32, st)
                rbo = [sb(f"rbo{i}", [128, 512], BF16, st) for i in range(2)]
                b.ms("dve", hbs[:], 0.0, [hbs])
                b.ms("dve", rbs[:], 0.0, [rbs])

                def b1_h1(c):
                    i2 = c % 2
                    first_of_seg = (c % SEGCH == 0)
                    last_of_seg = (c % SEGCH == SEGCH - 1)
                    b.ld(xbcs[:], chunk_rows(xs_d, c), [xbcs])
                    xs3 = xbcs[:, 0:1024].rearrange("p (e d) -> p e d", e=16)
                    b.ld(dtr[:], chunk_rows(dtr_d, c), [dtr])
                    b.tt("dve", dtr[:], dtr[:], dtbias, ALU.add, [dtr, prm], [dtr])
                    b.ts("dve", dt2[:], dtr[:], -1.0, None, ALU.mult, None, [dtr], [dt2])
                    b.tt("dve", dt2[:], dt2[:], dtr[:], ALU.max, [dtr, dt2], [dt2])
                    b.act(dt2[:], dt2[:], AF.Exp, [dt2], [dt2], scale=-1.0)
                    b.act(dt2[:], dt2[:], AF.Ln, [dt2], [dt2], bias=1.0)
                    b.ts("dve", dts[:], dtr[:], 0.0, None, ALU.max, None, [dtr], [dts])
                    b.tt("dve", dts[:], dts[:], dt2[:], ALU.add, [dts, dt2], [dts])
                    b.tt("dve", la[:], dts[:], avec, ALU.mult, [dts, prm], [la])
                    pc = P[6]
                    b.mm(pc[:, 0:16], Um, la[:, 0:16], True, True, [cst, la], [pc])
                    b.mm(pc[:, 16:32], Lm, la[:, 16:32], True, True, [cst, la], [pc])
                    b.mm(pc[:, 32:64], ONES, la[:, 0:32], True, True, [cst, la], [pc])
                    b.cp("act", csb[:], pc[:, 0:64], [pc], [csb])
                    sc = scs[i2]
                    b.act(sc[:], csb[:, 0:32], AF.Exp, [csb], [sc])
                    b.st(chunk_rows(sc_d, c), sc[:], [sc])
                    b.act(dec[i2][:], csb[:, 32:64], AF.Exp, [csb], [dec[i2]])
                    b.tt("dve", wte[:], csb[:, 32:64], csb[:, 0:32], ALU.subtract, [csb], [wte])
                    b.act(wte[:], wte[:], AF.Exp, [wte], [wte])
                    b.tt("dve", wte[:], wte[:], dts[:], ALU.mult, [wte, dts], [wte])
                    for d in range(2):
                        b.tt("dve", xdt[:, d, :].rearrange("p (e d) -> p e d", e=16), xs3,
                             dts[:, d * 16:(d + 1) * 16].unsqueeze(2).to_broadcast([128, 16, 64]), ALU.mult,
                             [xbcs, dts], [xdt])
                        b.tt("dve", xw[i2][:, d, :].rearrange("p (e d) -> p e d", e=16), xs3,
                             wte[:, d * 16:(d + 1) * 16].unsqueeze(2).to_broadcast([128, 16, 64]), ALU.mult,
                             [xbcs, wte], [xw[i2]])
                    bct = BCT[i2]
                    b.ld(bct[:], bct_d[c], [bct])
                    for g in range(2):
                        b.mm(pc[:, 128 + g * 128:128 + (g + 1) * 128], bct[:, g, :], bct[:, 2 + g, :], True, True,
                             [bct], [pc])
                    pcb = pc[:, 128:384].rearrange("p (g l) -> p g l", g=2)
                    b.tt("dve", cbm[:, 0:2, :], pcb, Um.unsqueeze(1).to_broadcast([128, 2, 128]), ALU.mult,
                         [pc, cst], [cbm])
                    b.tt("dve", cbm[:, 2:4, :], pcb, Lm.unsqueeze(1).to_broadcast([128, 2, 128]), ALU.mult,
                         [pc, cst], [cbm])
                    u = 0
                    for d in range(2):
                        tri = Um if d == 0 else Lm
                        Amat = Af if d == 0 else Ab
                        for q4 in range(4):
                            rq = rhsq[d * 4 + q4]
                            for e4 in range(4):
                                col = d * 16 + q4 * 4 + e4
                                b.act(rq[:, e4, :], tri, AF.Copy, [cst, la], [rq], scale=la[:, col:col + 1])
                        for q4 in range(4):
                            Pq = P[u % 2]
                            dtt = DT[u % 2]
                            u += 1
                            rq = rhsq[d * 4 + q4]
                            b.mm(Pq[:], Amat, rq[:].rearrange("p a b -> p (a b)"), True, True,
                                 [cst, rq], [Pq])
                            b.act(dtt[:], Pq[:], AF.Exp, [Pq], [dtt])
                            g = q4 // 2
                            b.tt("dve", M[:, d, q4 * 4:(q4 + 1) * 4, :], dtt[:].rearrange("p (a b) -> p a b", a=4),
                                 cbm[:, 2 * d + g, :].unsqueeze(1).to_broadcast([128, 4, 128]), ALU.mult,
                                 [dtt, cbm], [M])
                    for e in range(16):
                        Pq = P[2 + e // 8]
                        sl = slice((e % 8) * 64, (e % 8) * 64 + 64)
                        b.mm(Pq[:, sl], M[:, 0, e, :], xdt[:, 0, e * 64:(e + 1) * 64], True, False, [M, xdt], [Pq])
                        b.mm(Pq[:, sl], M[:, 1, e, :], xdt[:, 1, e * 64:(e + 1) * 64], False, True, [M, xdt], [Pq])
                    b.tt("dve", ydt[:].rearrange("p (e d) -> p e d", e=16), xs3,
                         dskip.unsqueeze(2).to_broadcast([128, 16, 64]), ALU.mult, [xbcs, prm], [ydt])
                    b.tt("dve", ydo[i2][:, 0:512], P[2][:], ydt[:, 0:512], ALU.add, [P[2], ydt], [ydo[i2]])
                    b.tt("dve", ydo[i2][:, 512:1024], P[3][:], ydt[:, 512:1024], ALU.add, [P[3], ydt], [ydo[i2]])
                    b.st(chunk_rows(yd_d, c), ydo[i2][:], [ydo[i2]])

                def b1_h2(c):
                    i2 = c % 2
                    first_of_seg = (c % SEGCH == 0)
                    last_of_seg = (c % SEGCH == SEGCH - 1)
                    bt = bcb[i2]
                    b.ld(bt[:], chunk_rows(btok_d, c), [bt])
                    for g in range(2):
                        b.mm(P[4 + g][:], bt[:, g * 128:(g + 1) * 128], xw[i2][:, 0, g * 512:(g + 1) * 512], True, True,
                             [bcb[i2], xw[i2]], [P[4 + g]])
                    for g in range(2):
                        b.cp("act", Sfo[i2][:, g * 512:(g + 1) * 512], P[4 + g][:], [P[4 + g]], [Sfo[i2]])
                    b.st(Sf_d[c], Sfo[i2][:], [Sfo[i2]])
                    b.st(decf_d[c], dec[i2][:, 0:16], [dec[i2]])
                    if last_of_seg:
                        if c == SEGCH - 1:
                            b.ts("dve", hbs[:], hbs[:], link[:, 0:1], None, ALU.mult, None, [hbs, link], [hbs])
                            b.ts("dve", rbs[:], rbs[:], link[:, 0:1], None, ALU.mult, None, [rbs, link], [rbs])
                        else:
                            b.ms("dve", hbs[:], 0.0, [hbs])
                            b.ms("dve", rbs[:], 0.0, [rbs])
                    b.cp("act", hbo[i2][:], hbs[:], [hbs], [hbo[i2]])
                    b.st(Hb_d[c], hbo[i2][:], [hbo[i2]])
                    for g in range(2):
                        b.mm(P[4 + g][:], bt[:, g * 128:(g + 1) * 128], xw[i2][:, 1, g * 512:(g + 1) * 512], True, True,
                             [bcb[i2], xw[i2]], [P[4 + g]])
                    b.tt("dve", hbs[:].rearrange("p (e d) -> p e d", e=16), hbs[:].rearrange("p (e d) -> p e d", e=16),
                         dec[i2][:, 16:32].unsqueeze(2).to_broadcast([128, 16, 64]), ALU.mult, [hbs, dec[i2]], [hbs])
                    for g in range(2):
                        b.tt("dve", hbs[:, g * 512:(g + 1) * 512], hbs[:, g * 512:(g + 1) * 512], P[4 + g][:], ALU.add,
                             [hbs, P[4 + g]], [hbs])
                    b.ld(qTl[:], qT0_d[c], [qTl])
                    b.ld(kTl[:], kT0_d[c], [kTl])
                    b.ld(kl[:], chunk_rows(k0_d, c), [kl])
                    b.ld(vl[:], chunk_rows(v0_d, c), [vl])
                    for par in range(2):
                        for hp in range(4):
                            base = par * 64
                            Pq = P[4 + par]
                            b.mm(Pq[:, hp * 128:(hp + 1) * 128], kTl[base:base + 64, hp, :], qTl[base:base + 64, hp, :],
                                 True, True, [kTl, qTl], [Pq])
                    for par in range(2):
                        b.tt("dve", SM[:, par:8:2, :], P[4 + par][:].rearrange("p (a b) -> p a b", a=4),
                             Dfb[:, par:8:2, :], ALU.mult, [P[4 + par], Dfb], [SM])
                    for h in range(8):
                        Pq = P[4 + h // 4]
                        b.mm(Pq[:, (h % 4) * 128:(h % 4 + 1) * 128], SM[:, h, :], vl[:, h * 128:(h + 1) * 128], True, True,
                             [SM, vl], [Pq])
                    for hh in range(2):
                        b.cp("act", yio[i2][:, hh * 512:(hh + 1) * 512], P[4 + hh][:], [P[4 + hh]], [yio[i2]])
                    b.st(chunk_rows(yi_d, c), yio[i2][:], [yio[i2]])
                    for d in range(2):
                        b.tt("dve", kd[:, d, :].rearrange("p (h d) -> p h d", h=8), kl[:].rearrange("p (h d) -> p h d", h=8),
                             rc[:, d * 8:(d + 1) * 8].unsqueeze(2).to_broadcast([128, 8, 64]), ALU.mult, [kl, rc], [kd])
                    def rstates(d, Pq):
                        for h in range(8):
                            hp, base = h // 2, (h % 2) * 64
                            b.mm(Pq[base:base + 64, hp * 128:(hp + 1) * 128], kd[:, d, h * 64:(h + 1) * 64],
                                 vl[:, h * 128:(h + 1) * 128], True, True, [kd, vl], [Pq])
                    rstates(0, P[4])
                    b.cp("act", Rfo[i2][:], P[4][:], [P[4]], [Rfo[i2]])
                    b.st(Rf_d[c], Rfo[i2][:], [Rfo[i2]])
                    b.cp("act", rbo[i2][:], rbs[:], [rbs], [rbo[i2]])
                    b.st(Rb_d[c], rbo[i2][:], [rbo[i2]])
                    rstates(1, P[5])
                    for par in range(2):
                        sl = slice(par * 64, par * 64 + 64)
                        gcol = rc[sl, 24 + par:32:2]
                        b.tt("dve", rbs[sl, :].rearrange("p (a v) -> p a v", a=4), rbs[sl, :].rearrange("p (a v) -> p a v", a=4),
                             gcol.unsqueeze(2).to_broadcast([64, 4, 128]), ALU.mult, [rbs, rc], [rbs])
                    b.tt("dve", rbs[:], rbs[:], P[5][:], ALU.add, [rbs, P[5]], [rbs])

                b.two_stage(b1_h1, b1_h2, (range(NCH - 1, -1, -1) if LOOPN.get("B1", NCH) == NCH else range(LOOPN["B1"] - 1, -1, -1)))
            S.barrier()
            with contextlib.ExitStack() as st:
                P, PTs = alloc_psum(st, 7, 1)
                PT = PTs[0]
                Wo = sb("Wo0", [128, 16, 1024], BF16, st)
                b.ldw(Wo, evout_d[0], 16, 1024)
                gpost = sb("gpost", [128, 1024], F32, st)
                b.ld(gpost[:], g_mix_post[0].partition_broadcast(128), [gpost])
                snw = sb("snw", [128, 1024], F32, st)
                b.ld(snw[:], snw_d[0].partition_broadcast(128), [snw])
                rgn = sb("rgn", [128, 1024], F32, st)
                b.ld(rgn[:], rgn_d[0].partition_broadcast(128), [rgn])
                prm = sb("prm2", [128, 16], F32, st)
                b.ld(prm[:, 0:16], rdec_d[0].rearrange("a b -> (a b)").partition_broadcast(128), [prm])
                b.act(prm[:], prm[:], AF.Exp, [prm], [prm], scale=-1.0)
                b.act(prm[:], prm[:], AF.Ln, [prm], [prm], bias=1.0)
                b.ts("dve", prm[:], prm[:], -1.0, None, ALU.mult, None, [prm], [prm])
                rc = sb("rc2", [128, 32], F32, st)
                b.ts("dve", rc[:, 0:8], prm[:, 0:8], POS[:, 2:3], None, ALU.mult, None, [prm, cst], [rc])
                b.ts("dve", rc[:, 8:16], prm[:, 8:16], POS[:, 3:4], None, ALU.mult, None, [prm, cst], [rc])
                b.ts("dve", rc[:, 16:32], prm[:, 0:16], 128.0, None, ALU.mult, None, [prm], [rc])
                b.act(rc[:], rc[:], AF.Exp, [rc], [rc])

                ydl = [sb(f"ydl{i}", [128, 1024], F32, st) for i in range(2)]
                yil = [sb(f"yil{i}", [128, 1024], F32, st) for i in range(2)]
                CTl = [sb(f"CTl{i}", [128, 2, 128], BF16, st) for i in range(2)]
                scl = [sb(f"scl{i}", [128, 32], F32, st) for i in range(2)]
                qTl = [sb(f"qTl{i}", [128, 4, 128], BF16, st) for i in range(2)]
                Hbl = [sb(f"Hbl{i}", [128, 1024], BF16, st) for i in range(2)]
                Rbl = [sb(f"Rbl{i}", [128, 512], BF16, st) for i in range(2)]
                Sfl = [sb(f"Sfl{i}", [128, 1024], F32, st) for i in range(2)]
                dfl = [sb(f"dfl{i}", [128, 16], F32, st) for i in range(2)]
                Rfl = [sb(f"Rfl{i}", [128, 512], F32, st) for i in range(2)]
                szl = [sb(f"szl{i}", [128, 1024], F32, st) for i in range(2)]
                sgl = [sb(f"sgl{i}", [128, 1024], F32, st) for i in range(2)]
                xt = [sb(f"xt{i}", [128, 1024], F32, st) for i in range(2)]
                hfs = sb("hfs", [128, 1024], F32, st)
                hfb = sb("hfb", [128, 1024], BF16, st)
                rfs = sb("rfs", [128, 512], F32, st)
                rfb = sb("rfb", [128, 512], BF16, st)
                t1 = sb("t1", [128, 1024], F32, st)
                ys_ = [sb(f"ys{i}", [128, 1024], F32, st) for i in range(2)]
                yr_ = [sb(f"yr{i}", [128, 1024], F32, st) for i in range(2)]
                st8 = sb("st8", [128, 24], F32, st)
                mix = sb("mix", [128, 2048], BF16, st)
                mixT = sb("mixT", [128, 16, 128], BF16, st)
                junk = sb("junk", [128, 1024], BF16, st)
                jf = sb("jf", [128, 1024], F32, st)
                ss = sb("ss", [128, 2], F32, st)
                tn = sb("tn", [128, 1024], F32, st)
                xo = [sb(f"xo{i}", [128, 1024], F32, st) for i in range(2)]
                b.ms("dve", hfs[:], 0.0, [hfs])
                b.ms("dve", rfs[:], 0.0, [rfs])
                def b2_g1(c):
                    i2 = c % 2
                    ys = ys_[i2]
                    yr = yr_[i2]
                    b.ld(ydl[i2][:], chunk_rows(yd_d, c), [ydl[i2]])
                    b.ld(yil[i2][:], chunk_rows(yi_d, c), [yil[i2]])
                    b.ld(CTl[i2][:], bct_d[c][:, 2:4, :], [CTl[i2]])
                    b.ld(scl[i2][:], chunk_rows(sc_d, c), [scl[i2]])
                    b.ld(qTl[i2][:], qT0_d[c], [qTl[i2]])
                    b.ld(Hbl[i2][:], Hb_d[c], [Hbl[i2]])
                    b.ld(Rbl[i2][:], Rb_d[c], [Rbl[i2]])
                    b.ld(Sfl[i2][:], Sf_d[c], [Sfl[i2]])
                    b.ld(dfl[i2][:], decf_d[c], [dfl[i2]])
                    b.ld(Rfl[i2][:], Rf_d[c], [Rfl[i2]])
                    b.ld(szl[i2][:], chunk_rows(sz_d, c), [szl[i2]])
                    b.ld(sgl[i2][:], chunk_rows(sg_d, c), [sgl[i2]])
                    b.ld(xt[i2][:], chunk_rows(src, c), [xt[i2]])
                    if c % SEGCH == 0 and c > 0:
                        if c == SEGCH:
                            b.ts("dve", hfs[:], hfs[:], link[:, 0:1], None, ALU.mult, None, [hfs, link], [hfs])
                            b.ts("dve", rfs[:], rfs[:], link[:, 0:1], None, ALU.mult, None, [rfs, link], [rfs])
                        else:
                            b.ms("dve", hfs[:], 0.0, [hfs])
                            b.ms("dve", rfs[:], 0.0, [rfs])
                    b.cp("act", hfb[:], hfs[:], [hfs], [hfb])
                    b.cp("act", rfb[:], rfs[:], [rfs], [rfb])
                    for g in range(2):
                        b.mm(P[g][:], CTl[i2][:, g, :], hfb[:, g * 512:(g + 1) * 512], True, True, [CTl[i2], hfb], [P[g]])
                        b.mm(P[2 + g][:], CTl[i2][:, g, :], Hbl[i2][:, g * 512:(g + 1) * 512], True, True,
                             [CTl[i2], Hbl[i2]], [P[2 + g]])
                    for g in range(2):
                        for d in range(2):
                            Pq = P[2 * d + g]
                            b.tt("dve", t1[:, g * 512:(g + 1) * 512].rearrange("p (e d) -> p e d", e=8),
                                 Pq[:].rearrange("p (e d) -> p e d", e=8),
                                 scl[i2][:, d * 16 + g * 8:d * 16 + (g + 1) * 8].unsqueeze(2).to_broadcast([128, 8, 64]),
                                 ALU.mult, [Pq, scl[i2]], [t1])
                            src_y = ydl[i2] if d == 0 else ys
                            b.tt("dve", ys[:, g * 512:(g + 1) * 512], t1[:, g * 512:(g + 1) * 512],
                                 src_y[:, g * 512:(g + 1) * 512], ALU.add, [t1, src_y], [ys])
                    b.tt("dve", hfs[:].rearrange("p (e d) -> p e d", e=16), hfs[:].rearrange("p (e d) -> p e d", e=16),
                         dfl[i2][:].unsqueeze(2).to_broadcast([128, 16, 64]), ALU.mult, [hfs, dfl[i2], hfb], [hfs])
                    b.tt("dve", hfs[:], hfs[:], Sfl[i2][:], ALU.add, [hfs, Sfl[i2]], [hfs])
                    for d, (Pa, Pb2), rsrc in ((0, (P[4], P[0]), rfb), (1, (P[1], P[2]), Rbl[i2])):
                        t1v = t1[:].rearrange("p (h v) -> p h v", h=8)
                        for par, Pq in ((0, Pa), (1, Pb2)):
                            base = par * 64
                            for hp in range(4):
                                b.mm(Pq[:, hp * 128:(hp + 1) * 128], qTl[i2][base:base + 64, hp, :],
                                     rsrc[base:base + 64, hp * 128:(hp + 1) * 128], True, True, [qTl[i2], rsrc], [Pq])
                            b.tt("dve", t1v[:, par:8:2, :], Pq[:].rearrange("p (a v) -> p a v", a=4),
                                 rc[:, d * 8 + par:d * 8 + 8:2].unsqueeze(2).to_broadcast([128, 4, 128]),
                                 ALU.mult, [Pq, rc], [t1])
                        src_y = yil[i2] if d == 0 else yr
                        b.tt("dve", yr[:], t1[:], src_y[:], ALU.add, [t1, src_y], [yr])
                    for par in range(2):
                        sl = slice(par * 64, par * 64 + 64)
                        gcol = rc[sl, 16 + par:24:2]
                        b.tt("dve", rfs[sl, :].rearrange("p (a v) -> p a v", a=4), rfs[sl, :].rearrange("p (a v) -> p a v", a=4),
                             gcol.unsqueeze(2).to_broadcast([64, 4, 128]), ALU.mult, [rfs, rc, rfb], [rfs])
                    b.tt("dve", rfs[:], rfs[:], Rfl[i2][:], ALU.add, [rfs, Rfl[i2]], [rfs])

                def b2_g2(c):
                    i2 = c % 2
                    ys = ys_[i2]
                    yr = yr_[i2]
                    b.tt("dve", ys[:], ys[:], szl[i2][:], ALU.mult, [ys, szl[i2]], [ys])
                    for g in range(2):
                        b.act(junk[:, g * 512:(g + 1) * 512], ys[:, g * 512:(g + 1) * 512], AF.Square, [ys], [junk, st8],
                              accum=st8[:, g:g + 1])
                    b.act(st8[:, 0:2], st8[:, 0:2], AF.Ln, [st8], [st8], scale=1.0 / 512, bias=EPS)
                    b.act(st8[:, 0:2], st8[:, 0:2], AF.Exp, [st8], [st8], scale=-0.5)
                    for g in range(2):
                        b.stt(mix[:, g * 512:(g + 1) * 512], ys[:, g * 512:(g + 1) * 512], st8[:, g:g + 1],
                              snw[:, g * 512:(g + 1) * 512], ALU.mult, ALU.mult, [ys, st8, snw], [mix])
                    yr3 = yr[:].rearrange("p (h v) -> p h v", h=8)
                    b.red(st8[:, 8:16], yr3, [yr], [st8])
                    b.act(jf[:], yr[:], AF.Square, [yr], [jf])
                    b.red(st8[:, 16:24], jf[:].rearrange("p (h v) -> p h v", h=8), [jf], [st8])
                    b.ts("dve", st8[:, 8:24], st8[:, 8:24], 1.0 / 128, None, ALU.mult, None, [st8], [st8])
                    b.tt("dve", jf[:, 0:8], st8[:, 8:16], st8[:, 8:16], ALU.mult, [st8], [jf])
                    b.tt("dve", st8[:, 16:24], st8[:, 16:24], jf[:, 0:8], ALU.subtract, [st8, jf], [st8])
                    b.act(st8[:, 16:24], st8[:, 16:24], AF.Ln, [st8], [st8], bias=EPS)
                    b.act(st8[:, 16:24], st8[:, 16:24], AF.Exp, [st8], [st8], scale=-0.5)
                    b.tt("dve", yr3, yr3, st8[:, 8:16].unsqueeze(2).to_broadcast([128, 8, 128]), ALU.subtract,
                         [yr, st8], [yr])
                    b.tt("dve", yr3, yr3, st8[:, 16:24].unsqueeze(2).to_broadcast([128, 8, 128]), ALU.mult,
                         [yr, st8], [yr])
                    b.tt("dve", yr[:], yr[:], rgn[:], ALU.mult, [yr, rgn], [yr])
                    b.tt("dve", mix[:, 1024:2048], yr[:], sgl[i2][:], ALU.mult, [yr, sgl[i2]], [mix])
                    for half in range(2):
                        for k in range(8):
                            kk = half * 8 + k
                            b.tr(PT[:, k * 128:(k + 1) * 128], mix[:, kk * 128:(kk + 1) * 128], idb[:], [mix, idb], [PT])
                        b.cp("act", mixT[:, half * 8:(half + 1) * 8, :],
                             PT[:, 0:1024].rearrange("p (k t) -> p k t", k=8), [PT], [mixT])
                    for nt, Pq in ((0, P[5]), (1, P[6])):
                        for k in range(16):
                            b.mm(Pq[:], mixT[:, k, :], Wo[:, k, nt * 512:(nt + 1) * 512], k == 0, k == 15, [mixT, Wo], [Pq])
                    post_norm_residual(P[5], P[6], gpost, xt[i2][:], xt[i2], junk, ss, tn, xo[i2], chunk_rows(dst, c))

                b.two_stage(b2_g1, b2_g2, crange("B2"))
            S.barrier()

        hm = sb("hm", [128, 8], F32)
        b.ld(hm[:], cst_d[:, 8, 8:16], [hm])
        b.ts("dve", hm[:, 0:2], hm[:, 4:6], -1.0, 1.0, ALU.mult, ALU.add, [hm], [hm])
        b.ts("dve", hm[:, 0:2], hm[:, 0:2], link[:, 0:1], None, ALU.mult, None, [hm, link], [hm])
        b.tt("dve", hm[:, 6:8], hm[:, 0:2], hm[:, 4:6], ALU.add, [hm], [hm])

        stages = []
        if 0 in layers:
            if do_mix:
                stages.append(("mix0",))
            if do_ffn:
                stages.append(("ffn0",))
        if 1 in layers:
            if do_mix:
                stages.append(("mix1",))
            if do_ffn:
                stages.append(("ffn1",))
        bufs = [xa, xb]
        cur = xin
        S.barrier()
        for si, (sname,) in enumerate(stages):
            dst = yout if si == len(stages) - 1 else bufs[si % 2]
            if sname == "mix0":
                even_layer(cur, dst)
            elif sname == "mix1":
                odd_layer(cur, dst)
            elif sname == "ffn0":
                ffn_loop(0, cur, dst)
            elif sname == "ffn1":
                ffn_loop(1, cur, dst)
            cur = dst
        counts = S.emit()
    return nc, counts


def _consts():
    p = np.arange(128)
    r, l = p[:, None], p[None, :]
    cst = np.zeros((128, 9, 128), np.float32)
    cst[:, 0] = (r == l)
    cst[:, 1] = (r <= l)
    cst[:, 2] = (r >= l)
    cst[:, 3] = (r > l)
    cst[:, 4] = (r < l)
    cst[:, 5] = 1.0
    cst[:, 6] = np.maximum(l - r, 0)
    cst[:, 7] = np.maximum(r - l, 0)
    cst[:, 8, 0] = 127 - p
    cst[:, 8, 1] = p
    cst[:, 8, 2] = p + 1
    cst[:, 8, 3] = 128 - p
    cst[:, 8, 12] = (p != 127)
    cst[:, 8, 13] = (p < 126)
    am = np.zeros((128, 17, 128), np.float32)
    for o in range(17):
        j = o - 8
        d = np.abs(l - r - 128 * j)
        am[:, o, :] = (d <= 64).astype(np.float32) + ((d % 4 == 0) & (d <= 256)) + ((d % 16 == 0) & (d <= 1024))
    return cst, am


def _rope(pos):
    inv_freq = (10000.0 ** (-np.arange(0, 64, 2, dtype=np.float32) / np.float32(64))).astype(np.float32)
    ang = pos.astype(np.float32)[:, None] * inv_freq[None, :]
    return np.cos(ang).astype(np.float32), np.sin(ang).astype(np.float32)


WEIGHT_NAMES = ["norm_mix_pre", "norm_mix_post", "norm_ffn_pre", "norm_ffn_post", "ffn_w1", "ffn_w2",
                "ev_in_proj", "ev_conv_w", "ev_conv_b", "ssd_dt_bias", "ssd_a_log", "ssd_d", "ssd_norm_w",
                "ret_decay", "ret_gn_w", "ev_out_proj", "od_in_proj", "gmlp_norm_w", "gmlp_ws", "gmlp_bs",
                "od_out_proj"]


def make_in_maps(inputs):
    xp = np.asarray(inputs["x_prompt"], np.float32)
    xs = np.asarray(inputs["x_sample"], np.float32)
    cst, am = _consts()
    pos_a = np.concatenate([np.arange(4096), np.arange(2048)])
    pos_b = np.concatenate([np.arange(2048)] * 3)
    ca, sa = _rope(pos_a)
    cb, sb_ = _rope(pos_b)
    w = {k: np.ascontiguousarray(np.asarray(inputs[k], np.float32)) for k in WEIGHT_NAMES}
    maps = []
    for c in range(NCORES):
        if c < 4:
            x = np.concatenate([xp[c], xs[c]], axis=0)
            lk = np.ones((128, 1), np.float32)
            rc, rs = ca, sa
        else:
            i0 = 4 + 3 * (c - 4)
            x = np.concatenate([xs[i0], xs[i0 + 1], xs[i0 + 2]], axis=0)
            lk = np.zeros((128, 1), np.float32)
            rc, rs = cb, sb_
        m = {"xin": np.ascontiguousarray(x), "link": lk, "rcos": rc, "rsin": rs, "cst": cst, "amask": am}
        m.update(w)
        maps.append(m)
    return maps


def gather(results):
    yp = np.zeros((4, 4096, D), np.float32)
    ys = np.zeros((16, 2048, D), np.float32)
    for c in range(NCORES):
        y = np.asarray(results[c]["yout"], np.float32)
        if c < 4:
            yp[c] = y[0:4096]
            ys[c] = y[4096:6144]
        else:
            i0 = 4 + 3 * (c - 4)
            for k in range(3):
                ys[i0 + k] = y[k * 2048:(k + 1) * 2048]
    return yp, ys


_NC_CACHE = {}


def kernel(**inputs):
    if "nc" not in _NC_CACHE:
        _NC_CACHE["nc"] = build_nc()[0]
    nc = _NC_CACHE["nc"]
    maps = make_in_maps(inputs)
    res = run_bass_kernel_spmd(nc, maps, core_ids=list(range(NCORES)))
    return gather(res.results)
```

```python
import contextlib
import numpy as np
import concourse.bass as bass
import concourse.mybir as mybir
from concourse.bass_utils import run_bass_kernel_spmd

F32 = mybir.dt.float32
BF16 = mybir.dt.bfloat16
ALU = mybir.AluOpType
AF = mybir.ActivationFunctionType
AX = mybir.AxisListType

ENGS = ("pe", "act", "dve", "pool", "sp")
DMA_ENGS = ("sp", "act", "pool")
EPOCH = 30000
NDSEM = 8

NCORES = 8
T = 6144
NCH = 48
SEGCH = 16
D = 1024
EPS = 1e-6
LOOPN = {}
USE_SCHED = True
SCHED_W = 128
SCHED_HOP = 120.0
AT_LAG = 8
AT_KEEP = False


def crange(name):
    return range(LOOPN.get(name, NCH))


class Tok:
    __slots__ = ("w", "rs")

    def __init__(self):
        self.w = None
        self.rs = []


class Op:
    __slots__ = ("eng", "fn", "deps", "dma", "needed", "seq", "dsem", "dval", "barrier", "snap", "cost", "lat")

    def __init__(self, eng, fn, deps, dma, barrier=False, cost=300.0, lat=0.0):
        self.cost = cost
        self.lat = lat
        self.eng = eng
        self.fn = fn
        self.deps = deps
        self.dma = dma
        self.needed = False
        self.seq = None
        self.dsem = None
        self.dval = None
        self.barrier = barrier
        self.snap = None


class Sched:
    def __init__(self, nc):
        self.nc = nc
        self.ops = []
        self.keep = False
        self.keep_set = set()

    def op(self, eng, fn, reads=(), writes=(), dma=False, cost=300.0, lat=0.0):
        idx = len(self.ops)
        deps = set()
        for t in reads:
            if t.w is not None:
                deps.add(t.w)
        for t in writes:
            if t.w is not None:
                deps.add(t.w)
            for r in t.rs:
                deps.add(r)
        self.ops.append(Op(eng, fn, deps, dma, cost=cost, lat=lat))
        if self.keep:
            self.keep_set.add(idx)
        for t in reads:
            t.rs.append(idx)
        for t in writes:
            t.w = idx
            t.rs = []
        return idx

    def barrier(self):
        for e in ENGS:
            self.ops.append(Op(e, None, set(), False, barrier=True))

    def schedule(self, W=12, hop=120.0):
        ops = self.ops
        n = len(ops)
        fin = [0.0] * n
        done = [False] * n
        order = {e: [] for e in ENGS}
        seg_start = 0
        bounds = []
        i = 0
        while i < n:
            if ops[i].barrier:
                bounds.append((seg_start, i))
                j = i
                while j < n and ops[j].barrier:
                    j += 1
                bounds.append(("barrier", i, j))
                seg_start = j
                i = j
            else:
                i += 1
        bounds.append((seg_start, n))
        for bnd in bounds:
            if bnd[0] == "barrier":
                for k in range(bnd[1], bnd[2]):
                    order[ops[k].eng].append(k)
                    done[k] = True
                continue
            lo, hi = bnd
            if hi <= lo:
                continue
            pend = {e: [] for e in ENGS}
            for k in range(lo, hi):
                pend[ops[k].eng].append(k)
            if lo in self.keep_set:
                for e in ENGS:
                    order[e].extend(pend[e])
                for k in range(lo, hi):
                    done[k] = True
                continue
            pos = {e: 0 for e in ENGS}
            et = {e: 0.0 for e in ENGS}
            remaining = hi - lo
            while remaining:
                best = None
                for e in ENGS:
                    lst = pend[e]
                    cnt = 0
                    p = pos[e]
                    while p < len(lst) and done[lst[p]]:
                        p += 1
                    pos[e] = p
                    q = p
                    while q < len(lst) and cnt < W:
                        k = lst[q]
                        q += 1
                        if done[k]:
                            continue
                        cnt += 1
                        o = ops[k]
                        rdy = 0.0
                        ok = True
                        for d in o.deps:
                            if not done[d]:
                                ok = False
                                break
                            f = fin[d] + (0.0 if ops[d].eng == e and not ops[d].dma else hop)
                            if f > rdy:
                                rdy = f
                        if not ok:
                            continue
                        stt = rdy if rdy > et[e] else et[e]
                        key = (stt, k)
                        if best is None or key < best[0]:
                            best = (key, e, k)
                assert best is not None, "scheduler deadlock"
                (stt, k), e, k = best
                o = ops[k]
                et[e] = stt + o.cost
                fin[k] = stt + o.cost + o.lat
                done[k] = True
                order[e].append(k)
                remaining -= 1
        return order

    def emit(self):
        nc = self.nc
        ops = self.ops
        sched_order = self.schedule(W=SCHED_W, hop=SCHED_HOP) if USE_SCHED else None
        for o in ops:
            for d in o.deps:
                ops[d].needed = True
        cnt = {e: 0 for e in ENGS}
        dcnt = {e: 0 for e in DMA_ENGS}
        if sched_order is not None:
            walk = []
            ptr = {e: 0 for e in ENGS}
            nb = sum(1 for o in ops if o.barrier) // len(ENGS)
            for _ in range(nb + 1):
                for e in ENGS:
                    lst = sched_order[e]
                    p = ptr[e]
                    while p < len(lst) and not ops[lst[p]].barrier:
                        walk.append(lst[p])
                        p += 1
                    ptr[e] = p
                for e in ENGS:
                    lst = sched_order[e]
                    if ptr[e] < len(lst):
                        walk.append(lst[ptr[e]])
                        ptr[e] += 1
            walk_ops = [ops[k] for k in walk]
        else:
            walk_ops = ops
        last = {e: None for e in ENGS}
        for o in walk_ops:
            if o.barrier:
                for e in ENGS:
                    if last[e] is not None:
                        last[e].needed = True
            elif not o.dma:
                last[o.eng] = o
        for o in walk_ops:
            if o.barrier:
                o.snap = (dict(cnt), dict(dcnt))
            elif o.dma:
                i = dcnt[o.eng]
                dcnt[o.eng] += 1
                o.dsem = i % NDSEM
                o.dval = 16 * (i // NDSEM + 1)
                assert o.dval < 60000, "too many DMAs on one queue"
            elif o.needed:
                cnt[o.eng] += 1
                o.seq = cnt[o.eng]
        nep = {e: (cnt[e] + EPOCH - 1) // EPOCH for e in ENGS}
        with contextlib.ExitStack() as st:
            sems = {e: [st.enter_context(nc.semaphore(f"s_{e}_{k}")) for k in range(max(1, nep[e]))]
                    for e in ENGS}
            dsems = {e: [st.enter_context(nc.semaphore(f"d_{e}_{k}")) for k in range(NDSEM)]
                     for e in DMA_ENGS}
            block = st.enter_context(nc.Block())
            if sched_order is not None:
                per_eng = sched_order
            else:
                per_eng = {e: [] for e in ENGS}
                for i, o in enumerate(ops):
                    per_eng[o.eng].append(i)

            def run_engine(ename, engobj):
                seen = {}

                def wait(key, sem, val):
                    if val <= 0 or seen.get(key, 0) >= val:
                        return
                    seen[key] = val
                    engobj.wait_ge(sem, val)

                def wait_all(c_snap, d_snap):
                    for e in ENGS:
                        n = c_snap[e]
                        if n > 0:
                            ep = (n - 1) // EPOCH
                            wait(("c", e, ep), sems[e][ep], n - ep * EPOCH)
                    for e in DMA_ENGS:
                        n = d_snap[e]
                        for k in range(min(n, NDSEM)):
                            lastk = ((n - 1 - k) // NDSEM) * NDSEM + k
                            wait(("d", e, k), dsems[e][k], 16 * (lastk // NDSEM + 1))

                for i in per_eng[ename]:
                    o = ops[i]
                    if o.barrier:
                        wait_all(*o.snap)
                        continue
                    for d in sorted(o.deps):
                        p = ops[d]
                        if ename == "pe" and p.eng == "pe" and not p.dma:
                            continue
                        if p.dma:
                            wait(("d", p.eng, p.dsem), dsems[p.eng][p.dsem], p.dval)
                        else:
                            ep = (p.seq - 1) // EPOCH
                            wait(("c", p.eng, ep), sems[p.eng][ep], p.seq - ep * EPOCH)
                    if o.dma:
                        wait(("d", o.eng, o.dsem), dsems[o.eng][o.dsem], o.dval - 16)
                        ins = o.fn(engobj)
                        ins.then_inc(dsems[o.eng][o.dsem], 16)
                    else:
                        ins = o.fn(engobj)
                        if o.needed:
                            ep = (o.seq - 1) // EPOCH
                            ins.then_inc(sems[o.eng][ep], 1)
                wait_all({e: 0 for e in ENGS}, dcnt)

            @block.tensor
            def _(e):
                run_engine("pe", e)

            @block.scalar
            def _(e):
                run_engine("act", e)

            @block.vector
            def _(e):
                run_engine("dve", e)

            @block.gpsimd
            def _(e):
                run_engine("pool", e)

            @block.sync
            def _(e):
                run_engine("sp", e)
        return {e: len(per_eng[e]) for e in ENGS}


class Tl:
    def __init__(self, h):
        self.h = h
        self.t = Tok()

    def __getitem__(self, k):
        return self.h[k]


class B:
    def __init__(self, nc):
        self.nc = nc
        self.S = Sched(nc)
        self.cap = None

    def _rec(self, eng, fn, R, W, dma=False, cost=300.0, lat=0.0):
        reads = [x.t for x in R]
        writes = [x.t for x in W]
        if self.cap is not None:
            self.cap.append((eng, fn, reads, writes, dma, cost, lat))
        else:
            self.S.op(eng, fn, reads, writes, dma, cost, lat)

    @staticmethod
    def _fd(ap):
        n = 1
        for d in ap.shape[1:]:
            n *= int(d)
        return n

    def _ecost(self, eng, out):
        n = self._fd(out)
        if eng == "act":
            return (200.0 + n) / 1.2
        if eng == "dve":
            return (150.0 + n) / 0.96
        return 150.0 + 2.2 * n

    def capture(self, f, *args):
        old = self.cap
        self.cap = []
        f(*args)
        lst = self.cap
        self.cap = old
        return lst

    def emit_merged(self, la, lb):
        i = j = 0
        while i < len(la) or j < len(lb):
            if j >= len(lb) or (i < len(la) and i * len(lb) <= j * len(la)):
                self.S.op(*la[i])
                i += 1
            else:
                self.S.op(*lb[j])
                j += 1

    def two_stage(self, f1, f2, items):
        prev = None
        for k in items:
            l1 = self.capture(f1, k)
            l2 = self.capture(f2, prev) if prev is not None else []
            self.emit_merged(l2, l1)
            prev = k
        if prev is not None:
            self.emit_merged(self.capture(f2, prev), [])

    def mm(self, out, lhsT, rhs, start, stop, R, W):
        c = max(64, self._fd(rhs)) / 2.4 * (4.0 if lhsT.dtype == F32 else 1.0)
        self._rec("pe", lambda e: e.matmul(out, lhsT=lhsT, rhs=rhs, start=start, stop=stop),
                  R, W, cost=c, lat=60.0)

    def tr(self, out, in_, ident, R, W):
        self._rec("pe", lambda e: e.transpose(out=out, in_=in_, identity=ident),
                  R, W, cost=60.0, lat=60.0)

    def act(self, out, in_, func, R, W, scale=1.0, bias=0.0, accum=None):
        if accum is None:
            self._rec("act", lambda e: e.activation(out=out, in_=in_, func=func, scale=scale, bias=bias),
                      R, W, cost=self._ecost("act", out))
        else:
            self._rec("act", lambda e: e.activation(out=out, in_=in_, func=func, scale=scale, bias=bias,
                                                    accum_out=accum),
                      R, W, cost=self._ecost("act", out) + 80.0)

    def tt(self, eng, out, in0, in1, op, R, W):
        self._rec(eng, lambda e: e.tensor_tensor(out=out, in0=in0, in1=in1, op=op),
                  R, W, cost=self._ecost(eng, out))

    def ts(self, eng, out, in0, s1, s2, op0, op1, R, W):
        if op1 is None:
            self._rec(eng, lambda e: e.tensor_scalar(out=out, in0=in0, scalar1=s1, scalar2=None, op0=op0),
                      R, W, cost=self._ecost(eng, out))
        else:
            self._rec(eng, lambda e: e.tensor_scalar(out=out, in0=in0, scalar1=s1, scalar2=s2, op0=op0, op1=op1),
                      R, W, cost=self._ecost(eng, out))

    def stt(self, out, in0, scalar, in1, op0, op1, R, W):
        self._rec("dve", lambda e: e.scalar_tensor_tensor(out=out, in0=in0, scalar=scalar, in1=in1, op0=op0, op1=op1),
                  R, W, cost=self._ecost("dve", out))

    def cp(self, eng, out, in_, R, W):
        if eng == "act":
            self._rec("act", lambda e: e.copy(out=out, in_=in_), R, W, cost=self._ecost("act", out))
        else:
            self._rec(eng, lambda e: e.tensor_copy(out=out, in_=in_), R, W, cost=self._ecost(eng, out))

    def ms(self, eng, ap, val, W):
        self._rec(eng, lambda e: e.memset(ap, val), [], W, cost=self._ecost(eng, ap))

    def red(self, out, in_, R, W):
        self._rec("dve", lambda e: e.tensor_reduce(out=out, in_=in_, axis=AX.X, op=ALU.add),
                  R, W, cost=self._ecost("dve", in_))

    def ld(self, out, in_, W, eng="sp"):
        nbytes = self._fd(out) * 4 * 128
        self._rec(eng, lambda e: e.dma_start(out=out, in_=in_), [], W, dma=True,
                  cost=(400.0 if eng == "sp" else 700.0), lat=2000.0 + nbytes / 150.0)

    def st(self, out, in_, R, eng="pool"):
        nbytes = self._fd(in_) * 4 * 128
        self._rec(eng, lambda e: e.dma_start(out=out, in_=in_), R, [], dma=True,
                  cost=(400.0 if eng == "sp" else 700.0), lat=2000.0 + nbytes / 150.0)

    def ldw(self, wt, wd, kc_n, ncols):
        for kc in range(kc_n):
            for c0 in range(0, ncols, 2048):
                c1 = min(ncols, c0 + 2048)
                self.ld(wt[:, kc, c0:c1], wd[kc * 128:(kc + 1) * 128, c0:c1], [wt], eng="pool")


def seg_of(c):
    return c // SEGCH


def build_nc(layers=(0, 1), do_mix=True, do_ffn=True):
    nc = bass.Bass("TRN2", target_bir_lowering=False)
    b = B(nc)
    S = b.S

    def din(name, shape):
        return nc.dram_tensor(name, list(shape), F32, kind="ExternalInput").ap()

    def dscr(name, shape, dt=F32):
        return nc.dram_tensor(name, list(shape), dt, kind="Internal").ap()

    xin = din("xin", [T, D])
    yout = nc.dram_tensor("yout", [T, D], F32, kind="ExternalOutput").ap()
    link_d = din("link", [128, 1])
    cos_d = din("rcos", [T, 32])
    sin_d = din("rsin", [T, 32])
    cst_d = din("cst", [128, 9, 128])
    amask_d = din("amask", [128, 17, 128])
    g_mix_pre = din("norm_mix_pre", [2, D])
    g_mix_post = din("norm_mix_post", [2, D])
    g_ffn_pre = din("norm_ffn_pre", [2, D])
    g_ffn_post = din("norm_ffn_post", [2, D])
    w1_d = din("ffn_w1", [2, D, 4096])
    w2_d = din("ffn_w2", [2, 4096, D])
    evin_d = din("ev_in_proj", [1, D, 5664])
    convw_d = din("ev_conv_w", [1, 5, 1536])
    convb_d = din("ev_conv_b", [1, 1536])
    dtb_d = din("ssd_dt_bias", [1, 2, 16])
    alog_d = din("ssd_a_log", [1, 2, 16])
    dsk_d = din("ssd_d", [1, 16])
    snw_d = din("ssd_norm_w", [1, 1024])
    rdec_d = din("ret_decay", [1, 2, 8])
    rgn_d = din("ret_gn_w", [1, 1024])
    evout_d = din("ev_out_proj", [1, 2048, D])
    odin_d = din("od_in_proj", [1, D, 4096])
    gnw_d = din("gmlp_norm_w", [1, 512])
    gws_d = din("gmlp_ws", [1, 8, 128, 128])
    gbs_d = din("gmlp_bs", [1, 8, 128])
    odout_d = din("od_out_proj", [1, 1536, D])

    xa = dscr("xa", [T, D])
    xb = dscr("xb", [T, D])
    sz_d = dscr("sz", [T, 1024])
    sg_d = dscr("sgt", [T, 1024])
    xbc_d = dscr("xbc", [T + 4, 1536])
    dtr_d = dscr("dtr", [T, 32])
    qT0_d = dscr("qT0", [NCH, 128, 4, 128], BF16)
    kT0_d = dscr("kT0", [NCH, 128, 4, 128], BF16)
    k0_d = dscr("k0", [T, 512], BF16)
    v0_d = dscr("v0", [T, 1024], BF16)
    yd_d = dscr("yd", [T, 1024])
    yi_d = dscr("yi", [T, 1024])
    CT_d = dscr("CTd", [NCH, 128, 2, 128], BF16)
    sc_d = dscr("scd", [T, 32])
    Sf_d = dscr("Sfd", [NCH, 128, 1024])
    decf_d = dscr("decf", [NCH, 128, 16])
    Rf_d = dscr("Rfd", [NCH, 128, 512])
    Hb_d = dscr("Hbd", [NCH, 128, 1024], BF16)
    Rb_d = dscr("Rbd", [NCH, 128, 512], BF16)
    xs_d = dscr("xsd", [T, 1024])
    bct_d = dscr("bctd", [NCH, 128, 4, 128], BF16)
    btok_d = dscr("btokd", [T, 256], BF16)
    qT1_d = dscr("qT1", [NCH, 128, 8, 128], BF16)
    kT1_d = dscr("kT1", [NCH, 128, 8, 128], BF16)
    v1_d = dscr("v1", [NCH, 128, 16 * 65], BF16)
    sg1_d = dscr("sg1", [T, 512], BF16)

    def chunk_rows(ap, c):
        return ap[c * 128:(c + 1) * 128, :]

    with contextlib.ExitStack() as gst:
        uid = [0]

        def sb(name, shape, dt, st=gst):
            uid[0] += 1
            return Tl(st.enter_context(nc.sbuf_tensor(f"s{uid[0]}_{name}", list(shape), dt)))

        def psum(name, shape, dt, st=gst):
            uid[0] += 1
            return Tl(st.enter_context(nc.psum_tensor(f"p{uid[0]}_{name}", list(shape), dt)))

        cst = sb("cst", [128, 9, 128], F32)
        b.ld(cst[:], cst_d[:, :, :], [cst])
        idb = sb("idb", [128, 128], BF16)
        b.cp("dve", idb[:], cst[:, 0, :], [cst], [idb])
        Um, Lm, Af, Ab, ONES = (cst[:, 1, :], cst[:, 2, :], cst[:, 3, :], cst[:, 4, :], cst[:, 5, :])
        RDp, RDn = cst[:, 6, :], cst[:, 7, :]
        POS = cst[:, 8, :]
        link = sb("link", [128, 1], F32)
        b.ld(link[:], link_d[:, :], [link])
        def alloc_psum(st, nf32, nbf):
            assert nf32 + nbf <= 8
            return ([psum(f"P{i}", [128, 512], F32, st) for i in range(nf32)],
                    [psum(f"PT{i}", [128, 1024], BF16, st) for i in range(nbf)])

        def rstd_from_ss(ss, n, R):
            b.act(ss[:, 0:1], ss[:, 0:1], AF.Ln, [ss] + R, [ss], scale=1.0 / n, bias=EPS)
            b.act(ss[:, 0:1], ss[:, 0:1], AF.Exp, [ss], [ss], scale=-0.5)

        def norm_to_hT(x_ap, xT, gain, junk, ss, hb, hT_ap, hT, PT):
            b.act(junk[:], x_ap, AF.Square, [xT], [junk, ss], accum=ss[:, 0:1])
            rstd_from_ss(ss, 1024.0, [])
            b.stt(hb[:], x_ap, ss[:, 0:1], gain[:], ALU.mult, ALU.mult, [xT, ss, gain], [hb])
            for k in range(8):
                b.tr(PT[:, k * 128:(k + 1) * 128], hb[:, k * 128:(k + 1) * 128], idb[:], [hb, idb], [PT])
            b.cp("act", hT_ap, PT[:, 0:1024].rearrange("p (k t) -> p k t", k=8), [PT], [hT])

        def post_norm_residual(Pa, Pb, gain, x_ap, xT, junk, ss, tn, xo, out_dram):
            ss2 = ss
            b.act(junk[:, 0:512], Pa[:], AF.Square, [Pa], [junk, ss2], accum=ss2[:, 0:1])
            b.act(junk[:, 512:1024], Pb[:], AF.Square, [Pb], [junk, ss2], accum=ss2[:, 1:2])
            b.tt("dve", ss2[:, 0:1], ss2[:, 0:1], ss2[:, 1:2], ALU.add, [ss2], [ss2])
            b.act(ss2[:, 0:1], ss2[:, 0:1], AF.Ln, [ss2], [ss2], scale=1.0 / 1024, bias=EPS)
            b.act(ss2[:, 0:1], ss2[:, 0:1], AF.Exp, [ss2], [ss2], scale=-0.5)
            b.stt(tn[:, 0:512], Pa[:], ss2[:, 0:1], gain[:, 0:512], ALU.mult, ALU.mult, [Pa, ss2, gain], [tn])
            b.stt(tn[:, 512:1024], Pb[:], ss2[:, 0:1], gain[:, 512:1024], ALU.mult, ALU.mult, [Pb, ss2, gain], [tn])
            b.tt("dve", xo[:], tn[:], x_ap, ALU.add, [tn, xT], [xo])
            b.st(out_dram, xo[:], [xo])

        def rotary(eng, out, src_ap, srcT, cs, H, tmp):
            s3 = src_ap.rearrange("p (h d) -> p h d", h=H)
            t1, t2 = s3[:, :, 0:32], s3[:, :, 32:64]
            cb = cs[:, 0:32].unsqueeze(1).to_broadcast([128, H, 32])
            sn = cs[:, 32:64].unsqueeze(1).to_broadcast([128, H, 32])
            ta = tmp[:, 0:H * 32].rearrange("p (h d) -> p h d", h=H)
            tb = tmp[:, H * 32:H * 64].rearrange("p (h d) -> p h d", h=H)
            b.tt(eng, ta, t1, cb, ALU.mult, [srcT, cs], [tmp])
            b.tt(eng, tb, t2, sn, ALU.mult, [srcT, cs], [tmp])
            b.tt(eng, out[:, :, 0:32], ta, tb, ALU.subtract, [tmp], [out])
            b.tt(eng, ta, t2, cb, ALU.mult, [srcT, cs, out], [tmp])
            b.tt(eng, tb, t1, sn, ALU.mult, [srcT, cs, out], [tmp])
            b.tt(eng, out[:, :, 32:64], ta, tb, ALU.add, [tmp], [out])

        def ffn_loop(li, src, dst):
            with contextlib.ExitStack() as st:
                W1b = [sb(f"W1b{i}", [128, 8, 512], BF16, st) for i in range(8)]
                W2b = [sb(f"W2b{i}", [128, 4, 1024], BF16, st) for i in range(8)]
                for cbk in range(8):
                    for kc in range(8):
                        b.ld(W1b[cbk][:, kc, :], w1_d[li][kc * 128:(kc + 1) * 128, cbk * 512:(cbk + 1) * 512], [W1b[cbk]],
                             eng="pool")
                for g8 in range(8):
                    for i4 in range(4):
                        r0 = (g8 * 4 + i4) * 128
                        b.ld(W2b[g8][:, i4, :], w2_d[li][r0:r0 + 128, :], [W2b[g8]], eng="pool")
                gpre = sb("gpre", [128, 1024], F32, st)
                gpost = sb("gpost", [128, 1024], F32, st)
                b.ld(gpre[:], g_ffn_pre[li].partition_broadcast(128), [gpre])
                b.ld(gpost[:], g_ffn_post[li].partition_broadcast(128), [gpost])
                P, PTs = alloc_psum(st, 6, 1)
                xm = [sb(f"xm{i}", [128, 2, 1024], F32, st) for i in range(2)]
                junk = sb("junk", [128, 1024], BF16, st)
                junk2 = sb("junk2", [128, 1024], BF16, st)
                ss = sb("ss", [128, 2], F32, st)
                ss2 = sb("ss2", [128, 2], F32, st)
                hb = sb("hb", [128, 1024], BF16, st)
                hT = [sb(f"hT{i}", [128, 8, 256], BF16, st) for i in range(2)]
                uT = sb("uT", [128, 32, 256], BF16, st)
                rl = [sb(f"rl{i}", [128, 256], F32, st) for i in range(2)]
                tn = sb("tn", [128, 1024], F32, st)
                xo = [sb(f"xo{i}", [128, 1024], F32, st) for i in range(2)]

                def ffn_n(mt):
                    xmt = xm[mt % 2]
                    b.ld(xmt[:], src[mt * 256:(mt + 1) * 256, :].rearrange("(j p) d -> p j d", p=128), [xmt])
                    for j in range(2):
                        norm_to_hT(xmt[:, j, :], xmt, gpre, junk, ss, hb, hT[mt % 2][:, :, j * 128:(j + 1) * 128],
                                   hT[mt % 2], PTs[0])

                def ffn_r(mt):
                    xmt = xm[mt % 2]
                    hTc = hT[mt % 2]
                    for fc in range(32):
                        pu = P[fc % 2]
                        for k in range(8):
                            b.mm(pu[:, 0:256], W1b[fc // 4][:, k, (fc % 4) * 128:(fc % 4 + 1) * 128], hTc[:, k, :], k == 0, k == 7,
                                 [W1b[fc // 4], hTc], [pu])
                        r = rl[fc % 2]
                        b.act(r[:], pu[:, 0:256], AF.Relu, [pu], [r])
                        b.tt("dve", uT[:, fc, :], r[:], r[:], ALU.mult, [r], [uT])
                    for j in range(2):
                        Pa, Pb = P[2 + 2 * j], P[3 + 2 * j]
                        for nt, Pq in ((0, Pa), (1, Pb)):
                            for fc in range(32):
                                b.mm(Pq[:], uT[:, fc, j * 128:(j + 1) * 128], W2b[fc // 4][:, fc % 4, nt * 512:(nt + 1) * 512],
                                     fc == 0, fc == 31, [uT, W2b[fc // 4]], [Pq])
                        c = mt * 2 + j
                        post_norm_residual(Pa, Pb, gpost, xmt[:, j, :], xmt, junk2, ss2, tn, xo[j], chunk_rows(dst, c))

                b.two_stage(ffn_n, ffn_r, range(NCH // 2))
            S.barrier()

        def odd_layer(src, dst):
            with contextlib.ExitStack() as st:
                P, PTs = alloc_psum(st, 6, 2)
                PT = PTs[0]
                Wi = sb("Wi1", [128, 8, 4096], BF16, st)
                b.ldw(Wi, odin_d[0], 8, 4096)
                gpre = sb("gpre", [128, 1024], F32, st)
                b.ld(gpre[:], g_mix_pre[1].partition_broadcast(128), [gpre])
                gnw = sb("gnw", [128, 512], F32, st)
                b.ld(gnw[:], gnw_d[0].partition_broadcast(128), [gnw])
                wsf = sb("wsf", [128, 8, 128], F32, st)
                b.ld(wsf[:], gws_d[0].rearrange("g t s -> t g s"), [wsf])
                wsb = sb("wsb", [128, 8, 128], BF16, st)
                b.cp("dve", wsb[:], wsf[:], [wsf], [wsb])
                wsT = sb("wsT", [128, 8, 128], BF16, st)
                for g in range(8):
                    b.tr(PT[:, g * 128:(g + 1) * 128], wsb[:, g, :], idb[:], [wsb, idb], [PT])
                b.cp("act", wsT[:], PT[:, 0:1024].rearrange("p (g t) -> p g t", g=8), [PT], [wsT])
                bsf = sb("bsf", [8, 128], F32, st)
                b.ld(bsf[:], gbs_d[0], [bsf])
                bsT = sb("bsT", [128, 8], F32, st)
                b.tr(P[0][:, 0:8], bsf[:], cst[0:8, 0, 0:8], [bsf, cst], [P[0]])
                b.cp("act", bsT[:], P[0][:, 0:8], [P[0]], [bsT])

                xt = [sb(f"xt{i}", [128, 1024], F32, st) for i in range(2)]
                cs = [sb(f"cs{i}", [128, 64], F32, st) for i in range(2)]
                junk = sb("junk", [128, 1024], BF16, st)
                junk2 = sb("junk2", [128, 512], BF16, st)
                ss = sb("ss", [128, 2], F32, st)
                hb = sb("hb", [128, 1024], BF16, st)
                hT = [sb(f"hT{i}", [128, 8, 128], BF16, st) for i in range(2)]
                qsq = sb("qsq", [128, 1024], F32, st)
                qsk = sb("qsk", [128, 1024], F32, st)
                rtq = sb("rtq", [128, 1024], F32, st)
                rtk = sb("rtk", [128, 1024], F32, st)
                qrq = sb("qrq", [128, 16, 64], BF16, st)
                qrk = sb("qrk", [128, 16, 64], BF16, st)
                qTs = [sb(f"qTs{i}", [128, 8, 128], BF16, st) for i in range(2)]
                kTs = [sb(f"kTs{i}", [128, 8, 128], BF16, st) for i in range(2)]
                v1s = [sb(f"v1s{i}", [128, 16, 65], BF16, st) for i in range(2)]
                for i in range(2):
                    b.ms("pool", v1s[i][:], 1.0, [v1s[i]])
                us = sb("us", [128, 512], F32, st)
                vgs = sb("vgs", [128, 512], F32, st)
                st4 = sb("st4", [128, 4], F32, st)
                vn = sb("vn", [128, 512], BF16, st)
                sgs = [sb(f"sgs{i}", [128, 512], BF16, st) for i in range(2)]

                def a1_n(c):
                    i2 = c % 2
                    b.ld(xt[i2][:], chunk_rows(src, c), [xt[i2]])
                    b.ld(cs[i2][:, 0:32], chunk_rows(cos_d, c), [cs[i2]])
                    b.ld(cs[i2][:, 32:64], chunk_rows(sin_d, c), [cs[i2]])
                    norm_to_hT(xt[i2][:], xt[i2], gpre, junk, ss, hb, hT[i2][:], hT[i2], PTs[1])

                def a1_r(c):
                    i2 = c % 2
                    hTc = hT[i2]
                    PT = PTs[0]

                    def proj(nt, Pq):
                        for k in range(8):
                            b.mm(Pq[:], hTc[:, k, :], Wi[:, k, nt * 512:(nt + 1) * 512], k == 0, k == 7, [hTc, Wi], [Pq])
                    for nt in range(6):
                        proj(nt, P[nt])
                    b.cp("act", qsq[:, 0:512], P[0][:], [P[0]], [qsq])
                    b.cp("act", qsq[:, 512:1024], P[1][:], [P[1]], [qsq])
                    b.cp("act", qsk[:, 0:512], P[2][:], [P[2]], [qsk])
                    b.cp("act", qsk[:, 512:1024], P[3][:], [P[3]], [qsk])
                    proj(6, P[0])
                    proj(7, P[1])
                    vt = v1s[i2]
                    b.cp("dve", vt[:, 0:8, 0:64], P[4][:].rearrange("p (h d) -> p h d", h=8), [P[4]], [vt])
                    b.cp("dve", vt[:, 8:16, 0:64], P[5][:].rearrange("p (h d) -> p h d", h=8), [P[5]], [vt])
                    b.st(v1_d[c], vt[:].rearrange("p h d -> p (h d)"), [vt])
                    rotary("dve", qrq, qsq[:], qsq, cs[i2], 16, rtq)
                    rotary("dve", qrk, qsk[:], qsk, cs[i2], 16, rtk)
                    b.cp("act", us[:], P[0][:], [P[0]], [us])
                    b.act(vgs[:], P[1][:], AF.Copy, [P[1]], [vgs, st4], accum=st4[:, 0:1])
                    b.act(junk2[:, 0:512], P[1][:], AF.Square, [P[1]], [junk2, st4], accum=st4[:, 1:2])
                    for qr_, dstT, dram in ((qrq, qTs[i2], qT1_d), (qrk, kTs[i2], kT1_d)):
                        qf = qr_[:].rearrange("p h d -> p (h d)")
                        for k in range(8):
                            b.tr(PT[:, k * 128:(k + 1) * 128], qf[:, k * 128:(k + 1) * 128], idb[:], [qr_, idb], [PT])
                        b.cp("act", dstT[:], PT[:, 0:1024].rearrange("p (k t) -> p k t", k=8), [PT], [dstT])
                        b.st(dram[c], dstT[:], [dstT])
                    b.ts("dve", st4[:, 0:2], st4[:, 0:2], 1.0 / 512, None, ALU.mult, None, [st4], [st4])
                    b.tt("dve", st4[:, 2:3], st4[:, 0:1], st4[:, 0:1], ALU.mult, [st4], [st4])
                    b.tt("dve", st4[:, 1:2], st4[:, 1:2], st4[:, 2:3], ALU.subtract, [st4], [st4])
                    b.act(st4[:, 1:2], st4[:, 1:2], AF.Ln, [st4], [st4], scale=1.0, bias=EPS)
                    b.act(st4[:, 1:2], st4[:, 1:2], AF.Exp, [st4], [st4], scale=-0.5)
                    b.ts("dve", vgs[:], vgs[:], st4[:, 0:1], st4[:, 1:2], ALU.subtract, ALU.mult, [vgs, st4], [vgs])
                    b.tt("dve", vn[:], vgs[:], gnw[:], ALU.mult, [vgs, gnw], [vn])
                    for g in range(8):
                        b.mm(P[2][:, g * 64:(g + 1) * 64], wsT[:, g, :], vn[:, g * 64:(g + 1) * 64], True, True,
                             [wsT, vn], [P[2]])
                    b.tt("dve", vgs[:].rearrange("p (g d) -> p g d", g=8), P[2][:].rearrange("p (g d) -> p g d", g=8),
                         bsT[:].unsqueeze(2).to_broadcast([128, 8, 64]), ALU.add, [P[2], bsT, vn], [vgs])
                    b.tt("dve", sgs[i2][:], vgs[:], us[:], ALU.mult, [vgs, us], [sgs[i2]])
                    b.st(chunk_rows(sg1_d, c), sgs[i2][:], [sgs[i2]])

                b.two_stage(a1_n, a1_r, crange("A1"))
            S.barrier()
            with contextlib.ExitStack() as st:
                NS = 18
                S.keep = AT_KEEP
                P, PTs = alloc_psum(st, 7, 1)
                PT = PTs[0]
                Wo = sb("Wo1", [128, 12, 1024], BF16, st)
                b.ldw(Wo, odout_d[0], 12, 1024)
                gpost = sb("gpost", [128, 1024], F32, st)
                b.ld(gpost[:], g_mix_post[1].partition_broadcast(128), [gpost])
                amf = sb("amf", [128, 17, 128], F32, st)
                b.ld(amf[:], amask_d[:, :, :], [amf])
                am = sb("am", [128, 17, 128], BF16, st)
                amL = sb("amL", [128, 17, 128], BF16, st)
                b.cp("dve", am[:], amf[:], [amf], [am])
                b.ts("dve", amL[:], amf[:], link[:, 0:1], None, ALU.mult, None, [amf, link], [amL])
                kr = [sb(f"kr{i}", [128, 8, 128], BF16, st) for i in range(NS)]
                vr = [sb(f"vr{i}", [128, 16 * 65], BF16, st) for i in range(NS)]
                qze = [sb(f"qze{i}", [128, 8, 128], BF16, st) for i in range(2)]
                qzo = [sb(f"qzo{i}", [128, 8, 128], BF16, st) for i in range(2)]
                for i in range(2):
                    b.ms("dve", qze[i][:], 0.0, [qze[i]])
                    b.ms("dve", qzo[i][:], 0.0, [qzo[i]])
                pe_sb = [sb(f"pe{i}", [128, 512], BF16, st) for i in range(AT_LAG + 1)]
                pm_sb = [sb(f"pm{i}", [128, 512], BF16, st) for i in range(AT_LAG + 1)]
                rden = sb("rden", [128, 4], F32, st)
                mix = [sb(f"mix{i}", [128, 1536], BF16, st) for i in range(2)]
                mixT = sb("mixT", [128, 12, 128], BF16, st)
                xt = [sb(f"xt{i}", [128, 1024], F32, st) for i in range(2)]
                junk = sb("junk", [128, 1024], BF16, st)
                ss = sb("ss", [128, 2], F32, st)
                tn = sb("tn", [128, 1024], F32, st)
                xo = [sb(f"xo{i}", [128, 1024], F32, st) for i in range(2)]
                loaded = set()
                gi = 0
                pend = []
                LAG = AT_LAG
                for c in crange("AT"):
                    i2 = c % 2
                    sgc = seg_of(c)
                    lo, hi = (0, 2 * SEGCH) if sgc < 2 else (2 * SEGCH, NCH)
                    blocks = [j for j in range(c - 8, c + 9) if lo <= j < hi]
                    for j in blocks:
                        if j not in loaded:
                            loaded.add(j)
                            b.ld(kr[j % NS][:], kT1_d[j], [kr[j % NS]])
                            b.ld(vr[j % NS][:], v1_d[j], [vr[j % NS]])
                    b.ld(qze[i2][0:64], qT1_d[c][0:64], [qze[i2]])
                    b.ld(qzo[i2][64:128], qT1_d[c][64:128], [qzo[i2]])
                    b.ld(mix[i2][:, 1024:1536], chunk_rows(sg1_d, c), [mix[i2]])
                    b.ld(xt[i2][:], chunk_rows(src, c), [xt[i2]])
                    groups = []
                    cur = []
                    for j in blocks:
                        cross = (seg_of(j) != sgc)
                        if cur and (len(cur) == 4 or cur[0][1] != cross):
                            groups.append(cur)
                            cur = []
                        cur.append((j, cross))
                    if cur:
                        groups.append(cur)
                    for h in range(16):
                        hp, base = h // 2, (h % 2) * 64
                        po = P[4 + (h // 4) % 2]
                        pcol = (h % 4) * 65
                        nb = len(blocks)
                        bi = 0
                        for gx, grp in enumerate(groups):
                            ps = P[gi % 3]
                            pe_t = pe_sb[gi % (AT_LAG + 1)]
                            pm_t = pm_sb[gi % (AT_LAG + 1)]
                            gi += 1
                            n = len(grp)
                            for i, (j, cross) in enumerate(grp):
                                qz = (qze if h % 2 == 0 else qzo)[i2]
                                b.mm(ps[:, i * 128:(i + 1) * 128], kr[j % NS][:, hp, :],
                                     qz[:, hp, :], True, True, [kr[j % NS], qz], [ps])
                            b.act(pe_t[:, 0:n * 128], ps[:, 0:n * 128], AF.Exp, [ps], [pe_t], scale=0.125)
                            o0 = grp[0][0] - c + 8
                            msk = amL if grp[0][1] else am
                            b.tt("dve", pm_t[:, 0:n * 128], pe_t[:, 0:n * 128],
                                 msk[:, o0:o0 + n, :].rearrange("p a b -> p (a b)"), ALU.mult, [pe_t, msk], [pm_t])

                            def pv(grp=grp, pm_t=pm_t, po=po, pcol=pcol, bi=bi, nb=nb, h=h, i2=i2, c=c,
                                   last=(gx == len(groups) - 1)):
                                for i, (j, cross) in enumerate(grp):
                                    b.mm(po[:, pcol:pcol + 65], pm_t[:, i * 128:(i + 1) * 128],
                                         vr[j % NS][:, h * 65:(h + 1) * 65], bi + i == 0, bi + i == nb - 1,
                                         [pm_t, vr[j % NS]], [po])
                                if last and h % 4 == 3:
                                    po3 = po[:, 0:260].rearrange("p (h d) -> p h d", h=4)
                                    b.S.op("dve", lambda e, o=rden[:].unsqueeze(2), i_=po3[:, :, 64:65]: e.reciprocal(out=o, in_=i_),
                                           [po.t], [rden.t])
                                    h0 = h - 3
                                    b.tt("dve", mix[i2][:, h0 * 64:(h0 + 4) * 64].rearrange("p (h d) -> p h d", h=4),
                                         po3[:, :, 0:64], rden[:].unsqueeze(2).to_broadcast([128, 4, 64]), ALU.mult,
                                         [po, rden], [mix[i2]])
                                if last and h == 15:
                                    for half in range(2):
                                        for k in range(6):
                                            kk = half * 6 + k
                                            b.tr(PT[:, k * 128:(k + 1) * 128], mix[i2][:, kk * 128:(kk + 1) * 128], idb[:],
                                                 [mix[i2], idb], [PT])
                                        b.cp("act", mixT[:, half * 6:(half + 1) * 6, :],
                                             PT[:, 0:768].rearrange("p (k t) -> p k t", k=6), [PT], [mixT])
                                    for nt, Pq in ((0, P[3]), (1, P[6])):
                                        for k in range(12):
                                            b.mm(Pq[:], mixT[:, k, :], Wo[:, k, nt * 512:(nt + 1) * 512], k == 0, k == 11,
                                                 [mixT, Wo], [Pq])
                                    post_norm_residual(P[3], P[6], gpost, xt[i2][:], xt[i2], junk, ss, tn, xo[i2],
                                                       chunk_rows(dst, c))
                            bi += n
                            pend.append(pv)
                            while len(pend) > LAG:
                                pend.pop(0)()
                while pend:
                    pend.pop(0)()
            S.keep = False
            S.barrier()

        def even_layer(src, dst):
            with contextlib.ExitStack() as st:
                P, PTs = alloc_psum(st, 6, 2)
                Wi = sb("Wi0", [128, 8, 5664], BF16, st)
                b.ldw(Wi, evin_d[0], 8, 5664)
                gpre = sb("gpre", [128, 1024], F32, st)
                b.ld(gpre[:], g_mix_pre[0].partition_broadcast(128), [gpre])
                xt = [sb(f"xt{i}", [128, 1024], F32, st) for i in range(2)]
                cs = [sb(f"cs{i}", [128, 64], F32, st) for i in range(2)]
                junk = sb("junk", [128, 1024], BF16, st)
                ss = sb("ss", [128, 2], F32, st)
                hb = sb("hb", [128, 1024], BF16, st)
                hT = [sb(f"hT{i}", [128, 8, 128], BF16, st) for i in range(2)]
                zo = [sb(f"zo{i}", [128, 1024], F32, st) for i in range(2)]
                go = [sb(f"go{i}", [128, 1024], F32, st) for i in range(2)]
                xpre = [sb(f"xpre{i}", [128, 12, 132], BF16, st) for i in range(3)]
                xsT = sb("xsT", [128, 8, 128], F32, st)
                bcT = [sb(f"bcT{i}", [128, 4, 128], BF16, st) for i in range(2)]
                xso = [sb(f"xso{i}", [128, 1024], F32, st) for i in range(2)]
                btk = [sb(f"btk{i}", [128, 256], BF16, st) for i in range(2)]
                cw = sb("cw", [8, 1536], F32, st)
                b.ms("dve", cw[:], 0.0, [cw])
                b.ld(cw[0:5, :], convw_d[0], [cw])
                cbr = sb("cbr", [12, 128], F32, st)
                b.ld(cbr[:], convb_d[0].rearrange("(a p) -> a p", p=128), [cbr])
                wT = sb("wT", [128, 12, 8], F32, st)
                cbT = sb("cbT", [128, 12], F32, st)
                for cb in range(12):
                    b.tr(P[0][:, cb * 8:(cb + 1) * 8], cw[:, cb * 128:(cb + 1) * 128], cst[0:8, 0, 0:8], [cw, cst], [P[0]])
                b.cp("act", wT[:], P[0][:, 0:96].rearrange("p (a j) -> p a j", a=12), [P[0]], [wT])
                b.tr(P[1][:, 0:12], cbr[:], cst[0:12, 0, 0:12], [cbr, cst], [P[1]])
                b.cp("act", cbT[:], P[1][:, 0:12], [P[1]], [cbT])
                Wd = sb("Wd", [128, 12, 5, 128], BF16, st)
                for cb in range(12):
                    for j in range(5):
                        b.ts("dve", Wd[:, cb, j, :], cst[:, 0, :], wT[:, cb, j:j + 1], None, ALU.mult, None, [cst, wT], [Wd])
                b.ms("dve", xpre[0][:, :, 0:2], 0.0, [xpre[0]])
                dto = [sb(f"dto{i}", [128, 32], F32, st) for i in range(2)]
                qs = sb("qs", [128, 512], F32, st)
                rtmp = sb("rtmp", [128, 512], F32, st)
                qr = sb("qr", [128, 8, 64], BF16, st)
                kro = [sb(f"kro{i}", [128, 8, 64], BF16, st) for i in range(2)]
                qTs = [sb(f"qTs{i}", [128, 4, 128], BF16, st) for i in range(2)]
                kTs = [sb(f"kTs{i}", [128, 4, 128], BF16, st) for i in range(2)]
                vo = [sb(f"vo{i}", [128, 1024], BF16, st) for i in range(2)]
                qs2 = sb("qs2", [128, 512], F32, st)
                rtmp2 = sb("rtmp2", [128, 512], F32, st)

                def a0_n(c):
                    i2 = c % 2
                    b.ld(xt[i2][:], chunk_rows(src, c), [xt[i2]])
                    b.ld(cs[i2][:, 0:32], chunk_rows(cos_d, c), [cs[i2]])
                    b.ld(cs[i2][:, 32:64], chunk_rows(sin_d, c), [cs[i2]])
                    norm_to_hT(xt[i2][:], xt[i2], gpre, junk, ss, hb, hT[i2][:], hT[i2], PTs[1])

                def a0_r(c):
                    i2 = c % 2
                    hTc = hT[i2]
                    PT = PTs[0]
                    pi = [0]

                    def proj(c0, ncol):
                        Pq = P[pi[0] % 6]
                        pi[0] += 1
                        for k in range(8):
                            b.mm(Pq[:, 0:ncol], hTc[:, k, :], Wi[:, k, c0:c0 + ncol], k == 0, k == 7, [hTc, Wi], [Pq])
                        return Pq
                    Pq = proj(2592, 512)
                    b.cp("act", qs[:], Pq[:], [Pq], [qs])
                    Pq = proj(3104, 512)
                    b.act(qs2[:], Pq[:], AF.Copy, [Pq], [qs2], scale=0.125)
                    rotary("dve", qr, qs[:], qs, cs[i2], 8, rtmp)
                    rotary("dve", kro[i2], qs2[:], qs2, cs[i2], 8, rtmp2)
                    for nt in range(2):
                        Pq = proj(nt * 512, 512)
                        b.act(zo[i2][:, nt * 512:(nt + 1) * 512], Pq[:], AF.Silu, [Pq], [zo[i2]])
                    b.st(chunk_rows(sz_d, c), zo[i2][:], [zo[i2]])
                    xp = xpre[c % 3]
                    for q4 in range(3):
                        Pq = P[pi[0] % 6]
                        pi[0] += 1
                        for cbi in range(4):
                            cb = q4 * 4 + cbi
                            for k in range(8):
                                b.mm(Pq[:, cbi * 128:(cbi + 1) * 128], Wi[:, k, 1024 + cb * 128:1024 + (cb + 1) * 128],
                                     hTc[:, k, :], k == 0, k == 7, [hTc, Wi], [Pq])
                        b.cp("dve", xp[:, q4 * 4:(q4 + 1) * 4, 2:130], Pq[:].rearrange("p (a t) -> p a t", a=4), [Pq], [xp])
                    if c > 0:
                        xl = xpre[(c - 1) % 3]
                        if c % SEGCH != 0:
                            b.cp("dve", xl[:, :, 130:132], xp[:, :, 2:4], [xp], [xl])
                        elif c == SEGCH:
                            b.ts("dve", xl[:, :, 130:132], xp[:, :, 2:4], link[:, 0:1], None, ALU.mult, None, [xp, link], [xl])
                        else:
                            b.ms("dve", xl[:, :, 130:132], 0.0, [xl])
                    if c + 1 < NCH:
                        xn = xpre[(c + 1) % 3]
                        if (c + 1) % SEGCH != 0:
                            b.cp("dve", xn[:, :, 0:2], xp[:, :, 128:130], [xp], [xn])
                        elif c + 1 == SEGCH:
                            b.ts("dve", xn[:, :, 0:2], xp[:, :, 128:130], link[:, 0:1], None, ALU.mult, None, [xp, link], [xn])
                        else:
                            b.ms("dve", xn[:, :, 0:2], 0.0, [xn])
                    else:
                        b.ms("dve", xp[:, :, 130:132], 0.0, [xp])
                    Pq = proj(2560, 32)
                    b.cp("dve", dto[i2][:], Pq[:, 0:32], [Pq], [dto[i2]])
                    b.st(chunk_rows(dtr_d, c), dto[i2][:], [dto[i2]])
                    for nt in range(2):
                        Pq = proj(3616 + nt * 512, 512)
                        b.cp("act" if nt == 0 else "dve", vo[i2][:, nt * 512:(nt + 1) * 512], Pq[:], [Pq], [vo[i2]])
                    b.st(chunk_rows(v0_d, c), vo[i2][:], [vo[i2]])
                    for nt in range(2):
                        Pq = proj(4640 + nt * 512, 512)
                        b.act(go[i2][:, nt * 512:(nt + 1) * 512], Pq[:], AF.Silu, [Pq], [go[i2]])
                    b.st(chunk_rows(sg_d, c), go[i2][:], [go[i2]])
                    qf = qr[:].rearrange("p h d -> p (h d)")
                    for k in range(4):
                        b.tr(PT[:, k * 128:(k + 1) * 128], qf[:, k * 128:(k + 1) * 128], idb[:], [qr, idb], [PT])
                    b.cp("act", qTs[i2][:], PT[:, 0:512].rearrange("p (k t) -> p k t", k=4), [PT], [qTs[i2]])
                    b.st(qT0_d[c], qTs[i2][:], [qTs[i2]])
                    kf = kro[i2][:].rearrange("p h d -> p (h d)")
                    b.st(chunk_rows(k0_d, c), kf, [kro[i2]])
                    for k in range(4):
                        b.tr(PT[:, 512 + k * 128:512 + (k + 1) * 128], kf[:, k * 128:(k + 1) * 128], idb[:],
                             [kro[i2], idb], [PT])
                    b.cp("act", kTs[i2][:], PT[:, 512:1024].rearrange("p (k t) -> p k t", k=4), [PT], [kTs[i2]])
                    b.st(kT0_d[c], kTs[i2][:], [kTs[i2]])
                    return pi

                def a0_conv(c, pi):
                    i2 = c % 2
                    xp = xpre[c % 3]
                    PT = PTs[0]
                    for q4 in range(3):
                        Pq = P[pi[0] % 6]
                        pi[0] += 1
                        for cbi in range(4):
                            cb = q4 * 4 + cbi
                            for j in range(5):
                                b.mm(Pq[:, cbi * 128:(cbi + 1) * 128], Wd[:, cb, j, :], xp[:, cb, j:j + 128], j == 0, j == 4,
                                     [Wd, xp], [Pq])
                        for cbi in range(4):
                            cb = q4 * 4 + cbi
                            if cb < 8:
                                b.act(xsT[:, cb, :], Pq[:, cbi * 128:(cbi + 1) * 128], AF.Silu, [Pq, cbT], [xsT],
                                      bias=cbT[:, cb:cb + 1])
                            else:
                                b.act(bcT[i2][:, cb - 8, :], Pq[:, cbi * 128:(cbi + 1) * 128], AF.Silu, [Pq, cbT], [bcT[i2]],
                                      bias=cbT[:, cb:cb + 1])
                    b.st(bct_d[c], bcT[i2][:], [bcT[i2]])
                    for half in range(2):
                        Pq = P[pi[0] % 6]
                        pi[0] += 1
                        for e4 in range(4):
                            cb = half * 4 + e4
                            b.tr(Pq[:, e4 * 128:(e4 + 1) * 128], xsT[:, cb, :], cst[:, 0, :], [xsT, cst], [Pq])
                        b.cp("act", xso[i2][:, half * 512:(half + 1) * 512], Pq[:], [Pq], [xso[i2]])
                    b.st(chunk_rows(xs_d, c), xso[i2][:], [xso[i2]])
                    for g in range(2):
                        b.tr(PT[:, 512 + g * 128:512 + (g + 1) * 128], bcT[i2][:, g, :], idb[:], [bcT[i2], idb], [PT])
                    b.cp("act", btk[i2][:], PT[:, 512:768], [PT], [btk[i2]])
                    b.st(chunk_rows(btok_d, c), btk[i2][:], [btk[i2]])

                def a0_r2(c):
                    pi = a0_r(c)
                    if c > 0:
                        a0_conv(c - 1, pi)
                    if c == len(crange("A")) - 1:
                        a0_conv(c, pi)

                b.two_stage(a0_n, a0_r2, crange("A"))
            S.barrier()
            with contextlib.ExitStack() as st:
                P, PTs = alloc_psum(st, 7, 1)
                PT = PTs[0]
                prm = sb("prm", [128, 96], F32, st)
                b.ld(prm[:, 0:32], dtb_d[0].rearrange("a b -> (a b)").partition_broadcast(128), [prm])
                b.ld(prm[:, 32:64], alog_d[0].rearrange("a b -> (a b)").partition_broadcast(128), [prm])
                b.ld(prm[:, 64:80], dsk_d[0].partition_broadcast(128), [prm])
                b.ld(prm[:, 80:96], rdec_d[0].rearrange("a b -> (a b)").partition_broadcast(128), [prm])
                b.act(prm[:, 32:64], prm[:, 32:64], AF.Exp, [prm], [prm])
                b.ts("dve", prm[:, 32:64], prm[:, 32:64], -1.0, None, ALU.mult, None, [prm], [prm])
                b.act(prm[:, 80:96], prm[:, 80:96], AF.Exp, [prm], [prm], scale=-1.0)
                b.act(prm[:, 80:96], prm[:, 80:96], AF.Ln, [prm], [prm], bias=1.0)
                b.ts("dve", prm[:, 80:96], prm[:, 80:96], -1.0, None, ALU.mult, None, [prm], [prm])
                dtbias, avec, dskip, lg = prm[:, 0:32], prm[:, 32:64], prm[:, 64:80], prm[:, 80:96]
                Dfb = sb("Dfb", [128, 8, 128], F32, st)
                Dtmp = sb("Dtmp", [128, 8, 128], F32, st)
                b.tt("dve", Dfb[:], RDp.unsqueeze(1).to_broadcast([128, 8, 128]),
                     prm[:, 80:88].unsqueeze(2).to_broadcast([128, 8, 128]), ALU.mult, [cst, prm], [Dfb])
                b.act(Dfb[:], Dfb[:], AF.Exp, [Dfb], [Dfb])
                b.tt("dve", Dfb[:], Dfb[:], Um.unsqueeze(1).to_broadcast([128, 8, 128]), ALU.mult, [Dfb, cst], [Dfb])
                b.tt("dve", Dtmp[:], RDn.unsqueeze(1).to_broadcast([128, 8, 128]),
                     prm[:, 88:96].unsqueeze(2).to_broadcast([128, 8, 128]), ALU.mult, [cst, prm], [Dtmp])
                b.act(Dtmp[:], Dtmp[:], AF.Exp, [Dtmp], [Dtmp])
                b.tt("dve", Dtmp[:], Dtmp[:], Lm.unsqueeze(1).to_broadcast([128, 8, 128]), ALU.mult, [Dtmp, cst], [Dtmp])
                b.tt("dve", Dfb[:], Dfb[:], Dtmp[:], ALU.add, [Dfb, Dtmp], [Dfb])
                rc = sb("rc", [128, 32], F32, st)
                b.ts("dve", rc[:, 0:8], prm[:, 80:88], POS[:, 0:1], None, ALU.mult, None, [prm, cst], [rc])
                b.ts("dve", rc[:, 8:16], prm[:, 88:96], POS[:, 1:2], None, ALU.mult, None, [prm, cst], [rc])
                b.ts("dve", rc[:, 16:32], prm[:, 80:96], 128.0, None, ALU.mult, None, [prm], [rc])
                b.act(rc[:], rc[:], AF.Exp, [rc], [rc])

                xbcs = sb("xbcs", [128, 1024], F32, st)
                dtr = sb("dtr", [128, 32], F32, st)
                dts = sb("dts", [128, 32], F32, st)
                dt2 = sb("dt2", [128, 32], F32, st)
                la = sb("la", [128, 32], F32, st)
                csb = sb("csb", [128, 64], F32, st)
                scs = [sb(f"scs{i}", [128, 32], F32, st) for i in range(2)]
                wte = sb("wte", [128, 32], F32, st)
                dec = [sb(f"dec{i}", [128, 32], F32, st) for i in range(2)]
                xdt = sb("xdt", [128, 2, 1024], BF16, st)
                xw = [sb(f"xw{i}", [128, 2, 1024], BF16, st) for i in range(2)]
                bcb = [sb(f"bcb{i}", [128, 256], BF16, st) for i in range(2)]
                BCT = [sb(f"BCT{i}", [128, 4, 128], BF16, st) for i in range(2)]
                cbm = sb("cbm", [128, 4, 128], F32, st)
                rhsq = [sb(f"rhsq{i}", [128, 4, 128], F32, st) for i in range(8)]
                DT = [sb(f"DT{i}", [128, 512], BF16, st) for i in range(2)]
                M = sb("M", [128, 2, 16, 128], BF16, st)
                ydt = sb("ydt", [128, 1024], F32, st)
                ydo = [sb(f"ydo{i}", [128, 1024], F32, st) for i in range(2)]
                Sfo = [sb(f"Sfo{i}", [128, 1024], F32, st) for i in range(2)]
                hbs = sb("hbs", [128, 1024], F32, st)
                hbo = [sb(f"hbo{i}", [128, 1024], BF16, st) for i in range(2)]
                qTl = sb("qTl", [128, 4, 128], BF16, st)
                kTl = sb("kTl", [128, 4, 128], BF16, st)
                kl = sb("kl", [128, 512], BF16, st)
                vl = sb("vl", [128, 1024], BF16, st)
                SM = sb("SM", [128, 8, 128], BF16, st)
                kd = sb("kd", [128, 2, 512], BF16, st)
                yio = [sb(f"yio{i}", [128, 1024], F32, st) for i in range(2)]
                Rfo = [sb(f"Rfo{i}", [128, 512], F32, st) for i in range(2)]
                rbs = sb("rbs", [128, 512], F32, st)
                rbo = [sb(f"rbo{i}", [128, 512], BF16, st) for i in range(2)]
                b.ms("dve", hbs[:], 0.0, [hbs])
                b.ms("dve", rbs[:], 0.0, [rbs])

                def b1_h1(c):
                    i2 = c % 2
                    first_of_seg = (c % SEGCH == 0)
                    last_of_seg = (c % SEGCH == SEGCH - 1)
                    b.ld(xbcs[:], chunk_rows(xs_d, c), [xbcs])
                    xs3 = xbcs[:, 0:1024].rearrange("p (e d) -> p e d", e=16)
                    b.ld(dtr[:], chunk_rows(dtr_d, c), [dtr])
                    b.tt("dve", dtr[:], dtr[:], dtbias, ALU.add, [dtr, prm], [dtr])
                    b.ts("dve", dt2[:], dtr[:], -1.0, None, ALU.mult, None, [dtr], [dt2])
                    b.tt("dve", dt2[:], dt2[:], dtr[:], ALU.max, [dtr, dt2], [dt2])
                    b.act(dt2[:], dt2[:], AF.Exp, [dt2], [dt2], scale=-1.0)
                    b.act(dt2[:], dt2[:], AF.Ln, [dt2], [dt2], bias=1.0)
                    b.ts("dve", dts[:], dtr[:], 0.0, None, ALU.max, None, [dtr], [dts])
                    b.tt("dve", dts[:], dts[:], dt2[:], ALU.add, [dts, dt2], [dts])
                    b.tt("dve", la[:], dts[:], avec, ALU.mult, [dts, prm], [la])
                    pc = P[6]
                    b.mm(pc[:, 0:16], Um, la[:, 0:16], True, True, [cst, la], [pc])
                    b.mm(pc[:, 16:32], Lm, la[:, 16:32], True, True, [cst, la], [pc])
                    b.mm(pc[:, 32:64], ONES, la[:, 0:32], True, True, [cst, la], [pc])
                    b.cp("act", csb[:], pc[:, 0:64], [pc], [csb])
                    sc = scs[i2]
                    b.act(sc[:], csb[:, 0:32], AF.Exp, [csb], [sc])
                    b.st(chunk_rows(sc_d, c), sc[:], [sc])
                    b.act(dec[i2][:], csb[:, 32:64], AF.Exp, [csb], [dec[i2]])
                    b.tt("dve", wte[:], csb[:, 32:64], csb[:, 0:32], ALU.subtract, [csb], [wte])
                    b.act(wte[:], wte[:], AF.Exp, [wte], [wte])
                    b.tt("dve", wte[:], wte[:], dts[:], ALU.mult, [wte, dts], [wte])
                    for d in range(2):
                        b.tt("dve", xdt[:, d, :].rearrange("p (e d) -> p e d", e=16), xs3,
                             dts[:, d * 16:(d + 1) * 16].unsqueeze(2).to_broadcast([128, 16, 64]), ALU.mult,
                             [xbcs, dts], [xdt])
                        b.tt("dve", xw[i2][:, d, :].rearrange("p (e d) -> p e d", e=16), xs3,
                             wte[:, d * 16:(d + 1) * 16].unsqueeze(2).to_broadcast([128, 16, 64]), ALU.mult,
                             [xbcs, wte], [xw[i2]])
                    bct = BCT[i2]
                    b.ld(bct[:], bct_d[c], [bct])
                    for g in range(2):
                        b.mm(pc[:, 128 + g * 128:128 + (g + 1) * 128], bct[:, g, :], bct[:, 2 + g, :], True, True,
                             [bct], [pc])
                    pcb = pc[:, 128:384].rearrange("p (g l) -> p g l", g=2)
                    b.tt("dve", cbm[:, 0:2, :], pcb, Um.unsqueeze(1).to_broadcast([128, 2, 128]), ALU.mult,
                         [pc, cst], [cbm])
                    b.tt("dve", cbm[:, 2:4, :], pcb, Lm.unsqueeze(1).to_broadcast([128, 2, 128]), ALU.mult,
                         [pc, cst], [cbm])
                    u = 0
                    for d in range(2):
                        tri = Um if d == 0 else Lm
                        Amat = Af if d == 0 else Ab
                        for q4 in range(4):
                            rq = rhsq[d * 4 + q4]
                            for e4 in range(4):
                                col = d * 16 + q4 * 4 + e4
                                b.act(rq[:, e4, :], tri, AF.Copy, [cst, la], [rq], scale=la[:, col:col + 1])
                        for q4 in range(4):
                            Pq = P[u % 2]
                            dtt = DT[u % 2]
                            u += 1
                            rq = rhsq[d * 4 + q4]
                            b.mm(Pq[:], Amat, rq[:].rearrange("p a b -> p (a b)"), True, True,
                                 [cst, rq], [Pq])
                            b.act(dtt[:], Pq[:], AF.Exp, [Pq], [dtt])
                            g = q4 // 2
                            b.tt("dve", M[:, d, q4 * 4:(q4 + 1) * 4, :], dtt[:].rearrange("p (a b) -> p a b", a=4),
                                 cbm[:, 2 * d + g, :].unsqueeze(1).to_broadcast([128, 4, 128]), ALU.mult,
                                 [dtt, cbm], [M])
                    for e in range(16):
                        Pq = P[2 + e // 8]
                        sl = slice((e % 8) * 64, (e % 8) * 64 + 64)
                        b.mm(Pq[:, sl], M[:, 0, e, :], xdt[:, 0, e * 64:(e + 1) * 64], True, False, [M, xdt], [Pq])
                        b.mm(Pq[:, sl], M[:, 1, e, :], xdt[:, 1, e * 64:(e + 1) * 64], False, True, [M, xdt], [Pq])
                    b.tt("dve", ydt[:].rearrange("p (e d) -> p e d", e=16), xs3,
                         dskip.unsqueeze(2).to_broadcast([128, 16, 64]), ALU.mult, [xbcs, prm], [ydt])
                    b.tt("dve", ydo[i2][:, 0:512], P[2][:], ydt[:, 0:512], ALU.add, [P[2], ydt], [ydo[i2]])
                    b.tt("dve", ydo[i2][:, 512:1024], P[3][:], ydt[:, 512:1024], ALU.add, [P[3], ydt], [ydo[i2]])
                    b.st(chunk_rows(yd_d, c), ydo[i2][:], [ydo[i2]])

                def b1_h2(c):
                    i2 = c % 2
                    first_of_seg = (c % SEGCH == 0)
                    last_of_seg = (c % SEGCH == SEGCH - 1)
                    bt = bcb[i2]
                    b.ld(bt[:], chunk_rows(btok_d, c), [bt])
                    for g in range(2):
                        b.mm(P[4 + g][:], bt[:, g * 128:(g + 1) * 128], xw[i2][:, 0, g * 512:(g + 1) * 512], True, True,
                             [bcb[i2], xw[i2]], [P[4 + g]])
                    for g in range(2):
                        b.cp("act", Sfo[i2][:, g * 512:(g + 1) * 512], P[4 + g][:], [P[4 + g]], [Sfo[i2]])
                    b.st(Sf_d[c], Sfo[i2][:], [Sfo[i2]])
                    b.st(decf_d[c], dec[i2][:, 0:16], [dec[i2]])
                    if last_of_seg:
                        if c == SEGCH - 1:
                            b.ts("dve", hbs[:], hbs[:], link[:, 0:1], None, ALU.mult, None, [hbs, link], [hbs])
                            b.ts("dve", rbs[:], rbs[:], link[:, 0:1], None, ALU.mult, None, [rbs, link], [rbs])
                        else:
                            b.ms("dve", hbs[:], 0.0, [hbs])
                            b.ms("dve", rbs[:], 0.0, [rbs])
                    b.cp("act", hbo[i2][:], hbs[:], [hbs], [hbo[i2]])
                    b.st(Hb_d[c], hbo[i2][:], [hbo[i2]])
                    for g in range(2):
                        b.mm(P[4 + g][:], bt[:, g * 128:(g + 1) * 128], xw[i2][:, 1, g * 512:(g + 1) * 512], True, True,
                             [bcb[i2], xw[i2]], [P[4 + g]])
                    b.tt("dve", hbs[:].rearrange("p (e d) -> p e d", e=16), hbs[:].rearrange("p (e d) -> p e d", e=16),
                         dec[i2][:, 16:32].unsqueeze(2).to_broadcast([128, 16, 64]), ALU.mult, [hbs, dec[i2]], [hbs])
                    for g in range(2):
                        b.tt("dve", hbs[:, g * 512:(g + 1) * 512], hbs[:, g * 512:(g + 1) * 512], P[4 + g][:], ALU.add,
                             [hbs, P[4 + g]], [hbs])
                    b.ld(qTl[:], qT0_d[c], [qTl])
                    b.ld(kTl[:], kT0_d[c], [kTl])
                    b.ld(kl[:], chunk_rows(k0_d, c), [kl])
                    b.ld(vl[:], chunk_rows(v0_d, c), [vl])
                    for par in range(2):
                        for hp in range(4):
                            base = par * 64
                            Pq = P[4 + par]
                            b.mm(Pq[:, hp * 128:(hp + 1) * 128], kTl[base:base + 64, hp, :], qTl[base:base + 64, hp, :],
                                 True, True, [kTl, qTl], [Pq])
                    for par in range(2):
                        b.tt("dve", SM[:, par:8:2, :], P[4 + par][:].rearrange("p (a b) -> p a b", a=4),
                             Dfb[:, par:8:2, :], ALU.mult, [P[4 + par], Dfb], [SM])
                    for h in range(8):
                        Pq = P[4 + h // 4]
                        b.mm(Pq[:, (h % 4) * 128:(h % 4 + 1) * 128], SM[:, h, :], vl[:, h * 128:(h + 1) * 128], True, True,
                             [SM, vl], [Pq])
                    for hh in range(2):
                        b.cp("act", yio[i2][:, hh * 512:(hh + 1) * 512], P[4 + hh][:], [P[4 + hh]], [yio[i2]])
                    b.st(chunk_rows(yi_d, c), yio[i2][:], [yio[i2]])
                    for d in range(2):
                        b.tt("dve", kd[:, d, :].rearrange("p (h d) -> p h d", h=8), kl[:].rearrange("p (h d) -> p h d", h=8),
                             rc[:, d * 8:(d + 1) * 8].unsqueeze(2).to_broadcast([128, 8, 64]), ALU.mult, [kl, rc], [kd])
                    def rstates(d, Pq):
                        for h in range(8):
                            hp, base = h // 2, (h % 2) * 64
                            b.mm(Pq[base:base + 64, hp * 128:(hp + 1) * 128], kd[:, d, h * 64:(h + 1) * 64],
                                 vl[:, h * 128:(h + 1) * 128], True, True, [kd, vl], [Pq])
                    rstates(0, P[4])
                    b.cp("act", Rfo[i2][:], P[4][:], [P[4]], [Rfo[i2]])
                    b.st(Rf_d[c], Rfo[i2][:], [Rfo[i2]])
                    b.cp("act", rbo[i2][:], rbs[:], [rbs], [rbo[i2]])
                    b.st(Rb_d[c], rbo[i2][:], [rbo[i2]])
                    rstates(1, P[5])
                    for par in range(2):
                        sl = slice(par * 64, par * 64 + 64)
                        gcol = rc[sl, 24 + par:32:2]
                        b.tt("dve", rbs[sl, :].rearrange("p (a v) -> p a v", a=4), rbs[sl, :].rearrange("p (a v) -> p a v", a=4),
                             gcol.unsqueeze(2).to_broadcast([64, 4, 128]), ALU.mult, [rbs, rc], [rbs])
                    b.tt("dve", rbs[:], rbs[:], P[5][:], ALU.add, [rbs, P[5]], [rbs])

                b.two_stage(b1_h1, b1_h2, (range(NCH - 1, -1, -1) if LOOPN.get("B1", NCH) == NCH else range(LOOPN["B1"] - 1, -1, -1)))
            S.barrier()
            with contextlib.ExitStack() as st:
                P, PTs = alloc_psum(st, 7, 1)
                PT = PTs[0]
                Wo = sb("Wo0", [128, 16, 1024], BF16, st)
                b.ldw(Wo, evout_d[0], 16, 1024)
                gpost = sb("gpost", [128, 1024], F32, st)
                b.ld(gpost[:], g_mix_post[0].partition_broadcast(128), [gpost])
                snw = sb("snw", [128, 1024], F32, st)
                b.ld(snw[:], snw_d[0].partition_broadcast(128), [snw])
                rgn = sb("rgn", [128, 1024], F32, st)
                b.ld(rgn[:], rgn_d[0].partition_broadcast(128), [rgn])
                prm = sb("prm2", [128, 16], F32, st)
                b.ld(prm[:, 0:16], rdec_d[0].rearrange("a b -> (a b)").partition_broadcast(128), [prm])
                b.act(prm[:], prm[:], AF.Exp, [prm], [prm], scale=-1.0)
                b.act(prm[:], prm[:], AF.Ln, [prm], [prm], bias=1.0)
                b.ts("dve", prm[:], prm[:], -1.0, None, ALU.mult, None, [prm], [prm])
                rc = sb("rc2", [128, 32], F32, st)
                b.ts("dve", rc[:, 0:8], prm[:, 0:8], POS[:, 2:3], None, ALU.mult, None, [prm, cst], [rc])
                b.ts("dve", rc[:, 8:16], prm[:, 8:16], POS[:, 3:4], None, ALU.mult, None, [prm, cst], [rc])
                b.ts("dve", rc[:, 16:32], prm[:, 0:16], 128.0, None, ALU.mult, None, [prm], [rc])
                b.act(rc[:], rc[:], AF.Exp, [rc], [rc])

                ydl = [sb(f"ydl{i}", [128, 1024], F32, st) for i in range(2)]
                yil = [sb(f"yil{i}", [128, 1024], F32, st) for i in range(2)]
                CTl = [sb(f"CTl{i}", [128, 2, 128], BF16, st) for i in range(2)]
                scl = [sb(f"scl{i}", [128, 32], F32, st) for i in range(2)]
                qTl = [sb(f"qTl{i}", [128, 4, 128], BF16, st) for i in range(2)]
                Hbl = [sb(f"Hbl{i}", [128, 1024], BF16, st) for i in range(2)]
                Rbl = [sb(f"Rbl{i}", [128, 512], BF16, st) for i in range(2)]
                Sfl = [sb(f"Sfl{i}", [128, 1024], F32, st) for i in range(2)]
                dfl = [sb(f"dfl{i}", [128, 16], F32, st) for i in range(2)]
                Rfl = [sb(f"Rfl{i}", [128, 512], F32, st) for i in range(2)]
                szl = [sb(f"szl{i}", [128, 1024], F32, st) for i in range(2)]
                sgl = [sb(f"sgl{i}", [128, 1024], F32, st) for i in range(2)]
                xt = [sb(f"xt{i}", [128, 1024], F32, st) for i in range(2)]
                hfs = sb("hfs", [128, 1024], F32, st)
                hfb = sb("hfb", [128, 1024], BF16, st)
                rfs = sb("rfs", [128, 512], F32, st)
                rfb = sb("rfb", [128, 512], BF16, st)
                t1 = sb("t1", [128, 1024], F32, st)
                ys_ = [sb(f"ys{i}", [128, 1024], F32, st) for i in range(2)]
                yr_ = [sb(f"yr{i}", [128, 1024], F32, st) for i in range(2)]
                st8 = sb("st8", [128, 24], F32, st)
                mix = sb("mix", [128, 2048], BF16, st)
                mixT = sb("mixT", [128, 16, 128], BF16, st)
                junk = sb("junk", [128, 1024], BF16, st)
                jf = sb("jf", [128, 1024], F32, st)
                ss = sb("ss", [128, 2], F32, st)
                tn = sb("tn", [128, 1024], F32, st)
                xo = [sb(f"xo{i}", [128, 1024], F32, st) for i in range(2)]
                b.ms("dve", hfs[:], 0.0, [hfs])
                b.ms("dve", rfs[:], 0.0, [rfs])
                def b2_g1(c):
                    i2 = c % 2
                    ys = ys_[i2]
                    yr = yr_[i2]
                    b.ld(ydl[i2][:], chunk_rows(yd_d, c), [ydl[i2]])
                    b.ld(yil[i2][:], chunk_rows(yi_d, c), [yil[i2]])
                    b.ld(CTl[i2][:], bct_d[c][:, 2:4, :], [CTl[i2]])
                    b.ld(scl[i2][:], chunk_rows(sc_d, c), [scl[i2]])
                    b.ld(qTl[i2][:], qT0_d[c], [qTl[i2]])
                    b.ld(Hbl[i2][:], Hb_d[c], [Hbl[i2]])
                    b.ld(Rbl[i2][:], Rb_d[c], [Rbl[i2]])
                    b.ld(Sfl[i2][:], Sf_d[c], [Sfl[i2]])
                    b.ld(dfl[i2][:], decf_d[c], [dfl[i2]])
                    b.ld(Rfl[i2][:], Rf_d[c], [Rfl[i2]])
                    b.ld(szl[i2][:], chunk_rows(sz_d, c), [szl[i2]])
                    b.ld(sgl[i2][:], chunk_rows(sg_d, c), [sgl[i2]])
                    b.ld(xt[i2][:], chunk_rows(src, c), [xt[i2]])
                    if c % SEGCH == 0 and c > 0:
                        if c == SEGCH:
                            b.ts("dve", hfs[:], hfs[:], link[:, 0:1], None, ALU.mult, None, [hfs, link], [hfs])
                            b.ts("dve", rfs[:], rfs[:], link[:, 0:1], None, ALU.mult, None, [rfs, link], [rfs])
                        else:
                            b.ms("dve", hfs[:], 0.0, [hfs])
                            b.ms("dve", rfs[:], 0.0, [rfs])
                    b.cp("act", hfb[:], hfs[:], [hfs], [hfb])
                    b.cp("act", rfb[:], rfs[:], [rfs], [rfb])
                    for g in range(2):
                        b.mm(P[g][:], CTl[i2][:, g, :], hfb[:, g * 512:(g + 1) * 512], True, True, [CTl[i2], hfb], [P[g]])
                        b.mm(P[2 + g][:], CTl[i2][:, g, :], Hbl[i2][:, g * 512:(g + 1) * 512], True, True,
                             [CTl[i2], Hbl[i2]], [P[2 + g]])
                    for g in range(2):
                        for d in range(2):
                            Pq = P[2 * d + g]
                            b.tt("dve", t1[:, g * 512:(g + 1) * 512].rearrange("p (e d) -> p e d", e=8),
                                 Pq[:].rearrange("p (e d) -> p e d", e=8),
                                 scl[i2][:, d * 16 + g * 8:d * 16 + (g + 1) * 8].unsqueeze(2).to_broadcast([128, 8, 64]),
                                 ALU.mult, [Pq, scl[i2]], [t1])
                            src_y = ydl[i2] if d == 0 else ys
                            b.tt("dve", ys[:, g * 512:(g + 1) * 512], t1[:, g * 512:(g + 1) * 512],
                                 src_y[:, g * 512:(g + 1) * 512], ALU.add, [t1, src_y], [ys])
                    b.tt("dve", hfs[:].rearrange("p (e d) -> p e d", e=16), hfs[:].rearrange("p (e d) -> p e d", e=16),
                         dfl[i2][:].unsqueeze(2).to_broadcast([128, 16, 64]), ALU.mult, [hfs, dfl[i2], hfb], [hfs])
                    b.tt("dve", hfs[:], hfs[:], Sfl[i2][:], ALU.add, [hfs, Sfl[i2]], [hfs])
                    for d, (Pa, Pb2), rsrc in ((0, (P[4], P[0]), rfb), (1, (P[1], P[2]), Rbl[i2])):
                        t1v = t1[:].rearrange("p (h v) -> p h v", h=8)
                        for par, Pq in ((0, Pa), (1, Pb2)):
                            base = par * 64
                            for hp in range(4):
                                b.mm(Pq[:, hp * 128:(hp + 1) * 128], qTl[i2][base:base + 64, hp, :],
                                     rsrc[base:base + 64, hp * 128:(hp + 1) * 128], True, True, [qTl[i2], rsrc], [Pq])
                            b.tt("dve", t1v[:, par:8:2, :], Pq[:].rearrange("p (a v) -> p a v", a=4),
                                 rc[:, d * 8 + par:d * 8 + 8:2].unsqueeze(2).to_broadcast([128, 4, 128]),
                                 ALU.mult, [Pq, rc], [t1])
                        src_y = yil[i2] if d == 0 else yr
                        b.tt("dve", yr[:], t1[:], src_y[:], ALU.add, [t1, src_y], [yr])
                    for par in range(2):
                        sl = slice(par * 64, par * 64 + 64)
                        gcol = rc[sl, 16 + par:24:2]
                        b.tt("dve", rfs[sl, :].rearrange("p (a v) -> p a v", a=4), rfs[sl, :].rearrange("p (a v) -> p a v", a=4),
                             gcol.unsqueeze(2).to_broadcast([64, 4, 128]), ALU.mult, [rfs, rc, rfb], [rfs])
                    b.tt("dve", rfs[:], rfs[:], Rfl[i2][:], ALU.add, [rfs, Rfl[i2]], [rfs])

                def b2_g2(c):
                    i2 = c % 2
                    ys = ys_[i2]
                    yr = yr_[i2]
                    b.tt("dve", ys[:], ys[:], szl[i2][:], ALU.mult, [ys, szl[i2]], [ys])
                    for g in range(2):
                        b.act(junk[:, g * 512:(g + 1) * 512], ys[:, g * 512:(g + 1) * 512], AF.Square, [ys], [junk, st8],
                              accum=st8[:, g:g + 1])
                    b.act(st8[:, 0:2], st8[:, 0:2], AF.Ln, [st8], [st8], scale=1.0 / 512, bias=EPS)
                    b.act(st8[:, 0:2], st8[:, 0:2], AF.Exp, [st8], [st8], scale=-0.5)
                    for g in range(2):
                        b.stt(mix[:, g * 512:(g + 1) * 512], ys[:, g * 512:(g + 1) * 512], st8[:, g:g + 1],
                              snw[:, g * 512:(g + 1) * 512], ALU.mult, ALU.mult, [ys, st8, snw], [mix])
                    yr3 = yr[:].rearrange("p (h v) -> p h v", h=8)
                    b.red(st8[:, 8:16], yr3, [yr], [st8])
                    b.act(jf[:], yr[:], AF.Square, [yr], [jf])
                    b.red(st8[:, 16:24], jf[:].rearrange("p (h v) -> p h v", h=8), [jf], [st8])
                    b.ts("dve", st8[:, 8:24], st8[:, 8:24], 1.0 / 128, None, ALU.mult, None, [st8], [st8])
                    b.tt("dve", jf[:, 0:8], st8[:, 8:16], st8[:, 8:16], ALU.mult, [st8], [jf])
                    b.tt("dve", st8[:, 16:24], st8[:, 16:24], jf[:, 0:8], ALU.subtract, [st8, jf], [st8])
                    b.act(st8[:, 16:24], st8[:, 16:24], AF.Ln, [st8], [st8], bias=EPS)
                    b.act(st8[:, 16:24], st8[:, 16:24], AF.Exp, [st8], [st8], scale=-0.5)
                    b.tt("dve", yr3, yr3, st8[:, 8:16].unsqueeze(2).to_broadcast([128, 8, 128]), ALU.subtract,
                         [yr, st8], [yr])
                    b.tt("dve", yr3, yr3, st8[:, 16:24].unsqueeze(2).to_broadcast([128, 8, 128]), ALU.mult,
                         [yr, st8], [yr])
                    b.tt("dve", yr[:], yr[:], rgn[:], ALU.mult, [yr, rgn], [yr])
                    b.tt("dve", mix[:, 1024:2048], yr[:], sgl[i2][:], ALU.mult, [yr, sgl[i2]], [mix])
                    for half in range(2):
                        for k in range(8):
                            kk = half * 8 + k
                            b.tr(PT[:, k * 128:(k + 1) * 128], mix[:, kk * 128:(kk + 1) * 128], idb[:], [mix, idb], [PT])
                        b.cp("act", mixT[:, half * 8:(half + 1) * 8, :],
                             PT[:, 0:1024].rearrange("p (k t) -> p k t", k=8), [PT], [mixT])
                    for nt, Pq in ((0, P[5]), (1, P[6])):
                        for k in range(16):
                            b.mm(Pq[:], mixT[:, k, :], Wo[:, k, nt * 512:(nt + 1) * 512], k == 0, k == 15, [mixT, Wo], [Pq])
                    post_norm_residual(P[5], P[6], gpost, xt[i2][:], xt[i2], junk, ss, tn, xo[i2], chunk_rows(dst, c))

                b.two_stage(b2_g1, b2_g2, crange("B2"))
            S.barrier()

        hm = sb("hm", [128, 8], F32)
        b.ld(hm[:], cst_d[:, 8, 8:16], [hm])
        b.ts("dve", hm[:, 0:2], hm[:, 4:6], -1.0, 1.0, ALU.mult, ALU.add, [hm], [hm])
        b.ts("dve", hm[:, 0:2], hm[:, 0:2], link[:, 0:1], None, ALU.mult, None, [hm, link], [hm])
        b.tt("dve", hm[:, 6:8], hm[:, 0:2], hm[:, 4:6], ALU.add, [hm], [hm])

        stages = []
        if 0 in layers:
            if do_mix:
                stages.append(("mix0",))
            if do_ffn:
                stages.append(("ffn0",))
        if 1 in layers:
            if do_mix:
                stages.append(("mix1",))
            if do_ffn:
                stages.append(("ffn1",))
        bufs = [xa, xb]
        cur = xin
        S.barrier()
        for si, (sname,) in enumerate(stages):
            dst = yout if si == len(stages) - 1 else bufs[si % 2]
            if sname == "mix0":
                even_layer(cur, dst)
            elif sname == "mix1":
                odd_layer(cur, dst)
            elif sname == "ffn0":
                ffn_loop(0, cur, dst)
            elif sname == "ffn1":
                ffn_loop(1, cur, dst)
            cur = dst
        counts = S.emit()
    return nc, counts


def _consts():
    p = np.arange(128)
    r, l = p[:, None], p[None, :]
    cst = np.zeros((128, 9, 128), np.float32)
    cst[:, 0] = (r == l)
    cst[:, 1] = (r <= l)
    cst[:, 2] = (r >= l)
    cst[:, 3] = (r > l)
    cst[:, 4] = (r < l)
    cst[:, 5] = 1.0
    cst[:, 6] = np.maximum(l - r, 0)
    cst[:, 7] = np.maximum(r - l, 0)
    cst[:, 8, 0] = 127 - p
    cst[:, 8, 1] = p
    cst[:, 8, 2] = p + 1
    cst[:, 8, 3] = 128 - p
    cst[:, 8, 12] = (p != 127)
    cst[:, 8, 13] = (p < 126)
    am = np.zeros((128, 17, 128), np.float32)
    for o in range(17):
        j = o - 8
        d = np.abs(l - r - 128 * j)
        am[:, o, :] = (d <= 64).astype(np.float32) + ((d % 4 == 0) & (d <= 256)) + ((d % 16 == 0) & (d <= 1024))
    return cst, am


def _rope(pos):
    inv_freq = (10000.0 ** (-np.arange(0, 64, 2, dtype=np.float32) / np.float32(64))).astype(np.float32)
    ang = pos.astype(np.float32)[:, None] * inv_freq[None, :]
    return np.cos(ang).astype(np.float32), np.sin(ang).astype(np.float32)


WEIGHT_NAMES = ["norm_mix_pre", "norm_mix_post", "norm_ffn_pre", "norm_ffn_post", "ffn_w1", "ffn_w2",
                "ev_in_proj", "ev_conv_w", "ev_conv_b", "ssd_dt_bias", "ssd_a_log", "ssd_d", "ssd_norm_w",
                "ret_decay", "ret_gn_w", "ev_out_proj", "od_in_proj", "gmlp_norm_w", "gmlp_ws", "gmlp_bs",
                "od_out_proj"]


def make_in_maps(inputs):
    xp = np.asarray(inputs["x_prompt"], np.float32)
    xs = np.asarray(inputs["x_sample"], np.float32)
    cst, am = _consts()
    pos_a = np.concatenate([np.arange(4096), np.arange(2048)])
    pos_b = np.concatenate([np.arange(2048)] * 3)
    ca, sa = _rope(pos_a)
    cb, sb_ = _rope(pos_b)
    w = {k: np.ascontiguousarray(np.asarray(inputs[k], np.float32)) for k in WEIGHT_NAMES}
    maps = []
    for c in range(NCORES):
        if c < 4:
            x = np.concatenate([xp[c], xs[c]], axis=0)
            lk = np.ones((128, 1), np.float32)
            rc, rs = ca, sa
        else:
            i0 = 4 + 3 * (c - 4)
            x = np.concatenate([xs[i0], xs[i0 + 1], xs[i0 + 2]], axis=0)
            lk = np.zeros((128, 1), np.float32)
            rc, rs = cb, sb_
        m = {"xin": np.ascontiguousarray(x), "link": lk, "rcos": rc, "rsin": rs, "cst": cst, "amask": am}
        m.update(w)
        maps.append(m)
    return maps


def gather(results):
    yp = np.zeros((4, 4096, D), np.float32)
    ys = np.zeros((16, 2048, D), np.float32)
    for c in range(NCORES):
        y = np.asarray(results[c]["yout"], np.float32)
        if c < 4:
            yp[c] = y[0:4096]
            ys[c] = y[4096:6144]
        else:
            i0 = 4 + 3 * (c - 4)
            for k in range(3):
                ys[i0 + k] = y[k * 2048:(k + 1) * 2048]
    return yp, ys


_NC_CACHE = {}


def kernel(**inputs):
    if "nc" not in _NC_CACHE:
        _NC_CACHE["nc"] = build_nc()[0]
    nc = _NC_CACHE["nc"]
    maps = make_in_maps(inputs)
    res = run_bass_kernel_spmd(nc, maps, core_ids=list(range(NCORES)))
    return gather(res.results)
```
